# Optimizing a Trainium2 kernel written in Bass

```python
import jax
import jax.numpy as jnp
from jax import lax
import numpy as np

D_MODEL = 1024
BATCH = 8
SEQ = 2048
DEPTH = 1

GRID_W = 64
CTX_LEN = 256
POOL_GROUPS = 4
POOL_WINDOWS = (2, 4, 8, 16)
POOL_WIDTH = D_MODEL // 2
POOL_GROUP_DIM = POOL_WIDTH // POOL_GROUPS
RET_HEADS = 8
RET_QK_DIM = D_MODEL // 16
RET_V_DIM = D_MODEL // 8
RET_QK_WIDTH = RET_HEADS * RET_QK_DIM
RET_V_WIDTH = RET_HEADS * RET_V_DIM
RET_CHUNK = 128
K_SCALE = RET_QK_DIM ** -0.5
ROPE_BASE = 10000.0
D_FF = ((8 * D_MODEL + 3 * 256 - 1) // (3 * 256)) * 256
EPS = 1e-6

P_OFF = 0
Q_OFF = P_OFF + POOL_WIDTH
K_OFF = Q_OFF + RET_QK_WIDTH
V_OFF = K_OFF + RET_QK_WIDTH
G_OFF = V_OFF + RET_V_WIDTH
GA_OFF = G_OFF + RET_V_WIDTH
GB_OFF = GA_OFF + D_MODEL
IN_WIDTH = GB_OFF + D_MODEL

kernel_name = 'hybrid_pool_retention_dit_block'


def rms_norm(x, gain):
    xf = x.astype(jnp.float32)
    y = xf * lax.rsqrt(jnp.mean(xf * xf, axis=-1, keepdims=True) + EPS)
    return (y * gain.astype(jnp.float32)).astype(x.dtype)


def modulate(h, shift, scale):
    return h * (1.0 + scale) + shift


def swiglu(h, w1, w3, w2):
    return (jax.nn.silu(h @ w1) * (h @ w3)) @ w2


def centred_box_mean(u, window, axis):
    n = u.shape[axis]
    cs = jnp.cumsum(u.astype(jnp.float32), axis=axis)
    pad = [(0, 0)] * u.ndim
    pad[axis] = (1, 0)
    cs = jnp.pad(cs, pad)
    pos = jnp.arange(n)
    lo = jnp.clip(pos - window // 2, 0, n)
    hi = jnp.clip(pos + (window - window // 2), 0, n)
    total = jnp.take(cs, hi, axis=axis) - jnp.take(cs, lo, axis=axis)
    shape = [1] * u.ndim
    shape[axis] = n
    count = (hi - lo).astype(jnp.float32).reshape(shape)
    return (total / count).astype(u.dtype)


def pool_mixer(u, rows, w_pool, pool_scale):
    B, L, _ = u.shape
    groups = jnp.split(u, POOL_GROUPS, axis=-1)
    diffs = []
    for g, w in enumerate(POOL_WINDOWS):
        ug = groups[g]
        if rows is None:
            m = centred_box_mean(ug, w, 1)
        else:
            grid = ug.reshape(B, rows, GRID_W, POOL_GROUP_DIM)
            m = centred_box_mean(centred_box_mean(grid, w, 2), w, 1).reshape(B, L, POOL_GROUP_DIM)
        diffs.append(m - ug)
    d = jnp.stack(diffs, axis=2)
    y = jnp.einsum('blgc,gcd->blgd', d, w_pool)
    return y.reshape(B, L, POOL_WIDTH) * pool_scale


def rope_2d(length, dtype):
    t = jnp.arange(length)
    row = (t // GRID_W).astype(jnp.float32)
    col = (t % GRID_W).astype(jnp.float32)
    n_freq = RET_QK_DIM // 4
    inv_freq = ROPE_BASE ** (-jnp.arange(n_freq, dtype=jnp.float32) / n_freq)
    ang = jnp.concatenate([row[:, None] * inv_freq, col[:, None] * inv_freq], axis=-1)
    return jnp.cos(ang).astype(dtype), jnp.sin(ang).astype(dtype)


def apply_rope(t, cos, sin):
    half = t.shape[-1] // 2
    t1, t2 = t[..., :half], t[..., half:]
    cos = cos[None, :, None, :]
    sin = sin[None, :, None, :]
    return jnp.concatenate([t1 * cos - t2 * sin, t1 * sin + t2 * cos], axis=-1)


def chunk_retention(q, k, v, log_gamma, s0, include_diag):
    B, L, H, _ = q.shape
    Dv = v.shape[-1]
    C = RET_CHUNK
    n = L // C
    lg = log_gamma.astype(jnp.float32)
    idx = jnp.arange(C, dtype=jnp.float32)
    diff = idx[:, None] - idx[None, :]
    keep = (diff >= 0) if include_diag else (diff > 0)
    intra = jnp.where(keep[None], jnp.exp(lg[:, None, None] * jnp.maximum(diff, 0.0)[None]), 0.0).astype(q.dtype)
    q_dec = jnp.exp(lg[:, None] * (idx + 1.0)).astype(q.dtype)
    k_dec = jnp.exp(lg[:, None] * (C - 1.0 - idx)).astype(q.dtype)
    c_dec = jnp.exp(lg * C).astype(q.dtype)

    def to_chunks(t):
        return t.reshape(B, n, C, H, t.shape[-1]).transpose(1, 0, 3, 2, 4)

    def step(state, chunk):
        qi, ki, vi = chunk
        scores = jnp.einsum('bhqd,bhkd->bhqk', qi, ki) * intra[None]
        out = (jnp.einsum('bhqk,bhkv->bhqv', scores, vi)
               + jnp.einsum('bhqd,bhdv->bhqv', qi * q_dec[None, :, :, None], state))
        state = (state * c_dec[None, :, None, None]
                 + jnp.einsum('bhkd,bhkv->bhdv', ki * k_dec[None, :, :, None], vi))
        return state, out

    _, out = lax.scan(step, s0, (to_chunks(q), to_chunks(k), to_chunks(v)))
    return out.transpose(1, 0, 3, 2, 4).reshape(B, L, H, Dv)


def bidir_retention(q, k, v, lg_f, lg_b, s_f, s_b):
    y_f = chunk_retention(q, k, v, lg_f, s_f, True)
    y_b = chunk_retention(jnp.flip(q, 1), jnp.flip(k, 1), jnp.flip(v, 1), lg_b, s_b, False)
    return y_f + jnp.flip(y_b, 1)


def context_final_states(k, v, lg_f, lg_b):
    Lc = k.shape[1]
    pos = jnp.arange(Lc, dtype=jnp.float32)
    w_f = jnp.exp(lg_f.astype(jnp.float32)[:, None] * (Lc - 1.0 - pos)).astype(k.dtype)
    w_b = jnp.exp(lg_b.astype(jnp.float32)[:, None] * pos).astype(k.dtype)
    s_f = jnp.einsum('hl,blhd,blhv->bhdv', w_f, k, v)
    s_b = jnp.einsum('hl,blhd,blhv->bhdv', w_b, k, v)
    return s_f, s_b


def head_group_norm(y, gain):
    B, L, H, Dv = y.shape
    yf = y.astype(jnp.float32)
    mu = jnp.mean(yf, axis=-1, keepdims=True)
    var = jnp.mean(jnp.square(yf - mu), axis=-1, keepdims=True)
    yn = ((yf - mu) * lax.rsqrt(var + EPS)).reshape(B, L, H * Dv)
    return (yn * gain.astype(jnp.float32)).astype(y.dtype)


def token_mixers(z, rows, rope, s_f, s_b, lg_f, lg_b, w_pool, pool_scale, gn_w, w_pa, w_rb, w_o):
    B, L, _ = z.shape
    u = z[..., P_OFF:Q_OFF]
    q = z[..., Q_OFF:K_OFF].reshape(B, L, RET_HEADS, RET_QK_DIM)
    k = z[..., K_OFF:V_OFF].reshape(B, L, RET_HEADS, RET_QK_DIM) * K_SCALE
    v = z[..., V_OFF:G_OFF].reshape(B, L, RET_HEADS, RET_V_DIM)
    g = z[..., G_OFF:GA_OFF]
    gate_a = jax.nn.sigmoid(z[..., GA_OFF:GB_OFF])
    gate_b = jax.nn.sigmoid(z[..., GB_OFF:IN_WIDTH])
    branch_a = pool_mixer(u, rows, w_pool, pool_scale) @ w_pa
    if rope is not None:
        q = apply_rope(q, rope[0], rope[1])
        k = apply_rope(k, rope[0], rope[1])
    y = bidir_retention(q, k, v, lg_f, lg_b, s_f, s_b)
    y = head_group_norm(y, gn_w) * jax.nn.silu(g)
    branch_b = y @ w_rb
    return (gate_a * branch_a + gate_b * branch_b) @ w_o


def setup_inputs(seed: int = 0) -> dict:
    key = jax.random.key(seed)
    ks = jax.random.split(key, 21)

    def nrm(k, shape, scale):
        return jax.random.normal(k, shape, jnp.float32) * scale

    decay_logit = jnp.log(2.0 ** (5.0 + jnp.arange(RET_HEADS, dtype=jnp.float32)) - 1.0)
    return {
        'x': nrm(ks[0], (BATCH, SEQ, D_MODEL), 1.0),
        'c': nrm(ks[1], (BATCH, D_MODEL), 1.0),
        'ctx': nrm(ks[2], (BATCH, CTX_LEN, D_MODEL), 1.0),
        'c_ctx': nrm(ks[3], (D_MODEL,), 1.0),
        'w_ada': nrm(ks[4], (DEPTH, D_MODEL, 6 * D_MODEL), 0.5 * D_MODEL ** -0.5),
        'b_ada': nrm(ks[5], (DEPTH, 6 * D_MODEL), 0.01),
        'norm_mix': 1.0 + nrm(ks[6], (DEPTH, D_MODEL), 0.05),
        'norm_ffn': 1.0 + nrm(ks[7], (DEPTH, D_MODEL), 0.05),
        'w_in': nrm(ks[8], (DEPTH, D_MODEL, IN_WIDTH), D_MODEL ** -0.5),
        'w_pool': nrm(ks[9], (DEPTH, POOL_GROUPS, POOL_GROUP_DIM, POOL_GROUP_DIM), POOL_GROUP_DIM ** -0.5),
        'pool_scale': 1.0 + nrm(ks[10], (DEPTH, POOL_WIDTH), 0.1),
        'ret_decay_f': decay_logit + nrm(ks[11], (DEPTH, RET_HEADS), 0.1),
        'ret_decay_b': decay_logit + nrm(ks[12], (DEPTH, RET_HEADS), 0.1),
        'ret_gn_w': 1.0 + nrm(ks[13], (DEPTH, RET_V_WIDTH), 0.05),
        'w_pa': nrm(ks[14], (DEPTH, POOL_WIDTH, D_MODEL), POOL_WIDTH ** -0.5),
        'w_rb': nrm(ks[15], (DEPTH, RET_V_WIDTH, D_MODEL), RET_V_WIDTH ** -0.5),
        'w_o': nrm(ks[16], (DEPTH, D_MODEL, D_MODEL), D_MODEL ** -0.5),
        'w_ff1': nrm(ks[17], (DEPTH, D_MODEL, D_FF), D_MODEL ** -0.5),
        'w_ff3': nrm(ks[18], (DEPTH, D_MODEL, D_FF), D_MODEL ** -0.5),
        'w_ff2': nrm(ks[19], (DEPTH, D_FF, D_MODEL), D_FF ** -0.5),
        'norm_final': 1.0 + nrm(ks[20], (D_MODEL,), 0.05),
    }


def reference(x, c, ctx, c_ctx, w_ada, b_ada, norm_mix, norm_ffn, w_in, w_pool, pool_scale,
              ret_decay_f, ret_decay_b, ret_gn_w, w_pa, w_rb, w_o, w_ff1, w_ff3, w_ff2, norm_final):
    B, L, _ = x.shape
    rows = L // GRID_W
    rope = rope_2d(L, x.dtype)
    silu_c = jax.nn.silu(c)
    silu_cc = jax.nn.silu(c_ctx)
    for l in range(DEPTH):
        update_ctx = l < DEPTH - 1
        mod = silu_c @ w_ada[l] + b_ada[l]
        sh_m, sc_m, g_m, sh_f, sc_f, g_f = [m[:, None, :] for m in jnp.split(mod, 6, axis=-1)]
        mod_c = silu_cc @ w_ada[l] + b_ada[l]
        shc_m, scc_m, gc_m, shc_f, scc_f, gc_f = jnp.split(mod_c, 6, axis=-1)
        lg_f = jax.nn.log_sigmoid(ret_decay_f[l])
        lg_b = jax.nn.log_sigmoid(ret_decay_b[l])
        mixer_params = (lg_f, lg_b, w_pool[l], pool_scale[l], ret_gn_w[l], w_pa[l], w_rb[l], w_o[l])

        hc = modulate(rms_norm(ctx, norm_mix[l]), shc_m, scc_m)
        if update_ctx:
            zc = hc @ w_in[l]
            kc_raw, vc_raw = zc[..., K_OFF:V_OFF], zc[..., V_OFF:G_OFF]
        else:
            kvc = hc @ w_in[l][:, K_OFF:G_OFF]
            kc_raw, vc_raw = kvc[..., :RET_QK_WIDTH], kvc[..., RET_QK_WIDTH:]
        Lc = ctx.shape[1]
        kc = kc_raw.reshape(B, Lc, RET_HEADS, RET_QK_DIM) * K_SCALE
        vc = vc_raw.reshape(B, Lc, RET_HEADS, RET_V_DIM)
        s_f, s_b = context_final_states(kc, vc, lg_f, lg_b)

        h = modulate(rms_norm(x, norm_mix[l]), sh_m, sc_m)
        x = x + g_m * token_mixers(h @ w_in[l], rows, rope, s_f, s_b, *mixer_params)
        h = modulate(rms_norm(x, norm_ffn[l]), sh_f, sc_f)
        x = x + g_f * swiglu(h, w_ff1[l], w_ff3[l], w_ff2[l])

        if update_ctx:
            zero_state = jnp.zeros((B, RET_HEADS, RET_QK_DIM, RET_V_DIM), ctx.dtype)
            ctx = ctx + gc_m * token_mixers(zc, None, None, zero_state, zero_state, *mixer_params)
            hc = modulate(rms_norm(ctx, norm_ffn[l]), shc_f, scc_f)
            ctx = ctx + gc_f * swiglu(hc, w_ff1[l], w_ff3[l], w_ff2[l])
    return rms_norm(x, norm_final)
```

```python
import contextlib
import os
import types
import numpy as np
import ml_dtypes
import concourse.bass as bass
import concourse.mybir as mybir
from concourse.bass_utils import run_bass_kernel_spmd

F32 = mybir.dt.float32
BF16 = mybir.dt.bfloat16
AF = mybir.ActivationFunctionType
ALU = mybir.AluOpType
AX = mybir.AxisListType

D = 1024
L = 2048
NT = 16
LC = 256
C = 128
GRID_W = 64
H = 8
DK = 64
DV = 128
DFF = 2816
NFF = 22
EPS = 1e-6
K_SCALE = DK ** -0.5
POOL_WINDOWS = (2, 4, 8, 16)
FF_GROUPS = ((0, 5), (5, 5), (10, 5), (15, 5), (20, 2))


class Op:
    __slots__ = ("eng", "fn", "reads", "writes", "dma_key", "waits", "signal", "cnt", "n_dma")

    def __init__(self, eng, fn, reads, writes, dma_key, n_dma):
        self.eng = eng
        self.fn = fn
        self.reads = reads
        self.writes = writes
        self.dma_key = dma_key
        self.n_dma = n_dma
        self.waits = []
        self.signal = False
        self.cnt = None


class Sched:
    ENGS = ("pe", "act", "dve", "pool", "sp")

    def __init__(self):
        self.ops = []
        self.last_writer = {}
        self.readers = {}
        self.dma_cum = {}
        self.bank_rd = {}

    def _dep(self, op, d, kind):
        if d is op:
            return
        if d.dma_key is not None:
            op.waits.append(("dma:" + d.dma_key, self.dma_cum[d.dma_key]))
            return
        if d.eng == op.eng and op.dma_key is None:
            if d.eng == "pe" or kind != "RAW":
                return
        d.signal = True
        op.waits.append(("eng:" + d.eng, d))

    @staticmethod
    def _snapshot(fn):
        if fn.__closure__ is None:
            return fn
        cells = []
        for c in fn.__closure__:
            try:
                cells.append(types.CellType(c.cell_contents))
            except ValueError:
                cells.append(c)
        return types.FunctionType(fn.__code__, fn.__globals__, fn.__name__, fn.__defaults__, tuple(cells))

    def fence(self):
        self._nfence = getattr(self, "_nfence", 0) + 1
        key = "__fence%d" % self._nfence
        last = {}
        for o in self.ops:
            last[o.dma_key if o.dma_key is not None else "eng:" + o.eng] = o
        self.readers[key] = list(last.values())
        return key

    @staticmethod
    def _expand(keys):
        out = []
        for k in keys:
            if isinstance(k, str) and len(k) == 3 and k.startswith("ps"):
                out.extend((k + "a", k + "b"))
            else:
                out.append(k)
        return tuple(out)

    def op(self, eng, fn, reads=(), writes=(), dma_key=None, n_dma=1):
        fn = self._snapshot(fn)
        o = Op(eng, fn, self._expand(reads), self._expand(writes), dma_key, n_dma)
        for k in o.reads:
            w = self.last_writer.get(k)
            if w is not None:
                self._dep(o, w, "RAW")
            if isinstance(k, str) and k.startswith("ps") and eng in ("act", "dve"):
                bk = k[:3]
                lr = self.bank_rd.setdefault(bk, {})
                for e2, r in lr.items():
                    if e2 != eng:
                        self._dep(o, r, "XRD")
                lr[eng] = o
        for k in o.writes:
            w = self.last_writer.get(k)
            if w is not None:
                self._dep(o, w, "WAW")
            for r in self.readers.get(k, ()):
                self._dep(o, r, "WAR")
        for k in o.reads:
            self.readers.setdefault(k, []).append(o)
        for k in o.writes:
            self.last_writer[k] = o
            self.readers[k] = []
        if dma_key is not None:
            self.dma_cum[dma_key] = self.dma_cum.get(dma_key, 0) + 16 * n_dma
            o.cnt = self.dma_cum[dma_key]
        self.ops.append(o)
        return o

    def pe(self, fn, reads=(), writes=()):
        return self.op("pe", fn, reads, writes)

    def act(self, fn, reads=(), writes=()):
        return self.op("act", fn, reads, writes)

    def dve(self, fn, reads=(), writes=()):
        return self.op("dve", fn, reads, writes)

    def pool(self, fn, reads=(), writes=()):
        return self.op("pool", fn, reads, writes)

    def pool_or(self, tag, alt, fn, reads=(), writes=()):
        on = os.environ.get("KPOOL", "").split(",")
        return self.op("pool" if tag in on else alt, fn, reads, writes)

    def dma(self, eng, key, fn, reads=(), writes=()):
        return self.op(eng, fn, reads, writes, dma_key=key)

    def emit(self, nc, final_dma_keys=()):
        cnt = {e: 0 for e in self.ENGS}
        for o in self.ops:
            if o.dma_key is None and o.signal:
                cnt[o.eng] += 1
                o.cnt = cnt[o.eng]
        semnames = set()
        for o in self.ops:
            for (s, v) in o.waits:
                semnames.add(s)
            if o.dma_key is not None:
                semnames.add("dma:" + o.dma_key)
            elif o.signal:
                semnames.add("eng:" + o.eng)
        with contextlib.ExitStack() as es:
            sems = {}
            for s in sorted(semnames):
                sems[s] = es.enter_context(nc.semaphore(s.replace(":", "_")))
            block = es.enter_context(nc.Block())
            streams = {e: [o for o in self.ops if o.eng == e] for e in self.ENGS}

            def run(engname, e):
                waited = {}
                for o in streams[engname]:
                    need = {}
                    for (s, v) in o.waits:
                        val = v.cnt if isinstance(v, Op) else v
                        if val > need.get(s, 0):
                            need[s] = val
                    for s, val in need.items():
                        if waited.get(s, 0) >= val:
                            continue
                        e.wait_ge(sems[s], val)
                        waited[s] = val
                    ins = o.fn(e)
                    if o.dma_key is not None:
                        ins.then_inc(sems["dma:" + o.dma_key], 16 * o.n_dma)
                    elif o.signal:
                        ins.then_inc(sems["eng:" + o.eng], 1)
                if engname == "sp":
                    for k in final_dma_keys:
                        e.wait_ge(sems["dma:" + k], self.dma_cum[k])

            @block.tensor
            def _(e):
                run("pe", e)

            @block.scalar
            def _(e):
                run("act", e)

            @block.vector
            def _(e):
                run("dve", e)

            @block.gpsimd
            def _(e):
                run("pool", e)

            @block.sync
            def _(e):
                run("sp", e)


def _box_matrix(n, w):
    pos = np.arange(n)
    lo = np.clip(pos - w // 2, 0, n)
    hi = np.clip(pos + (w - w // 2), 0, n)
    a = np.zeros((n, n), np.float64)
    for t in range(n):
        a[t, lo[t]:hi[t]] = 1.0 / (hi[t] - lo[t])
    return a


def _pool_blocks():
    rows = L // GRID_W
    blocks = []
    seen = {}
    plan = []
    for g, w in enumerate(POOL_WINDOWS):
        a = np.kron(_box_matrix(rows, w), _box_matrix(GRID_W, w)) - np.eye(L)
        at = a.T
        pg = []
        for j in range(4):
            lst = []
            for t in range(NT):
                blk = at[t * 128:(t + 1) * 128, j * 512:(j + 1) * 512]
                if np.any(blk != 0.0):
                    b32 = np.ascontiguousarray(blk.astype(np.float32))
                    key = (g, b32.tobytes())
                    if key not in seen:
                        seen[key] = len(blocks)
                        blocks.append(b32)
                    lst.append((t, seen[key]))
            pg.append(lst)
        plan.append(pg)
    arr = np.stack(blocks, axis=1)
    return np.ascontiguousarray(arr).astype(ml_dtypes.bfloat16), plan


def _rope_tables():
    t = np.arange(L)
    row = (t // GRID_W).astype(np.float32)
    col = (t % GRID_W).astype(np.float32)
    n_freq = DK // 4
    inv_freq = (10000.0 ** (-np.arange(n_freq, dtype=np.float32) / n_freq)).astype(np.float32)
    ang = np.concatenate([row[:, None] * inv_freq, col[:, None] * inv_freq], axis=-1).astype(np.float32)
    cos = np.cos(ang).astype(np.float32).reshape(NT, 128, 32).transpose(1, 0, 2)
    sin = np.sin(ang).astype(np.float32).reshape(NT, 128, 32).transpose(1, 0, 2)
    return np.ascontiguousarray(cos), np.ascontiguousarray(sin)


_POOL_CACHE = None


def _get_pool():
    global _POOL_CACHE
    if _POOL_CACHE is None:
        _POOL_CACHE = _pool_blocks()
    return _POOL_CACHE


CO_COS = 0
CO_SIN = CO_COS + NT * 32
CO_DPOS = CO_SIN + NT * 32
CO_DNEG = CO_DPOS + 128
CO_POSQ = CO_DNEG + 128
CO_SM = CO_POSQ + 128
NCONST = CO_SM + 8

VO_C = 0
VO_NMIX = 16
VO_NFFN = 24
VO_PSC = 32
VO_GNW = 36
VO_BADA = 44
NVEC = VO_BADA + 48

RO_BGM = 0
RO_BGF = 1024
RO_NF = 2048
RO_DEC = 3072
NROW = RO_DEC + 16


def _host_consts():
    cos, sin = _rope_tables()
    c = np.zeros((128, NCONST), np.float32)
    c[:, CO_COS:CO_COS + NT * 32] = cos.reshape(128, -1)
    c[:, CO_SIN:CO_SIN + NT * 32] = sin.reshape(128, -1)
    i = np.arange(128, dtype=np.float32)
    dmat = i[None, :] - i[:, None]
    c[:, CO_DPOS:CO_DPOS + 128] = np.maximum(dmat, 0)
    c[:, CO_DNEG:CO_DNEG + 128] = np.maximum(-dmat, 0)
    c[0:64, CO_POSQ:CO_POSQ + 128] = (i + 1.0)[None, :]
    c[64:128, CO_POSQ:CO_POSQ + 128] = (C - i)[None, :]
    c[:, CO_SM + 0] = C - 1.0 - i
    c[:, CO_SM + 1] = i
    c[:, CO_SM + 2] = LC - 1.0 - i
    c[:, CO_SM + 3] = LC - 1.0 - (i + 128)
    c[:, CO_SM + 4] = i
    c[:, CO_SM + 5] = i + 128
    return c


def build_program(pool_plan, n_pool_blk, stop=None):
    nc = bass.Bass("TRN2", target_bir_lowering=False)
    DBGN = 16384

    def din(name, shape, dt=F32):
        return nc.dram_tensor(name, list(shape), dt, kind="ExternalInput").ap()

    x_d = din("x", [L, D])
    ctx_d = din("ctx", [LC, D])
    vecs_d = din("vecs", [128, NVEC])
    rows_d = din("rows", [1, NROW])
    consts_d = din("consts", [128, NCONST])
    ident_d = din("ident", [128, 128], BF16)
    wada_d = din("wada", [128, 8, 6 * D])
    wu_d = din("wu", [128, 8, 512])
    wpairs_d = din("wpairs", [4, 128, 8, 768])
    wtail_d = din("wtail", [8, 128, 28, 128])
    wpool_d = din("wpool", [128, 4, 128])
    wo_d = din("wo", [8, 128, D])
    w13_d = din("w13", [NFF, 128, 2, 8, 128])
    w2_d = din("w2", [NFF, 128, D])
    pblk_d = din("pblk", [128, n_pool_blk, 512], BF16)
    out_d = nc.dram_tensor("out", [L, D], F32, kind="ExternalOutput").ap()
    dbg_d = nc.dram_tensor("dbg", [128, DBGN], F32, kind="ExternalOutput").ap() if stop is not None else None

    S = Sched()
    SB_BYTES = 207 * 1024
    arena = nc.alloc_sbuf_tensor("arena", [128, SB_BYTES // 2], BF16).ap()
    ps = nc.alloc_psum_tensor("ps", [128, 4096], F32).ap()

    class Region:
        def __init__(self, start, size):
            self.start = start
            self.size = size
            self.off = 0

        def carve(self, nbytes, dt=BF16):
            want = nbytes
            nbytes = (nbytes + 31) // 32 * 32
            assert self.off + nbytes <= self.size, (self.off, nbytes, self.size)
            a = (self.start + self.off) // 2
            self.off += nbytes
            v = arena[:, a:a + want // 2]
            return v if dt == BF16 else v.bitcast(dt)

        def reset(self):
            self.off = 0

    KB = 1024
    R_PERS = Region(0, 32 * KB)
    R_X = Region(32 * KB, 64 * KB)
    R_M = Region(96 * KB, 32 * KB)
    R_PAIR = Region(128 * KB, 63 * KB)
    R_YP = Region(191 * KB, 16 * KB)
    assert 207 * KB <= SB_BYTES

    def bank(b, dt=F32):
        v = ps[:, b * 512:(b + 1) * 512]
        return v if dt == F32 else v.bitcast(dt)

    def PK(b):
        return "ps%d" % b

    def finish_debug(items):
        off = 0
        fk = S.fence()
        for ap, n, in items:
            for c0 in range(0, n, 1024):
                c1 = min(n, c0 + 1024)
                S.dma("pool", "dbg", lambda e, ap=ap, off=off, c0=c0, c1=c1: e.dma_start(
                    out=dbg_d[:, off + c0:off + c1], in_=ap[:, c0:c1]), writes=[fk])
            off += n
        S.dma("sp", "out", lambda e: e.dma_start(out=out_d[0:128, :], in_=gm_bc), writes=[fk])
        S.emit(nc, final_dma_keys=["out", "dbg"])
        return nc

    vecs = R_PERS.carve(NVEC * 4, F32)
    consts = R_PERS.carve(NCONST * 4, F32)
    ident = R_PERS.carve(256)
    decbc = R_PERS.carve(64, F32)
    lgbc = R_PERS.carve(64, F32)
    lgsel = R_PERS.carve(32, F32)
    cdec = R_PERS.carve(32, F32)
    kdec = R_PERS.carve(64, F32)
    ctxw = R_PERS.carve(128, F32)
    modT = R_PERS.carve(48 * 2 * 4, F32)
    modAB = R_PERS.carve(6 * 8 * 4, F32)
    scT = R_PERS.carve(8 * 2 * 2)
    scbc = R_PERS.carve(8 * 128 * 2)
    qdec = R_PERS.carve(8 * 128 * 4, F32)
    mask = R_PERS.carve(8 * 128 * 4, F32)
    gm_bc = R_PERS.carve(D * 4, F32)
    gf_bc = R_PERS.carve(D * 4, F32)
    hcT = R_PERS.carve(8 * LC * 2)
    nf_bc = hcT.bitcast(F32)
    wpool = R_PERS.carve(4 * 128 * 2)
    stat = R_PERS.carve(64 * 4, F32)
    epsb = R_PERS.carve(32, F32)
    gnw128 = R_PERS.carve(32, F32)
    sfp = R_PERS.carve(384 * 4, F32)

    modT3 = modT.rearrange("p (j t) -> p j t", t=2)
    modAB3 = modAB.rearrange("p (a k) -> p a k", k=8)
    scT3 = scT.rearrange("p (k t) -> p k t", t=2)
    scbc3 = scbc.rearrange("p (k m) -> p k m", m=128)
    qdec3 = qdec.rearrange("p (h t) -> p h t", t=128)
    mask3 = mask.rearrange("p (h t) -> p h t", t=128)
    hcT3 = hcT.rearrange("p (k t) -> p k t", t=LC)
    wpool3 = wpool.rearrange("p (g n) -> p g n", n=128)
    kdec3 = kdec.rearrange("p (d h) -> p d h", h=8)
    ctxw4 = ctxw.rearrange("p (t d h) -> p t d h", d=2, h=8)
    cos3 = consts[:, CO_COS:CO_COS + NT * 32].rearrange("p (i f) -> p i f", f=32)
    sin3 = consts[:, CO_SIN:CO_SIN + NT * 32].rearrange("p (i f) -> p i f", f=32)
    dpos = consts[:, CO_DPOS:CO_DPOS + 128]
    dneg = consts[:, CO_DNEG:CO_DNEG + 128]
    posq = consts[:, CO_POSQ:CO_POSQ + 128]

    def csm(i):
        return consts[:, CO_SM + i:CO_SM + i + 1]

    A_M, B_M, A_C, B_C, A_F, B_F = range(6)

    hT = R_X.carve(32 * KB).rearrange("p (k t) -> p k t", t=L)
    ygT = R_X.carve(32 * KB).rearrange("p (k t) -> p k t", t=L)
    R_X.reset()
    x1 = R_X.carve(64 * KB, F32).rearrange("p (i f) -> p i f", f=D)
    mT = R_M.carve(32 * KB).rearrange("p (k t) -> p k t", t=L)
    R_M.reset()
    _ux = R_X.start + 32 * KB
    u_tok = arena[:, _ux // 2:(_ux + 16 * KB) // 2].rearrange("p (i c) -> p i c", c=512)
    dT_sb = [arena[:, (_ux + (16 + i) * KB) // 2:(_ux + (17 + i) * KB) // 2] for i in range(2)]
    R_M.reset()
    wpair_buf = [R_M.carve(12 * KB).rearrange("p (k n) -> p k n", n=768) for _ in range(2)]
    wada_buf = [arena[:, (R_M.start + i * 8 * KB) // 2:(R_M.start + (i + 1) * 8 * KB) // 2].rearrange("p (k n) -> p k n", n=512)
                for i in range(4)]
    wada_buf += [arena[:, (R_YP.start + i * 8 * KB) // 2:(R_YP.start + (i + 1) * 8 * KB) // 2].rearrange("p (k n) -> p k n", n=512)
                 for i in range(2)]
    ypT = R_YP.carve(16 * KB).rearrange("p (g t) -> p g t", t=L)

    S.dma("sp", "c0", lambda e: e.dma_start(out=vecs, in_=vecs_d), writes=["vecs"])
    S.dma("sp", "c1", lambda e: e.dma_start(out=consts, in_=consts_d), writes=["consts"])
    S.dma("sp", "c2", lambda e: e.dma_start(out=ident, in_=ident_d), writes=["ident"])
    S.dma("sp", "c3", lambda e: e.dma_start(out=decbc, in_=rows_d[0:1, RO_DEC:RO_DEC + 16].to_broadcast([128, 16])),
          writes=["decbc"])
    S.dma("sp", "c4", lambda e: e.dma_start(out=gm_bc, in_=rows_d[0:1, RO_BGM:RO_BGM + D].to_broadcast([128, D])),
          writes=["gm_bc"])
    S.dma("sp", "c5", lambda e: e.dma_start(out=gf_bc, in_=rows_d[0:1, RO_BGF:RO_BGF + D].to_broadcast([128, D])),
          writes=["gf_bc"])
    S.dma("pool", "wpool", lambda e: e.dma_start(out=wpool3, in_=wpool_d), writes=["wpool"])

    S.dve(lambda e: e.memset(epsb, float(DV * DV) * EPS), writes=["epsb"])
    S.dve(lambda e: e.tensor_scalar(out=gnw128, in0=vecs[:, VO_GNW:VO_GNW + 8], scalar1=float(DV), scalar2=None, op0=ALU.mult),
          reads=["vecs"], writes=["gnw128"])
    cv3 = vecs[:, VO_C:VO_C + 16].rearrange("p (k t) -> p k t", t=2)
    S.act(lambda e: e.activation(out=scT3, in_=cv3, func=AF.Silu), reads=["vecs"], writes=["scT"])
    S.act(lambda e: e.activation(out=scbc3, in_=cv3[:, :, 0:1].to_broadcast([128, 8, 128]), func=AF.Silu),
          reads=["vecs"], writes=["scbc"])

    def wada_group(gi):
        buf = wada_buf[gi % 6]
        bk = "wada%d" % (gi % 6)
        S.dma("pool", bk, lambda e, buf=buf, gi=gi: e.dma_start(out=buf, in_=wada_d[:, :, gi * 512:(gi + 1) * 512]),
              writes=[bk])
        return buf, bk

    def wada_compute(gi, buf, bk):
        if gi in (4, 5, 10, 11):
            dst = gm_bc if gi in (4, 5) else gf_bc
            dk = "gm_bc" if gi in (4, 5) else "gf_bc"
            half = gi % 2 if gi in (4, 5) else (gi - 10)
            pb = 5 + (gi % 2)
            for k in range(8):
                S.pe(lambda e, k=k, buf=buf, pb=pb: e.matmul(bank(pb), lhsT=scbc3[:, k, :], rhs=buf[:, k, :],
                                                              start=(k == 0), stop=(k == 7)),
                     reads=[bk, "scbc"], writes=[PK(pb)])
            S.dve(lambda e, dst=dst, half=half, pb=pb: e.tensor_tensor(
                out=dst[:, half * 512:(half + 1) * 512], in0=bank(pb), in1=dst[:, half * 512:(half + 1) * 512], op=ALU.add),
                reads=[PK(pb), dk], writes=[dk])
        else:
            for jj in range(4):
                j = gi * 4 + jj
                for k in range(8):
                    S.pe(lambda e, k=k, jj=jj, j=j, buf=buf: e.matmul(
                        bank(7)[:, 2 * j:2 * j + 2], lhsT=buf[:, k, jj * 128:(jj + 1) * 128], rhs=scT3[:, k, :],
                        start=(k == 0), stop=(k == 7)),
                        reads=[bk, "scT"], writes=[PK(7)])

    def modT_evac(j0, j1):
        S.dve(lambda e, j0=j0, j1=j1: e.tensor_tensor(
            out=modT3[:, j0:j1, :], in0=bank(7)[:, 2 * j0:2 * j1].rearrange("p (j t) -> p j t", t=2),
            in1=vecs[:, VO_BADA + j0:VO_BADA + j1].unsqueeze(2).to_broadcast([128, j1 - j0, 2]), op=ALU.add),
            reads=[PK(7), "vecs"], writes=["modT"])

    def mod_ab(ai, bi, nvo, sh_blk, sc_blk, col):
        S.dve(lambda e: e.scalar_tensor_tensor(out=modAB3[:, ai, :], in0=modT3[:, sc_blk:sc_blk + 8, col], scalar=1.0,
                                               in1=vecs[:, nvo:nvo + 8], op0=ALU.add, op1=ALU.mult),
              reads=["modT", "vecs"], writes=["modAB%d" % ai])
        S.dve(lambda e: e.tensor_copy(out=modAB3[:, bi, :], in_=modT3[:, sh_blk:sh_blk + 8, col]),
              reads=["modT"], writes=["modAB%d" % bi])

    wg = [wada_group(gi) for gi in range(4)]

    def setup_mod_mix():
        for gi in range(4):
            wada_compute(gi, *wg[gi])
        modT_evac(0, 16)
        mod_ab(A_M, B_M, VO_NMIX, 0, 8, 0)
        mod_ab(A_C, B_C, VO_NMIX, 0, 8, 1)

    if stop == 0:
        setup_mod_mix()

    ee = stat[:, 0:16]
    tt = stat[:, 16:32]
    S.act(lambda e: e.activation(out=ee, in_=decbc, func=AF.Exp, scale=-1.0), reads=["decbc"], writes=["ee"])
    S.dve(lambda e: e.tensor_scalar(out=tt, in0=ee, scalar1=-1.0 / 7, scalar2=1.0 / 6, op0=ALU.mult, op1=ALU.add),
          reads=["ee"], writes=["tt"])
    for cf in (1.0 / 5, 1.0 / 4, 1.0 / 3, 1.0 / 2, 1.0):
        S.dve(lambda e: e.tensor_tensor(out=tt, in0=tt, in1=ee, op=ALU.mult), reads=["tt", "ee"], writes=["tt"])
        S.dve(lambda e, cf=cf: e.tensor_scalar(out=tt, in0=tt, scalar1=-1.0, scalar2=cf, op0=ALU.mult, op1=ALU.add),
              reads=["tt"], writes=["tt"])
    S.dve(lambda e: e.scalar_tensor_tensor(out=lgbc, in0=tt, scalar=-1.0, in1=ee, op0=ALU.mult, op1=ALU.mult),
          reads=["tt", "ee"], writes=["lgbc"])
    S.dve(lambda e: e.tensor_copy(out=lgsel[0:64, :], in_=lgbc[0:64, 0:8]), reads=["lgbc"], writes=["lgsel"])
    S.dve(lambda e: e.tensor_copy(out=lgsel[64:128, :], in_=lgbc[64:128, 8:16]), reads=["lgbc"], writes=["lgsel"])
    S.act(lambda e: e.activation(out=cdec, in_=lgsel, func=AF.Exp, scale=float(C)), reads=["lgsel"], writes=["cdec"])
    for d in range(2):
        S.act(lambda e, d=d: e.activation(out=kdec3[:, d, :], in_=lgbc[:, d * 8:(d + 1) * 8], func=AF.Exp, scale=csm(d)),
              reads=["lgbc", "consts"], writes=["kdec"])
        for t in range(2):
            S.act(lambda e, d=d, t=t: e.activation(out=ctxw4[:, t, d, :], in_=lgbc[:, d * 8:(d + 1) * 8], func=AF.Exp,
                                                   scale=csm(2 + 2 * d + t)),
                  reads=["lgbc", "consts"], writes=["ctxw"])
    S.dve(lambda e: e.tensor_scalar(out=kdec, in0=kdec, scalar1=K_SCALE, scalar2=None, op0=ALU.mult),
          reads=["kdec"], writes=["kdec"])
    S.dve(lambda e: e.tensor_scalar(out=ctxw, in0=ctxw, scalar1=K_SCALE, scalar2=None, op0=ALU.mult),
          reads=["ctxw"], writes=["ctxw"])
    for h in range(H):
        S.act(lambda e, h=h: e.activation(out=qdec3[:, h, :], in_=posq, func=AF.Exp, scale=lgsel[:, h:h + 1]),
              reads=["lgsel", "consts"], writes=["qdec"])
        S.dve(lambda e, h=h: e.tensor_scalar(out=mask3[:, h, :], in0=dpos, scalar1=lgbc[:, h:h + 1], scalar2=None,
                                             op0=ALU.mult), reads=["lgbc", "consts"], writes=["mask"])
        S.dve(lambda e, h=h: e.scalar_tensor_tensor(out=mask3[:, h, :], in0=dneg, scalar=lgbc[:, 8 + h:9 + h],
                                                    in1=mask3[:, h, :], op0=ALU.mult, op1=ALU.add),
              reads=["lgbc", "consts", "mask"], writes=["mask"])
    S.act(lambda e: e.activation(out=mask, in_=mask, func=AF.Exp), reads=["mask"], writes=["mask"])
    S.dve(lambda e: e.tensor_scalar(out=mask, in0=mask, scalar1=K_SCALE, scalar2=None, op0=ALU.mult),
          reads=["mask"], writes=["mask"])

    if stop == 0:
        for gi in range(4, 12):
            wg.append(wada_group(gi))
            wada_compute(gi, *wg[gi])
        modT_evac(24, 40)
        mod_ab(A_F, B_F, VO_NFFN, 24, 32, 0)
        return finish_debug([(modAB, 48), (lgbc, 16), (mask, 1024), (qdec, 1024), (gm_bc, 1024), (gf_bc, 1024),
                             (kdec, 16), (ctxw, 32), (cdec, 8)])
    R_PAIR.reset()
    xbuf = [R_PAIR.carve(4 * KB, F32) for _ in range(3)]
    junk = R_PAIR.carve(2 * KB)
    xnb = [R_PAIR.carve(2 * KB) for _ in range(2)]
    nctr = [0]

    def norm_stats(src_ap, src_key, junk, xnb):
        n = nctr[0]
        nctr[0] += 1
        ss = stat[:, 32 + (n % 4) * 2:33 + (n % 4) * 2]
        rs = stat[:, 33 + (n % 4) * 2:34 + (n % 4) * 2]
        sk = "nstat%d" % (n % 4)
        xn = xnb[n % len(xnb)]
        xk = "xn%d_%d" % (len(xnb), n % len(xnb))
        S.act(lambda e: e.activation(out=junk, in_=src_ap, func=AF.Square, accum_out=ss),
              reads=[src_key], writes=["junk", sk])
        S.dve(lambda e: e.tensor_scalar(out=rs, in0=ss, scalar1=1.0 / D, scalar2=EPS, op0=ALU.mult, op1=ALU.add),
              reads=[sk], writes=[sk + "r"])
        S.act(lambda e: e.activation(out=rs, in_=rs, func=AF.Sqrt), reads=[sk + "r"], writes=[sk + "r"])
        S.dve(lambda e: e.reciprocal(out=rs, in_=rs), reads=[sk + "r"], writes=[sk + "r"])
        S.dve(lambda e: e.tensor_scalar(out=xn, in0=src_ap, scalar1=rs, scalar2=None, op0=ALU.mult),
              reads=[src_key, sk + "r"], writes=[xk])
        return (n, xn, xk)

    ACT_K = (1, 4, 6)

    def norm_tr(st, ai, bi, dst3, dst_key, tcol, pbase=0, fused=True):
        n, xn, xk = st
        par = n % 2
        pbD, pbA = pbase + 2 * par, pbase + 2 * par + 1
        pTD = bank(pbD, BF16)[:, 0:640].rearrange("p (k t) -> p k t", t=128)
        pTA = bank(pbA, BF16)[:, 0:384].rearrange("p (k t) -> p k t", t=128)
        slot = {}
        na = nd = 0
        for k in range(8):
            if k in ACT_K:
                slot[k] = (pTA, pbA, na)
                na += 1
            else:
                slot[k] = (pTD, pbD, nd)
                nd += 1
        for k in range(8):
            pt_, pbk, sl = slot[k]
            S.pe(lambda e, k=k, pt_=pt_, sl=sl: e.transpose(out=pt_[:, sl, :], in_=xn[:, k * 128:(k + 1) * 128], identity=ident),
                 reads=[xk, "ident"], writes=[PK(pbk)])
        for k in range(8):
            o = dst3[:, k, tcol * 128:(tcol + 1) * 128]
            pt_, pbk, sl = slot[k]
            if not fused:
                if k not in ACT_K:
                    S.dve(lambda e, o=o, pt_=pt_, sl=sl: e.tensor_copy(out=o, in_=pt_[:, sl, :]),
                          reads=[PK(pbk)], writes=[dst_key(k, tcol)])
                else:
                    S.act(lambda e, o=o, pt_=pt_, sl=sl: e.activation(out=o, in_=pt_[:, sl, :], func=AF.Copy),
                          reads=[PK(pbk)], writes=[dst_key(k, tcol)])
            elif k not in ACT_K:
                S.dve(lambda e, k=k, o=o, pt_=pt_, sl=sl: e.tensor_scalar(out=o, in0=pt_[:, sl, :], scalar1=modAB3[:, ai, k:k + 1],
                                                                          scalar2=modAB3[:, bi, k:k + 1], op0=ALU.mult, op1=ALU.add),
                      reads=[PK(pbk), "modAB%d" % ai, "modAB%d" % bi], writes=[dst_key(k, tcol)])
            else:
                S.act(lambda e, k=k, o=o, pt_=pt_, sl=sl: e.activation(out=o, in_=pt_[:, sl, :], func=AF.Identity,
                                                                       scale=modAB3[:, ai, k:k + 1], bias=modAB3[:, bi, k:k + 1]),
                      reads=[PK(pbk), "modAB%d" % ai, "modAB%d" % bi], writes=[dst_key(k, tcol)])

    def hT_key(k, i):
        return "XA_%d_%d" % (k, i)

    pendA = []
    xnb = xnb + [arena[:, R_YP.start // 2:(R_YP.start + 2 * KB) // 2]]
    for t in range(2 + NT):
        sbi = t % 3
        xb = xbuf[sbi]
        if t < 2:
            S.dma("sp", "xb%d" % sbi, lambda e, xb=xb, t=t: e.dma_start(out=xb, in_=ctx_d[t * 128:(t + 1) * 128, :]),
                  writes=["xb%d" % sbi])
            args = (A_C, B_C, hcT3, (lambda k, tc: "hcT%d" % k), t)
        else:
            i = t - 2
            S.dma("sp", "xb%d" % sbi, lambda e, xb=xb, i=i: e.dma_start(out=xb, in_=x_d[i * 128:(i + 1) * 128, :]),
                  writes=["xb%d" % sbi])
            args = (A_M, B_M, hT, hT_key, i)
        st = norm_stats(xb, "xb%d" % sbi, junk, xnb)
        pendA.append((st,) + args)
        if len(pendA) > 2:
            norm_tr(*pendA.pop(0), fused=False)
    while pendA:
        norm_tr(*pendA.pop(0), fused=False)
    setup_mod_mix()
    for hf in range(2):
        for k in range(8):
            S.dve(lambda e, k=k, hf=hf: e.tensor_scalar(out=hT[:, k, hf * 1024:(hf + 1) * 1024], in0=hT[:, k, hf * 1024:(hf + 1) * 1024],
                                                        scalar1=modAB3[:, A_M, k:k + 1], scalar2=modAB3[:, B_M, k:k + 1],
                                                        op0=ALU.mult, op1=ALU.add),
                  reads=[hT_key(k, i) for i in range(hf * 8, hf * 8 + 8)] + ["modAB%d" % A_M, "modAB%d" % B_M],
                  writes=[hT_key(k, i) for i in range(hf * 8, hf * 8 + 8)])
    for k in range(8):
        S.dve(lambda e, k=k: e.tensor_scalar(out=hcT3[:, k, :], in0=hcT3[:, k, :], scalar1=modAB3[:, A_C, k:k + 1],
                                             scalar2=modAB3[:, B_C, k:k + 1], op0=ALU.mult, op1=ALU.add),
              reads=["hcT%d" % k, "modAB%d" % A_C, "modAB%d" % B_C], writes=["hcT%d" % k])
    wlate = arena[:, (R_M.start + 24 * KB) // 2:(R_M.start + 32 * KB) // 2].rearrange("p (k n) -> p k n", n=512)
    WL_ALIAS = ["ysb0", "ysb1", "ysb2", "ysq"]

    def wada_late_load(gi):
        S.dma("pool", "wlate", lambda e: e.dma_start(out=wlate, in_=wada_d[:, :, gi * 512:(gi + 1) * 512]),
              writes=["wlate"] + WL_ALIAS)

    def wada_late_compute(gi):
        rd = ["wlate"] + WL_ALIAS
        if gi in (4, 5, 10, 11):
            dst = gm_bc if gi in (4, 5) else gf_bc
            dk = "gm_bc" if gi in (4, 5) else "gf_bc"
            half = gi % 2 if gi in (4, 5) else (gi - 10)
            for k in range(8):
                S.pe(lambda e, k=k: e.matmul(bank(7), lhsT=scbc3[:, k, :], rhs=wlate[:, k, :], start=(k == 0), stop=(k == 7)),
                     reads=rd + ["scbc"], writes=[PK(7)])
            S.dve(lambda e: e.tensor_tensor(out=dst[:, half * 512:(half + 1) * 512], in0=bank(7),
                                            in1=dst[:, half * 512:(half + 1) * 512], op=ALU.add),
                  reads=[PK(7), dk], writes=[dk])
        else:
            for jj in range(4):
                j = gi * 4 + jj
                for k in range(8):
                    S.pe(lambda e, k=k, jj=jj, j=j: e.matmul(bank(7)[:, 2 * j:2 * j + 2], lhsT=wlate[:, k, jj * 128:(jj + 1) * 128],
                                                             rhs=scT3[:, k, :], start=(k == 0), stop=(k == 7)),
                         reads=rd + ["scT"], writes=[PK(7)])
            modT_evac(gi * 4, gi * 4 + 4)
    hcT_keys = ["hcT%d" % k for k in range(8)]

    if stop == 1:
        return finish_debug([(hcT, 2048), (hT.rearrange("p k t -> p (k t)")[:, 0:8192], 8192)])

    def load_pair(p):
        buf = wpair_buf[p % 2]
        extra = [S.fence()] if p < 2 else []
        S.dma("pool", "wpair%d" % (p % 2), lambda e, buf=buf, p=p: e.dma_start(out=buf, in_=wpairs_d[p]),
              writes=["wpair%d" % (p % 2)] + extra)

    wu_sb = R_PAIR.carve(8 * KB).rearrange("p (k n) -> p k n", n=512)
    NPST = 9
    pstage = [R_PAIR.carve(4 * KB).rearrange("p (b t) -> p b t", t=512) for _ in range(NPST)]
    S.dma("pool", "wu", lambda e: e.dma_start(out=wu_sb, in_=wu_d), writes=["wu"])
    load_pair(0)
    for i in range(NT):
        pb = 2 + (i % 2)
        for k in range(8):
            S.pe(lambda e, k=k, i=i, pb=pb: e.matmul(bank(pb), lhsT=hT[:, k, i * 128:(i + 1) * 128], rhs=wu_sb[:, k, :],
                                                     start=(k == 0), stop=(k == 7)),
                 reads=[hT_key(k, i), "wu"], writes=[PK(pb)])
        if i % 2 == 0:
            S.act(lambda e, i=i, pb=pb: e.activation(out=u_tok[:, i, :], in_=bank(pb), func=AF.Copy),
                  reads=[PK(pb)], writes=["u%d" % i])
        else:
            S.dve(lambda e, i=i, pb=pb: e.tensor_copy(out=u_tok[:, i, :], in_=bank(pb)),
                  reads=[PK(pb)], writes=["u%d" % i])
    pst_n = [0]
    resident = {}

    def pool_head(pctr, g, j):
        lst = pool_plan[g][j]
        pd = 4 + (pctr % 2)
        dsb = dT_sb[pctr % 2]
        dk = "dT%d" % (pctr % 2)
        runs = []
        for ent in lst:
            if runs and len(runs[-1]) < 4 and runs[-1][-1][1] + 1 == ent[1]:
                runs[-1].append(ent)
            else:
                runs.append([ent])
        pos = 0
        for grp in runs:
            b0 = grp[0][1]
            nb = len(grp)
            assert all(grp[q][1] == b0 + q for q in range(nb))
            hit = resident.get((b0, nb))
            if hit is not None and pst_n[0] - hit < NPST:
                st = pstage[hit % NPST]
                sk = "pst%d" % (hit % NPST)
            else:
                st = pstage[pst_n[0] % NPST]
                sk = "pst%d" % (pst_n[0] % NPST)
                resident[(b0, nb)] = pst_n[0]
                pst_n[0] += 1
                S.dma("sp", sk, lambda e: e.dma_start(out=st[:, 0:nb, :], in_=pblk_d[:, b0:b0 + nb, :]), writes=[sk])
            for q, (t, bi_) in enumerate(grp):
                first = (pos == 0)
                last = (pos == len(lst) - 1)
                pos += 1
                S.pe(lambda e, t=t, q=q, first=first, last=last: e.matmul(
                    bank(pd), lhsT=u_tok[:, t, g * 128:(g + 1) * 128], rhs=st[:, q, :], start=first, stop=last),
                    reads=["u%d" % t, sk], writes=[PK(pd)])
        S.dve(lambda e: e.tensor_copy(out=dsb, in_=bank(pd)), reads=[PK(pd)], writes=[dk])

    def pool_tail(pctr, g, j):
        py = 6 + (pctr % 2)
        dsb = dT_sb[pctr % 2]
        dk = "dT%d" % (pctr % 2)
        S.pe(lambda e: e.matmul(bank(py), lhsT=wpool3[:, g, :], rhs=dsb, start=True, stop=True),
             reads=[dk, "wpool"], writes=[PK(py)])
        S.act(lambda e: e.activation(out=ypT[:, g, j * 512:(j + 1) * 512], in_=bank(py), func=AF.Copy,
                                     scale=vecs[:, VO_PSC + g:VO_PSC + g + 1]),
              reads=[PK(py), "vecs"], writes=["ypT"])

    pblocks = [(g, j) for g in range(4) for j in range(4)]
    pool_head(0, *pblocks[0])
    for c_ in range(len(pblocks)):
        if c_ + 1 < len(pblocks):
            pool_head(c_ + 1, *pblocks[c_ + 1])
        pool_tail(c_, *pblocks[c_])

    if stop == 2:
        return finish_debug([(ypT.rearrange("p g t -> p (g t)"), 8192)])
    R_PAIR.reset()
    kT2 = R_PAIR.carve(4 * KB)
    qTz = R_PAIR.carve(8 * KB).rearrange("p (n h t) -> p n h t", h=2, t=128)
    zf = S.fence()
    S.dve(lambda e: e.memset(qTz[64:128, :, 0, :], 0.0), writes=["qTz_zero", zf])
    S.dve(lambda e: e.memset(qTz[0:64, :, 1, :], 0.0), writes=["qTz_zero", zf])
    q2 = R_PAIR.carve(8 * KB).rearrange("p (h t) -> p h t", t=L)
    kfb = R_PAIR.carve(8 * KB).rearrange("p (i d c) -> p i d c", d=2, c=128)
    v_tok = R_PAIR.carve(8 * KB).rearrange("p (i c) -> p i c", c=256)
    sg_tok = R_PAIR.carve(8 * KB).rearrange("p (i c) -> p i c", c=256)
    s_all = R_PAIR.carve(8 * KB).rearrange("p (n h v) -> p n h v", h=2, v=128)
    rot = [R_PAIR.carve(1 * KB).rearrange("p (t a c) -> p t a c", t=2, c=64) for _ in range(2)]
    qdup = R_PAIR.carve(1 * KB).rearrange("p (t a u c) -> p t a u c", t=2, u=2, c=64)
    _rt0 = R_PAIR.off
    rtmp = [R_PAIR.carve(1 * KB, F32).rearrange("p (t a f) -> p t a f", t=2, f=32) for _ in range(4)]
    _rt1 = R_PAIR.off
    R_PAIR.off = _rt0
    kcw = R_PAIR.carve(1 * KB).rearrange("p (t d c) -> p t d c", d=2, c=128)
    vc = R_PAIR.carve(1 * KB).rearrange("p (t c) -> p t c", c=256)
    R_PAIR.off = _rt1
    pmat = [R_PAIR.carve(1 * KB).rearrange("p (c h t) -> p c h t", c=2, t=128) for _ in range(2)]
    ygb = [R_PAIR.carve(1 * KB).rearrange("p (c h v) -> p c h v", c=2, v=128) for _ in range(2)]
    _rm = R_M.start + 24 * KB
    y_sb = [arena[:, (_rm + i * 2 * KB) // 2:(_rm + (i + 1) * 2 * KB) // 2].bitcast(F32).rearrange("p (c h v) -> p c h v", c=2, v=128)
            for i in range(3)]
    ysq = arena[:, (_rm + 6 * KB) // 2:(_rm + 8 * KB) // 2].bitcast(F32).rearrange("p (c h v) -> p c h v", c=2, v=128)
    sfp3 = sfp[:, 0:256].rearrange("p (h v) -> p h v", v=128)
    gst = sfp[:, 256:384]

    def ygT_key(k, i):
        return "XB_%d_%d" % (k, i)

    def make_pair(p):
        wp = wpair_buf[p % 2]
        wk = "wpair%d" % (p % 2)

        def ctx_pe():
            if 1 <= p + 1 < 4:
                load_pair(p + 1)
            for t in range(2):
                pb = 6 + t
                for k in range(8):
                    S.pe(lambda e, k=k, t=t, pb=pb: e.matmul(bank(pb)[:, 0:384], lhsT=hcT3[:, k, t * 128:(t + 1) * 128],
                                                             rhs=wp[:, k, 128:512], start=(k == 0), stop=(k == 7)),
                         reads=["hcT%d" % k, wk], writes=[PK(pb)])

        def ctx_rest():
            for t in range(2):
                pb = 6 + t
                for d in range(2):
                    S.dve(lambda e, t=t, d=d, pb=pb: e.tensor_tensor(
                        out=kcw[:, t, d, :].rearrange("p (h c) -> p h c", c=64),
                        in0=bank(pb)[:, 0:128].rearrange("p (h c) -> p h c", c=64),
                        in1=ctxw4[:, t, d, 2 * p:2 * p + 2].unsqueeze(2).to_broadcast([128, 2, 64]), op=ALU.mult),
                        reads=[PK(pb), "ctxw"], writes=["kcw"])
                S.dve(lambda e, t=t, pb=pb: e.tensor_copy(out=vc[:, t, :], in_=bank(pb)[:, 128:384]),
                      reads=[PK(pb)], writes=["vc"])
            pS = bank(1)[:, 0:256].rearrange("p (h v) -> p h v", v=128)
            for h2 in range(2):
                for d in range(2):
                    for t in range(2):
                        S.pe(lambda e, h2=h2, d=d, t=t: e.matmul(pS[d * 64:(d + 1) * 64, h2, :],
                                                                 lhsT=kcw[:, t, d, h2 * 64:(h2 + 1) * 64],
                                                                 rhs=vc[:, t, h2 * 128:(h2 + 1) * 128],
                                                                 start=(t == 0), stop=(t == 1)),
                             reads=["kcw", "vc"], writes=["ps1a", "ps1b"])
            S.dve(lambda e: e.tensor_copy(out=sfp3, in_=pS), reads=["ps1a", "ps1b"], writes=["sfp"])
            S.dve(lambda e: e.tensor_copy(out=s_all[0:64, 0, :, :], in_=pS[0:64, :, :]),
                  reads=["ps1a", "ps1b"], writes=["sall_f0"])
            S.dve(lambda e: e.tensor_copy(out=s_all[64:128, NT - 1, :, :], in_=pS[64:128, :, :]),
                  reads=["ps1a", "ps1b"], writes=["sall_b%d" % (NT - 1)])


        def sb_proj(it, i, pe_only=False):
            par = it % 2
            pq = 2 + par
            for t in range(2):
                for k in range(8):
                    S.pe(lambda e, k=k, t=t: e.matmul(bank(pq)[:, t * 256:(t + 1) * 256], lhsT=hT[:, k, (i + t) * 128:(i + t + 1) * 128],
                                                      rhs=wp[:, k, 0:256], start=(k == 0), stop=(k == 7)),
                         reads=[hT_key(k, i + t), wk], writes=[PK(pq)])
            for t in range(2):
                for k in range(8):
                    S.pe(lambda e, k=k, t=t: e.matmul(bank(4 + t), lhsT=hT[:, k, (i + t) * 128:(i + t + 1) * 128],
                                                      rhs=wp[:, k, 256:768], start=(k == 0), stop=(k == 7)),
                         reads=[hT_key(k, i + t), wk], writes=[PK(4 + t)])
            if not pe_only:
                sb_proj_evac(it, i)

        def sb_proj_evac(it, i):
            vg = ps[:, 4 * 512:6 * 512].rearrange("p (t c) -> p t c", t=2)
            S.act(lambda e: e.activation(out=v_tok[:, i:i + 2, :], in_=vg[:, :, 0:256], func=AF.Copy),
                  reads=[PK(4), PK(5)], writes=["v%d" % i, "v%d" % (i + 1)])
            S.act(lambda e: e.activation(out=sg_tok[:, i:i + 2, :], in_=vg[:, :, 256:512], func=AF.Silu),
                  reads=[PK(4), PK(5)], writes=["sg%d" % i, "sg%d" % (i + 1)])

        def sb_rope(it, i):
            par = it % 2
            pq = 2 + par
            qk4 = bank(pq).rearrange("p (t a c) -> p t a c", t=2, c=64)
            t1 = qk4[:, :, :, 0:32]
            t2 = qk4[:, :, :, 32:64]
            cs = cos3[:, i:i + 2, :].unsqueeze(2).to_broadcast([128, 2, 4, 32])
            sn = sin3[:, i:i + 2, :].unsqueeze(2).to_broadcast([128, 2, 4, 32])
            rt = rot[par]
            rk = "rot%d" % par
            ta, tb_, tc_, td = rtmp
            S.dve(lambda e: e.tensor_tensor(out=ta, in0=t1, in1=cs, op=ALU.mult), reads=[PK(pq), "consts"], writes=["rta", "kcw", "vc"])
            S.dve(lambda e: e.tensor_tensor(out=tb_, in0=t2, in1=sn, op=ALU.mult), reads=[PK(pq), "consts"], writes=["rtb", "kcw", "vc"])
            S.dve(lambda e: e.tensor_tensor(out=tc_, in0=t1, in1=sn, op=ALU.mult), reads=[PK(pq), "consts"], writes=["rtc", "kcw", "vc"])
            S.dve(lambda e: e.tensor_tensor(out=td, in0=t2, in1=cs, op=ALU.mult), reads=[PK(pq), "consts"], writes=["rtd", "kcw", "vc"])
            S.dve(lambda e: e.tensor_tensor(out=rt[:, :, :, 0:32], in0=ta, in1=tb_, op=ALU.subtract),
                  reads=["rta", "rtb"], writes=[rk])
            S.dve(lambda e: e.tensor_tensor(out=rt[:, :, :, 32:64], in0=tc_, in1=td, op=ALU.add),
                  reads=["rtc", "rtd"], writes=[rk])
            for t in range(2):
                S.act(lambda e, t=t: e.activation(out=qdup[:, t, :, :, :], in_=rt[:, t, 0:2, :].unsqueeze(2).to_broadcast([128, 2, 2, 64]),
                                                  func=AF.Copy), reads=[rk], writes=["qdup"])
            for d in range(2):
                S.dve(lambda e, d=d: e.tensor_tensor(
                    out=kfb[:, i:i + 2, d, :].rearrange("p t (h c) -> p t h c", c=64), in0=rt[:, :, 2:4, :],
                    in1=kdec3[:, d, 2 * p:2 * p + 2].unsqueeze(1).unsqueeze(3).to_broadcast([128, 2, 2, 64]), op=ALU.mult),
                    reads=[rk, "kdec"], writes=["kfb%d" % i, "kfb%d" % (i + 1)])

        def sb_tr(it, i):
            par = it % 2
            rt = rot[par]
            rk = "rot%d" % par
            sfx = "ab"[par]
            pTA = bank(0, BF16)[:, par * 512:(par + 1) * 512].rearrange("p (t a x) -> p t a x", t=2, x=128)
            pTD = bank(1, BF16)[:, par * 512:(par + 1) * 512].rearrange("p (t h x) -> p t h x", t=2, x=128)
            for t in range(2):
                S.pe(lambda e, t=t: e.transpose(out=pTA[:, t, 0, :], in_=rt[:, t, 0:2, :].rearrange("p a c -> p (a c)"), identity=ident),
                     reads=[rk, "ident"], writes=["ps0" + sfx])
                S.pe(lambda e, t=t: e.transpose(out=pTA[:, t, 1, :], in_=rt[:, t, 2:4, :].rearrange("p a c -> p (a c)"), identity=ident),
                     reads=[rk, "ident"], writes=["ps0" + sfx])
                for h2 in range(2):
                    S.pe(lambda e, t=t, h2=h2: e.transpose(out=pTD[:, t, h2, :], in_=qdup[:, t, h2, :, :].rearrange("p u c -> p (u c)"),
                                                           identity=ident),
                         reads=["qdup", "ident"], writes=["ps1" + sfx])
            qk_keys = ["qkT%d" % i, "qkT%d" % (i + 1)]
            S.act(lambda e: e.activation(out=kT2[:, i * 128:(i + 2) * 128].rearrange("p (t x) -> p t x", t=2), in_=pTA[:, :, 1, :], func=AF.Copy),
                  reads=["ps0" + sfx], writes=qk_keys)
            S.act(lambda e: e.activation(out=qTz[0:64, i:i + 2, 0, :], in_=pTA[0:64, :, 0, :], func=AF.Copy),
                  reads=["ps0" + sfx, "qTz_zero"], writes=qk_keys)
            S.act(lambda e: e.activation(out=qTz[64:128, i:i + 2, 1, :], in_=pTA[64:128, :, 0, :], func=AF.Copy),
                  reads=["ps0" + sfx, "qTz_zero"], writes=qk_keys)
            S.dve(lambda e: e.tensor_tensor(out=q2[:, :, i * 128:(i + 2) * 128].rearrange("p h (t x) -> p t h x", t=2), in0=pTD,
                                            in1=qdec3[:, 2 * p:2 * p + 2, :].unsqueeze(1).to_broadcast([128, 2, 2, 128]), op=ALU.mult),
                  reads=["ps1" + sfx, "qdec"], writes=["q2_%d" % i, "q2_%d" % (i + 1)])

        def scan_pe(j):
            sfx = "ab"[j % 2]
            pD = bank(6)[:, (j % 2) * 256:(j % 2) * 256 + 256].rearrange("p (h v) -> p h v", v=128)
            nf, nb = j, NT - 1 - j
            for h2 in range(2):
                S.pe(lambda e, h2=h2: e.matmul(pD[0:64, h2, :], lhsT=kfb[:, nf, 0, h2 * 64:(h2 + 1) * 64],
                                               rhs=v_tok[:, nf, h2 * 128:(h2 + 1) * 128], start=True, stop=True),
                     reads=["kfb%d" % nf, "v%d" % nf], writes=["ps6" + sfx])
                S.pe(lambda e, h2=h2: e.matmul(pD[64:128, h2, :], lhsT=kfb[:, nb, 1, h2 * 64:(h2 + 1) * 64],
                                               rhs=v_tok[:, nb, h2 * 128:(h2 + 1) * 128], start=True, stop=True),
                     reads=["kfb%d" % nb, "v%d" % nb], writes=["ps6" + sfx])

        def scan_ew(j):
            sfx = "ab"[j % 2]
            pD = bank(6)[:, (j % 2) * 256:(j % 2) * 256 + 256].rearrange("p (h v) -> p h v", v=128)
            nf, nb = j, NT - 1 - j
            for h2 in range(2):
                S.dve(lambda e, h2=h2: e.scalar_tensor_tensor(
                    out=sfp3[:, h2, :], in0=sfp3[:, h2, :], scalar=cdec[:, 2 * p + h2:2 * p + h2 + 1], in1=pD[:, h2, :],
                    op0=ALU.mult, op1=ALU.add), reads=["sfp", "ps6" + sfx, "cdec"], writes=["sfp"])
            if os.environ.get("KDBG_CAST", "dve") == "dve":
                S.dve(lambda e: e.tensor_copy(out=s_all[0:64, nf + 1, :, :], in_=sfp3[0:64, :, :]),
                      reads=["sfp"], writes=["sall_f%d" % (nf + 1)])
                S.dve(lambda e: e.tensor_copy(out=s_all[64:128, nb - 1, :, :], in_=sfp3[64:128, :, :]),
                      reads=["sfp"], writes=["sall_b%d" % (nb - 1)])
            else:
                S.act(lambda e: e.activation(out=s_all[0:64, nf + 1, :, :], in_=sfp3[0:64, :, :], func=AF.Copy),
                      reads=["sfp"], writes=["sall_f%d" % (nf + 1)])
                S.act(lambda e: e.activation(out=s_all[64:128, nb - 1, :, :], in_=sfp3[64:128, :, :], func=AF.Copy),
                      reads=["sfp"], writes=["sall_b%d" % (nb - 1)])

        def scan_step(j):
            scan_pe(j)
            scan_ew(j)

        class OutStep:
            def __init__(self, it, n0):
                self.it, self.n0 = it, n0
                par = it % 2
                self.psc = 2 + par
                self.pyb = 4 + par
                self.pS4 = bank(self.psc).rearrange("p (c h t) -> p c h t", c=2, t=128)
                self.pY4 = bank(self.pyb).rearrange("p (c h v) -> p c h v", c=2, v=128)
                self.pm = pmat[par]
                self.pmk = "pmat%d" % par
                self.ysb = y_sb[it % 3]
                self.yk = "ysb%d" % (it % 3)
                self.gs = gst[:, (it % 4) * 32:(it % 4) * 32 + 32]
                self.gk = "gst%d" % (it % 4)
                self.yg = ygb[par]
                self.ygk = "ygb%d" % par
                self.sfx = "ab"[par]
                self.pT4 = bank(0, BF16)[:, par * 512:(par + 1) * 512].rearrange("p (h c t) -> p h c t", h=2, t=128)

            def scores(o):
                for c in range(2):
                    n = o.n0 + c
                    tsl = slice(n * 128, (n + 1) * 128)
                    S.pe(lambda e, c=c, n=n, tsl=tsl: e.matmul(bank(o.psc)[:, c * 256:(c + 1) * 256], lhsT=kT2[:, tsl],
                                                               rhs=qTz[:, n, :, :].rearrange("p h t -> p (h t)"), start=True, stop=True),
                         reads=["qkT%d" % n, "qTz_zero"], writes=[PK(o.psc)])

            def mask(o):
                S.dve(lambda e: e.tensor_tensor(out=o.pm, in0=o.pS4,
                                                in1=mask3[:, 2 * p:2 * p + 2, :].unsqueeze(1).to_broadcast([128, 2, 2, 128]),
                                                op=ALU.mult), reads=[PK(o.psc), "mask"], writes=[o.pmk])

            def av(o):
                for c in range(2):
                    n = o.n0 + c
                    tsl = slice(n * 128, (n + 1) * 128)
                    for h2 in range(2):
                        S.pe(lambda e, c=c, n=n, h2=h2: e.matmul(o.pY4[:, c, h2, :], lhsT=o.pm[:, c, h2, :],
                                                                 rhs=v_tok[:, n, h2 * 128:(h2 + 1) * 128], start=True, stop=False),
                             reads=[o.pmk, "v%d" % n], writes=[PK(o.pyb)])
                        S.pe(lambda e, c=c, n=n, h2=h2, tsl=tsl: e.matmul(o.pY4[:, c, h2, :], lhsT=q2[:, h2, tsl],
                                                                          rhs=s_all[:, n, h2, :], start=False, stop=True),
                             reads=["q2_%d" % n, "sall_f%d" % n, "sall_b%d" % n], writes=[PK(o.pyb)])

            def copy_sq(o):
                S.act(lambda e: e.activation(out=o.ysb, in_=o.pY4, func=AF.Copy), reads=[PK(o.pyb)], writes=[o.yk])
                for c in range(2):
                    for h2 in range(2):
                        q_ = c * 2 + h2
                        S.act(lambda e, c=c, h2=h2, q_=q_: e.activation(out=ysq[:, c, h2, :], in_=o.pY4[:, c, h2, :], func=AF.Square,
                                                                        accum_out=o.gs[:, 4 + q_:5 + q_]),
                              reads=[PK(o.pyb)], writes=["ysq", o.gk + "q"])

            def reduce(o):
                S.dve(lambda e: e.tensor_reduce(out=o.gs[:, 0:4], in_=o.ysb.rearrange("p c h v -> p (c h) v"), axis=AX.X, op=ALU.add),
                      reads=[o.yk], writes=[o.gk + "s"])

            def var(o):
                S.dve(lambda e: e.tensor_tensor(out=o.gs[:, 8:12], in0=o.gs[:, 0:4], in1=o.gs[:, 0:4], op=ALU.mult),
                      reads=[o.gk + "s"], writes=[o.gk + "ss"])
                S.dve(lambda e: e.scalar_tensor_tensor(out=o.gs[:, 12:16], in0=o.gs[:, 4:8], scalar=float(DV), in1=o.gs[:, 8:12],
                                                       op0=ALU.mult, op1=ALU.subtract),
                      reads=[o.gk + "q", o.gk + "ss"], writes=[o.gk + "v"])

            def sqrt(o):
                S.act(lambda e: e.activation(out=o.gs[:, 16:20], in_=o.gs[:, 12:16], func=AF.Sqrt, bias=epsb[:, 0:1]),
                      reads=[o.gk + "v", "epsb"], writes=[o.gk + "sd"])

            def rstd(o):
                S.dve(lambda e: e.reciprocal(out=o.gs[:, 24:28], in_=o.gs[:, 16:20]), reads=[o.gk + "sd"], writes=[o.gk + "r"])
                S.dve(lambda e: e.scalar_tensor_tensor(out=o.gs[:, 28:32], in0=o.gs[:, 0:4], scalar=-1.0 / DV, in1=o.gs[:, 24:28],
                                                       op0=ALU.mult, op1=ALU.mult),
                      reads=[o.gk + "s", o.gk + "r"], writes=[o.gk + "nb"])

            def normalize(o):
                for c in range(2):
                    for h2 in range(2):
                        q_ = c * 2 + h2
                        S.act(lambda e, c=c, h2=h2, q_=q_: e.activation(out=o.yg[:, c, h2, :], in_=o.ysb[:, c, h2, :], func=AF.Identity,
                                                                        scale=o.gs[:, 24 + q_:25 + q_], bias=o.gs[:, 28 + q_:29 + q_]),
                              reads=[o.yk, o.gk + "r", o.gk + "nb"], writes=[o.ygk])

            def gate(o):
                S.dve(lambda e: e.tensor_tensor(out=o.yg.rearrange("p c h v -> p c (h v)"),
                                                in0=o.yg.rearrange("p c h v -> p c (h v)"),
                                                in1=sg_tok[:, o.n0:o.n0 + 2, :], op=ALU.mult),
                      reads=[o.ygk, "sg%d" % o.n0, "sg%d" % (o.n0 + 1)], writes=[o.ygk])

            def transposes(o):
                for c in range(2):
                    for h2 in range(2):
                        S.pe(lambda e, c=c, h2=h2: e.transpose(out=o.pT4[:, h2, c, :], in_=o.yg[:, c, h2, :], identity=ident),
                             reads=[o.ygk, "ident"], writes=["ps0" + o.sfx])

            def evac(o):
                for h2 in range(2):
                    kk = 2 * p + h2
                    S.act(lambda e, h2=h2, kk=kk: e.activation(out=ygT[:, kk, o.n0 * 128:(o.n0 + 2) * 128],
                                                               in_=o.pT4[:, h2, :, :].rearrange("p c t -> p (c t)"), func=AF.Copy,
                                                               scale=gnw128[:, kk:kk + 1]),
                          reads=["ps0" + o.sfx, "gnw128"], writes=[ygT_key(kk, o.n0), ygT_key(kk, o.n0 + 1)])

        order_b = []
        for j in range(NT // 4):
            order_b += [2 * j, NT - 2 - 2 * j]
        def head_pe():
            sb_proj(0, order_b[0], pe_only=True)

        def head_rest():
            sb_proj_evac(0, order_b[0])
            sb_rope(0, order_b[0])

        def body(nxt):
            wada_late_load(4 + 2 * p)
            for it, i in enumerate(order_b):
                if it + 1 < len(order_b):
                    sb_proj(it + 1, order_b[it + 1])
                if it == 2:
                    wada_late_compute(4 + 2 * p)
                    wada_late_load(5 + 2 * p)
                if it == 5:
                    wada_late_compute(5 + 2 * p)
                    if p == 2:
                        mod_ab(A_F, B_F, VO_NFFN, 24, 32, 0)
                sb_tr(it, i)
                if it + 1 < len(order_b):
                    sb_rope(it + 1, order_b[it + 1])
                if it % 2 == 1:
                    scan_step(it - 1)
                    scan_step(it)
            order_o = [6, 8, 4, 10, 2, 12, 0, 14]
            NO = len(order_o)
            steps = [OutStep(it, order_o[it]) for it in range(NO)]

            def g(k):
                return steps[k] if 0 <= k < NO else None

            scan_step(NT // 2)
            def iteration(it):
                    a_, b_, c_, d_ = g(it), g(it - 1), g(it - 2), g(it - 3)
                    sj = NT // 2 + 1 + it if NT // 2 + 1 + it <= NT - 2 else None
                    ordm = os.environ.get("KDBG_ORD", "m1c")
                    if ordm == "r7":
                        if c_:
                            c_.var(); c_.sqrt(); c_.rstd()
                        if d_:
                            d_.normalize(); d_.gate(); d_.transposes(); d_.evac()
                        if b_:
                            b_.copy_sq(); b_.reduce()
                        if sj is not None:
                            scan_step(sj)
                        if a_:
                            a_.scores(); a_.mask(); a_.av()
                        return
                    if ordm == "m1":
                        if a_:
                            a_.scores()
                        if sj is not None:
                            scan_pe(sj)
                        if c_:
                            c_.var(); c_.sqrt(); c_.rstd()
                        if d_:
                            d_.normalize(); d_.gate(); d_.transposes(); d_.evac()
                        if b_:
                            b_.copy_sq(); b_.reduce()
                        if sj is not None:
                            scan_ew(sj)
                        if a_:
                            a_.mask(); a_.av()
                        return
                    if ordm == "m1c":
                        if a_:
                            a_.scores()
                        if sj is not None:
                            scan_pe(sj)
                        if c_:
                            c_.var(); c_.sqrt()
                        if sj is not None:
                            scan_ew(sj)
                        if c_:
                            c_.rstd()
                        if d_:
                            d_.normalize()
                        if b_:
                            b_.copy_sq()
                        if d_:
                            d_.gate(); d_.transposes()
                        if b_:
                            b_.reduce()
                        if d_:
                            d_.evac()
                        if a_:
                            a_.mask(); a_.av()
                        return
                    if ordm == "m1b":
                        if a_:
                            a_.scores()
                        if sj is not None:
                            scan_pe(sj)
                        if c_:
                            c_.var(); c_.sqrt(); c_.rstd()
                        if d_:
                            d_.normalize()
                        if b_:
                            b_.copy_sq()
                        if d_:
                            d_.gate(); d_.transposes()
                        if b_:
                            b_.reduce()
                        if d_:
                            d_.evac()
                        if sj is not None:
                            scan_ew(sj)
                        if a_:
                            a_.mask(); a_.av()
                        return
                    if ordm in ("m3", "m4"):
                        if a_:
                            a_.scores()
                        if sj is not None:
                            scan_pe(sj)
                        if c_:
                            c_.var(); c_.sqrt(); c_.rstd()
                        if ordm == "m4" and a_:
                            a_.mask(); a_.av()
                        if d_:
                            d_.normalize(); d_.gate(); d_.transposes(); d_.evac()
                        if ordm == "m3" and a_:
                            a_.mask(); a_.av()
                        if b_:
                            b_.copy_sq(); b_.reduce()
                        if sj is not None:
                            scan_ew(sj)
                        return
                    if ordm == "m2":
                        if a_:
                            a_.scores()
                        if sj is not None:
                            scan_pe(sj)
                        if c_:
                            c_.var(); c_.sqrt()
                        if a_:
                            a_.mask(); a_.av()
                        if c_:
                            c_.rstd()
                        if d_:
                            d_.normalize(); d_.gate(); d_.transposes(); d_.evac()
                        if b_:
                            b_.copy_sq(); b_.reduce()
                        if sj is not None:
                            scan_ew(sj)
                        return
                    if a_:
                        a_.scores()
                    if sj is not None:
                        scan_pe(sj)
                    if c_:
                        c_.var()
                        c_.sqrt()
                    if d_:
                        d_.normalize()
                    if a_:
                        a_.mask()
                        a_.av()
                    if c_:
                        c_.rstd()
                    if sj is not None:
                        scan_ew(sj)
                    if b_:
                        b_.copy_sq()
                    if d_:
                        d_.gate()
                        d_.transposes()
                    if b_:
                        b_.reduce()
                    if d_:
                        d_.evac()

            for it in range(NO + 3):
                iteration(it)
                if nxt is not None and it == NO - 1:
                    nxt.ctx_pe()
                if nxt is not None and it == NO:
                    nxt.head_pe()
            if nxt is not None:
                nxt.ctx_rest()
                nxt.head_rest()

        class _P:
            pass
        o_ = _P()
        o_.ctx_pe, o_.ctx_rest, o_.head_pe, o_.head_rest, o_.body = ctx_pe, ctx_rest, head_pe, head_rest, body
        return o_

    pairs = [make_pair(p) for p in range(4)]
    pairs[0].ctx_pe()
    pairs[0].ctx_rest()
    pairs[0].head_pe()
    pairs[0].head_rest()
    for p in range(4):
        pairs[p].body(pairs[p + 1] if p < 3 else None)

    if stop == 3:
        return finish_debug([(ygT.rearrange("p k t -> p (k t)"), 16384)])
    R_PAIR.reset()
    def _rp(off_kb, size_kb, dt=BF16):
        v = arena[:, (R_PAIR.start + off_kb * KB) // 2:(R_PAIR.start + (off_kb + size_kb) * KB) // 2]
        return v if dt == BF16 else v.bitcast(dt)

    wt_buf = [_rp(20, 7).rearrange("p (r n) -> p r n", n=128), _rp(0, 7).rearrange("p (r n) -> p r n", n=128)]
    wt_buf = [wt_buf[0], wt_buf[1]]
    sgab = [_rp(7 + 2 * i, 2, F32) for i in range(4)]
    wo_sb = _rp(28, 16).rearrange("p (k n) -> p k n", n=D)
    wo_stage = [_rp(44 + 4 * i, 4, F32) for i in range(2)]
    xres = [_rp(52 + 4 * i, 4, F32) for i in range(2)]
    R_PAIR.off = 60 * KB

    def mT_key(k, i):
        return "M_%d_%d" % (k, i)

    def load_tail(cb):
        extra = [S.fence()] if cb == 1 else (["kfb%d" % i for i in range(NT)] if cb == 0 else [])
        S.dma("pool", "wt%d" % (cb % 2), lambda e, cb=cb: e.dma_start(out=wt_buf[cb % 2], in_=wtail_d[cb]),
              writes=["wt%d" % (cb % 2)] + extra)

    def prep_wo(k):
        st = wo_stage[k % 2]
        sk = "wost%d" % (k % 2)
        S.dma("sp", sk, lambda e, st=st, k=k: e.dma_start(out=st, in_=wo_d[k]), writes=[sk] + ([S.fence()] if k < 2 else []))
        S.dve(lambda e, st=st, k=k: e.tensor_tensor(out=wo_sb[:, k, :], in0=st, in1=gm_bc, op=ALU.mult),
              reads=[sk, "gm_bc"], writes=["wo%d" % k])

    load_tail(0)
    ctr = 0
    for cb in range(8):
        wt = wt_buf[cb % 2]
        wtk = "wt%d" % (cb % 2)
        if cb + 1 < 8:
            load_tail(cb + 1)
        prep_wo(cb)
        for tb in range(4):
            ts_ = slice(tb * 512, (tb + 1) * 512)
            par = ctr % 2
            ctr += 1
            pga, pgb, pba, pbb = par, 2 + par, 4 + par, 6 + par
            hreads = lambda k: [hT_key(k, 4 * tb + q) for q in range(4)]
            for k in range(8):
                S.pe(lambda e, k=k, ts_=ts_, pga=pga, wt=wt: e.matmul(bank(pga), lhsT=wt[:, k, :], rhs=hT[:, k, ts_],
                                                                     start=(k == 0), stop=(k == 7)),
                     reads=[wtk] + hreads(k), writes=[PK(pga)])
            for g in range(4):
                S.pe(lambda e, g=g, ts_=ts_, pba=pba, wt=wt: e.matmul(bank(pba), lhsT=wt[:, 24 + g, :], rhs=ypT[:, g, ts_],
                                                                     start=(g == 0), stop=(g == 3)),
                     reads=[wtk, "ypT"], writes=[PK(pba)])
            for k in range(8):
                S.pe(lambda e, k=k, ts_=ts_, pgb=pgb, wt=wt: e.matmul(bank(pgb), lhsT=wt[:, 8 + k, :], rhs=hT[:, k, ts_],
                                                                     start=(k == 0), stop=(k == 7)),
                     reads=[wtk] + hreads(k), writes=[PK(pgb)])
            for k in range(8):
                S.pe(lambda e, k=k, ts_=ts_, pbb=pbb, wt=wt: e.matmul(bank(pbb), lhsT=wt[:, 16 + k, :], rhs=ygT[:, k, ts_],
                                                                     start=(k == 0), stop=(k == 7)),
                     reads=[wtk] + [ygT_key(k, 4 * tb + q) for q in range(4)], writes=[PK(pbb)])
            sa = sgab[par * 2]
            sb_ = sgab[par * 2 + 1]
            sak = "sga%d" % par
            sbk = "sgb%d" % par
            S.act(lambda e, sa=sa, pga=pga: e.activation(out=sa, in_=bank(pga), func=AF.Sigmoid), reads=[PK(pga)], writes=[sak])
            S.act(lambda e, sb_=sb_, pgb=pgb: e.activation(out=sb_, in_=bank(pgb), func=AF.Sigmoid), reads=[PK(pgb)], writes=[sbk])
            S.dve(lambda e, sa=sa, pba=pba: e.tensor_tensor(out=sa, in0=sa, in1=bank(pba), op=ALU.mult),
                  reads=[sak, PK(pba)], writes=[sak])
            S.dve(lambda e, sb_=sb_, pbb=pbb: e.tensor_tensor(out=sb_, in0=sb_, in1=bank(pbb), op=ALU.mult),
                  reads=[sbk, PK(pbb)], writes=[sbk])
            S.dve(lambda e, sa=sa, sb_=sb_, cb=cb, ts_=ts_: e.tensor_tensor(out=mT[:, cb, ts_], in0=sa, in1=sb_, op=ALU.add),
                  reads=[sak, sbk], writes=[mT_key(cb, 4 * tb + q) for q in range(4)])

    if stop == 4:
        return finish_debug([(mT.rearrange("p k t -> p (k t)"), 16384)])
    w13_buf = [arena[:, (R_PAIR.start + i * 4 * KB) // 2:(R_PAIR.start + (i + 1) * 4 * KB) // 2].rearrange(
        "p (a k n) -> p a k n", a=2, n=128) for i in range(2)]

    def load_w13(c):
        extra = [S.fence()] if c < 2 else []
        S.dma("pool", "w13_%d" % (c % 2), lambda e, c=c: e.dma_start(out=w13_buf[c % 2], in_=w13_d[c]),
              writes=["w13_%d" % (c % 2)] + extra)

    if stop is None:
        load_w13(0)
        load_w13(1)
    def x1_key(i):
        return "X1_%d" % i

    h2T = mT

    def h2_key(k, i):
        return mT_key(k, i)

    junk2 = _rp(48, 2)
    a2 = {}

    def a2_square(i):
        n = nctr[0]
        nctr[0] += 1
        c = dict(n=n, i=i, ss=stat[:, 32 + (n % 4) * 2:33 + (n % 4) * 2], rs=stat[:, 33 + (n % 4) * 2:34 + (n % 4) * 2],
                 sk="nstat%d" % (n % 4), xn=xnb2[n % 3], xk="xn3_%d" % (n % 3))
        S.act(lambda e: e.activation(out=junk2, in_=x1[:, i, :], func=AF.Square, accum_out=c["ss"]),
              reads=[x1_key(i)], writes=["junk", c["sk"]])
        return c

    def a2_ts(c):
        S.dve(lambda e: e.tensor_scalar(out=c["rs"], in0=c["ss"], scalar1=1.0 / D, scalar2=EPS, op0=ALU.mult, op1=ALU.add),
              reads=[c["sk"]], writes=[c["sk"] + "r"])
        S.act(lambda e: e.activation(out=c["rs"], in_=c["rs"], func=AF.Sqrt), reads=[c["sk"] + "r"], writes=[c["sk"] + "r"])

    def a2_fin(c):
        S.dve(lambda e: e.reciprocal(out=c["rs"], in_=c["rs"]), reads=[c["sk"] + "r"], writes=[c["sk"] + "r"])
        S.dve(lambda e: e.tensor_scalar(out=c["xn"], in0=x1[:, c["i"], :], scalar1=c["rs"], scalar2=None, op0=ALU.mult),
              reads=[x1_key(c["i"]), c["sk"] + "r"], writes=[c["xk"]])

    xnb2 = [_rp(50, 2), _rp(60, 2), arena[:, (R_YP.start + 14 * KB) // 2:(R_YP.start + 16 * KB) // 2]]
    pend = None
    for i in range(NT):
        pa = (i % 2) * 2
        xr = xres[i % 2]
        xrk = "xres%d" % (i % 2)
        S.dma("sp", xrk, lambda e, xr=xr, i=i: e.dma_start(out=xr, in_=x_d[i * 128:(i + 1) * 128, :]),
              writes=[xrk])
        for hf in range(2):
            for k in range(8):
                S.pe(lambda e, k=k, hf=hf, i=i, pa=pa: e.matmul(bank(pa + hf), lhsT=mT[:, k, i * 128:(i + 1) * 128],
                                                                rhs=wo_sb[:, k, hf * 512:(hf + 1) * 512],
                                                                start=(k == 0), stop=(k == 7)),
                     reads=[mT_key(k, i), "wo%d" % k], writes=[PK(pa + hf)])
        S.dve(lambda e, i=i, xr=xr, pa=pa: e.tensor_tensor(out=x1[:, i, :], in0=ps[:, pa * 512:(pa + 2) * 512], in1=xr, op=ALU.add),
              reads=[PK(pa), PK(pa + 1), xrk],
              writes=[x1_key(i)] + [(hT_key(i, t) if i < 8 else ygT_key(i - 8, t)) for t in range(NT)])
        if stop != 5:
            a2[i] = a2_square(i)
            if i >= 1:
                a2_ts(a2[i - 1])
            if i >= 3:
                norm_tr((a2[i - 3]["n"], a2[i - 3]["xn"], a2[i - 3]["xk"]), A_F, B_F, h2T, h2_key, i - 3, 4)
            if i >= 1:
                a2_fin(a2[i - 1])
    if stop != 5:
        a2_ts(a2[NT - 1])
        norm_tr((a2[NT - 3]["n"], a2[NT - 3]["xn"], a2[NT - 3]["xk"]), A_F, B_F, h2T, h2_key, NT - 3, 4)
        a2_fin(a2[NT - 1])
        for i_ in (NT - 2, NT - 1):
            norm_tr((a2[i_]["n"], a2[i_]["xn"], a2[i_]["xk"]), A_F, B_F, h2T, h2_key, i_, 4)

    if stop == 5:
        return finish_debug([(x1.rearrange("p i f -> p (i f)"), 16384)])
    R_PAIR.reset()
    R_PAIR.carve(8 * KB)
    actT = [R_PAIR.carve(20 * KB).rearrange("p (c t) -> p c t", t=L) for _ in range(2)]
    R_YP.reset()
    w2_sb = R_YP.carve(10 * KB).rearrange("p (c n) -> p c n", n=D)
    sa_sb = [R_YP.carve(2 * KB, F32) for _ in range(2)]
    w2_stage = [qdec, mask]

    S.dma("sp", "c6", lambda e: e.dma_start(out=nf_bc, in_=rows_d[0:1, RO_NF:RO_NF + D].to_broadcast([128, D])),
          writes=["nf_bc", S.fence()] + hcT_keys)
    fctr = 0
    for gi, (c0, ng) in enumerate(FF_GROUPS):
        at = actT[gi % 2]
        atk = "actT%d" % (gi % 2)
        for cc in range(ng):
            c = c0 + cc
            wb = w13_buf[c % 2]
            wbk = "w13_%d" % (c % 2)
            if 2 <= c + 1 < NFF:
                load_w13(c + 1)
            st = w2_stage[c % 2]
            stk = "w2st%d" % (c % 2)
            S.dma("sp", stk, lambda e, st=st, c=c: e.dma_start(out=st, in_=w2_d[c]), writes=[stk, "qdec" if c % 2 == 0 else "mask"])
            S.dve(lambda e, st=st, cc=cc: e.tensor_tensor(out=w2_sb[:, cc, :], in0=st, in1=gf_bc, op=ALU.mult),
                  reads=[stk, "gf_bc"], writes=["w2sb%d" % cc])
            for tb in range(4):
                ts_ = slice(tb * 512, (tb + 1) * 512)
                par = fctr % 2
                fctr += 1
                pa_, pb_ = par, 2 + par
                for a in range(2):
                    pbk = pa_ if a == 0 else pb_
                    for k in range(8):
                        S.pe(lambda e, a=a, k=k, ts_=ts_, pbk=pbk, wb=wb: e.matmul(bank(pbk), lhsT=wb[:, a, k, :], rhs=h2T[:, k, ts_],
                                                                                  start=(k == 0), stop=(k == 7)),
                             reads=[wbk] + [h2_key(k, 4 * tb + q) for q in range(4)], writes=[PK(pbk)])
                sa = sa_sb[par]
                sak = "ffsa%d" % par
                S.act(lambda e, sa=sa, pa_=pa_: e.activation(out=sa, in_=bank(pa_), func=AF.Silu), reads=[PK(pa_)], writes=[sak])
                S.dve(lambda e, sa=sa, pb_=pb_, cc=cc, ts_=ts_, at=at: e.tensor_tensor(out=at[:, cc, ts_], in0=sa, in1=bank(pb_), op=ALU.mult),
                      reads=[sak, PK(pb_)], writes=[atk + "_%d_%d" % (cc, tb)])
        last = (gi == len(FF_GROUPS) - 1)
        for i in range(NT):
            pa = 4 + (i % 2) * 2
            for hf in range(2):
                for cc in range(ng):
                    S.pe(lambda e, cc=cc, hf=hf, i=i, pa=pa, at=at: e.matmul(bank(pa + hf), lhsT=at[:, cc, i * 128:(i + 1) * 128],
                                                                             rhs=w2_sb[:, cc, hf * 512:(hf + 1) * 512],
                                                                             start=(cc == 0), stop=(cc == ng - 1)),
                         reads=[atk + "_%d_%d" % (cc, i // 4), "w2sb%d" % cc], writes=[PK(pa + hf)])
            S.dve(lambda e, i=i, pa=pa: e.tensor_tensor(out=x1[:, i, :], in0=ps[:, pa * 512:(pa + 2) * 512], in1=x1[:, i, :], op=ALU.add),
                  reads=[PK(pa), PK(pa + 1), x1_key(i)], writes=[x1_key(i)])
            if last:
                def fin_a(i):
                    ss = stat[:, 32 + (i % 4) * 2:33 + (i % 4) * 2]
                    S.act(lambda e: e.activation(out=junk2, in_=x1[:, i, :], func=AF.Square, accum_out=ss),
                          reads=[x1_key(i)], writes=["junk2", "fstat%d" % (i % 4)])

                def fin_b(i):
                    ss = stat[:, 32 + (i % 4) * 2:33 + (i % 4) * 2]
                    rs = stat[:, 33 + (i % 4) * 2:34 + (i % 4) * 2]
                    sk = "fstat%d" % (i % 4)
                    S.dve(lambda e: e.tensor_scalar(out=rs, in0=ss, scalar1=1.0 / D, scalar2=EPS, op0=ALU.mult, op1=ALU.add),
                          reads=[sk], writes=[sk + "r"])
                    S.act(lambda e: e.activation(out=rs, in_=rs, func=AF.Sqrt), reads=[sk + "r"], writes=[sk + "r"])

                def fin_c(i):
                    rs = stat[:, 33 + (i % 4) * 2:34 + (i % 4) * 2]
                    sk = "fstat%d" % (i % 4)
                    S.dve(lambda e: e.reciprocal(out=rs, in_=rs), reads=[sk + "r"], writes=[sk + "r"])
                    S.dve(lambda e: e.scalar_tensor_tensor(out=x1[:, i, :], in0=x1[:, i, :], scalar=rs, in1=nf_bc,
                                                           op0=ALU.mult, op1=ALU.mult),
                          reads=[x1_key(i), sk + "r", "nf_bc"], writes=[x1_key(i)])
                    S.dma("sp", "out", lambda e: e.dma_start(out=out_d[i * 128:(i + 1) * 128, :], in_=x1[:, i, :]),
                          reads=[x1_key(i)])

                fin_a(i)
                if i >= 1:
                    fin_b(i - 1)
                if i >= 2:
                    fin_c(i - 2)
                if i == NT - 1:
                    fin_b(i)
                    fin_c(i - 1)
                    fin_c(i)

    S.emit(nc, final_dma_keys=["out"])
    return nc


_Q_OFF, _K_OFF, _V_OFF, _G_OFF, _GA_OFF, _GB_OFF = 512, 1024, 1536, 2560, 3584, 4608


def _kchunk(w):
    kk = w.shape[0] // 128
    return np.ascontiguousarray(w.reshape(kk, 128, w.shape[1]).transpose(1, 0, 2))


def _prep(x, c, ctx, c_ctx, w_ada, b_ada, norm_mix, norm_ffn, w_in, w_pool, pool_scale,
          ret_decay_f, ret_decay_b, ret_gn_w, w_pa, w_rb, w_o, w_ff1, w_ff3, w_ff2, norm_final):
    f32 = np.float32
    x = np.asarray(x, f32)
    B = x.shape[0]
    pblk, plan = _get_pool()

    w_in0 = np.asarray(w_in, f32)[0]
    wada = _kchunk(np.asarray(w_ada, f32)[0])
    wu = _kchunk(w_in0[:, 0:512])
    wpairs = []
    for p in range(4):
        cols = np.concatenate([
            w_in0[:, _Q_OFF + p * 128:_Q_OFF + (p + 1) * 128],
            w_in0[:, _K_OFF + p * 128:_K_OFF + (p + 1) * 128],
            w_in0[:, _V_OFF + p * 256:_V_OFF + (p + 1) * 256],
            w_in0[:, _G_OFF + p * 256:_G_OFF + (p + 1) * 256]], axis=1)
        wpairs.append(_kchunk(cols))
    wpairs = np.stack(wpairs, 0)
    w_rb0 = np.asarray(w_rb, f32)[0]
    w_pa0 = np.asarray(w_pa, f32)[0]
    wtail = []
    for cb in range(8):
        cs = slice(cb * 128, (cb + 1) * 128)
        ga = _kchunk(w_in0[:, _GA_OFF:_GA_OFF + D][:, cs])
        gb = _kchunk(w_in0[:, _GB_OFF:_GB_OFF + D][:, cs])
        rb = _kchunk(w_rb0[:, cs])
        pa = _kchunk(w_pa0[:, cs])
        wtail.append(np.concatenate([ga, gb, rb, pa], axis=1))
    wtail = np.ascontiguousarray(np.stack(wtail, 0))
    wpool = np.ascontiguousarray(np.asarray(w_pool, f32)[0].transpose(1, 0, 2))
    wo = np.ascontiguousarray(np.asarray(w_o, f32)[0].reshape(8, 128, D))
    w1 = np.asarray(w_ff1, f32)[0]
    w3 = np.asarray(w_ff3, f32)[0]
    w13 = np.stack([np.stack([_kchunk(w1[:, cc * 128:(cc + 1) * 128]), _kchunk(w3[:, cc * 128:(cc + 1) * 128])], axis=1)
                    for cc in range(NFF)], 0)
    w13 = np.ascontiguousarray(w13)
    w2 = np.ascontiguousarray(np.asarray(w_ff2, f32)[0].reshape(NFF, 128, D))
    consts = _host_consts()
    ident = np.eye(128, dtype=np.float32).astype(ml_dtypes.bfloat16)

    def pp(v, k):
        return np.asarray(v, f32).reshape(k, 128).T

    b_ada0 = np.asarray(b_ada, f32)[0]
    rows = np.zeros((1, NROW), f32)
    rows[0, RO_BGM:RO_BGM + D] = b_ada0[2 * D:3 * D]
    rows[0, RO_BGF:RO_BGF + D] = b_ada0[5 * D:6 * D]
    rows[0, RO_NF:RO_NF + D] = np.asarray(norm_final, f32)
    rows[0, RO_DEC:RO_DEC + 8] = np.asarray(ret_decay_f, f32)[0]
    rows[0, RO_DEC + 8:RO_DEC + 16] = np.asarray(ret_decay_b, f32)[0]

    in_maps = []
    for b in range(B):
        vecs = np.zeros((128, NVEC), f32)
        vecs[:, VO_C:VO_C + 16:2] = pp(np.asarray(c, f32)[b], 8)
        vecs[:, VO_C + 1:VO_C + 16:2] = pp(np.asarray(c_ctx, f32), 8)
        vecs[:, VO_NMIX:VO_NMIX + 8] = pp(np.asarray(norm_mix, f32)[0], 8)
        vecs[:, VO_NFFN:VO_NFFN + 8] = pp(np.asarray(norm_ffn, f32)[0], 8)
        vecs[:, VO_PSC:VO_PSC + 4] = pp(np.asarray(pool_scale, f32)[0], 4)
        vecs[:, VO_GNW:VO_GNW + 8] = pp(np.asarray(ret_gn_w, f32)[0], 8)
        vecs[:, VO_BADA:VO_BADA + 48] = pp(b_ada0, 48)
        in_maps.append({
            "x": np.ascontiguousarray(x[b]), "ctx": np.ascontiguousarray(np.asarray(ctx, f32)[b]),
            "vecs": vecs, "rows": rows, "consts": consts, "ident": ident,
            "wada": wada, "wu": wu, "wpairs": wpairs, "wtail": wtail, "wpool": wpool, "wo": wo,
            "w13": w13, "w2": w2, "pblk": pblk,
        })
    return in_maps


def kernel(**inputs):
    in_maps = _prep(**inputs)
    pblk, plan = _get_pool()
    nc = build_program(plan, pblk.shape[1])
    B = len(in_maps)
    res = run_bass_kernel_spmd(nc, in_maps, core_ids=list(range(B)))
    out = np.stack([np.asarray(res.results[b]["out"], np.float32) for b in range(B)], 0)
    return out
```

```python
import contextlib
import os
import types
import numpy as np
import ml_dtypes
import concourse.bass as bass
import concourse.mybir as mybir
from concourse.bass_utils import run_bass_kernel_spmd

F32 = mybir.dt.float32
BF16 = mybir.dt.bfloat16
AF = mybir.ActivationFunctionType
ALU = mybir.AluOpType
AX = mybir.AxisListType

D = 1024
L = 2048
NT = 16
LC = 256
C = 128
GRID_W = 64
H = 8
DK = 64
DV = 128
DFF = 2816
NFF = 22
EPS = 1e-6
K_SCALE = DK ** -0.5
POOL_WINDOWS = (2, 4, 8, 16)
FF_GROUPS = ((0, 5), (5, 5), (10, 5), (15, 5), (20, 2))


class Op:
    __slots__ = ("eng", "fn", "reads", "writes", "dma_key", "waits", "signal", "cnt", "n_dma")

    def __init__(self, eng, fn, reads, writes, dma_key, n_dma):
        self.eng = eng
        self.fn = fn
        self.reads = reads
        self.writes = writes
        self.dma_key = dma_key
        self.n_dma = n_dma
        self.waits = []
        self.signal = False
        self.cnt = None


class Sched:
    ENGS = ("pe", "act", "dve", "pool", "sp")

    def __init__(self):
        self.ops = []
        self.last_writer = {}
        self.readers = {}
        self.dma_cum = {}
        self.bank_rd = {}

    def _dep(self, op, d, kind):
        if d is op:
            return
        if d.dma_key is not None:
            op.waits.append(("dma:" + d.dma_key, self.dma_cum[d.dma_key]))
            return
        if d.eng == op.eng and op.dma_key is None:
            if d.eng == "pe" or kind != "RAW":
                return
        d.signal = True
        op.waits.append(("eng:" + d.eng, d))

    @staticmethod
    def _snapshot(fn):
        if fn.__closure__ is None:
            return fn
        cells = []
        for c in fn.__closure__:
            try:
                cells.append(types.CellType(c.cell_contents))
            except ValueError:
                cells.append(c)
        return types.FunctionType(fn.__code__, fn.__globals__, fn.__name__, fn.__defaults__, tuple(cells))

    def fence(self):
        self._nfence = getattr(self, "_nfence", 0) + 1
        key = "__fence%d" % self._nfence
        last = {}
        for o in self.ops:
            last[o.dma_key if o.dma_key is not None else "eng:" + o.eng] = o
        self.readers[key] = list(last.values())
        return key

    @staticmethod
    def _expand(keys):
        out = []
        for k in keys:
            if isinstance(k, str) and len(k) == 3 and k.startswith("ps"):
                out.extend((k + "a", k + "b"))
            else:
                out.append(k)
        return tuple(out)

    def op(self, eng, fn, reads=(), writes=(), dma_key=None, n_dma=1):
        fn = self._snapshot(fn)
        o = Op(eng, fn, self._expand(reads), self._expand(writes), dma_key, n_dma)
        for k in o.reads:
            w = self.last_writer.get(k)
            if w is not None:
                self._dep(o, w, "RAW")
            if isinstance(k, str) and k.startswith("ps") and eng in ("act", "dve"):
                bk = k[:3]
                lr = self.bank_rd.setdefault(bk, {})
                for e2, r in lr.items():
                    if e2 != eng:
                        self._dep(o, r, "XRD")
                lr[eng] = o
        for k in o.writes:
            w = self.last_writer.get(k)
            if w is not None:
                self._dep(o, w, "WAW")
            for r in self.readers.get(k, ()):
                self._dep(o, r, "WAR")
        for k in o.reads:
            self.readers.setdefault(k, []).append(o)
        for k in o.writes:
            self.last_writer[k] = o
            self.readers[k] = []
        if dma_key is not None:
            self.dma_cum[dma_key] = self.dma_cum.get(dma_key, 0) + 16 * n_dma
            o.cnt = self.dma_cum[dma_key]
        self.ops.append(o)
        return o

    def pe(self, fn, reads=(), writes=()):
        return self.op("pe", fn, reads, writes)

    def act(self, fn, reads=(), writes=()):
        return self.op("act", fn, reads, writes)

    def dve(self, fn, reads=(), writes=()):
        return self.op("dve", fn, reads, writes)

    def pool(self, fn, reads=(), writes=()):
        return self.op("pool", fn, reads, writes)

    def pool_or(self, tag, alt, fn, reads=(), writes=()):
        on = os.environ.get("KPOOL", "").split(",")
        return self.op("pool" if tag in on else alt, fn, reads, writes)

    def dma(self, eng, key, fn, reads=(), writes=()):
        return self.op(eng, fn, reads, writes, dma_key=key)

    def emit(self, nc, final_dma_keys=()):
        cnt = {e: 0 for e in self.ENGS}
        for o in self.ops:
            if o.dma_key is None and o.signal:
                cnt[o.eng] += 1
                o.cnt = cnt[o.eng]
        semnames = set()
        for o in self.ops:
            for (s, v) in o.waits:
                semnames.add(s)
            if o.dma_key is not None:
                semnames.add("dma:" + o.dma_key)
            elif o.signal:
                semnames.add("eng:" + o.eng)
        with contextlib.ExitStack() as es:
            sems = {}
            for s in sorted(semnames):
                sems[s] = es.enter_context(nc.semaphore(s.replace(":", "_")))
            block = es.enter_context(nc.Block())
            streams = {e: [o for o in self.ops if o.eng == e] for e in self.ENGS}

            def run(engname, e):
                waited = {}
                for o in streams[engname]:
                    need = {}
                    for (s, v) in o.waits:
                        val = v.cnt if isinstance(v, Op) else v
                        if val > need.get(s, 0):
                            need[s] = val
                    for s, val in need.items():
                        if waited.get(s, 0) >= val:
                            continue
                        e.wait_ge(sems[s], val)
                        waited[s] = val
                    ins = o.fn(e)
                    if o.dma_key is not None:
                        ins.then_inc(sems["dma:" + o.dma_key], 16 * o.n_dma)
                    elif o.signal:
                        ins.then_inc(sems["eng:" + o.eng], 1)
                if engname == "sp":
                    for k in final_dma_keys:
                        e.wait_ge(sems["dma:" + k], self.dma_cum[k])

            @block.tensor
            def _(e):
                run("pe", e)

            @block.scalar
            def _(e):
                run("act", e)

            @block.vector
            def _(e):
                run("dve", e)

            @block.gpsimd
            def _(e):
                run("pool", e)

            @block.sync
            def _(e):
                run("sp", e)


def _box_matrix(n, w):
    pos = np.arange(n)
    lo = np.clip(pos - w // 2, 0, n)
    hi = np.clip(pos + (w - w // 2), 0, n)
    a = np.zeros((n, n), np.float64)
    for t in range(n):
        a[t, lo[t]:hi[t]] = 1.0 / (hi[t] - lo[t])
    return a


def _pool_blocks():
    rows = L // GRID_W
    blocks = []
    seen = {}
    plan = []
    for g, w in enumerate(POOL_WINDOWS):
        a = np.kron(_box_matrix(rows, w), _box_matrix(GRID_W, w)) - np.eye(L)
        at = a.T
        pg = []
        for j in range(4):
            lst = []
            for t in range(NT):
                blk = at[t * 128:(t + 1) * 128, j * 512:(j + 1) * 512]
                if np.any(blk != 0.0):
                    b32 = np.ascontiguousarray(blk.astype(np.float32))
                    key = (g, b32.tobytes())
                    if key not in seen:
                        seen[key] = len(blocks)
                        blocks.append(b32)
                    lst.append((t, seen[key]))
            pg.append(lst)
        plan.append(pg)
    arr = np.stack(blocks, axis=1)
    return np.ascontiguousarray(arr).astype(ml_dtypes.bfloat16), plan


def _rope_tables():
    t = np.arange(L)
    row = (t // GRID_W).astype(np.float32)
    col = (t % GRID_W).astype(np.float32)
    n_freq = DK // 4
    inv_freq = (10000.0 ** (-np.arange(n_freq, dtype=np.float32) / n_freq)).astype(np.float32)
    ang = np.concatenate([row[:, None] * inv_freq, col[:, None] * inv_freq], axis=-1).astype(np.float32)
    cos = np.cos(ang).astype(np.float32).reshape(NT, 128, 32).transpose(1, 0, 2)
    sin = np.sin(ang).astype(np.float32).reshape(NT, 128, 32).transpose(1, 0, 2)
    return np.ascontiguousarray(cos), np.ascontiguousarray(sin)


_POOL_CACHE = None


def _get_pool():
    global _POOL_CACHE
    if _POOL_CACHE is None:
        _POOL_CACHE = _pool_blocks()
    return _POOL_CACHE


CO_COS = 0
CO_SIN = CO_COS + NT * 32
CO_DPOS = CO_SIN + NT * 32
CO_DNEG = CO_DPOS + 128
CO_POSQ = CO_DNEG + 128
CO_SM = CO_POSQ + 128
NCONST = CO_SM + 8

VO_C = 0
VO_NMIX = 16
VO_NFFN = 24
VO_PSC = 32
VO_GNW = 36
VO_BADA = 44
NVEC = VO_BADA + 48

RO_BGM = 0
RO_BGF = 1024
RO_NF = 2048
RO_DEC = 3072
NROW = RO_DEC + 16


def _host_consts():
    cos, sin = _rope_tables()
    c = np.zeros((128, NCONST), np.float32)
    c[:, CO_COS:CO_COS + NT * 32] = cos.reshape(128, -1)
    c[:, CO_SIN:CO_SIN + NT * 32] = sin.reshape(128, -1)
    i = np.arange(128, dtype=np.float32)
    dmat = i[None, :] - i[:, None]
    c[:, CO_DPOS:CO_DPOS + 128] = np.maximum(dmat, 0)
    c[:, CO_DNEG:CO_DNEG + 128] = np.maximum(-dmat, 0)
    c[0:64, CO_POSQ:CO_POSQ + 128] = (i + 1.0)[None, :]
    c[64:128, CO_POSQ:CO_POSQ + 128] = (C - i)[None, :]
    c[:, CO_SM + 0] = C - 1.0 - i
    c[:, CO_SM + 1] = i
    c[:, CO_SM + 2] = LC - 1.0 - i
    c[:, CO_SM + 3] = LC - 1.0 - (i + 128)
    c[:, CO_SM + 4] = i
    c[:, CO_SM + 5] = i + 128
    return c


def build_program(pool_plan, n_pool_blk, stop=None):
    nc = bass.Bass("TRN2", target_bir_lowering=False)
    DBGN = 16384

    def din(name, shape, dt=F32):
        return nc.dram_tensor(name, list(shape), dt, kind="ExternalInput").ap()

    x_d = din("x", [L, D])
    ctx_d = din("ctx", [LC, D])
    vecs_d = din("vecs", [128, NVEC])
    rows_d = din("rows", [1, NROW])
    consts_d = din("consts", [128, NCONST])
    ident_d = din("ident", [128, 128], BF16)
    wada_d = din("wada", [128, 8, 6 * D])
    wu_d = din("wu", [128, 8, 512])
    wpairs_d = din("wpairs", [4, 128, 8, 768])
    wtail_d = din("wtail", [8, 128, 28, 128])
    wpool_d = din("wpool", [128, 4, 128])
    wo_d = din("wo", [8, 128, D])
    w13_d = din("w13", [NFF, 128, 2, 8, 128])
    w2_d = din("w2", [NFF, 128, D])
    pblk_d = din("pblk", [128, n_pool_blk, 512], BF16)
    out_d = nc.dram_tensor("out", [L, D], F32, kind="ExternalOutput").ap()
    dbg_d = nc.dram_tensor("dbg", [128, DBGN], F32, kind="ExternalOutput").ap() if stop is not None else None

    S = Sched()
    SB_BYTES = 207 * 1024
    arena = nc.alloc_sbuf_tensor("arena", [128, SB_BYTES // 2], BF16).ap()
    ps = nc.alloc_psum_tensor("ps", [128, 4096], F32).ap()

    class Region:
        def __init__(self, start, size):
            self.start = start
            self.size = size
            self.off = 0

        def carve(self, nbytes, dt=BF16):
            want = nbytes
            nbytes = (nbytes + 31) // 32 * 32
            assert self.off + nbytes <= self.size, (self.off, nbytes, self.size)
            a = (self.start + self.off) // 2
            self.off += nbytes
            v = arena[:, a:a + want // 2]
            return v if dt == BF16 else v.bitcast(dt)

        def reset(self):
            self.off = 0

    KB = 1024
    R_PERS = Region(0, 32 * KB)
    R_X = Region(32 * KB, 64 * KB)
    R_M = Region(96 * KB, 32 * KB)
    R_PAIR = Region(128 * KB, 63 * KB)
    R_YP = Region(191 * KB, 16 * KB)
    assert 207 * KB <= SB_BYTES

    def bank(b, dt=F32):
        v = ps[:, b * 512:(b + 1) * 512]
        return v if dt == F32 else v.bitcast(dt)

    def PK(b):
        return "ps%d" % b

    def finish_debug(items):
        off = 0
        fk = S.fence()
        for ap, n, in items:
            for c0 in range(0, n, 1024):
                c1 = min(n, c0 + 1024)
                S.dma("pool", "dbg", lambda e, ap=ap, off=off, c0=c0, c1=c1: e.dma_start(
                    out=dbg_d[:, off + c0:off + c1], in_=ap[:, c0:c1]), writes=[fk])
            off += n
        S.dma("sp", "out", lambda e: e.dma_start(out=out_d[0:128, :], in_=gm_bc), writes=[fk])
        S.emit(nc, final_dma_keys=["out", "dbg"])
        return nc

    vecs = R_PERS.carve(NVEC * 4, F32)
    consts = R_PERS.carve(NCONST * 4, F32)
    ident = R_PERS.carve(256)
    decbc = R_PERS.carve(64, F32)
    lgbc = R_PERS.carve(64, F32)
    lgsel = R_PERS.carve(32, F32)
    cdec = R_PERS.carve(32, F32)
    kdec = R_PERS.carve(64, F32)
    ctxw = R_PERS.carve(128, F32)
    modT = R_PERS.carve(48 * 2 * 4, F32)
    modAB = R_PERS.carve(6 * 8 * 4, F32)
    scT = R_PERS.carve(8 * 2 * 2)
    scbc = R_PERS.carve(8 * 128 * 2)
    qdec = R_PERS.carve(8 * 128 * 4, F32)
    mask = R_PERS.carve(8 * 128 * 4, F32)
    gm_bc = R_PERS.carve(D * 4, F32)
    gf_bc = R_PERS.carve(D * 4, F32)
    hcT = R_PERS.carve(8 * LC * 2)
    nf_bc = hcT.bitcast(F32)
    wpool = R_PERS.carve(4 * 128 * 2)
    stat = R_PERS.carve(64 * 4, F32)
    epsb = R_PERS.carve(32, F32)
    gnw128 = R_PERS.carve(32, F32)
    sfp = R_PERS.carve(384 * 4, F32)

    modT3 = modT.rearrange("p (j t) -> p j t", t=2)
    modAB3 = modAB.rearrange("p (a k) -> p a k", k=8)
    scT3 = scT.rearrange("p (k t) -> p k t", t=2)
    scbc3 = scbc.rearrange("p (k m) -> p k m", m=128)
    qdec3 = qdec.rearrange("p (h t) -> p h t", t=128)
    mask3 = mask.rearrange("p (h t) -> p h t", t=128)
    hcT3 = hcT.rearrange("p (k t) -> p k t", t=LC)
    wpool3 = wpool.rearrange("p (g n) -> p g n", n=128)
    kdec3 = kdec.rearrange("p (d h) -> p d h", h=8)
    ctxw4 = ctxw.rearrange("p (t d h) -> p t d h", d=2, h=8)
    cos3 = consts[:, CO_COS:CO_COS + NT * 32].rearrange("p (i f) -> p i f", f=32)
    sin3 = consts[:, CO_SIN:CO_SIN + NT * 32].rearrange("p (i f) -> p i f", f=32)
    dpos = consts[:, CO_DPOS:CO_DPOS + 128]
    dneg = consts[:, CO_DNEG:CO_DNEG + 128]
    posq = consts[:, CO_POSQ:CO_POSQ + 128]

    def csm(i):
        return consts[:, CO_SM + i:CO_SM + i + 1]

    A_M, B_M, A_C, B_C, A_F, B_F = range(6)

    hT = R_X.carve(32 * KB).rearrange("p (k t) -> p k t", t=L)
    ygT = R_X.carve(32 * KB).rearrange("p (k t) -> p k t", t=L)
    R_X.reset()
    x1 = R_X.carve(64 * KB, F32).rearrange("p (i f) -> p i f", f=D)
    mT = R_M.carve(32 * KB).rearrange("p (k t) -> p k t", t=L)
    R_M.reset()
    _ux = R_X.start + 32 * KB
    u_tok = arena[:, _ux // 2:(_ux + 16 * KB) // 2].rearrange("p (i c) -> p i c", c=512)
    dT_sb = [arena[:, (_ux + (16 + i) * KB) // 2:(_ux + (17 + i) * KB) // 2] for i in range(2)]
    R_M.reset()
    wpair_buf = [R_M.carve(12 * KB).rearrange("p (k n) -> p k n", n=768) for _ in range(2)]
    wada_buf = [arena[:, (R_M.start + i * 8 * KB) // 2:(R_M.start + (i + 1) * 8 * KB) // 2].rearrange("p (k n) -> p k n", n=512)
                for i in range(4)]
    wada_buf += [arena[:, (R_YP.start + i * 8 * KB) // 2:(R_YP.start + (i + 1) * 8 * KB) // 2].rearrange("p (k n) -> p k n", n=512)
                 for i in range(2)]
    ypT = R_YP.carve(16 * KB).rearrange("p (g t) -> p g t", t=L)

    S.dma("sp", "c0", lambda e: e.dma_start(out=vecs, in_=vecs_d), writes=["vecs"])
    S.dma("sp", "c1", lambda e: e.dma_start(out=consts, in_=consts_d), writes=["consts"])
    S.dma("sp", "c2", lambda e: e.dma_start(out=ident, in_=ident_d), writes=["ident"])
    S.dma("sp", "c3", lambda e: e.dma_start(out=decbc, in_=rows_d[0:1, RO_DEC:RO_DEC + 16].to_broadcast([128, 16])),
          writes=["decbc"])
    S.dma("sp", "c4", lambda e: e.dma_start(out=gm_bc, in_=rows_d[0:1, RO_BGM:RO_BGM + D].to_broadcast([128, D])),
          writes=["gm_bc"])
    S.dma("sp", "c5", lambda e: e.dma_start(out=gf_bc, in_=rows_d[0:1, RO_BGF:RO_BGF + D].to_broadcast([128, D])),
          writes=["gf_bc"])
    S.dma("pool", "wpool", lambda e: e.dma_start(out=wpool3, in_=wpool_d), writes=["wpool"])

    S.dve(lambda e: e.memset(epsb, float(DV * DV) * EPS), writes=["epsb"])
    S.dve(lambda e: e.tensor_scalar(out=gnw128, in0=vecs[:, VO_GNW:VO_GNW + 8], scalar1=float(DV), scalar2=None, op0=ALU.mult),
          reads=["vecs"], writes=["gnw128"])
    cv3 = vecs[:, VO_C:VO_C + 16].rearrange("p (k t) -> p k t", t=2)
    S.act(lambda e: e.activation(out=scT3, in_=cv3, func=AF.Silu), reads=["vecs"], writes=["scT"])
    S.act(lambda e: e.activation(out=scbc3, in_=cv3[:, :, 0:1].to_broadcast([128, 8, 128]), func=AF.Silu),
          reads=["vecs"], writes=["scbc"])

    def wada_group(gi):
        buf = wada_buf[gi % 6]
        bk = "wada%d" % (gi % 6)
        S.dma("pool", bk, lambda e, buf=buf, gi=gi: e.dma_start(out=buf, in_=wada_d[:, :, gi * 512:(gi + 1) * 512]),
              writes=[bk])
        return buf, bk

    def wada_compute(gi, buf, bk):
        if gi in (4, 5, 10, 11):
            dst = gm_bc if gi in (4, 5) else gf_bc
            dk = "gm_bc" if gi in (4, 5) else "gf_bc"
            half = gi % 2 if gi in (4, 5) else (gi - 10)
            pb = 5 + (gi % 2)
            for k in range(8):
                S.pe(lambda e, k=k, buf=buf, pb=pb: e.matmul(bank(pb), lhsT=scbc3[:, k, :], rhs=buf[:, k, :],
                                                              start=(k == 0), stop=(k == 7)),
                     reads=[bk, "scbc"], writes=[PK(pb)])
            S.dve(lambda e, dst=dst, half=half, pb=pb: e.tensor_tensor(
                out=dst[:, half * 512:(half + 1) * 512], in0=bank(pb), in1=dst[:, half * 512:(half + 1) * 512], op=ALU.add),
                reads=[PK(pb), dk], writes=[dk])
        else:
            for jj in range(4):
                j = gi * 4 + jj
                for k in range(8):
                    S.pe(lambda e, k=k, jj=jj, j=j, buf=buf: e.matmul(
                        bank(7)[:, 2 * j:2 * j + 2], lhsT=buf[:, k, jj * 128:(jj + 1) * 128], rhs=scT3[:, k, :],
                        start=(k == 0), stop=(k == 7)),
                        reads=[bk, "scT"], writes=[PK(7)])

    def modT_evac(j0, j1):
        S.dve(lambda e, j0=j0, j1=j1: e.tensor_tensor(
            out=modT3[:, j0:j1, :], in0=bank(7)[:, 2 * j0:2 * j1].rearrange("p (j t) -> p j t", t=2),
            in1=vecs[:, VO_BADA + j0:VO_BADA + j1].unsqueeze(2).to_broadcast([128, j1 - j0, 2]), op=ALU.add),
            reads=[PK(7), "vecs"], writes=["modT"])

    def mod_ab(ai, bi, nvo, sh_blk, sc_blk, col):
        S.dve(lambda e: e.scalar_tensor_tensor(out=modAB3[:, ai, :], in0=modT3[:, sc_blk:sc_blk + 8, col], scalar=1.0,
                                               in1=vecs[:, nvo:nvo + 8], op0=ALU.add, op1=ALU.mult),
              reads=["modT", "vecs"], writes=["modAB%d" % ai])
        S.dve(lambda e: e.tensor_copy(out=modAB3[:, bi, :], in_=modT3[:, sh_blk:sh_blk + 8, col]),
              reads=["modT"], writes=["modAB%d" % bi])

    wg = [wada_group(gi) for gi in range(4)]

    def setup_mod_mix():
        for gi in range(4):
            wada_compute(gi, *wg[gi])
        modT_evac(0, 16)
        mod_ab(A_M, B_M, VO_NMIX, 0, 8, 0)
        mod_ab(A_C, B_C, VO_NMIX, 0, 8, 1)

    if stop == 0:
        setup_mod_mix()

    ee = stat[:, 0:16]
    tt = stat[:, 16:32]
    S.act(lambda e: e.activation(out=ee, in_=decbc, func=AF.Exp, scale=-1.0), reads=["decbc"], writes=["ee"])
    S.dve(lambda e: e.tensor_scalar(out=tt, in0=ee, scalar1=-1.0 / 7, scalar2=1.0 / 6, op0=ALU.mult, op1=ALU.add),
          reads=["ee"], writes=["tt"])
    for cf in (1.0 / 5, 1.0 / 4, 1.0 / 3, 1.0 / 2, 1.0):
        S.dve(lambda e: e.tensor_tensor(out=tt, in0=tt, in1=ee, op=ALU.mult), reads=["tt", "ee"], writes=["tt"])
        S.dve(lambda e, cf=cf: e.tensor_scalar(out=tt, in0=tt, scalar1=-1.0, scalar2=cf, op0=ALU.mult, op1=ALU.add),
              reads=["tt"], writes=["tt"])
    S.dve(lambda e: e.scalar_tensor_tensor(out=lgbc, in0=tt, scalar=-1.0, in1=ee, op0=ALU.mult, op1=ALU.mult),
          reads=["tt", "ee"], writes=["lgbc"])
    S.dve(lambda e: e.tensor_copy(out=lgsel[0:64, :], in_=lgbc[0:64, 0:8]), reads=["lgbc"], writes=["lgsel"])
    S.dve(lambda e: e.tensor_copy(out=lgsel[64:128, :], in_=lgbc[64:128, 8:16]), reads=["lgbc"], writes=["lgsel"])
    S.act(lambda e: e.activation(out=cdec, in_=lgsel, func=AF.Exp, scale=float(C)), reads=["lgsel"], writes=["cdec"])
    for d in range(2):
        S.act(lambda e, d=d: e.activation(out=kdec3[:, d, :], in_=lgbc[:, d * 8:(d + 1) * 8], func=AF.Exp, scale=csm(d)),
              reads=["lgbc", "consts"], writes=["kdec"])
        for t in range(2):
            S.act(lambda e, d=d, t=t: e.activation(out=ctxw4[:, t, d, :], in_=lgbc[:, d * 8:(d + 1) * 8], func=AF.Exp,
                                                   scale=csm(2 + 2 * d + t)),
                  reads=["lgbc", "consts"], writes=["ctxw"])
    S.dve(lambda e: e.tensor_scalar(out=kdec, in0=kdec, scalar1=K_SCALE, scalar2=None, op0=ALU.mult),
          reads=["kdec"], writes=["kdec"])
    S.dve(lambda e: e.tensor_scalar(out=ctxw, in0=ctxw, scalar1=K_SCALE, scalar2=None, op0=ALU.mult),
          reads=["ctxw"], writes=["ctxw"])
    for h in range(H):
        S.act(lambda e, h=h: e.activation(out=qdec3[:, h, :], in_=posq, func=AF.Exp, scale=lgsel[:, h:h + 1]),
              reads=["lgsel", "consts"], writes=["qdec"])
        S.dve(lambda e, h=h: e.tensor_scalar(out=mask3[:, h, :], in0=dpos, scalar1=lgbc[:, h:h + 1], scalar2=None,
                                             op0=ALU.mult), reads=["lgbc", "consts"], writes=["mask"])
        S.dve(lambda e, h=h: e.scalar_tensor_tensor(out=mask3[:, h, :], in0=dneg, scalar=lgbc[:, 8 + h:9 + h],
                                                    in1=mask3[:, h, :], op0=ALU.mult, op1=ALU.add),
              reads=["lgbc", "consts", "mask"], writes=["mask"])
    S.act(lambda e: e.activation(out=mask, in_=mask, func=AF.Exp), reads=["mask"], writes=["mask"])
    S.dve(lambda e: e.tensor_scalar(out=mask, in0=mask, scalar1=K_SCALE, scalar2=None, op0=ALU.mult),
          reads=["mask"], writes=["mask"])

    if stop == 0:
        for gi in range(4, 12):
            wg.append(wada_group(gi))
            wada_compute(gi, *wg[gi])
        modT_evac(24, 40)
        mod_ab(A_F, B_F, VO_NFFN, 24, 32, 0)
        return finish_debug([(modAB, 48), (lgbc, 16), (mask, 1024), (qdec, 1024), (gm_bc, 1024), (gf_bc, 1024),
                             (kdec, 16), (ctxw, 32), (cdec, 8)])
    R_PAIR.reset()
    xbuf = [R_PAIR.carve(4 * KB, F32) for _ in range(3)]
    junk = R_PAIR.carve(2 * KB)
    xnb = [R_PAIR.carve(2 * KB) for _ in range(2)]
    nctr = [0]

    def norm_stats(src_ap, src_key, junk, xnb):
        n = nctr[0]
        nctr[0] += 1
        ss = stat[:, 32 + (n % 4) * 2:33 + (n % 4) * 2]
        rs = stat[:, 33 + (n % 4) * 2:34 + (n % 4) * 2]
        sk = "nstat%d" % (n % 4)
        xn = xnb[n % len(xnb)]
        xk = "xn%d_%d" % (len(xnb), n % len(xnb))
        S.act(lambda e: e.activation(out=junk, in_=src_ap, func=AF.Square, accum_out=ss),
              reads=[src_key], writes=["junk", sk])
        S.dve(lambda e: e.tensor_scalar(out=rs, in0=ss, scalar1=1.0 / D, scalar2=EPS, op0=ALU.mult, op1=ALU.add),
              reads=[sk], writes=[sk + "r"])
        S.act(lambda e: e.activation(out=rs, in_=rs, func=AF.Sqrt), reads=[sk + "r"], writes=[sk + "r"])
        S.dve(lambda e: e.reciprocal(out=rs, in_=rs), reads=[sk + "r"], writes=[sk + "r"])
        S.dve(lambda e: e.tensor_scalar(out=xn, in0=src_ap, scalar1=rs, scalar2=None, op0=ALU.mult),
              reads=[src_key, sk + "r"], writes=[xk])
        return (n, xn, xk)

    ACT_K = (1, 4, 6)

    def norm_tr(st, ai, bi, dst3, dst_key, tcol, pbase=0, fused=True):
        n, xn, xk = st
        par = n % 2
        pbD, pbA = pbase + 2 * par, pbase + 2 * par + 1
        pTD = bank(pbD, BF16)[:, 0:640].rearrange("p (k t) -> p k t", t=128)
        pTA = bank(pbA, BF16)[:, 0:384].rearrange("p (k t) -> p k t", t=128)
        slot = {}
        na = nd = 0
        for k in range(8):
            if k in ACT_K:
                slot[k] = (pTA, pbA, na)
                na += 1
            else:
                slot[k] = (pTD, pbD, nd)
                nd += 1
        for k in range(8):
            pt_, pbk, sl = slot[k]
            S.pe(lambda e, k=k, pt_=pt_, sl=sl: e.transpose(out=pt_[:, sl, :], in_=xn[:, k * 128:(k + 1) * 128], identity=ident),
                 reads=[xk, "ident"], writes=[PK(pbk)])
        for k in range(8):
            o = dst3[:, k, tcol * 128:(tcol + 1) * 128]
            pt_, pbk, sl = slot[k]
            if not fused:
                if k not in ACT_K:
                    S.dve(lambda e, o=o, pt_=pt_, sl=sl: e.tensor_copy(out=o, in_=pt_[:, sl, :]),
                          reads=[PK(pbk)], writes=[dst_key(k, tcol)])
                else:
                    S.act(lambda e, o=o, pt_=pt_, sl=sl: e.activation(out=o, in_=pt_[:, sl, :], func=AF.Copy),
                          reads=[PK(pbk)], writes=[dst_key(k, tcol)])
            elif k not in ACT_K:
                S.dve(lambda e, k=k, o=o, pt_=pt_, sl=sl: e.tensor_scalar(out=o, in0=pt_[:, sl, :], scalar1=modAB3[:, ai, k:k + 1],
                                                                          scalar2=modAB3[:, bi, k:k + 1], op0=ALU.mult, op1=ALU.add),
                      reads=[PK(pbk), "modAB%d" % ai, "modAB%d" % bi], writes=[dst_key(k, tcol)])
            else:
                S.act(lambda e, k=k, o=o, pt_=pt_, sl=sl: e.activation(out=o, in_=pt_[:, sl, :], func=AF.Identity,
                                                                       scale=modAB3[:, ai, k:k + 1], bias=modAB3[:, bi, k:k + 1]),
                      reads=[PK(pbk), "modAB%d" % ai, "modAB%d" % bi], writes=[dst_key(k, tcol)])

    def hT_key(k, i):
        return "XA_%d_%d" % (k, i)

    pendA = []
    xnb = xnb + [arena[:, R_YP.start // 2:(R_YP.start + 2 * KB) // 2]]
    xbuf = xbuf + [arena[:, (R_YP.start + (4 + 4 * i) * KB) // 2:(R_YP.start + (8 + 4 * i) * KB) // 2].bitcast(F32) for i in range(2)]
    for t in range(2 + NT):
        sbi = t % len(xbuf)
        xb = xbuf[sbi]
        if t < 2:
            S.dma("sp", "xb%d" % sbi, lambda e, xb=xb, t=t: e.dma_start(out=xb, in_=ctx_d[t * 128:(t + 1) * 128, :]),
                  writes=["xb%d" % sbi])
            args = (A_C, B_C, hcT3, (lambda k, tc: "hcT%d" % k), t)
        else:
            i = t - 2
            S.dma("sp", "xb%d" % sbi, lambda e, xb=xb, i=i: e.dma_start(out=xb, in_=x_d[i * 128:(i + 1) * 128, :]),
                  writes=["xb%d" % sbi])
            args = (A_M, B_M, hT, hT_key, i)
        st = norm_stats(xb, "xb%d" % sbi, junk, xnb)
        pendA.append((st,) + args)
        if len(pendA) > 2:
            norm_tr(*pendA.pop(0), fused=False)
    while pendA:
        norm_tr(*pendA.pop(0), fused=False)
    setup_mod_mix()
    for hf in range(2):
        for k in range(8):
            S.dve(lambda e, k=k, hf=hf: e.tensor_scalar(out=hT[:, k, hf * 1024:(hf + 1) * 1024], in0=hT[:, k, hf * 1024:(hf + 1) * 1024],
                                                        scalar1=modAB3[:, A_M, k:k + 1], scalar2=modAB3[:, B_M, k:k + 1],
                                                        op0=ALU.mult, op1=ALU.add),
                  reads=[hT_key(k, i) for i in range(hf * 8, hf * 8 + 8)] + ["modAB%d" % A_M, "modAB%d" % B_M],
                  writes=[hT_key(k, i) for i in range(hf * 8, hf * 8 + 8)])
    for k in range(8):
        S.dve(lambda e, k=k: e.tensor_scalar(out=hcT3[:, k, :], in0=hcT3[:, k, :], scalar1=modAB3[:, A_C, k:k + 1],
                                             scalar2=modAB3[:, B_C, k:k + 1], op0=ALU.mult, op1=ALU.add),
              reads=["hcT%d" % k, "modAB%d" % A_C, "modAB%d" % B_C], writes=["hcT%d" % k])
    wlate = arena[:, (R_M.start + 24 * KB) // 2:(R_M.start + 32 * KB) // 2].rearrange("p (k n) -> p k n", n=512)
    WL_ALIAS = ["ysb0", "ysb1", "ysb2", "ysq"]

    def wada_late_load(gi):
        S.dma("pool", "wlate", lambda e: e.dma_start(out=wlate, in_=wada_d[:, :, gi * 512:(gi + 1) * 512]),
              writes=["wlate"] + WL_ALIAS)

    def wada_late_compute(gi):
        rd = ["wlate"] + WL_ALIAS
        if gi in (4, 5, 10, 11):
            dst = gm_bc if gi in (4, 5) else gf_bc
            dk = "gm_bc" if gi in (4, 5) else "gf_bc"
            half = gi % 2 if gi in (4, 5) else (gi - 10)
            for k in range(8):
                S.pe(lambda e, k=k: e.matmul(bank(7), lhsT=scbc3[:, k, :], rhs=wlate[:, k, :], start=(k == 0), stop=(k == 7)),
                     reads=rd + ["scbc"], writes=[PK(7)])
            S.dve(lambda e: e.tensor_tensor(out=dst[:, half * 512:(half + 1) * 512], in0=bank(7),
                                            in1=dst[:, half * 512:(half + 1) * 512], op=ALU.add),
                  reads=[PK(7), dk], writes=[dk])
        else:
            for jj in range(4):
                j = gi * 4 + jj
                for k in range(8):
                    S.pe(lambda e, k=k, jj=jj, j=j: e.matmul(bank(7)[:, 2 * j:2 * j + 2], lhsT=wlate[:, k, jj * 128:(jj + 1) * 128],
                                                             rhs=scT3[:, k, :], start=(k == 0), stop=(k == 7)),
                         reads=rd + ["scT"], writes=[PK(7)])
            modT_evac(gi * 4, gi * 4 + 4)
    hcT_keys = ["hcT%d" % k for k in range(8)]

    if stop == 1:
        return finish_debug([(hcT, 2048), (hT.rearrange("p k t -> p (k t)")[:, 0:8192], 8192)])

    def load_pair(p):
        buf = wpair_buf[p % 2]
        extra = [S.fence()] if p < 2 else []
        S.dma("pool", "wpair%d" % (p % 2), lambda e, buf=buf, p=p: e.dma_start(out=buf, in_=wpairs_d[p]),
              writes=["wpair%d" % (p % 2)] + extra)

    wu_sb = R_PAIR.carve(8 * KB).rearrange("p (k n) -> p k n", n=512)
    NPST = 9
    pstage = [R_PAIR.carve(4 * KB).rearrange("p (b t) -> p b t", t=512) for _ in range(NPST)]
    S.dma("pool", "wu", lambda e: e.dma_start(out=wu_sb, in_=wu_d), writes=["wu"])
    load_pair(0)
    for i in range(NT):
        pb = 2 + (i % 2)
        for k in range(8):
            S.pe(lambda e, k=k, i=i, pb=pb: e.matmul(bank(pb), lhsT=hT[:, k, i * 128:(i + 1) * 128], rhs=wu_sb[:, k, :],
                                                     start=(k == 0), stop=(k == 7)),
                 reads=[hT_key(k, i), "wu"], writes=[PK(pb)])
        if i % 2 == 0:
            S.act(lambda e, i=i, pb=pb: e.activation(out=u_tok[:, i, :], in_=bank(pb), func=AF.Copy),
                  reads=[PK(pb)], writes=["u%d" % i])
        else:
            S.dve(lambda e, i=i, pb=pb: e.tensor_copy(out=u_tok[:, i, :], in_=bank(pb)),
                  reads=[PK(pb)], writes=["u%d" % i])
    pst_n = [0]
    resident = {}

    def pool_head(pctr, g, j):
        lst = pool_plan[g][j]
        pd = 4 + (pctr % 2)
        dsb = dT_sb[pctr % 2]
        dk = "dT%d" % (pctr % 2)
        runs = []
        for ent in lst:
            if runs and len(runs[-1]) < 4 and runs[-1][-1][1] + 1 == ent[1]:
                runs[-1].append(ent)
            else:
                runs.append([ent])
        pos = 0
        for grp in runs:
            b0 = grp[0][1]
            nb = len(grp)
            assert all(grp[q][1] == b0 + q for q in range(nb))
            hit = resident.get((b0, nb))
            if hit is not None and pst_n[0] - hit < NPST:
                st = pstage[hit % NPST]
                sk = "pst%d" % (hit % NPST)
            else:
                st = pstage[pst_n[0] % NPST]
                sk = "pst%d" % (pst_n[0] % NPST)
                resident[(b0, nb)] = pst_n[0]
                pst_n[0] += 1
                S.dma("sp", sk, lambda e: e.dma_start(out=st[:, 0:nb, :], in_=pblk_d[:, b0:b0 + nb, :]), writes=[sk])
            for q, (t, bi_) in enumerate(grp):
                first = (pos == 0)
                last = (pos == len(lst) - 1)
                pos += 1
                S.pe(lambda e, t=t, q=q, first=first, last=last: e.matmul(
                    bank(pd), lhsT=u_tok[:, t, g * 128:(g + 1) * 128], rhs=st[:, q, :], start=first, stop=last),
                    reads=["u%d" % t, sk], writes=[PK(pd)])
        S.dve(lambda e: e.tensor_copy(out=dsb, in_=bank(pd)), reads=[PK(pd)], writes=[dk])

    def pool_tail(pctr, g, j):
        py = 6 + (pctr % 2)
        dsb = dT_sb[pctr % 2]
        dk = "dT%d" % (pctr % 2)
        S.pe(lambda e: e.matmul(bank(py), lhsT=wpool3[:, g, :], rhs=dsb, start=True, stop=True),
             reads=[dk, "wpool"], writes=[PK(py)])
        S.act(lambda e: e.activation(out=ypT[:, g, j * 512:(j + 1) * 512], in_=bank(py), func=AF.Copy,
                                     scale=vecs[:, VO_PSC + g:VO_PSC + g + 1]),
              reads=[PK(py), "vecs"], writes=["ypT"])

    pblocks = [(g, j) for g in range(4) for j in range(4)]
    pool_head(0, *pblocks[0])
    for c_ in range(len(pblocks)):
        if c_ + 1 < len(pblocks):
            pool_head(c_ + 1, *pblocks[c_ + 1])
        pool_tail(c_, *pblocks[c_])

    if stop == 2:
        return finish_debug([(ypT.rearrange("p g t -> p (g t)"), 8192)])
    R_PAIR.reset()
    kT2 = R_PAIR.carve(4 * KB)
    qTz = R_PAIR.carve(8 * KB).rearrange("p (n h t) -> p n h t", h=2, t=128)
    zf = S.fence()
    S.dve(lambda e: e.memset(qTz[64:128, :, 0, :], 0.0), writes=["qTz_zero", zf])
    S.dve(lambda e: e.memset(qTz[0:64, :, 1, :], 0.0), writes=["qTz_zero", zf])
    q2 = R_PAIR.carve(8 * KB).rearrange("p (h t) -> p h t", t=L)
    kfb = R_PAIR.carve(8 * KB).rearrange("p (i d c) -> p i d c", d=2, c=128)
    v_tok = R_PAIR.carve(8 * KB).rearrange("p (i c) -> p i c", c=256)
    sg_tok = R_PAIR.carve(8 * KB).rearrange("p (i c) -> p i c", c=256)
    s_all = R_PAIR.carve(8 * KB).rearrange("p (n h v) -> p n h v", h=2, v=128)
    rot = [R_PAIR.carve(1 * KB).rearrange("p (t a c) -> p t a c", t=2, c=64) for _ in range(2)]
    qdup = R_PAIR.carve(1 * KB).rearrange("p (t a u c) -> p t a u c", t=2, u=2, c=64)
    _rt0 = R_PAIR.off
    rtmp = [R_PAIR.carve(1 * KB, F32).rearrange("p (t a f) -> p t a f", t=2, f=32) for _ in range(4)]
    _rt1 = R_PAIR.off
    R_PAIR.off = _rt0
    kcw = R_PAIR.carve(1 * KB).rearrange("p (t d c) -> p t d c", d=2, c=128)
    vc = R_PAIR.carve(1 * KB).rearrange("p (t c) -> p t c", c=256)
    R_PAIR.off = _rt1
    pmat = [R_PAIR.carve(1 * KB).rearrange("p (c h t) -> p c h t", c=2, t=128) for _ in range(2)]
    ygb = [R_PAIR.carve(1 * KB).rearrange("p (c h v) -> p c h v", c=2, v=128) for _ in range(2)]
    _rm = R_M.start + 24 * KB
    y_sb = [arena[:, (_rm + i * 2 * KB) // 2:(_rm + (i + 1) * 2 * KB) // 2].bitcast(F32).rearrange("p (c h v) -> p c h v", c=2, v=128)
            for i in range(3)]
    ysq = arena[:, (_rm + 6 * KB) // 2:(_rm + 8 * KB) // 2].bitcast(F32).rearrange("p (c h v) -> p c h v", c=2, v=128)
    sfp3 = sfp[:, 0:256].rearrange("p (h v) -> p h v", v=128)
    gst = sfp[:, 256:384]

    def ygT_key(k, i):
        return "XB_%d_%d" % (k, i)

    def make_pair(p):
        wp = wpair_buf[p % 2]
        wk = "wpair%d" % (p % 2)

        def ctx_pe():
            if 1 <= p + 1 < 4:
                load_pair(p + 1)
            for t in range(2):
                pb = 6 + t
                for k in range(8):
                    S.pe(lambda e, k=k, t=t, pb=pb: e.matmul(bank(pb)[:, 0:384], lhsT=hcT3[:, k, t * 128:(t + 1) * 128],
                                                             rhs=wp[:, k, 128:512], start=(k == 0), stop=(k == 7)),
                         reads=["hcT%d" % k, wk], writes=[PK(pb)])

        def ctx_rest():
            for t in range(2):
                pb = 6 + t
                for d in range(2):
                    S.dve(lambda e, t=t, d=d, pb=pb: e.tensor_tensor(
                        out=kcw[:, t, d, :].rearrange("p (h c) -> p h c", c=64),
                        in0=bank(pb)[:, 0:128].rearrange("p (h c) -> p h c", c=64),
                        in1=ctxw4[:, t, d, 2 * p:2 * p + 2].unsqueeze(2).to_broadcast([128, 2, 64]), op=ALU.mult),
                        reads=[PK(pb), "ctxw"], writes=["kcw"])
                S.dve(lambda e, t=t, pb=pb: e.tensor_copy(out=vc[:, t, :], in_=bank(pb)[:, 128:384]),
                      reads=[PK(pb)], writes=["vc"])
            pS = bank(1)[:, 0:256].rearrange("p (h v) -> p h v", v=128)
            for h2 in range(2):
                for d in range(2):
                    for t in range(2):
                        S.pe(lambda e, h2=h2, d=d, t=t: e.matmul(pS[d * 64:(d + 1) * 64, h2, :],
                                                                 lhsT=kcw[:, t, d, h2 * 64:(h2 + 1) * 64],
                                                                 rhs=vc[:, t, h2 * 128:(h2 + 1) * 128],
                                                                 start=(t == 0), stop=(t == 1)),
                             reads=["kcw", "vc"], writes=["ps1a", "ps1b"])
            S.dve(lambda e: e.tensor_copy(out=sfp3, in_=pS), reads=["ps1a", "ps1b"], writes=["sfp"])
            S.dve(lambda e: e.tensor_copy(out=s_all[0:64, 0, :, :], in_=pS[0:64, :, :]),
                  reads=["ps1a", "ps1b"], writes=["sall_f0"])
            S.dve(lambda e: e.tensor_copy(out=s_all[64:128, NT - 1, :, :], in_=pS[64:128, :, :]),
                  reads=["ps1a", "ps1b"], writes=["sall_b%d" % (NT - 1)])


        def sb_proj(it, i, pe_only=False):
            par = it % 2
            pq = 2 + par
            for t in range(2):
                for k in range(8):
                    S.pe(lambda e, k=k, t=t: e.matmul(bank(pq)[:, t * 256:(t + 1) * 256], lhsT=hT[:, k, (i + t) * 128:(i + t + 1) * 128],
                                                      rhs=wp[:, k, 0:256], start=(k == 0), stop=(k == 7)),
                         reads=[hT_key(k, i + t), wk], writes=[PK(pq)])
            for t in range(2):
                for k in range(8):
                    S.pe(lambda e, k=k, t=t: e.matmul(bank(4 + t), lhsT=hT[:, k, (i + t) * 128:(i + t + 1) * 128],
                                                      rhs=wp[:, k, 256:768], start=(k == 0), stop=(k == 7)),
                         reads=[hT_key(k, i + t), wk], writes=[PK(4 + t)])
            if not pe_only:
                sb_proj_evac(it, i)

        def sb_proj_evac(it, i):
            vg = ps[:, 4 * 512:6 * 512].rearrange("p (t c) -> p t c", t=2)
            S.act(lambda e: e.activation(out=v_tok[:, i:i + 2, :], in_=vg[:, :, 0:256], func=AF.Copy),
                  reads=[PK(4), PK(5)], writes=["v%d" % i, "v%d" % (i + 1)])
            S.act(lambda e: e.activation(out=sg_tok[:, i:i + 2, :], in_=vg[:, :, 256:512], func=AF.Silu),
                  reads=[PK(4), PK(5)], writes=["sg%d" % i, "sg%d" % (i + 1)])

        def sb_rope(it, i):
            par = it % 2
            pq = 2 + par
            qk4 = bank(pq).rearrange("p (t a c) -> p t a c", t=2, c=64)
            t1 = qk4[:, :, :, 0:32]
            t2 = qk4[:, :, :, 32:64]
            cs = cos3[:, i:i + 2, :].unsqueeze(2).to_broadcast([128, 2, 4, 32])
            sn = sin3[:, i:i + 2, :].unsqueeze(2).to_broadcast([128, 2, 4, 32])
            rt = rot[par]
            rk = "rot%d" % par
            ta, tb_, tc_, td = rtmp
            S.dve(lambda e: e.tensor_tensor(out=ta, in0=t1, in1=cs, op=ALU.mult), reads=[PK(pq), "consts"], writes=["rta", "kcw", "vc"])
            S.dve(lambda e: e.tensor_tensor(out=tb_, in0=t2, in1=sn, op=ALU.mult), reads=[PK(pq), "consts"], writes=["rtb", "kcw", "vc"])
            S.dve(lambda e: e.tensor_tensor(out=tc_, in0=t1, in1=sn, op=ALU.mult), reads=[PK(pq), "consts"], writes=["rtc", "kcw", "vc"])
            S.dve(lambda e: e.tensor_tensor(out=td, in0=t2, in1=cs, op=ALU.mult), reads=[PK(pq), "consts"], writes=["rtd", "kcw", "vc"])
            S.dve(lambda e: e.tensor_tensor(out=rt[:, :, :, 0:32], in0=ta, in1=tb_, op=ALU.subtract),
                  reads=["rta", "rtb"], writes=[rk])
            S.dve(lambda e: e.tensor_tensor(out=rt[:, :, :, 32:64], in0=tc_, in1=td, op=ALU.add),
                  reads=["rtc", "rtd"], writes=[rk])
            for t in range(2):
                S.act(lambda e, t=t: e.activation(out=qdup[:, t, :, :, :], in_=rt[:, t, 0:2, :].unsqueeze(2).to_broadcast([128, 2, 2, 64]),
                                                  func=AF.Copy), reads=[rk], writes=["qdup"])
            for d in range(2):
                S.dve(lambda e, d=d: e.tensor_tensor(
                    out=kfb[:, i:i + 2, d, :].rearrange("p t (h c) -> p t h c", c=64), in0=rt[:, :, 2:4, :],
                    in1=kdec3[:, d, 2 * p:2 * p + 2].unsqueeze(1).unsqueeze(3).to_broadcast([128, 2, 2, 64]), op=ALU.mult),
                    reads=[rk, "kdec"], writes=["kfb%d" % i, "kfb%d" % (i + 1)])

        def sb_tr(it, i):
            par = it % 2
            rt = rot[par]
            rk = "rot%d" % par
            sfx = "ab"[par]
            pTA = bank(0, BF16)[:, par * 512:(par + 1) * 512].rearrange("p (t a x) -> p t a x", t=2, x=128)
            pTD = bank(1, BF16)[:, par * 512:(par + 1) * 512].rearrange("p (t h x) -> p t h x", t=2, x=128)
            for t in range(2):
                S.pe(lambda e, t=t: e.transpose(out=pTA[:, t, 0, :], in_=rt[:, t, 0:2, :].rearrange("p a c -> p (a c)"), identity=ident),
                     reads=[rk, "ident"], writes=["ps0" + sfx])
                S.pe(lambda e, t=t: e.transpose(out=pTA[:, t, 1, :], in_=rt[:, t, 2:4, :].rearrange("p a c -> p (a c)"), identity=ident),
                     reads=[rk, "ident"], writes=["ps0" + sfx])
                for h2 in range(2):
                    S.pe(lambda e, t=t, h2=h2: e.transpose(out=pTD[:, t, h2, :], in_=qdup[:, t, h2, :, :].rearrange("p u c -> p (u c)"),
                                                           identity=ident),
                         reads=["qdup", "ident"], writes=["ps1" + sfx])
            qk_keys = ["qkT%d" % i, "qkT%d" % (i + 1)]
            S.act(lambda e: e.activation(out=kT2[:, i * 128:(i + 2) * 128].rearrange("p (t x) -> p t x", t=2), in_=pTA[:, :, 1, :], func=AF.Copy),
                  reads=["ps0" + sfx], writes=qk_keys)
            S.act(lambda e: e.activation(out=qTz[0:64, i:i + 2, 0, :], in_=pTA[0:64, :, 0, :], func=AF.Copy),
                  reads=["ps0" + sfx, "qTz_zero"], writes=qk_keys)
            S.act(lambda e: e.activation(out=qTz[64:128, i:i + 2, 1, :], in_=pTA[64:128, :, 0, :], func=AF.Copy),
                  reads=["ps0" + sfx, "qTz_zero"], writes=qk_keys)
            S.dve(lambda e: e.tensor_tensor(out=q2[:, :, i * 128:(i + 2) * 128].rearrange("p h (t x) -> p t h x", t=2), in0=pTD,
                                            in1=qdec3[:, 2 * p:2 * p + 2, :].unsqueeze(1).to_broadcast([128, 2, 2, 128]), op=ALU.mult),
                  reads=["ps1" + sfx, "qdec"], writes=["q2_%d" % i, "q2_%d" % (i + 1)])

        def scan_pe(j):
            sfx = "ab"[j % 2]
            pD = bank(6)[:, (j % 2) * 256:(j % 2) * 256 + 256].rearrange("p (h v) -> p h v", v=128)
            nf, nb = j, NT - 1 - j
            for h2 in range(2):
                S.pe(lambda e, h2=h2: e.matmul(pD[0:64, h2, :], lhsT=kfb[:, nf, 0, h2 * 64:(h2 + 1) * 64],
                                               rhs=v_tok[:, nf, h2 * 128:(h2 + 1) * 128], start=True, stop=True),
                     reads=["kfb%d" % nf, "v%d" % nf], writes=["ps6" + sfx])
                S.pe(lambda e, h2=h2: e.matmul(pD[64:128, h2, :], lhsT=kfb[:, nb, 1, h2 * 64:(h2 + 1) * 64],
                                               rhs=v_tok[:, nb, h2 * 128:(h2 + 1) * 128], start=True, stop=True),
                     reads=["kfb%d" % nb, "v%d" % nb], writes=["ps6" + sfx])

        def scan_ew(j):
            sfx = "ab"[j % 2]
            pD = bank(6)[:, (j % 2) * 256:(j % 2) * 256 + 256].rearrange("p (h v) -> p h v", v=128)
            nf, nb = j, NT - 1 - j
            for h2 in range(2):
                S.dve(lambda e, h2=h2: e.scalar_tensor_tensor(
                    out=sfp3[:, h2, :], in0=sfp3[:, h2, :], scalar=cdec[:, 2 * p + h2:2 * p + h2 + 1], in1=pD[:, h2, :],
                    op0=ALU.mult, op1=ALU.add), reads=["sfp", "ps6" + sfx, "cdec"], writes=["sfp"])
            if os.environ.get("KDBG_CAST", "dve") == "dve":
                S.dve(lambda e: e.tensor_copy(out=s_all[0:64, nf + 1, :, :], in_=sfp3[0:64, :, :]),
                      reads=["sfp"], writes=["sall_f%d" % (nf + 1)])
                S.dve(lambda e: e.tensor_copy(out=s_all[64:128, nb - 1, :, :], in_=sfp3[64:128, :, :]),
                      reads=["sfp"], writes=["sall_b%d" % (nb - 1)])
            else:
                S.act(lambda e: e.activation(out=s_all[0:64, nf + 1, :, :], in_=sfp3[0:64, :, :], func=AF.Copy),
                      reads=["sfp"], writes=["sall_f%d" % (nf + 1)])
                S.act(lambda e: e.activation(out=s_all[64:128, nb - 1, :, :], in_=sfp3[64:128, :, :], func=AF.Copy),
                      reads=["sfp"], writes=["sall_b%d" % (nb - 1)])

        def scan_step(j):
            scan_pe(j)
            scan_ew(j)

        class OutStep:
            def __init__(self, it, n0):
                self.it, self.n0 = it, n0
                par = it % 2
                self.psc = 2 + par
                self.pyb = 4 + par
                self.pS4 = bank(self.psc).rearrange("p (c h t) -> p c h t", c=2, t=128)
                self.pY4 = bank(self.pyb).rearrange("p (c h v) -> p c h v", c=2, v=128)
                self.pm = pmat[par]
                self.pmk = "pmat%d" % par
                self.ysb = y_sb[it % 3]
                self.yk = "ysb%d" % (it % 3)
                self.gs = gst[:, (it % 4) * 32:(it % 4) * 32 + 32]
                self.gk = "gst%d" % (it % 4)
                self.yg = ygb[par]
                self.ygk = "ygb%d" % par
                self.sfx = "ab"[par]
                self.pT4 = bank(0, BF16)[:, par * 512:(par + 1) * 512].rearrange("p (h c t) -> p h c t", h=2, t=128)

            def scores(o):
                for c in range(2):
                    n = o.n0 + c
                    tsl = slice(n * 128, (n + 1) * 128)
                    S.pe(lambda e, c=c, n=n, tsl=tsl: e.matmul(bank(o.psc)[:, c * 256:(c + 1) * 256], lhsT=kT2[:, tsl],
                                                               rhs=qTz[:, n, :, :].rearrange("p h t -> p (h t)"), start=True, stop=True),
                         reads=["qkT%d" % n, "qTz_zero"], writes=[PK(o.psc)])

            def mask(o):
                S.dve(lambda e: e.tensor_tensor(out=o.pm, in0=o.pS4,
                                                in1=mask3[:, 2 * p:2 * p + 2, :].unsqueeze(1).to_broadcast([128, 2, 2, 128]),
                                                op=ALU.mult), reads=[PK(o.psc), "mask"], writes=[o.pmk])

            def av(o):
                for c in range(2):
                    n = o.n0 + c
                    tsl = slice(n * 128, (n + 1) * 128)
                    for h2 in range(2):
                        S.pe(lambda e, c=c, n=n, h2=h2: e.matmul(o.pY4[:, c, h2, :], lhsT=o.pm[:, c, h2, :],
                                                                 rhs=v_tok[:, n, h2 * 128:(h2 + 1) * 128], start=True, stop=False),
                             reads=[o.pmk, "v%d" % n], writes=[PK(o.pyb)])
                        S.pe(lambda e, c=c, n=n, h2=h2, tsl=tsl: e.matmul(o.pY4[:, c, h2, :], lhsT=q2[:, h2, tsl],
                                                                          rhs=s_all[:, n, h2, :], start=False, stop=True),
                             reads=["q2_%d" % n, "sall_f%d" % n, "sall_b%d" % n], writes=[PK(o.pyb)])

            def copy_sq(o):
                S.act(lambda e: e.activation(out=o.ysb, in_=o.pY4, func=AF.Copy), reads=[PK(o.pyb)], writes=[o.yk])
                for c in range(2):
                    for h2 in range(2):
                        q_ = c * 2 + h2
                        S.act(lambda e, c=c, h2=h2, q_=q_: e.activation(out=ysq[:, c, h2, :], in_=o.pY4[:, c, h2, :], func=AF.Square,
                                                                        accum_out=o.gs[:, 4 + q_:5 + q_]),
                              reads=[PK(o.pyb)], writes=["ysq", o.gk + "q"])

            def reduce(o):
                S.dve(lambda e: e.tensor_reduce(out=o.gs[:, 0:4], in_=o.ysb.rearrange("p c h v -> p (c h) v"), axis=AX.X, op=ALU.add),
                      reads=[o.yk], writes=[o.gk + "s"])

            def var(o):
                S.dve(lambda e: e.tensor_tensor(out=o.gs[:, 8:12], in0=o.gs[:, 0:4], in1=o.gs[:, 0:4], op=ALU.mult),
                      reads=[o.gk + "s"], writes=[o.gk + "ss"])
                S.dve(lambda e: e.scalar_tensor_tensor(out=o.gs[:, 12:16], in0=o.gs[:, 4:8], scalar=float(DV), in1=o.gs[:, 8:12],
                                                       op0=ALU.mult, op1=ALU.subtract),
                      reads=[o.gk + "q", o.gk + "ss"], writes=[o.gk + "v"])

            def sqrt(o):
                S.act(lambda e: e.activation(out=o.gs[:, 16:20], in_=o.gs[:, 12:16], func=AF.Sqrt, bias=epsb[:, 0:1]),
                      reads=[o.gk + "v", "epsb"], writes=[o.gk + "sd"])

            def rstd(o):
                S.dve(lambda e: e.reciprocal(out=o.gs[:, 24:28], in_=o.gs[:, 16:20]), reads=[o.gk + "sd"], writes=[o.gk + "r"])
                S.dve(lambda e: e.scalar_tensor_tensor(out=o.gs[:, 28:32], in0=o.gs[:, 0:4], scalar=-1.0 / DV, in1=o.gs[:, 24:28],
                                                       op0=ALU.mult, op1=ALU.mult),
                      reads=[o.gk + "s", o.gk + "r"], writes=[o.gk + "nb"])

            def normalize(o):
                for c in range(2):
                    for h2 in range(2):
                        q_ = c * 2 + h2
                        S.act(lambda e, c=c, h2=h2, q_=q_: e.activation(out=o.yg[:, c, h2, :], in_=o.ysb[:, c, h2, :], func=AF.Identity,
                                                                        scale=o.gs[:, 24 + q_:25 + q_], bias=o.gs[:, 28 + q_:29 + q_]),
                              reads=[o.yk, o.gk + "r", o.gk + "nb"], writes=[o.ygk])

            def gate(o):
                S.dve(lambda e: e.tensor_tensor(out=o.yg.rearrange("p c h v -> p c (h v)"),
                                                in0=o.yg.rearrange("p c h v -> p c (h v)"),
                                                in1=sg_tok[:, o.n0:o.n0 + 2, :], op=ALU.mult),
                      reads=[o.ygk, "sg%d" % o.n0, "sg%d" % (o.n0 + 1)], writes=[o.ygk])

            def transposes(o):
                for c in range(2):
                    for h2 in range(2):
                        S.pe(lambda e, c=c, h2=h2: e.transpose(out=o.pT4[:, h2, c, :], in_=o.yg[:, c, h2, :], identity=ident),
                             reads=[o.ygk, "ident"], writes=["ps0" + o.sfx])

            def evac(o):
                for h2 in range(2):
                    kk = 2 * p + h2
                    S.act(lambda e, h2=h2, kk=kk: e.activation(out=ygT[:, kk, o.n0 * 128:(o.n0 + 2) * 128],
                                                               in_=o.pT4[:, h2, :, :].rearrange("p c t -> p (c t)"), func=AF.Copy,
                                                               scale=gnw128[:, kk:kk + 1]),
                          reads=["ps0" + o.sfx, "gnw128"], writes=[ygT_key(kk, o.n0), ygT_key(kk, o.n0 + 1)])

        order_b = []
        for j in range(NT // 4):
            order_b += [2 * j, NT - 2 - 2 * j]
        def head_pe():
            sb_proj(0, order_b[0], pe_only=True)

        def head_rest():
            sb_proj_evac(0, order_b[0])
            sb_rope(0, order_b[0])

        def body(nxt):
            wada_late_load(4 + 2 * p)
            for it, i in enumerate(order_b):
                if it + 1 < len(order_b):
                    sb_proj(it + 1, order_b[it + 1])
                if it == 2:
                    wada_late_compute(4 + 2 * p)
                    wada_late_load(5 + 2 * p)
                if it == 5:
                    wada_late_compute(5 + 2 * p)
                    if p == 2:
                        mod_ab(A_F, B_F, VO_NFFN, 24, 32, 0)
                sb_tr(it, i)
                if it + 1 < len(order_b):
                    sb_rope(it + 1, order_b[it + 1])
                if it % 2 == 1:
                    scan_step(it - 1)
                    scan_step(it)
            order_o = [6, 8, 4, 10, 2, 12, 0, 14]
            NO = len(order_o)
            steps = [OutStep(it, order_o[it]) for it in range(NO)]

            def g(k):
                return steps[k] if 0 <= k < NO else None

            scan_step(NT // 2)
            def iteration(it):
                    a_, b_, c_, d_ = g(it), g(it - 1), g(it - 2), g(it - 3)
                    sj = NT // 2 + 1 + it if NT // 2 + 1 + it <= NT - 2 else None
                    ordm = os.environ.get("KDBG_ORD", "m1c")
                    if ordm == "r7":
                        if c_:
                            c_.var(); c_.sqrt(); c_.rstd()
                        if d_:
                            d_.normalize(); d_.gate(); d_.transposes(); d_.evac()
                        if b_:
                            b_.copy_sq(); b_.reduce()
                        if sj is not None:
                            scan_step(sj)
                        if a_:
                            a_.scores(); a_.mask(); a_.av()
                        return
                    if ordm == "m1":
                        if a_:
                            a_.scores()
                        if sj is not None:
                            scan_pe(sj)
                        if c_:
                            c_.var(); c_.sqrt(); c_.rstd()
                        if d_:
                            d_.normalize(); d_.gate(); d_.transposes(); d_.evac()
                        if b_:
                            b_.copy_sq(); b_.reduce()
                        if sj is not None:
                            scan_ew(sj)
                        if a_:
                            a_.mask(); a_.av()
                        return
                    if ordm == "m1c":
                        if a_:
                            a_.scores()
                        if sj is not None:
                            scan_pe(sj)
                        if c_:
                            c_.var(); c_.sqrt()
                        if sj is not None:
                            scan_ew(sj)
                        if c_:
                            c_.rstd()
                        if d_:
                            d_.normalize()
                        if b_:
                            b_.copy_sq()
                        if d_:
                            d_.gate(); d_.transposes()
                        if b_:
                            b_.reduce()
                        if d_:
                            d_.evac()
                        if a_:
                            a_.mask(); a_.av()
                        return
                    if ordm == "m1b":
                        if a_:
                            a_.scores()
                        if sj is not None:
                            scan_pe(sj)
                        if c_:
                            c_.var(); c_.sqrt(); c_.rstd()
                        if d_:
                            d_.normalize()
                        if b_:
                            b_.copy_sq()
                        if d_:
                            d_.gate(); d_.transposes()
                        if b_:
                            b_.reduce()
                        if d_:
                            d_.evac()
                        if sj is not None:
                            scan_ew(sj)
                        if a_:
                            a_.mask(); a_.av()
                        return
                    if ordm in ("m3", "m4"):
                        if a_:
                            a_.scores()
                        if sj is not None:
                            scan_pe(sj)
                        if c_:
                            c_.var(); c_.sqrt(); c_.rstd()
                        if ordm == "m4" and a_:
                            a_.mask(); a_.av()
                        if d_:
                            d_.normalize(); d_.gate(); d_.transposes(); d_.evac()
                        if ordm == "m3" and a_:
                            a_.mask(); a_.av()
                        if b_:
                            b_.copy_sq(); b_.reduce()
                        if sj is not None:
                            scan_ew(sj)
                        return
                    if ordm == "m2":
                        if a_:
                            a_.scores()
                        if sj is not None:
                            scan_pe(sj)
                        if c_:
                            c_.var(); c_.sqrt()
                        if a_:
                            a_.mask(); a_.av()
                        if c_:
                            c_.rstd()
                        if d_:
                            d_.normalize(); d_.gate(); d_.transposes(); d_.evac()
                        if b_:
                            b_.copy_sq(); b_.reduce()
                        if sj is not None:
                            scan_ew(sj)
                        return
                    if a_:
                        a_.scores()
                    if sj is not None:
                        scan_pe(sj)
                    if c_:
                        c_.var()
                        c_.sqrt()
                    if d_:
                        d_.normalize()
                    if a_:
                        a_.mask()
                        a_.av()
                    if c_:
                        c_.rstd()
                    if sj is not None:
                        scan_ew(sj)
                    if b_:
                        b_.copy_sq()
                    if d_:
                        d_.gate()
                        d_.transposes()
                    if b_:
                        b_.reduce()
                    if d_:
                        d_.evac()

            for it in range(NO + 3):
                iteration(it)
                if nxt is not None and it == NO:
                    nxt.ctx_pe()
                if nxt is not None and it == NO + 1:
                    nxt.head_pe()
            if nxt is not None:
                nxt.ctx_rest()
                nxt.head_rest()

        class _P:
            pass
        o_ = _P()
        o_.ctx_pe, o_.ctx_rest, o_.head_pe, o_.head_rest, o_.body = ctx_pe, ctx_rest, head_pe, head_rest, body
        return o_

    pairs = [make_pair(p) for p in range(4)]
    pairs[0].ctx_pe()
    pairs[0].ctx_rest()
    pairs[0].head_pe()
    pairs[0].head_rest()
    for p in range(4):
        pairs[p].body(pairs[p + 1] if p < 3 else None)

    if stop == 3:
        return finish_debug([(ygT.rearrange("p k t -> p (k t)"), 16384)])
    R_PAIR.reset()
    def _rp(off_kb, size_kb, dt=BF16):
        v = arena[:, (R_PAIR.start + off_kb * KB) // 2:(R_PAIR.start + (off_kb + size_kb) * KB) // 2]
        return v if dt == BF16 else v.bitcast(dt)

    wt_buf = [_rp(20, 7).rearrange("p (r n) -> p r n", n=128), _rp(0, 7).rearrange("p (r n) -> p r n", n=128)]
    wt_buf = [wt_buf[0], wt_buf[1]]
    sgab = [_rp(7 + 2 * i, 2, F32) for i in range(4)]
    wo_sb = _rp(28, 16).rearrange("p (k n) -> p k n", n=D)
    wo_stage = [_rp(44 + 4 * i, 4, F32) for i in range(2)]
    xres = [_rp(52 + 4 * i, 4, F32) for i in range(2)]
    R_PAIR.off = 60 * KB

    def mT_key(k, i):
        return "M_%d_%d" % (k, i)

    def load_tail(cb):
        extra = [S.fence()] if cb == 1 else (["kfb%d" % i for i in range(NT)] if cb == 0 else [])
        S.dma("pool", "wt%d" % (cb % 2), lambda e, cb=cb: e.dma_start(out=wt_buf[cb % 2], in_=wtail_d[cb]),
              writes=["wt%d" % (cb % 2)] + extra)

    def prep_wo(k):
        st = wo_stage[k % 2]
        sk = "wost%d" % (k % 2)
        S.dma("sp", sk, lambda e, st=st, k=k: e.dma_start(out=st, in_=wo_d[k]), writes=[sk] + ([S.fence()] if k < 2 else []))
        S.dve(lambda e, st=st, k=k: e.tensor_tensor(out=wo_sb[:, k, :], in0=st, in1=gm_bc, op=ALU.mult),
              reads=[sk, "gm_bc"], writes=["wo%d" % k])

    load_tail(0)
    ctr = 0
    for cb in range(8):
        wt = wt_buf[cb % 2]
        wtk = "wt%d" % (cb % 2)
        if cb + 1 < 8:
            load_tail(cb + 1)
        prep_wo(cb)
        for tb in range(4):
            ts_ = slice(tb * 512, (tb + 1) * 512)
            par = ctr % 2
            ctr += 1
            pga, pgb, pba, pbb = par, 2 + par, 4 + par, 6 + par
            hreads = lambda k: [hT_key(k, 4 * tb + q) for q in range(4)]
            for k in range(8):
                S.pe(lambda e, k=k, ts_=ts_, pga=pga, wt=wt: e.matmul(bank(pga), lhsT=wt[:, k, :], rhs=hT[:, k, ts_],
                                                                     start=(k == 0), stop=(k == 7)),
                     reads=[wtk] + hreads(k), writes=[PK(pga)])
            for g in range(4):
                S.pe(lambda e, g=g, ts_=ts_, pba=pba, wt=wt: e.matmul(bank(pba), lhsT=wt[:, 24 + g, :], rhs=ypT[:, g, ts_],
                                                                     start=(g == 0), stop=(g == 3)),
                     reads=[wtk, "ypT"], writes=[PK(pba)])
            for k in range(8):
                S.pe(lambda e, k=k, ts_=ts_, pgb=pgb, wt=wt: e.matmul(bank(pgb), lhsT=wt[:, 8 + k, :], rhs=hT[:, k, ts_],
                                                                     start=(k == 0), stop=(k == 7)),
                     reads=[wtk] + hreads(k), writes=[PK(pgb)])
            for k in range(8):
                S.pe(lambda e, k=k, ts_=ts_, pbb=pbb, wt=wt: e.matmul(bank(pbb), lhsT=wt[:, 16 + k, :], rhs=ygT[:, k, ts_],
                                                                     start=(k == 0), stop=(k == 7)),
                     reads=[wtk] + [ygT_key(k, 4 * tb + q) for q in range(4)], writes=[PK(pbb)])
            sa = sgab[par * 2]
            sb_ = sgab[par * 2 + 1]
            sak = "sga%d" % par
            sbk = "sgb%d" % par
            S.act(lambda e, sa=sa, pga=pga: e.activation(out=sa, in_=bank(pga), func=AF.Sigmoid), reads=[PK(pga)], writes=[sak])
            S.act(lambda e, sb_=sb_, pgb=pgb: e.activation(out=sb_, in_=bank(pgb), func=AF.Sigmoid), reads=[PK(pgb)], writes=[sbk])
            S.dve(lambda e, sa=sa, pba=pba: e.tensor_tensor(out=sa, in0=sa, in1=bank(pba), op=ALU.mult),
                  reads=[sak, PK(pba)], writes=[sak])
            S.dve(lambda e, sb_=sb_, pbb=pbb: e.tensor_tensor(out=sb_, in0=sb_, in1=bank(pbb), op=ALU.mult),
                  reads=[sbk, PK(pbb)], writes=[sbk])
            S.dve(lambda e, sa=sa, sb_=sb_, cb=cb, ts_=ts_: e.tensor_tensor(out=mT[:, cb, ts_], in0=sa, in1=sb_, op=ALU.add),
                  reads=[sak, sbk], writes=[mT_key(cb, 4 * tb + q) for q in range(4)])

    if stop == 4:
        return finish_debug([(mT.rearrange("p k t -> p (k t)"), 16384)])
    w13_buf = [arena[:, (R_PAIR.start + i * 4 * KB) // 2:(R_PAIR.start + (i + 1) * 4 * KB) // 2].rearrange(
        "p (a k n) -> p a k n", a=2, n=128) for i in range(2)]

    def load_w13(c):
        extra = [S.fence()] if c < 2 else []
        S.dma("pool", "w13_%d" % (c % 2), lambda e, c=c: e.dma_start(out=w13_buf[c % 2], in_=w13_d[c]),
              writes=["w13_%d" % (c % 2)] + extra)

    if stop is None:
        load_w13(0)
        load_w13(1)
    def x1_key(i):
        return "X1_%d" % i

    h2T = mT

    def h2_key(k, i):
        return mT_key(k, i)

    junk2 = _rp(48, 2)
    a2 = {}

    def a2_square(i):
        n = nctr[0]
        nctr[0] += 1
        c = dict(n=n, i=i, ss=stat[:, 32 + (n % 4) * 2:33 + (n % 4) * 2], rs=stat[:, 33 + (n % 4) * 2:34 + (n % 4) * 2],
                 sk="nstat%d" % (n % 4), xn=xnb2[n % 3], xk="xn3_%d" % (n % 3))
        S.act(lambda e: e.activation(out=junk2, in_=x1[:, i, :], func=AF.Square, accum_out=c["ss"]),
              reads=[x1_key(i)], writes=["junk", c["sk"]])
        return c

    def a2_ts(c):
        S.dve(lambda e: e.tensor_scalar(out=c["rs"], in0=c["ss"], scalar1=1.0 / D, scalar2=EPS, op0=ALU.mult, op1=ALU.add),
              reads=[c["sk"]], writes=[c["sk"] + "r"])
        S.act(lambda e: e.activation(out=c["rs"], in_=c["rs"], func=AF.Sqrt), reads=[c["sk"] + "r"], writes=[c["sk"] + "r"])

    def a2_fin(c):
        S.dve(lambda e: e.reciprocal(out=c["rs"], in_=c["rs"]), reads=[c["sk"] + "r"], writes=[c["sk"] + "r"])
        S.dve(lambda e: e.tensor_scalar(out=c["xn"], in0=x1[:, c["i"], :], scalar1=c["rs"], scalar2=None, op0=ALU.mult),
              reads=[x1_key(c["i"]), c["sk"] + "r"], writes=[c["xk"]])

    xnb2 = [_rp(50, 2), _rp(60, 2), arena[:, (R_YP.start + 14 * KB) // 2:(R_YP.start + 16 * KB) // 2]]
    pend = None
    for i in range(NT):
        pa = (i % 2) * 2
        xr = xres[i % 2]
        xrk = "xres%d" % (i % 2)
        S.dma("sp", xrk, lambda e, xr=xr, i=i: e.dma_start(out=xr, in_=x_d[i * 128:(i + 1) * 128, :]),
              writes=[xrk])
        for hf in range(2):
            for k in range(8):
                S.pe(lambda e, k=k, hf=hf, i=i, pa=pa: e.matmul(bank(pa + hf), lhsT=mT[:, k, i * 128:(i + 1) * 128],
                                                                rhs=wo_sb[:, k, hf * 512:(hf + 1) * 512],
                                                                start=(k == 0), stop=(k == 7)),
                     reads=[mT_key(k, i), "wo%d" % k], writes=[PK(pa + hf)])
        S.dve(lambda e, i=i, xr=xr, pa=pa: e.tensor_tensor(out=x1[:, i, :], in0=ps[:, pa * 512:(pa + 2) * 512], in1=xr, op=ALU.add),
              reads=[PK(pa), PK(pa + 1), xrk],
              writes=[x1_key(i)] + [(hT_key(i, t) if i < 8 else ygT_key(i - 8, t)) for t in range(NT)])
        if stop != 5:
            a2[i] = a2_square(i)
            if i >= 1:
                a2_ts(a2[i - 1])
            if i >= 3:
                norm_tr((a2[i - 3]["n"], a2[i - 3]["xn"], a2[i - 3]["xk"]), A_F, B_F, h2T, h2_key, i - 3, 4)
            if i >= 1:
                a2_fin(a2[i - 1])
    if stop != 5:
        a2_ts(a2[NT - 1])
        norm_tr((a2[NT - 3]["n"], a2[NT - 3]["xn"], a2[NT - 3]["xk"]), A_F, B_F, h2T, h2_key, NT - 3, 4)
        a2_fin(a2[NT - 1])
        for i_ in (NT - 2, NT - 1):
            norm_tr((a2[i_]["n"], a2[i_]["xn"], a2[i_]["xk"]), A_F, B_F, h2T, h2_key, i_, 4)

    if stop == 5:
        return finish_debug([(x1.rearrange("p i f -> p (i f)"), 16384)])
    R_PAIR.reset()
    R_PAIR.carve(8 * KB)
    actT = [R_PAIR.carve(20 * KB).rearrange("p (c t) -> p c t", t=L) for _ in range(2)]
    R_YP.reset()
    w2_sb = R_YP.carve(10 * KB).rearrange("p (c n) -> p c n", n=D)
    sa_sb = [R_YP.carve(2 * KB, F32) for _ in range(2)]
    w2_stage = [qdec, mask]

    S.dma("sp", "c6", lambda e: e.dma_start(out=nf_bc, in_=rows_d[0:1, RO_NF:RO_NF + D].to_broadcast([128, D])),
          writes=["nf_bc", S.fence()] + hcT_keys)
    fctr = 0
    for gi, (c0, ng) in enumerate(FF_GROUPS):
        at = actT[gi % 2]
        atk = "actT%d" % (gi % 2)
        for cc in range(ng):
            c = c0 + cc
            wb = w13_buf[c % 2]
            wbk = "w13_%d" % (c % 2)
            if 2 <= c + 1 < NFF:
                load_w13(c + 1)
            st = w2_stage[c % 2]
            stk = "w2st%d" % (c % 2)
            S.dma("sp", stk, lambda e, st=st, c=c: e.dma_start(out=st, in_=w2_d[c]), writes=[stk, "qdec" if c % 2 == 0 else "mask"])
            S.dve(lambda e, st=st, cc=cc: e.tensor_tensor(out=w2_sb[:, cc, :], in0=st, in1=gf_bc, op=ALU.mult),
                  reads=[stk, "gf_bc"], writes=["w2sb%d" % cc])
            for tb in range(4):
                ts_ = slice(tb * 512, (tb + 1) * 512)
                par = fctr % 2
                fctr += 1
                pa_, pb_ = par, 2 + par
                for a in range(2):
                    pbk = pa_ if a == 0 else pb_
                    for k in range(8):
                        S.pe(lambda e, a=a, k=k, ts_=ts_, pbk=pbk, wb=wb: e.matmul(bank(pbk), lhsT=wb[:, a, k, :], rhs=h2T[:, k, ts_],
                                                                                  start=(k == 0), stop=(k == 7)),
                             reads=[wbk] + [h2_key(k, 4 * tb + q) for q in range(4)], writes=[PK(pbk)])
                sa = sa_sb[par]
                sak = "ffsa%d" % par
                S.act(lambda e, sa=sa, pa_=pa_: e.activation(out=sa, in_=bank(pa_), func=AF.Silu), reads=[PK(pa_)], writes=[sak])
                S.dve(lambda e, sa=sa, pb_=pb_, cc=cc, ts_=ts_, at=at: e.tensor_tensor(out=at[:, cc, ts_], in0=sa, in1=bank(pb_), op=ALU.mult),
                      reads=[sak, PK(pb_)], writes=[atk + "_%d_%d" % (cc, tb)])
        last = (gi == len(FF_GROUPS) - 1)
        for i in range(NT):
            pa = 4 + (i % 2) * 2
            for hf in range(2):
                for cc in range(ng):
                    S.pe(lambda e, cc=cc, hf=hf, i=i, pa=pa, at=at: e.matmul(bank(pa + hf), lhsT=at[:, cc, i * 128:(i + 1) * 128],
                                                                             rhs=w2_sb[:, cc, hf * 512:(hf + 1) * 512],
                                                                             start=(cc == 0), stop=(cc == ng - 1)),
                         reads=[atk + "_%d_%d" % (cc, i // 4), "w2sb%d" % cc], writes=[PK(pa + hf)])
            S.dve(lambda e, i=i, pa=pa: e.tensor_tensor(out=x1[:, i, :], in0=ps[:, pa * 512:(pa + 2) * 512], in1=x1[:, i, :], op=ALU.add),
                  reads=[PK(pa), PK(pa + 1), x1_key(i)], writes=[x1_key(i)])
            if last:
                def fin_a(i):
                    ss = stat[:, 32 + (i % 4) * 2:33 + (i % 4) * 2]
                    S.act(lambda e: e.activation(out=junk2, in_=x1[:, i, :], func=AF.Square, accum_out=ss),
                          reads=[x1_key(i)], writes=["junk2", "fstat%d" % (i % 4)])

                def fin_b(i):
                    ss = stat[:, 32 + (i % 4) * 2:33 + (i % 4) * 2]
                    rs = stat[:, 33 + (i % 4) * 2:34 + (i % 4) * 2]
                    sk = "fstat%d" % (i % 4)
                    S.dve(lambda e: e.tensor_scalar(out=rs, in0=ss, scalar1=1.0 / D, scalar2=EPS, op0=ALU.mult, op1=ALU.add),
                          reads=[sk], writes=[sk + "r"])
                    S.act(lambda e: e.activation(out=rs, in_=rs, func=AF.Sqrt), reads=[sk + "r"], writes=[sk + "r"])

                def fin_c(i):
                    rs = stat[:, 33 + (i % 4) * 2:34 + (i % 4) * 2]
                    sk = "fstat%d" % (i % 4)
                    S.dve(lambda e: e.reciprocal(out=rs, in_=rs), reads=[sk + "r"], writes=[sk + "r"])
                    S.dve(lambda e: e.scalar_tensor_tensor(out=x1[:, i, :], in0=x1[:, i, :], scalar=rs, in1=nf_bc,
                                                           op0=ALU.mult, op1=ALU.mult),
                          reads=[x1_key(i), sk + "r", "nf_bc"], writes=[x1_key(i)])
                    S.dma("sp", "out", lambda e: e.dma_start(out=out_d[i * 128:(i + 1) * 128, :], in_=x1[:, i, :]),
                          reads=[x1_key(i)])

                fin_a(i)
                if i >= 1:
                    fin_b(i - 1)
                if i >= 2:
                    fin_c(i - 2)
                if i == NT - 1:
                    fin_b(i)
                    fin_c(i - 1)
                    fin_c(i)

    S.emit(nc, final_dma_keys=["out"])
    return nc


_Q_OFF, _K_OFF, _V_OFF, _G_OFF, _GA_OFF, _GB_OFF = 512, 1024, 1536, 2560, 3584, 4608


def _kchunk(w):
    kk = w.shape[0] // 128
    return np.ascontiguousarray(w.reshape(kk, 128, w.shape[1]).transpose(1, 0, 2))


def _prep(x, c, ctx, c_ctx, w_ada, b_ada, norm_mix, norm_ffn, w_in, w_pool, pool_scale,
          ret_decay_f, ret_decay_b, ret_gn_w, w_pa, w_rb, w_o, w_ff1, w_ff3, w_ff2, norm_final):
    f32 = np.float32
    x = np.asarray(x, f32)
    B = x.shape[0]
    pblk, plan = _get_pool()

    w_in0 = np.asarray(w_in, f32)[0]
    wada = _kchunk(np.asarray(w_ada, f32)[0])
    wu = _kchunk(w_in0[:, 0:512])
    wpairs = []
    for p in range(4):
        cols = np.concatenate([
            w_in0[:, _Q_OFF + p * 128:_Q_OFF + (p + 1) * 128],
            w_in0[:, _K_OFF + p * 128:_K_OFF + (p + 1) * 128],
            w_in0[:, _V_OFF + p * 256:_V_OFF + (p + 1) * 256],
            w_in0[:, _G_OFF + p * 256:_G_OFF + (p + 1) * 256]], axis=1)
        wpairs.append(_kchunk(cols))
    wpairs = np.stack(wpairs, 0)
    w_rb0 = np.asarray(w_rb, f32)[0]
    w_pa0 = np.asarray(w_pa, f32)[0]
    wtail = []
    for cb in range(8):
        cs = slice(cb * 128, (cb + 1) * 128)
        ga = _kchunk(w_in0[:, _GA_OFF:_GA_OFF + D][:, cs])
        gb = _kchunk(w_in0[:, _GB_OFF:_GB_OFF + D][:, cs])
        rb = _kchunk(w_rb0[:, cs])
        pa = _kchunk(w_pa0[:, cs])
        wtail.append(np.concatenate([ga, gb, rb, pa], axis=1))
    wtail = np.ascontiguousarray(np.stack(wtail, 0))
    wpool = np.ascontiguousarray(np.asarray(w_pool, f32)[0].transpose(1, 0, 2))
    wo = np.ascontiguousarray(np.asarray(w_o, f32)[0].reshape(8, 128, D))
    w1 = np.asarray(w_ff1, f32)[0]
    w3 = np.asarray(w_ff3, f32)[0]
    w13 = np.stack([np.stack([_kchunk(w1[:, cc * 128:(cc + 1) * 128]), _kchunk(w3[:, cc * 128:(cc + 1) * 128])], axis=1)
                    for cc in range(NFF)], 0)
    w13 = np.ascontiguousarray(w13)
    w2 = np.ascontiguousarray(np.asarray(w_ff2, f32)[0].reshape(NFF, 128, D))
    consts = _host_consts()
    ident = np.eye(128, dtype=np.float32).astype(ml_dtypes.bfloat16)

    def pp(v, k):
        return np.asarray(v, f32).reshape(k, 128).T

    b_ada0 = np.asarray(b_ada, f32)[0]
    rows = np.zeros((1, NROW), f32)
    rows[0, RO_BGM:RO_BGM + D] = b_ada0[2 * D:3 * D]
    rows[0, RO_BGF:RO_BGF + D] = b_ada0[5 * D:6 * D]
    rows[0, RO_NF:RO_NF + D] = np.asarray(norm_final, f32)
    rows[0, RO_DEC:RO_DEC + 8] = np.asarray(ret_decay_f, f32)[0]
    rows[0, RO_DEC + 8:RO_DEC + 16] = np.asarray(ret_decay_b, f32)[0]

    in_maps = []
    for b in range(B):
        vecs = np.zeros((128, NVEC), f32)
        vecs[:, VO_C:VO_C + 16:2] = pp(np.asarray(c, f32)[b], 8)
        vecs[:, VO_C + 1:VO_C + 16:2] = pp(np.asarray(c_ctx, f32), 8)
        vecs[:, VO_NMIX:VO_NMIX + 8] = pp(np.asarray(norm_mix, f32)[0], 8)
        vecs[:, VO_NFFN:VO_NFFN + 8] = pp(np.asarray(norm_ffn, f32)[0], 8)
        vecs[:, VO_PSC:VO_PSC + 4] = pp(np.asarray(pool_scale, f32)[0], 4)
        vecs[:, VO_GNW:VO_GNW + 8] = pp(np.asarray(ret_gn_w, f32)[0], 8)
        vecs[:, VO_BADA:VO_BADA + 48] = pp(b_ada0, 48)
        in_maps.append({
            "x": np.ascontiguousarray(x[b]), "ctx": np.ascontiguousarray(np.asarray(ctx, f32)[b]),
            "vecs": vecs, "rows": rows, "consts": consts, "ident": ident,
            "wada": wada, "wu": wu, "wpairs": wpairs, "wtail": wtail, "wpool": wpool, "wo": wo,
            "w13": w13, "w2": w2, "pblk": pblk,
        })
    return in_maps


def kernel(**inputs):
    in_maps = _prep(**inputs)
    pblk, plan = _get_pool()
    nc = build_program(plan, pblk.shape[1])
    B = len(in_maps)
    res = run_bass_kernel_spmd(nc, in_maps, core_ids=list(range(B)))
    out = np.stack([np.asarray(res.results[b]["out"], np.float32) for b in range(B)], 0)
    return out
```

```python
import contextlib
import os
import types
import numpy as np
import ml_dtypes
import concourse.bass as bass
import concourse.mybir as mybir
from concourse.bass_utils import run_bass_kernel_spmd

F32 = mybir.dt.float32
BF16 = mybir.dt.bfloat16
AF = mybir.ActivationFunctionType
ALU = mybir.AluOpType
AX = mybir.AxisListType

D = 1024
L = 2048
NT = 16
LC = 256
C = 128
GRID_W = 64
H = 8
DK = 64
DV = 128
DFF = 2816
NFF = 22
EPS = 1e-6
K_SCALE = DK ** -0.5
POOL_WINDOWS = (2, 4, 8, 16)
FF_GROUPS = ((0, 5), (5, 5), (10, 5), (15, 5), (20, 2))


class Op:
    __slots__ = ("eng", "fn", "reads", "writes", "dma_key", "waits", "signal", "cnt", "n_dma")

    def __init__(self, eng, fn, reads, writes, dma_key, n_dma):
        self.eng = eng
        self.fn = fn
        self.reads = reads
        self.writes = writes
        self.dma_key = dma_key
        self.n_dma = n_dma
        self.waits = []
        self.signal = False
        self.cnt = None


class Sched:
    ENGS = ("pe", "act", "dve", "pool", "sp")

    def __init__(self):
        self.ops = []
        self.last_writer = {}
        self.readers = {}
        self.dma_cum = {}
        self.bank_rd = {}

    def _dep(self, op, d, kind):
        if d is op:
            return
        if d.dma_key is not None:
            op.waits.append(("dma:" + d.dma_key, self.dma_cum[d.dma_key]))
            return
        if d.eng == op.eng and op.dma_key is None:
            if d.eng == "pe" or kind != "RAW":
                return
        d.signal = True
        op.waits.append(("eng:" + d.eng, d))

    @staticmethod
    def _snapshot(fn):
        if fn.__closure__ is None:
            return fn
        cells = []
        for c in fn.__closure__:
            try:
                cells.append(types.CellType(c.cell_contents))
            except ValueError:
                cells.append(c)
        return types.FunctionType(fn.__code__, fn.__globals__, fn.__name__, fn.__defaults__, tuple(cells))

    def fence(self):
        self._nfence = getattr(self, "_nfence", 0) + 1
        key = "__fence%d" % self._nfence
        last = {}
        for o in self.ops:
            last[o.dma_key if o.dma_key is not None else "eng:" + o.eng] = o
        self.readers[key] = list(last.values())
        return key

    @staticmethod
    def _expand(keys):
        out = []
        for k in keys:
            if isinstance(k, str) and len(k) == 3 and k.startswith("ps"):
                out.extend((k + "a", k + "b"))
            else:
                out.append(k)
        return tuple(out)

    def op(self, eng, fn, reads=(), writes=(), dma_key=None, n_dma=1):
        fn = self._snapshot(fn)
        o = Op(eng, fn, self._expand(reads), self._expand(writes), dma_key, n_dma)
        for k in o.reads:
            w = self.last_writer.get(k)
            if w is not None:
                self._dep(o, w, "RAW")
            if isinstance(k, str) and k.startswith("ps") and eng in ("act", "dve"):
                bk = k[:3]
                lr = self.bank_rd.setdefault(bk, {})
                for e2, r in lr.items():
                    if e2 != eng:
                        self._dep(o, r, "XRD")
                lr[eng] = o
        for k in o.writes:
            w = self.last_writer.get(k)
            if w is not None:
                self._dep(o, w, "WAW")
            for r in self.readers.get(k, ()):
                self._dep(o, r, "WAR")
        for k in o.reads:
            self.readers.setdefault(k, []).append(o)
        for k in o.writes:
            self.last_writer[k] = o
            self.readers[k] = []
        if dma_key is not None:
            self.dma_cum[dma_key] = self.dma_cum.get(dma_key, 0) + 16 * n_dma
            o.cnt = self.dma_cum[dma_key]
        self.ops.append(o)
        return o

    def pe(self, fn, reads=(), writes=()):
        return self.op("pe", fn, reads, writes)

    def act(self, fn, reads=(), writes=()):
        return self.op("act", fn, reads, writes)

    def dve(self, fn, reads=(), writes=()):
        return self.op("dve", fn, reads, writes)

    def pool(self, fn, reads=(), writes=()):
        return self.op("pool", fn, reads, writes)

    def pool_or(self, tag, alt, fn, reads=(), writes=()):
        on = os.environ.get("KPOOL", "").split(",")
        return self.op("pool" if tag in on else alt, fn, reads, writes)

    def dma(self, eng, key, fn, reads=(), writes=()):
        return self.op(eng, fn, reads, writes, dma_key=key)

    def emit(self, nc, final_dma_keys=()):
        cnt = {e: 0 for e in self.ENGS}
        for o in self.ops:
            if o.dma_key is None and o.signal:
                cnt[o.eng] += 1
                o.cnt = cnt[o.eng]
        semnames = set()
        for o in self.ops:
            for (s, v) in o.waits:
                semnames.add(s)
            if o.dma_key is not None:
                semnames.add("dma:" + o.dma_key)
            elif o.signal:
                semnames.add("eng:" + o.eng)
        with contextlib.ExitStack() as es:
            sems = {}
            for s in sorted(semnames):
                sems[s] = es.enter_context(nc.semaphore(s.replace(":", "_")))
            block = es.enter_context(nc.Block())
            streams = {e: [o for o in self.ops if o.eng == e] for e in self.ENGS}

            def run(engname, e):
                waited = {}
                for o in streams[engname]:
                    need = {}
                    for (s, v) in o.waits:
                        val = v.cnt if isinstance(v, Op) else v
                        if val > need.get(s, 0):
                            need[s] = val
                    for s, val in need.items():
                        if waited.get(s, 0) >= val:
                            continue
                        e.wait_ge(sems[s], val)
                        waited[s] = val
                    ins = o.fn(e)
                    if o.dma_key is not None:
                        ins.then_inc(sems["dma:" + o.dma_key], 16 * o.n_dma)
                    elif o.signal:
                        ins.then_inc(sems["eng:" + o.eng], 1)
                if engname == "sp":
                    for k in final_dma_keys:
                        e.wait_ge(sems["dma:" + k], self.dma_cum[k])

            @block.tensor
            def _(e):
                run("pe", e)

            @block.scalar
            def _(e):
                run("act", e)

            @block.vector
            def _(e):
                run("dve", e)

            @block.gpsimd
            def _(e):
                run("pool", e)

            @block.sync
            def _(e):
                run("sp", e)


def _box_matrix(n, w):
    pos = np.arange(n)
    lo = np.clip(pos - w // 2, 0, n)
    hi = np.clip(pos + (w - w // 2), 0, n)
    a = np.zeros((n, n), np.float64)
    for t in range(n):
        a[t, lo[t]:hi[t]] = 1.0 / (hi[t] - lo[t])
    return a


def _pool_blocks():
    rows = L // GRID_W
    blocks = []
    seen = {}
    plan = []
    for g, w in enumerate(POOL_WINDOWS):
        a = np.kron(_box_matrix(rows, w), _box_matrix(GRID_W, w)) - np.eye(L)
        at = a.T
        pg = []
        for j in range(4):
            lst = []
            for t in range(NT):
                blk = at[t * 128:(t + 1) * 128, j * 512:(j + 1) * 512]
                if np.any(blk != 0.0):
                    b32 = np.ascontiguousarray(blk.astype(np.float32))
                    key = (g, b32.tobytes())
                    if key not in seen:
                        seen[key] = len(blocks)
                        blocks.append(b32)
                    lst.append((t, seen[key]))
            pg.append(lst)
        plan.append(pg)
    arr = np.stack(blocks, axis=1)
    return np.ascontiguousarray(arr).astype(ml_dtypes.bfloat16), plan


def _rope_tables():
    t = np.arange(L)
    row = (t // GRID_W).astype(np.float32)
    col = (t % GRID_W).astype(np.float32)
    n_freq = DK // 4
    inv_freq = (10000.0 ** (-np.arange(n_freq, dtype=np.float32) / n_freq)).astype(np.float32)
    ang = np.concatenate([row[:, None] * inv_freq, col[:, None] * inv_freq], axis=-1).astype(np.float32)
    cos = np.cos(ang).astype(np.float32).reshape(NT, 128, 32).transpose(1, 0, 2)
    sin = np.sin(ang).astype(np.float32).reshape(NT, 128, 32).transpose(1, 0, 2)
    return np.ascontiguousarray(cos), np.ascontiguousarray(sin)


_POOL_CACHE = None


def _get_pool():
    global _POOL_CACHE
    if _POOL_CACHE is None:
        _POOL_CACHE = _pool_blocks()
    return _POOL_CACHE


CO_COS = 0
CO_SIN = CO_COS + NT * 32
CO_DPOS = CO_SIN + NT * 32
CO_DNEG = CO_DPOS + 128
CO_POSQ = CO_DNEG + 128
CO_SM = CO_POSQ + 128
NCONST = CO_SM + 8

VO_C = 0
VO_NMIX = 16
VO_NFFN = 24
VO_PSC = 32
VO_GNW = 36
VO_BADA = 44
NVEC = VO_BADA + 48

RO_BGM = 0
RO_BGF = 1024
RO_NF = 2048
RO_DEC = 3072
NROW = RO_DEC + 16


def _host_consts():
    cos, sin = _rope_tables()
    c = np.zeros((128, NCONST), np.float32)
    c[:, CO_COS:CO_COS + NT * 32] = cos.reshape(128, -1)
    c[:, CO_SIN:CO_SIN + NT * 32] = sin.reshape(128, -1)
    i = np.arange(128, dtype=np.float32)
    dmat = i[None, :] - i[:, None]
    c[:, CO_DPOS:CO_DPOS + 128] = np.maximum(dmat, 0)
    c[:, CO_DNEG:CO_DNEG + 128] = np.maximum(-dmat, 0)
    c[0:64, CO_POSQ:CO_POSQ + 128] = (i + 1.0)[None, :]
    c[64:128, CO_POSQ:CO_POSQ + 128] = (C - i)[None, :]
    c[:, CO_SM + 0] = C - 1.0 - i
    c[:, CO_SM + 1] = i
    c[:, CO_SM + 2] = LC - 1.0 - i
    c[:, CO_SM + 3] = LC - 1.0 - (i + 128)
    c[:, CO_SM + 4] = i
    c[:, CO_SM + 5] = i + 128
    return c


def build_program(pool_plan, n_pool_blk, stop=None):
    nc = bass.Bass("TRN2", target_bir_lowering=False)
    DBGN = 16384

    def din(name, shape, dt=F32):
        return nc.dram_tensor(name, list(shape), dt, kind="ExternalInput").ap()

    x_d = din("x", [L, D])
    ctx_d = din("ctx", [LC, D])
    vecs_d = din("vecs", [128, NVEC])
    rows_d = din("rows", [1, NROW])
    consts_d = din("consts", [128, NCONST])
    ident_d = din("ident", [128, 128], BF16)
    wada_d = din("wada", [128, 8, 6 * D])
    wu_d = din("wu", [128, 8, 512])
    wpairs_d = din("wpairs", [4, 128, 8, 768])
    wtail_d = din("wtail", [8, 128, 28, 128])
    wpool_d = din("wpool", [128, 4, 128])
    wo_d = din("wo", [8, 128, D])
    w13_d = din("w13", [NFF, 128, 2, 8, 128])
    w2_d = din("w2", [NFF, 128, D])
    pblk_d = din("pblk", [128, n_pool_blk, 512], BF16)
    out_d = nc.dram_tensor("out", [L, D], F32, kind="ExternalOutput").ap()
    dbg_d = nc.dram_tensor("dbg", [128, DBGN], F32, kind="ExternalOutput").ap() if stop is not None else None

    S = Sched()
    SB_BYTES = 207 * 1024
    arena = nc.alloc_sbuf_tensor("arena", [128, SB_BYTES // 2], BF16).ap()
    ps = nc.alloc_psum_tensor("ps", [128, 4096], F32).ap()

    class Region:
        def __init__(self, start, size):
            self.start = start
            self.size = size
            self.off = 0

        def carve(self, nbytes, dt=BF16):
            want = nbytes
            nbytes = (nbytes + 31) // 32 * 32
            assert self.off + nbytes <= self.size, (self.off, nbytes, self.size)
            a = (self.start + self.off) // 2
            self.off += nbytes
            v = arena[:, a:a + want // 2]
            return v if dt == BF16 else v.bitcast(dt)

        def reset(self):
            self.off = 0

    KB = 1024
    R_PERS = Region(0, 32 * KB)
    R_X = Region(32 * KB, 64 * KB)
    R_M = Region(96 * KB, 32 * KB)
    R_PAIR = Region(128 * KB, 63 * KB)
    R_YP = Region(191 * KB, 16 * KB)
    assert 207 * KB <= SB_BYTES

    def bank(b, dt=F32):
        v = ps[:, b * 512:(b + 1) * 512]
        return v if dt == F32 else v.bitcast(dt)

    def PK(b):
        return "ps%d" % b

    def finish_debug(items):
        off = 0
        fk = S.fence()
        for ap, n, in items:
            for c0 in range(0, n, 1024):
                c1 = min(n, c0 + 1024)
                S.dma("pool", "dbg", lambda e, ap=ap, off=off, c0=c0, c1=c1: e.dma_start(
                    out=dbg_d[:, off + c0:off + c1], in_=ap[:, c0:c1]), writes=[fk])
            off += n
        S.dma("sp", "out", lambda e: e.dma_start(out=out_d[0:128, :], in_=gm_bc), writes=[fk])
        S.emit(nc, final_dma_keys=["out", "dbg"])
        return nc

    vecs = R_PERS.carve(NVEC * 4, F32)
    consts = R_PERS.carve(NCONST * 4, F32)
    ident = R_PERS.carve(256)
    decbc = R_PERS.carve(64, F32)
    lgbc = R_PERS.carve(64, F32)
    lgsel = R_PERS.carve(32, F32)
    cdec = R_PERS.carve(32, F32)
    kdec = R_PERS.carve(64, F32)
    ctxw = R_PERS.carve(128, F32)
    modT = R_PERS.carve(48 * 2 * 4, F32)
    modAB = R_PERS.carve(6 * 8 * 4, F32)
    scT = R_PERS.carve(8 * 2 * 2)
    scbc = R_PERS.carve(8 * 128 * 2)
    qdec = R_PERS.carve(8 * 128 * 4, F32)
    mask = R_PERS.carve(8 * 128 * 4, F32)
    gm_bc = R_PERS.carve(D * 4, F32)
    gf_bc = R_PERS.carve(D * 4, F32)
    hcT = R_PERS.carve(8 * LC * 2)
    nf_bc = hcT.bitcast(F32)
    wpool = R_PERS.carve(4 * 128 * 2)
    stat = R_PERS.carve(64 * 4, F32)
    epsb = R_PERS.carve(32, F32)
    gnw128 = R_PERS.carve(32, F32)
    sfp = R_PERS.carve(384 * 4, F32)

    modT3 = modT.rearrange("p (j t) -> p j t", t=2)
    modAB3 = modAB.rearrange("p (a k) -> p a k", k=8)
    scT3 = scT.rearrange("p (k t) -> p k t", t=2)
    scbc3 = scbc.rearrange("p (k m) -> p k m", m=128)
    qdec3 = qdec.rearrange("p (h t) -> p h t", t=128)
    mask3 = mask.rearrange("p (h t) -> p h t", t=128)
    hcT3 = hcT.rearrange("p (k t) -> p k t", t=LC)
    wpool3 = wpool.rearrange("p (g n) -> p g n", n=128)
    kdec3 = kdec.rearrange("p (d h) -> p d h", h=8)
    ctxw4 = ctxw.rearrange("p (t d h) -> p t d h", d=2, h=8)
    cos3 = consts[:, CO_COS:CO_COS + NT * 32].rearrange("p (i f) -> p i f", f=32)
    sin3 = consts[:, CO_SIN:CO_SIN + NT * 32].rearrange("p (i f) -> p i f", f=32)
    dpos = consts[:, CO_DPOS:CO_DPOS + 128]
    dneg = consts[:, CO_DNEG:CO_DNEG + 128]
    posq = consts[:, CO_POSQ:CO_POSQ + 128]

    def csm(i):
        return consts[:, CO_SM + i:CO_SM + i + 1]

    A_M, B_M, A_C, B_C, A_F, B_F = range(6)

    hT = R_X.carve(32 * KB).rearrange("p (k t) -> p k t", t=L)
    ygT = R_X.carve(32 * KB).rearrange("p (k t) -> p k t", t=L)
    R_X.reset()
    x1 = R_X.carve(64 * KB, F32).rearrange("p (i f) -> p i f", f=D)
    mT = R_M.carve(32 * KB).rearrange("p (k t) -> p k t", t=L)
    R_M.reset()
    _ux = R_X.start + 32 * KB
    u_tok = arena[:, _ux // 2:(_ux + 16 * KB) // 2].rearrange("p (i c) -> p i c", c=512)
    dT_sb = [arena[:, (_ux + (16 + i) * KB) // 2:(_ux + (17 + i) * KB) // 2] for i in range(2)]
    R_M.reset()
    wpair_buf = [R_M.carve(12 * KB).rearrange("p (k n) -> p k n", n=768) for _ in range(2)]
    wada_buf = [arena[:, (R_M.start + i * 8 * KB) // 2:(R_M.start + (i + 1) * 8 * KB) // 2].rearrange("p (k n) -> p k n", n=512)
                for i in range(4)]
    wada_buf += [arena[:, (R_YP.start + i * 8 * KB) // 2:(R_YP.start + (i + 1) * 8 * KB) // 2].rearrange("p (k n) -> p k n", n=512)
                 for i in range(2)]
    ypT = R_YP.carve(16 * KB).rearrange("p (g t) -> p g t", t=L)

    S.dma("sp", "c0", lambda e: e.dma_start(out=vecs, in_=vecs_d), writes=["vecs"])
    S.dma("sp", "c1", lambda e: e.dma_start(out=consts, in_=consts_d), writes=["consts"])
    S.dma("sp", "c2", lambda e: e.dma_start(out=ident, in_=ident_d), writes=["ident"])
    S.dma("sp", "c3", lambda e: e.dma_start(out=decbc, in_=rows_d[0:1, RO_DEC:RO_DEC + 16].to_broadcast([128, 16])),
          writes=["decbc"])
    S.dma("sp", "c4", lambda e: e.dma_start(out=gm_bc, in_=rows_d[0:1, RO_BGM:RO_BGM + D].to_broadcast([128, D])),
          writes=["gm_bc"])
    S.dma("sp", "c5", lambda e: e.dma_start(out=gf_bc, in_=rows_d[0:1, RO_BGF:RO_BGF + D].to_broadcast([128, D])),
          writes=["gf_bc"])
    S.dma("pool", "wpool", lambda e: e.dma_start(out=wpool3, in_=wpool_d), writes=["wpool"])

    S.dve(lambda e: e.memset(epsb, float(DV * DV) * EPS), writes=["epsb"])
    S.dve(lambda e: e.tensor_scalar(out=gnw128, in0=vecs[:, VO_GNW:VO_GNW + 8], scalar1=float(DV), scalar2=None, op0=ALU.mult),
          reads=["vecs"], writes=["gnw128"])
    cv3 = vecs[:, VO_C:VO_C + 16].rearrange("p (k t) -> p k t", t=2)
    S.act(lambda e: e.activation(out=scT3, in_=cv3, func=AF.Silu), reads=["vecs"], writes=["scT"])
    S.act(lambda e: e.activation(out=scbc3, in_=cv3[:, :, 0:1].to_broadcast([128, 8, 128]), func=AF.Silu),
          reads=["vecs"], writes=["scbc"])

    def wada_group(gi):
        buf = wada_buf[gi % 6]
        bk = "wada%d" % (gi % 6)
        S.dma("pool", bk, lambda e, buf=buf, gi=gi: e.dma_start(out=buf, in_=wada_d[:, :, gi * 512:(gi + 1) * 512]),
              writes=[bk])
        return buf, bk

    def wada_compute(gi, buf, bk):
        if gi in (4, 5, 10, 11):
            dst = gm_bc if gi in (4, 5) else gf_bc
            dk = "gm_bc" if gi in (4, 5) else "gf_bc"
            half = gi % 2 if gi in (4, 5) else (gi - 10)
            pb = 5 + (gi % 2)
            for k in range(8):
                S.pe(lambda e, k=k, buf=buf, pb=pb: e.matmul(bank(pb), lhsT=scbc3[:, k, :], rhs=buf[:, k, :],
                                                              start=(k == 0), stop=(k == 7)),
                     reads=[bk, "scbc"], writes=[PK(pb)])
            S.dve(lambda e, dst=dst, half=half, pb=pb: e.tensor_tensor(
                out=dst[:, half * 512:(half + 1) * 512], in0=bank(pb), in1=dst[:, half * 512:(half + 1) * 512], op=ALU.add),
                reads=[PK(pb), dk], writes=[dk])
        else:
            for jj in range(4):
                j = gi * 4 + jj
                for k in range(8):
                    S.pe(lambda e, k=k, jj=jj, j=j, buf=buf: e.matmul(
                        bank(7)[:, 2 * j:2 * j + 2], lhsT=buf[:, k, jj * 128:(jj + 1) * 128], rhs=scT3[:, k, :],
                        start=(k == 0), stop=(k == 7)),
                        reads=[bk, "scT"], writes=[PK(7)])

    def modT_evac(j0, j1):
        S.dve(lambda e, j0=j0, j1=j1: e.tensor_tensor(
            out=modT3[:, j0:j1, :], in0=bank(7)[:, 2 * j0:2 * j1].rearrange("p (j t) -> p j t", t=2),
            in1=vecs[:, VO_BADA + j0:VO_BADA + j1].unsqueeze(2).to_broadcast([128, j1 - j0, 2]), op=ALU.add),
            reads=[PK(7), "vecs"], writes=["modT"])

    def mod_ab(ai, bi, nvo, sh_blk, sc_blk, col):
        S.dve(lambda e: e.scalar_tensor_tensor(out=modAB3[:, ai, :], in0=modT3[:, sc_blk:sc_blk + 8, col], scalar=1.0,
                                               in1=vecs[:, nvo:nvo + 8], op0=ALU.add, op1=ALU.mult),
              reads=["modT", "vecs"], writes=["modAB%d" % ai])
        S.dve(lambda e: e.tensor_copy(out=modAB3[:, bi, :], in_=modT3[:, sh_blk:sh_blk + 8, col]),
              reads=["modT"], writes=["modAB%d" % bi])

    wg = [wada_group(gi) for gi in range(4)]

    def setup_mod_mix():
        for gi in range(4):
            wada_compute(gi, *wg[gi])
        modT_evac(0, 16)
        mod_ab(A_M, B_M, VO_NMIX, 0, 8, 0)
        mod_ab(A_C, B_C, VO_NMIX, 0, 8, 1)

    if stop == 0:
        setup_mod_mix()

    ee = stat[:, 0:16]
    tt = stat[:, 16:32]
    S.act(lambda e: e.activation(out=ee, in_=decbc, func=AF.Exp, scale=-1.0), reads=["decbc"], writes=["ee"])
    S.dve(lambda e: e.tensor_scalar(out=tt, in0=ee, scalar1=-1.0 / 7, scalar2=1.0 / 6, op0=ALU.mult, op1=ALU.add),
          reads=["ee"], writes=["tt"])
    for cf in (1.0 / 5, 1.0 / 4, 1.0 / 3, 1.0 / 2, 1.0):
        S.dve(lambda e: e.tensor_tensor(out=tt, in0=tt, in1=ee, op=ALU.mult), reads=["tt", "ee"], writes=["tt"])
        S.dve(lambda e, cf=cf: e.tensor_scalar(out=tt, in0=tt, scalar1=-1.0, scalar2=cf, op0=ALU.mult, op1=ALU.add),
              reads=["tt"], writes=["tt"])
    S.dve(lambda e: e.scalar_tensor_tensor(out=lgbc, in0=tt, scalar=-1.0, in1=ee, op0=ALU.mult, op1=ALU.mult),
          reads=["tt", "ee"], writes=["lgbc"])
    S.dve(lambda e: e.tensor_copy(out=lgsel[0:64, :], in_=lgbc[0:64, 0:8]), reads=["lgbc"], writes=["lgsel"])
    S.dve(lambda e: e.tensor_copy(out=lgsel[64:128, :], in_=lgbc[64:128, 8:16]), reads=["lgbc"], writes=["lgsel"])
    S.act(lambda e: e.activation(out=cdec, in_=lgsel, func=AF.Exp, scale=float(C)), reads=["lgsel"], writes=["cdec"])
    for d in range(2):
        S.act(lambda e, d=d: e.activation(out=kdec3[:, d, :], in_=lgbc[:, d * 8:(d + 1) * 8], func=AF.Exp, scale=csm(d)),
              reads=["lgbc", "consts"], writes=["kdec"])
        for t in range(2):
            S.act(lambda e, d=d, t=t: e.activation(out=ctxw4[:, t, d, :], in_=lgbc[:, d * 8:(d + 1) * 8], func=AF.Exp,
                                                   scale=csm(2 + 2 * d + t)),
                  reads=["lgbc", "consts"], writes=["ctxw"])
    S.dve(lambda e: e.tensor_scalar(out=kdec, in0=kdec, scalar1=K_SCALE, scalar2=None, op0=ALU.mult),
          reads=["kdec"], writes=["kdec"])
    S.dve(lambda e: e.tensor_scalar(out=ctxw, in0=ctxw, scalar1=K_SCALE, scalar2=None, op0=ALU.mult),
          reads=["ctxw"], writes=["ctxw"])
    for h in range(H):
        S.act(lambda e, h=h: e.activation(out=qdec3[:, h, :], in_=posq, func=AF.Exp, scale=lgsel[:, h:h + 1]),
              reads=["lgsel", "consts"], writes=["qdec"])
        S.dve(lambda e, h=h: e.tensor_scalar(out=mask3[:, h, :], in0=dpos, scalar1=lgbc[:, h:h + 1], scalar2=None,
                                             op0=ALU.mult), reads=["lgbc", "consts"], writes=["mask"])
        S.dve(lambda e, h=h: e.scalar_tensor_tensor(out=mask3[:, h, :], in0=dneg, scalar=lgbc[:, 8 + h:9 + h],
                                                    in1=mask3[:, h, :], op0=ALU.mult, op1=ALU.add),
              reads=["lgbc", "consts", "mask"], writes=["mask"])
    S.act(lambda e: e.activation(out=mask, in_=mask, func=AF.Exp), reads=["mask"], writes=["mask"])
    S.dve(lambda e: e.tensor_scalar(out=mask, in0=mask, scalar1=K_SCALE, scalar2=None, op0=ALU.mult),
          reads=["mask"], writes=["mask"])

    if stop == 0:
        for gi in range(4, 12):
            wg.append(wada_group(gi))
            wada_compute(gi, *wg[gi])
        modT_evac(24, 40)
        mod_ab(A_F, B_F, VO_NFFN, 24, 32, 0)
        return finish_debug([(modAB, 48), (lgbc, 16), (mask, 1024), (qdec, 1024), (gm_bc, 1024), (gf_bc, 1024),
                             (kdec, 16), (ctxw, 32), (cdec, 8)])
    R_PAIR.reset()
    xbuf = [R_PAIR.carve(4 * KB, F32) for _ in range(3)]
    junk = R_PAIR.carve(2 * KB)
    xnb = [R_PAIR.carve(2 * KB) for _ in range(2)]
    nctr = [0]

    def norm_stats(src_ap, src_key, junk, xnb):
        n = nctr[0]
        nctr[0] += 1
        ss = stat[:, 32 + (n % 4) * 2:33 + (n % 4) * 2]
        rs = stat[:, 33 + (n % 4) * 2:34 + (n % 4) * 2]
        sk = "nstat%d" % (n % 4)
        xn = xnb[n % len(xnb)]
        xk = "xn%d_%d" % (len(xnb), n % len(xnb))
        S.act(lambda e: e.activation(out=junk, in_=src_ap, func=AF.Square, accum_out=ss),
              reads=[src_key], writes=["junk", sk])
        S.dve(lambda e: e.tensor_scalar(out=rs, in0=ss, scalar1=1.0 / D, scalar2=EPS, op0=ALU.mult, op1=ALU.add),
              reads=[sk], writes=[sk + "r"])
        S.act(lambda e: e.activation(out=rs, in_=rs, func=AF.Sqrt), reads=[sk + "r"], writes=[sk + "r"])
        S.dve(lambda e: e.reciprocal(out=rs, in_=rs), reads=[sk + "r"], writes=[sk + "r"])
        S.dve(lambda e: e.tensor_scalar(out=xn, in0=src_ap, scalar1=rs, scalar2=None, op0=ALU.mult),
              reads=[src_key, sk + "r"], writes=[xk])
        return (n, xn, xk)

    ACT_K = (1, 4, 6)

    def norm_tr(st, ai, bi, dst3, dst_key, tcol, pbase=0, fused=True):
        n, xn, xk = st
        par = n % 2
        pbD, pbA = pbase + 2 * par, pbase + 2 * par + 1
        pTD = bank(pbD, BF16)[:, 0:640].rearrange("p (k t) -> p k t", t=128)
        pTA = bank(pbA, BF16)[:, 0:384].rearrange("p (k t) -> p k t", t=128)
        slot = {}
        na = nd = 0
        for k in range(8):
            if k in ACT_K:
                slot[k] = (pTA, pbA, na)
                na += 1
            else:
                slot[k] = (pTD, pbD, nd)
                nd += 1
        for k in range(8):
            pt_, pbk, sl = slot[k]
            S.pe(lambda e, k=k, pt_=pt_, sl=sl: e.transpose(out=pt_[:, sl, :], in_=xn[:, k * 128:(k + 1) * 128], identity=ident),
                 reads=[xk, "ident"], writes=[PK(pbk)])
        for k in range(8):
            o = dst3[:, k, tcol * 128:(tcol + 1) * 128]
            pt_, pbk, sl = slot[k]
            if not fused:
                if k not in ACT_K:
                    S.dve(lambda e, o=o, pt_=pt_, sl=sl: e.tensor_copy(out=o, in_=pt_[:, sl, :]),
                          reads=[PK(pbk)], writes=[dst_key(k, tcol)])
                else:
                    S.act(lambda e, o=o, pt_=pt_, sl=sl: e.activation(out=o, in_=pt_[:, sl, :], func=AF.Copy),
                          reads=[PK(pbk)], writes=[dst_key(k, tcol)])
            elif k not in ACT_K:
                S.dve(lambda e, k=k, o=o, pt_=pt_, sl=sl: e.tensor_scalar(out=o, in0=pt_[:, sl, :], scalar1=modAB3[:, ai, k:k + 1],
                                                                          scalar2=modAB3[:, bi, k:k + 1], op0=ALU.mult, op1=ALU.add),
                      reads=[PK(pbk), "modAB%d" % ai, "modAB%d" % bi], writes=[dst_key(k, tcol)])
            else:
                S.act(lambda e, k=k, o=o, pt_=pt_, sl=sl: e.activation(out=o, in_=pt_[:, sl, :], func=AF.Identity,
                                                                       scale=modAB3[:, ai, k:k + 1], bias=modAB3[:, bi, k:k + 1]),
                      reads=[PK(pbk), "modAB%d" % ai, "modAB%d" % bi], writes=[dst_key(k, tcol)])

    def hT_key(k, i):
        return "XA_%d_%d" % (k, i)

    pendA = []
    xnb = xnb + [arena[:, R_YP.start // 2:(R_YP.start + 2 * KB) // 2]]
    xbuf = xbuf + [arena[:, (R_YP.start + (4 + 4 * i) * KB) // 2:(R_YP.start + (8 + 4 * i) * KB) // 2].bitcast(F32) for i in range(3)]
    for t in range(2 + NT):
        sbi = t % len(xbuf)
        xb = xbuf[sbi]
        if t < 2:
            S.dma("sp", "xb%d" % sbi, lambda e, xb=xb, t=t: e.dma_start(out=xb, in_=ctx_d[t * 128:(t + 1) * 128, :]),
                  writes=["xb%d" % sbi])
            args = (A_C, B_C, hcT3, (lambda k, tc: "hcT%d" % k), t)
        else:
            i = t - 2
            S.dma("sp", "xb%d" % sbi, lambda e, xb=xb, i=i: e.dma_start(out=xb, in_=x_d[i * 128:(i + 1) * 128, :]),
                  writes=["xb%d" % sbi])
            args = (A_M, B_M, hT, hT_key, i)
        st = norm_stats(xb, "xb%d" % sbi, junk, xnb)
        pendA.append((st,) + args)
        if len(pendA) > 2:
            norm_tr(*pendA.pop(0), fused=False)
    while pendA:
        norm_tr(*pendA.pop(0), fused=False)
    setup_mod_mix()
    for hf in range(2):
        for k in range(8):
            S.dve(lambda e, k=k, hf=hf: e.tensor_scalar(out=hT[:, k, hf * 1024:(hf + 1) * 1024], in0=hT[:, k, hf * 1024:(hf + 1) * 1024],
                                                        scalar1=modAB3[:, A_M, k:k + 1], scalar2=modAB3[:, B_M, k:k + 1],
                                                        op0=ALU.mult, op1=ALU.add),
                  reads=[hT_key(k, i) for i in range(hf * 8, hf * 8 + 8)] + ["modAB%d" % A_M, "modAB%d" % B_M],
                  writes=[hT_key(k, i) for i in range(hf * 8, hf * 8 + 8)])
    for k in range(8):
        S.dve(lambda e, k=k: e.tensor_scalar(out=hcT3[:, k, :], in0=hcT3[:, k, :], scalar1=modAB3[:, A_C, k:k + 1],
                                             scalar2=modAB3[:, B_C, k:k + 1], op0=ALU.mult, op1=ALU.add),
              reads=["hcT%d" % k, "modAB%d" % A_C, "modAB%d" % B_C], writes=["hcT%d" % k])
    wlate = arena[:, (R_M.start + 24 * KB) // 2:(R_M.start + 32 * KB) // 2].rearrange("p (k n) -> p k n", n=512)
    WL_ALIAS = ["ysb0", "ysb1", "ysb2", "ysq"]

    def wada_late_load(gi):
        S.dma("pool", "wlate", lambda e: e.dma_start(out=wlate, in_=wada_d[:, :, gi * 512:(gi + 1) * 512]),
              writes=["wlate"] + WL_ALIAS)

    def wada_late_compute(gi):
        rd = ["wlate"] + WL_ALIAS
        if gi in (4, 5, 10, 11):
            dst = gm_bc if gi in (4, 5) else gf_bc
            dk = "gm_bc" if gi in (4, 5) else "gf_bc"
            half = gi % 2 if gi in (4, 5) else (gi - 10)
            for k in range(8):
                S.pe(lambda e, k=k: e.matmul(bank(7), lhsT=scbc3[:, k, :], rhs=wlate[:, k, :], start=(k == 0), stop=(k == 7)),
                     reads=rd + ["scbc"], writes=[PK(7)])
            S.dve(lambda e: e.tensor_tensor(out=dst[:, half * 512:(half + 1) * 512], in0=bank(7),
                                            in1=dst[:, half * 512:(half + 1) * 512], op=ALU.add),
                  reads=[PK(7), dk], writes=[dk])
        else:
            for jj in range(4):
                j = gi * 4 + jj
                for k in range(8):
                    S.pe(lambda e, k=k, jj=jj, j=j: e.matmul(bank(7)[:, 2 * j:2 * j + 2], lhsT=wlate[:, k, jj * 128:(jj + 1) * 128],
                                                             rhs=scT3[:, k, :], start=(k == 0), stop=(k == 7)),
                         reads=rd + ["scT"], writes=[PK(7)])
            modT_evac(gi * 4, gi * 4 + 4)
    hcT_keys = ["hcT%d" % k for k in range(8)]

    if stop == 1:
        return finish_debug([(hcT, 2048), (hT.rearrange("p k t -> p (k t)")[:, 0:8192], 8192)])

    def load_pair(p):
        buf = wpair_buf[p % 2]
        extra = [S.fence()] if p < 2 else []
        S.dma("pool", "wpair%d" % (p % 2), lambda e, buf=buf, p=p: e.dma_start(out=buf, in_=wpairs_d[p]),
              writes=["wpair%d" % (p % 2)] + extra)

    wu_sb = R_PAIR.carve(8 * KB).rearrange("p (k n) -> p k n", n=512)
    NPST = 9
    pstage = [R_PAIR.carve(4 * KB).rearrange("p (b t) -> p b t", t=512) for _ in range(NPST)]
    S.dma("pool", "wu", lambda e: e.dma_start(out=wu_sb, in_=wu_d), writes=["wu"])
    load_pair(0)
    for i in range(NT):
        pb = 2 + (i % 2)
        for k in range(8):
            S.pe(lambda e, k=k, i=i, pb=pb: e.matmul(bank(pb), lhsT=hT[:, k, i * 128:(i + 1) * 128], rhs=wu_sb[:, k, :],
                                                     start=(k == 0), stop=(k == 7)),
                 reads=[hT_key(k, i), "wu"], writes=[PK(pb)])
        if i % 2 == 0:
            S.act(lambda e, i=i, pb=pb: e.activation(out=u_tok[:, i, :], in_=bank(pb), func=AF.Copy),
                  reads=[PK(pb)], writes=["u%d" % i])
        else:
            S.dve(lambda e, i=i, pb=pb: e.tensor_copy(out=u_tok[:, i, :], in_=bank(pb)),
                  reads=[PK(pb)], writes=["u%d" % i])
    pst_n = [0]
    resident = {}

    def pool_head(pctr, g, j):
        lst = pool_plan[g][j]
        pd = 4 + (pctr % 2)
        dsb = dT_sb[pctr % 2]
        dk = "dT%d" % (pctr % 2)
        runs = []
        for ent in lst:
            if runs and len(runs[-1]) < 4 and runs[-1][-1][1] + 1 == ent[1]:
                runs[-1].append(ent)
            else:
                runs.append([ent])
        pos = 0
        for grp in runs:
            b0 = grp[0][1]
            nb = len(grp)
            assert all(grp[q][1] == b0 + q for q in range(nb))
            hit = resident.get((b0, nb))
            if hit is not None and pst_n[0] - hit < NPST:
                st = pstage[hit % NPST]
                sk = "pst%d" % (hit % NPST)
            else:
                st = pstage[pst_n[0] % NPST]
                sk = "pst%d" % (pst_n[0] % NPST)
                resident[(b0, nb)] = pst_n[0]
                pst_n[0] += 1
                S.dma("sp", sk, lambda e: e.dma_start(out=st[:, 0:nb, :], in_=pblk_d[:, b0:b0 + nb, :]), writes=[sk])
            for q, (t, bi_) in enumerate(grp):
                first = (pos == 0)
                last = (pos == len(lst) - 1)
                pos += 1
                S.pe(lambda e, t=t, q=q, first=first, last=last: e.matmul(
                    bank(pd), lhsT=u_tok[:, t, g * 128:(g + 1) * 128], rhs=st[:, q, :], start=first, stop=last),
                    reads=["u%d" % t, sk], writes=[PK(pd)])
        S.dve(lambda e: e.tensor_copy(out=dsb, in_=bank(pd)), reads=[PK(pd)], writes=[dk])

    def pool_tail(pctr, g, j):
        py = 6 + (pctr % 2)
        dsb = dT_sb[pctr % 2]
        dk = "dT%d" % (pctr % 2)
        S.pe(lambda e: e.matmul(bank(py), lhsT=wpool3[:, g, :], rhs=dsb, start=True, stop=True),
             reads=[dk, "wpool"], writes=[PK(py)])
        S.act(lambda e: e.activation(out=ypT[:, g, j * 512:(j + 1) * 512], in_=bank(py), func=AF.Copy,
                                     scale=vecs[:, VO_PSC + g:VO_PSC + g + 1]),
              reads=[PK(py), "vecs"], writes=["ypT"])

    pblocks = [(g, j) for g in range(4) for j in range(4)]
    pool_head(0, *pblocks[0])
    for c_ in range(len(pblocks)):
        if c_ + 1 < len(pblocks):
            pool_head(c_ + 1, *pblocks[c_ + 1])
        pool_tail(c_, *pblocks[c_])

    if stop == 2:
        return finish_debug([(ypT.rearrange("p g t -> p (g t)"), 8192)])
    R_PAIR.reset()
    kT2 = R_PAIR.carve(4 * KB)
    qTz = R_PAIR.carve(8 * KB).rearrange("p (n h t) -> p n h t", h=2, t=128)
    zf = S.fence()
    S.dve(lambda e: e.memset(qTz[64:128, :, 0, :], 0.0), writes=["qTz_zero", zf])
    S.dve(lambda e: e.memset(qTz[0:64, :, 1, :], 0.0), writes=["qTz_zero", zf])
    q2 = R_PAIR.carve(8 * KB).rearrange("p (h t) -> p h t", t=L)
    kfb = R_PAIR.carve(8 * KB).rearrange("p (i d c) -> p i d c", d=2, c=128)
    v_tok = R_PAIR.carve(8 * KB).rearrange("p (i c) -> p i c", c=256)
    sg_tok = R_PAIR.carve(8 * KB).rearrange("p (i c) -> p i c", c=256)
    s_all = R_PAIR.carve(8 * KB).rearrange("p (n h v) -> p n h v", h=2, v=128)
    rot = [R_PAIR.carve(1 * KB).rearrange("p (t a c) -> p t a c", t=2, c=64) for _ in range(2)]
    qdup = R_PAIR.carve(1 * KB).rearrange("p (t a u c) -> p t a u c", t=2, u=2, c=64)
    _rt0 = R_PAIR.off
    rtmp = [R_PAIR.carve(1 * KB, F32).rearrange("p (t a f) -> p t a f", t=2, f=32) for _ in range(4)]
    _rt1 = R_PAIR.off
    R_PAIR.off = _rt0
    kcw = R_PAIR.carve(1 * KB).rearrange("p (t d c) -> p t d c", d=2, c=128)
    vc = R_PAIR.carve(1 * KB).rearrange("p (t c) -> p t c", c=256)
    R_PAIR.off = _rt1
    pmat = [R_PAIR.carve(1 * KB).rearrange("p (c h t) -> p c h t", c=2, t=128) for _ in range(2)]
    ygb = [R_PAIR.carve(1 * KB).rearrange("p (c h v) -> p c h v", c=2, v=128) for _ in range(2)]
    _rm = R_M.start + 24 * KB
    y_sb = [arena[:, (_rm + i * 2 * KB) // 2:(_rm + (i + 1) * 2 * KB) // 2].bitcast(F32).rearrange("p (c h v) -> p c h v", c=2, v=128)
            for i in range(3)]
    ysq = arena[:, (_rm + 6 * KB) // 2:(_rm + 8 * KB) // 2].bitcast(F32).rearrange("p (c h v) -> p c h v", c=2, v=128)
    sfp3 = sfp[:, 0:256].rearrange("p (h v) -> p h v", v=128)
    gst = sfp[:, 256:384]

    def ygT_key(k, i):
        return "XB_%d_%d" % (k, i)

    def make_pair(p):
        wp = wpair_buf[p % 2]
        wk = "wpair%d" % (p % 2)

        def ctx_pe():
            if 1 <= p + 1 < 4:
                load_pair(p + 1)
            for t in range(2):
                pb = 6 + t
                for k in range(8):
                    S.pe(lambda e, k=k, t=t, pb=pb: e.matmul(bank(pb)[:, 0:384], lhsT=hcT3[:, k, t * 128:(t + 1) * 128],
                                                             rhs=wp[:, k, 128:512], start=(k == 0), stop=(k == 7)),
                         reads=["hcT%d" % k, wk], writes=[PK(pb)])

        def ctx_rest():
            for t in range(2):
                pb = 6 + t
                for d in range(2):
                    S.dve(lambda e, t=t, d=d, pb=pb: e.tensor_tensor(
                        out=kcw[:, t, d, :].rearrange("p (h c) -> p h c", c=64),
                        in0=bank(pb)[:, 0:128].rearrange("p (h c) -> p h c", c=64),
                        in1=ctxw4[:, t, d, 2 * p:2 * p + 2].unsqueeze(2).to_broadcast([128, 2, 64]), op=ALU.mult),
                        reads=[PK(pb), "ctxw"], writes=["kcw"])
                S.dve(lambda e, t=t, pb=pb: e.tensor_copy(out=vc[:, t, :], in_=bank(pb)[:, 128:384]),
                      reads=[PK(pb)], writes=["vc"])
            pS = bank(1)[:, 0:256].rearrange("p (h v) -> p h v", v=128)
            for h2 in range(2):
                for d in range(2):
                    for t in range(2):
                        S.pe(lambda e, h2=h2, d=d, t=t: e.matmul(pS[d * 64:(d + 1) * 64, h2, :],
                                                                 lhsT=kcw[:, t, d, h2 * 64:(h2 + 1) * 64],
                                                                 rhs=vc[:, t, h2 * 128:(h2 + 1) * 128],
                                                                 start=(t == 0), stop=(t == 1)),
                             reads=["kcw", "vc"], writes=["ps1a", "ps1b"])
            S.dve(lambda e: e.tensor_copy(out=sfp3, in_=pS), reads=["ps1a", "ps1b"], writes=["sfp"])
            S.dve(lambda e: e.tensor_copy(out=s_all[0:64, 0, :, :], in_=pS[0:64, :, :]),
                  reads=["ps1a", "ps1b"], writes=["sall_f0"])
            S.dve(lambda e: e.tensor_copy(out=s_all[64:128, NT - 1, :, :], in_=pS[64:128, :, :]),
                  reads=["ps1a", "ps1b"], writes=["sall_b%d" % (NT - 1)])


        def sb_proj(it, i, pe_only=False):
            par = it % 2
            pq = 2 + par
            for t in range(2):
                for k in range(8):
                    S.pe(lambda e, k=k, t=t: e.matmul(bank(pq)[:, t * 256:(t + 1) * 256], lhsT=hT[:, k, (i + t) * 128:(i + t + 1) * 128],
                                                      rhs=wp[:, k, 0:256], start=(k == 0), stop=(k == 7)),
                         reads=[hT_key(k, i + t), wk], writes=[PK(pq)])
            for t in range(2):
                for k in range(8):
                    S.pe(lambda e, k=k, t=t: e.matmul(bank(4 + t), lhsT=hT[:, k, (i + t) * 128:(i + t + 1) * 128],
                                                      rhs=wp[:, k, 256:768], start=(k == 0), stop=(k == 7)),
                         reads=[hT_key(k, i + t), wk], writes=[PK(4 + t)])
            if not pe_only:
                sb_proj_evac(it, i)

        def sb_proj_evac(it, i):
            vg = ps[:, 4 * 512:6 * 512].rearrange("p (t c) -> p t c", t=2)
            S.act(lambda e: e.activation(out=v_tok[:, i:i + 2, :], in_=vg[:, :, 0:256], func=AF.Copy),
                  reads=[PK(4), PK(5)], writes=["v%d" % i, "v%d" % (i + 1)])
            S.act(lambda e: e.activation(out=sg_tok[:, i:i + 2, :], in_=vg[:, :, 256:512], func=AF.Silu),
                  reads=[PK(4), PK(5)], writes=["sg%d" % i, "sg%d" % (i + 1)])

        def sb_rope(it, i):
            par = it % 2
            pq = 2 + par
            qk4 = bank(pq).rearrange("p (t a c) -> p t a c", t=2, c=64)
            t1 = qk4[:, :, :, 0:32]
            t2 = qk4[:, :, :, 32:64]
            cs = cos3[:, i:i + 2, :].unsqueeze(2).to_broadcast([128, 2, 4, 32])
            sn = sin3[:, i:i + 2, :].unsqueeze(2).to_broadcast([128, 2, 4, 32])
            rt = rot[par]
            rk = "rot%d" % par
            ta, tb_, tc_, td = rtmp
            S.dve(lambda e: e.tensor_tensor(out=ta, in0=t1, in1=cs, op=ALU.mult), reads=[PK(pq), "consts"], writes=["rta", "kcw", "vc"])
            S.dve(lambda e: e.tensor_tensor(out=tb_, in0=t2, in1=sn, op=ALU.mult), reads=[PK(pq), "consts"], writes=["rtb", "kcw", "vc"])
            S.dve(lambda e: e.tensor_tensor(out=tc_, in0=t1, in1=sn, op=ALU.mult), reads=[PK(pq), "consts"], writes=["rtc", "kcw", "vc"])
            S.dve(lambda e: e.tensor_tensor(out=td, in0=t2, in1=cs, op=ALU.mult), reads=[PK(pq), "consts"], writes=["rtd", "kcw", "vc"])
            S.dve(lambda e: e.tensor_tensor(out=rt[:, :, :, 0:32], in0=ta, in1=tb_, op=ALU.subtract),
                  reads=["rta", "rtb"], writes=[rk])
            S.dve(lambda e: e.tensor_tensor(out=rt[:, :, :, 32:64], in0=tc_, in1=td, op=ALU.add),
                  reads=["rtc", "rtd"], writes=[rk])
            for t in range(2):
                S.act(lambda e, t=t: e.activation(out=qdup[:, t, :, :, :], in_=rt[:, t, 0:2, :].unsqueeze(2).to_broadcast([128, 2, 2, 64]),
                                                  func=AF.Copy), reads=[rk], writes=["qdup"])
            for d in range(2):
                S.dve(lambda e, d=d: e.tensor_tensor(
                    out=kfb[:, i:i + 2, d, :].rearrange("p t (h c) -> p t h c", c=64), in0=rt[:, :, 2:4, :],
                    in1=kdec3[:, d, 2 * p:2 * p + 2].unsqueeze(1).unsqueeze(3).to_broadcast([128, 2, 2, 64]), op=ALU.mult),
                    reads=[rk, "kdec"], writes=["kfb%d" % i, "kfb%d" % (i + 1)])

        def sb_tr(it, i):
            par = it % 2
            rt = rot[par]
            rk = "rot%d" % par
            sfx = "ab"[par]
            pTA = bank(0, BF16)[:, par * 512:(par + 1) * 512].rearrange("p (t a x) -> p t a x", t=2, x=128)
            pTD = bank(1, BF16)[:, par * 512:(par + 1) * 512].rearrange("p (t h x) -> p t h x", t=2, x=128)
            for t in range(2):
                S.pe(lambda e, t=t: e.transpose(out=pTA[:, t, 0, :], in_=rt[:, t, 0:2, :].rearrange("p a c -> p (a c)"), identity=ident),
                     reads=[rk, "ident"], writes=["ps0" + sfx])
                S.pe(lambda e, t=t: e.transpose(out=pTA[:, t, 1, :], in_=rt[:, t, 2:4, :].rearrange("p a c -> p (a c)"), identity=ident),
                     reads=[rk, "ident"], writes=["ps0" + sfx])
                for h2 in range(2):
                    S.pe(lambda e, t=t, h2=h2: e.transpose(out=pTD[:, t, h2, :], in_=qdup[:, t, h2, :, :].rearrange("p u c -> p (u c)"),
                                                           identity=ident),
                         reads=["qdup", "ident"], writes=["ps1" + sfx])
            qk_keys = ["qkT%d" % i, "qkT%d" % (i + 1)]
            S.act(lambda e: e.activation(out=kT2[:, i * 128:(i + 2) * 128].rearrange("p (t x) -> p t x", t=2), in_=pTA[:, :, 1, :], func=AF.Copy),
                  reads=["ps0" + sfx], writes=qk_keys)
            S.act(lambda e: e.activation(out=qTz[0:64, i:i + 2, 0, :], in_=pTA[0:64, :, 0, :], func=AF.Copy),
                  reads=["ps0" + sfx, "qTz_zero"], writes=qk_keys)
            S.act(lambda e: e.activation(out=qTz[64:128, i:i + 2, 1, :], in_=pTA[64:128, :, 0, :], func=AF.Copy),
                  reads=["ps0" + sfx, "qTz_zero"], writes=qk_keys)
            S.dve(lambda e: e.tensor_tensor(out=q2[:, :, i * 128:(i + 2) * 128].rearrange("p h (t x) -> p t h x", t=2), in0=pTD,
                                            in1=qdec3[:, 2 * p:2 * p + 2, :].unsqueeze(1).to_broadcast([128, 2, 2, 128]), op=ALU.mult),
                  reads=["ps1" + sfx, "qdec"], writes=["q2_%d" % i, "q2_%d" % (i + 1)])

        def scan_pe(j):
            sfx = "ab"[j % 2]
            pD = bank(6)[:, (j % 2) * 256:(j % 2) * 256 + 256].rearrange("p (h v) -> p h v", v=128)
            nf, nb = j, NT - 1 - j
            for h2 in range(2):
                S.pe(lambda e, h2=h2: e.matmul(pD[0:64, h2, :], lhsT=kfb[:, nf, 0, h2 * 64:(h2 + 1) * 64],
                                               rhs=v_tok[:, nf, h2 * 128:(h2 + 1) * 128], start=True, stop=True),
                     reads=["kfb%d" % nf, "v%d" % nf], writes=["ps6" + sfx])
                S.pe(lambda e, h2=h2: e.matmul(pD[64:128, h2, :], lhsT=kfb[:, nb, 1, h2 * 64:(h2 + 1) * 64],
                                               rhs=v_tok[:, nb, h2 * 128:(h2 + 1) * 128], start=True, stop=True),
                     reads=["kfb%d" % nb, "v%d" % nb], writes=["ps6" + sfx])

        def scan_ew(j):
            sfx = "ab"[j % 2]
            pD = bank(6)[:, (j % 2) * 256:(j % 2) * 256 + 256].rearrange("p (h v) -> p h v", v=128)
            nf, nb = j, NT - 1 - j
            for h2 in range(2):
                S.dve(lambda e, h2=h2: e.scalar_tensor_tensor(
                    out=sfp3[:, h2, :], in0=sfp3[:, h2, :], scalar=cdec[:, 2 * p + h2:2 * p + h2 + 1], in1=pD[:, h2, :],
                    op0=ALU.mult, op1=ALU.add), reads=["sfp", "ps6" + sfx, "cdec"], writes=["sfp"])
            if os.environ.get("KDBG_CAST", "dve") == "dve":
                S.dve(lambda e: e.tensor_copy(out=s_all[0:64, nf + 1, :, :], in_=sfp3[0:64, :, :]),
                      reads=["sfp"], writes=["sall_f%d" % (nf + 1)])
                S.dve(lambda e: e.tensor_copy(out=s_all[64:128, nb - 1, :, :], in_=sfp3[64:128, :, :]),
                      reads=["sfp"], writes=["sall_b%d" % (nb - 1)])
            else:
                S.act(lambda e: e.activation(out=s_all[0:64, nf + 1, :, :], in_=sfp3[0:64, :, :], func=AF.Copy),
                      reads=["sfp"], writes=["sall_f%d" % (nf + 1)])
                S.act(lambda e: e.activation(out=s_all[64:128, nb - 1, :, :], in_=sfp3[64:128, :, :], func=AF.Copy),
                      reads=["sfp"], writes=["sall_b%d" % (nb - 1)])

        def scan_step(j):
            scan_pe(j)
            scan_ew(j)

        class OutStep:
            def __init__(self, it, n0):
                self.it, self.n0 = it, n0
                par = it % 2
                self.psc = 2 + par
                self.pyb = 4 + par
                self.pS4 = bank(self.psc).rearrange("p (c h t) -> p c h t", c=2, t=128)
                self.pY4 = bank(self.pyb).rearrange("p (c h v) -> p c h v", c=2, v=128)
                self.pm = pmat[par]
                self.pmk = "pmat%d" % par
                self.ysb = y_sb[it % 3]
                self.yk = "ysb%d" % (it % 3)
                self.gs = gst[:, (it % 4) * 32:(it % 4) * 32 + 32]
                self.gk = "gst%d" % (it % 4)
                self.yg = ygb[par]
                self.ygk = "ygb%d" % par
                self.sfx = "ab"[par]
                self.pT4 = bank(0, BF16)[:, par * 512:(par + 1) * 512].rearrange("p (h c t) -> p h c t", h=2, t=128)

            def scores(o):
                for c in range(2):
                    n = o.n0 + c
                    tsl = slice(n * 128, (n + 1) * 128)
                    S.pe(lambda e, c=c, n=n, tsl=tsl: e.matmul(bank(o.psc)[:, c * 256:(c + 1) * 256], lhsT=kT2[:, tsl],
                                                               rhs=qTz[:, n, :, :].rearrange("p h t -> p (h t)"), start=True, stop=True),
                         reads=["qkT%d" % n, "qTz_zero"], writes=[PK(o.psc)])

            def mask(o):
                S.dve(lambda e: e.tensor_tensor(out=o.pm, in0=o.pS4,
                                                in1=mask3[:, 2 * p:2 * p + 2, :].unsqueeze(1).to_broadcast([128, 2, 2, 128]),
                                                op=ALU.mult), reads=[PK(o.psc), "mask"], writes=[o.pmk])

            def av(o):
                for c in range(2):
                    n = o.n0 + c
                    tsl = slice(n * 128, (n + 1) * 128)
                    for h2 in range(2):
                        S.pe(lambda e, c=c, n=n, h2=h2: e.matmul(o.pY4[:, c, h2, :], lhsT=o.pm[:, c, h2, :],
                                                                 rhs=v_tok[:, n, h2 * 128:(h2 + 1) * 128], start=True, stop=False),
                             reads=[o.pmk, "v%d" % n], writes=[PK(o.pyb)])
                        S.pe(lambda e, c=c, n=n, h2=h2, tsl=tsl: e.matmul(o.pY4[:, c, h2, :], lhsT=q2[:, h2, tsl],
                                                                          rhs=s_all[:, n, h2, :], start=False, stop=True),
                             reads=["q2_%d" % n, "sall_f%d" % n, "sall_b%d" % n], writes=[PK(o.pyb)])

            def copy_sq(o):
                S.act(lambda e: e.activation(out=o.ysb, in_=o.pY4, func=AF.Copy), reads=[PK(o.pyb)], writes=[o.yk])
                for c in range(2):
                    for h2 in range(2):
                        q_ = c * 2 + h2
                        S.act(lambda e, c=c, h2=h2, q_=q_: e.activation(out=ysq[:, c, h2, :], in_=o.pY4[:, c, h2, :], func=AF.Square,
                                                                        accum_out=o.gs[:, 4 + q_:5 + q_]),
                              reads=[PK(o.pyb)], writes=["ysq", o.gk + "q"])

            def reduce(o):
                S.dve(lambda e: e.tensor_reduce(out=o.gs[:, 0:4], in_=o.ysb.rearrange("p c h v -> p (c h) v"), axis=AX.X, op=ALU.add),
                      reads=[o.yk], writes=[o.gk + "s"])

            def var(o):
                S.dve(lambda e: e.tensor_tensor(out=o.gs[:, 8:12], in0=o.gs[:, 0:4], in1=o.gs[:, 0:4], op=ALU.mult),
                      reads=[o.gk + "s"], writes=[o.gk + "ss"])
                S.dve(lambda e: e.scalar_tensor_tensor(out=o.gs[:, 12:16], in0=o.gs[:, 4:8], scalar=float(DV), in1=o.gs[:, 8:12],
                                                       op0=ALU.mult, op1=ALU.subtract),
                      reads=[o.gk + "q", o.gk + "ss"], writes=[o.gk + "v"])

            def sqrt(o):
                S.act(lambda e: e.activation(out=o.gs[:, 16:20], in_=o.gs[:, 12:16], func=AF.Sqrt, bias=epsb[:, 0:1]),
                      reads=[o.gk + "v", "epsb"], writes=[o.gk + "sd"])

            def rstd(o):
                S.dve(lambda e: e.reciprocal(out=o.gs[:, 24:28], in_=o.gs[:, 16:20]), reads=[o.gk + "sd"], writes=[o.gk + "r"])
                S.dve(lambda e: e.scalar_tensor_tensor(out=o.gs[:, 28:32], in0=o.gs[:, 0:4], scalar=-1.0 / DV, in1=o.gs[:, 24:28],
                                                       op0=ALU.mult, op1=ALU.mult),
                      reads=[o.gk + "s", o.gk + "r"], writes=[o.gk + "nb"])

            def normalize(o):
                for c in range(2):
                    for h2 in range(2):
                        q_ = c * 2 + h2
                        S.act(lambda e, c=c, h2=h2, q_=q_: e.activation(out=o.yg[:, c, h2, :], in_=o.ysb[:, c, h2, :], func=AF.Identity,
                                                                        scale=o.gs[:, 24 + q_:25 + q_], bias=o.gs[:, 28 + q_:29 + q_]),
                              reads=[o.yk, o.gk + "r", o.gk + "nb"], writes=[o.ygk])

            def gate(o):
                S.dve(lambda e: e.tensor_tensor(out=o.yg.rearrange("p c h v -> p c (h v)"),
                                                in0=o.yg.rearrange("p c h v -> p c (h v)"),
                                                in1=sg_tok[:, o.n0:o.n0 + 2, :], op=ALU.mult),
                      reads=[o.ygk, "sg%d" % o.n0, "sg%d" % (o.n0 + 1)], writes=[o.ygk])

            def transposes(o):
                for c in range(2):
                    for h2 in range(2):
                        S.pe(lambda e, c=c, h2=h2: e.transpose(out=o.pT4[:, h2, c, :], in_=o.yg[:, c, h2, :], identity=ident),
                             reads=[o.ygk, "ident"], writes=["ps0" + o.sfx])

            def evac(o):
                for h2 in range(2):
                    kk = 2 * p + h2
                    S.act(lambda e, h2=h2, kk=kk: e.activation(out=ygT[:, kk, o.n0 * 128:(o.n0 + 2) * 128],
                                                               in_=o.pT4[:, h2, :, :].rearrange("p c t -> p (c t)"), func=AF.Copy,
                                                               scale=gnw128[:, kk:kk + 1]),
                          reads=["ps0" + o.sfx, "gnw128"], writes=[ygT_key(kk, o.n0), ygT_key(kk, o.n0 + 1)])

        order_b = []
        for j in range(NT // 4):
            order_b += [2 * j, NT - 2 - 2 * j]
        def head_pe():
            sb_proj(0, order_b[0], pe_only=True)

        def head_rest():
            sb_proj_evac(0, order_b[0])
            sb_rope(0, order_b[0])

        def body(nxt):
            wada_late_load(4 + 2 * p)
            for it, i in enumerate(order_b):
                if it + 1 < len(order_b):
                    sb_proj(it + 1, order_b[it + 1])
                if it == 2:
                    wada_late_compute(4 + 2 * p)
                    wada_late_load(5 + 2 * p)
                if it == 5:
                    wada_late_compute(5 + 2 * p)
                    if p == 2:
                        mod_ab(A_F, B_F, VO_NFFN, 24, 32, 0)
                sb_tr(it, i)
                if it + 1 < len(order_b):
                    sb_rope(it + 1, order_b[it + 1])
                if it % 2 == 1:
                    scan_step(it - 1)
                    scan_step(it)
            order_o = [6, 8, 4, 10, 2, 12, 0, 14]
            NO = len(order_o)
            steps = [OutStep(it, order_o[it]) for it in range(NO)]

            def g(k):
                return steps[k] if 0 <= k < NO else None

            scan_step(NT // 2)
            def iteration(it):
                    a_, b_, c_, d_ = g(it), g(it - 1), g(it - 2), g(it - 3)
                    sj = NT // 2 + 1 + it if NT // 2 + 1 + it <= NT - 2 else None
                    ordm = os.environ.get("KDBG_ORD", "m1c")
                    if ordm == "r7":
                        if c_:
                            c_.var(); c_.sqrt(); c_.rstd()
                        if d_:
                            d_.normalize(); d_.gate(); d_.transposes(); d_.evac()
                        if b_:
                            b_.copy_sq(); b_.reduce()
                        if sj is not None:
                            scan_step(sj)
                        if a_:
                            a_.scores(); a_.mask(); a_.av()
                        return
                    if ordm == "m1":
                        if a_:
                            a_.scores()
                        if sj is not None:
                            scan_pe(sj)
                        if c_:
                            c_.var(); c_.sqrt(); c_.rstd()
                        if d_:
                            d_.normalize(); d_.gate(); d_.transposes(); d_.evac()
                        if b_:
                            b_.copy_sq(); b_.reduce()
                        if sj is not None:
                            scan_ew(sj)
                        if a_:
                            a_.mask(); a_.av()
                        return
                    if ordm == "m1c":
                        if a_:
                            a_.scores()
                        if sj is not None:
                            scan_pe(sj)
                        if c_:
                            c_.var(); c_.sqrt()
                        if sj is not None:
                            scan_ew(sj)
                        if c_:
                            c_.rstd()
                        if d_:
                            d_.normalize()
                        if b_:
                            b_.copy_sq()
                        if d_:
                            d_.gate(); d_.transposes()
                        if b_:
                            b_.reduce()
                        if d_:
                            d_.evac()
                        if a_:
                            a_.mask(); a_.av()
                        return
                    if ordm == "m1b":
                        if a_:
                            a_.scores()
                        if sj is not None:
                            scan_pe(sj)
                        if c_:
                            c_.var(); c_.sqrt(); c_.rstd()
                        if d_:
                            d_.normalize()
                        if b_:
                            b_.copy_sq()
                        if d_:
                            d_.gate(); d_.transposes()
                        if b_:
                            b_.reduce()
                        if d_:
                            d_.evac()
                        if sj is not None:
                            scan_ew(sj)
                        if a_:
                            a_.mask(); a_.av()
                        return
                    if ordm in ("m3", "m4"):
                        if a_:
                            a_.scores()
                        if sj is not None:
                            scan_pe(sj)
                        if c_:
                            c_.var(); c_.sqrt(); c_.rstd()
                        if ordm == "m4" and a_:
                            a_.mask(); a_.av()
                        if d_:
                            d_.normalize(); d_.gate(); d_.transposes(); d_.evac()
                        if ordm == "m3" and a_:
                            a_.mask(); a_.av()
                        if b_:
                            b_.copy_sq(); b_.reduce()
                        if sj is not None:
                            scan_ew(sj)
                        return
                    if ordm == "m2":
                        if a_:
                            a_.scores()
                        if sj is not None:
                            scan_pe(sj)
                        if c_:
                            c_.var(); c_.sqrt()
                        if a_:
                            a_.mask(); a_.av()
                        if c_:
                            c_.rstd()
                        if d_:
                            d_.normalize(); d_.gate(); d_.transposes(); d_.evac()
                        if b_:
                            b_.copy_sq(); b_.reduce()
                        if sj is not None:
                            scan_ew(sj)
                        return
                    if a_:
                        a_.scores()
                    if sj is not None:
                        scan_pe(sj)
                    if c_:
                        c_.var()
                        c_.sqrt()
                    if d_:
                        d_.normalize()
                    if a_:
                        a_.mask()
                        a_.av()
                    if c_:
                        c_.rstd()
                    if sj is not None:
                        scan_ew(sj)
                    if b_:
                        b_.copy_sq()
                    if d_:
                        d_.gate()
                        d_.transposes()
                    if b_:
                        b_.reduce()
                    if d_:
                        d_.evac()

            for it in range(NO + 3):
                iteration(it)
                if nxt is not None and it == NO:
                    nxt.ctx_pe()
                if nxt is not None and it == NO + 1:
                    nxt.head_pe()
            if nxt is not None:
                nxt.ctx_rest()
                nxt.head_rest()

        class _P:
            pass
        o_ = _P()
        o_.ctx_pe, o_.ctx_rest, o_.head_pe, o_.head_rest, o_.body = ctx_pe, ctx_rest, head_pe, head_rest, body
        return o_

    pairs = [make_pair(p) for p in range(4)]
    pairs[0].ctx_pe()
    pairs[0].ctx_rest()
    pairs[0].head_pe()
    pairs[0].head_rest()
    for p in range(4):
        pairs[p].body(pairs[p + 1] if p < 3 else None)

    if stop == 3:
        return finish_debug([(ygT.rearrange("p k t -> p (k t)"), 16384)])
    R_PAIR.reset()
    def _rp(off_kb, size_kb, dt=BF16):
        v = arena[:, (R_PAIR.start + off_kb * KB) // 2:(R_PAIR.start + (off_kb + size_kb) * KB) // 2]
        return v if dt == BF16 else v.bitcast(dt)

    wt_buf = [_rp(20, 7).rearrange("p (r n) -> p r n", n=128), _rp(0, 7).rearrange("p (r n) -> p r n", n=128)]
    wt_buf = [wt_buf[0], wt_buf[1]]
    sgab = [_rp(7 + 2 * i, 2, F32) for i in range(4)]
    wo_sb = _rp(28, 16).rearrange("p (k n) -> p k n", n=D)
    wo_stage = [_rp(44 + 4 * i, 4, F32) for i in range(2)]
    xres = [_rp(52 + 4 * i, 4, F32) for i in range(2)]
    R_PAIR.off = 60 * KB

    def mT_key(k, i):
        return "M_%d_%d" % (k, i)

    def load_tail(cb):
        extra = [S.fence()] if cb == 1 else (["kfb%d" % i for i in range(NT)] if cb == 0 else [])
        S.dma("pool", "wt%d" % (cb % 2), lambda e, cb=cb: e.dma_start(out=wt_buf[cb % 2], in_=wtail_d[cb]),
              writes=["wt%d" % (cb % 2)] + extra)

    def prep_wo(k):
        st = wo_stage[k % 2]
        sk = "wost%d" % (k % 2)
        S.dma("sp", sk, lambda e, st=st, k=k: e.dma_start(out=st, in_=wo_d[k]), writes=[sk] + ([S.fence()] if k < 2 else []))
        S.dve(lambda e, st=st, k=k: e.tensor_tensor(out=wo_sb[:, k, :], in0=st, in1=gm_bc, op=ALU.mult),
              reads=[sk, "gm_bc"], writes=["wo%d" % k])

    load_tail(0)
    ctr = 0
    for cb in range(8):
        wt = wt_buf[cb % 2]
        wtk = "wt%d" % (cb % 2)
        if cb + 1 < 8:
            load_tail(cb + 1)
        prep_wo(cb)
        for tb in range(4):
            ts_ = slice(tb * 512, (tb + 1) * 512)
            par = ctr % 2
            ctr += 1
            pga, pgb, pba, pbb = par, 2 + par, 4 + par, 6 + par
            hreads = lambda k: [hT_key(k, 4 * tb + q) for q in range(4)]
            for k in range(8):
                S.pe(lambda e, k=k, ts_=ts_, pga=pga, wt=wt: e.matmul(bank(pga), lhsT=wt[:, k, :], rhs=hT[:, k, ts_],
                                                                     start=(k == 0), stop=(k == 7)),
                     reads=[wtk] + hreads(k), writes=[PK(pga)])
            for g in range(4):
                S.pe(lambda e, g=g, ts_=ts_, pba=pba, wt=wt: e.matmul(bank(pba), lhsT=wt[:, 24 + g, :], rhs=ypT[:, g, ts_],
                                                                     start=(g == 0), stop=(g == 3)),
                     reads=[wtk, "ypT"], writes=[PK(pba)])
            for k in range(8):
                S.pe(lambda e, k=k, ts_=ts_, pgb=pgb, wt=wt: e.matmul(bank(pgb), lhsT=wt[:, 8 + k, :], rhs=hT[:, k, ts_],
                                                                     start=(k == 0), stop=(k == 7)),
                     reads=[wtk] + hreads(k), writes=[PK(pgb)])
            for k in range(8):
                S.pe(lambda e, k=k, ts_=ts_, pbb=pbb, wt=wt: e.matmul(bank(pbb), lhsT=wt[:, 16 + k, :], rhs=ygT[:, k, ts_],
                                                                     start=(k == 0), stop=(k == 7)),
                     reads=[wtk] + [ygT_key(k, 4 * tb + q) for q in range(4)], writes=[PK(pbb)])
            sa = sgab[par * 2]
            sb_ = sgab[par * 2 + 1]
            sak = "sga%d" % par
            sbk = "sgb%d" % par
            S.act(lambda e, sa=sa, pga=pga: e.activation(out=sa, in_=bank(pga), func=AF.Sigmoid), reads=[PK(pga)], writes=[sak])
            S.act(lambda e, sb_=sb_, pgb=pgb: e.activation(out=sb_, in_=bank(pgb), func=AF.Sigmoid), reads=[PK(pgb)], writes=[sbk])
            S.dve(lambda e, sa=sa, pba=pba: e.tensor_tensor(out=sa, in0=sa, in1=bank(pba), op=ALU.mult),
                  reads=[sak, PK(pba)], writes=[sak])
            S.dve(lambda e, sb_=sb_, pbb=pbb: e.tensor_tensor(out=sb_, in0=sb_, in1=bank(pbb), op=ALU.mult),
                  reads=[sbk, PK(pbb)], writes=[sbk])
            S.dve(lambda e, sa=sa, sb_=sb_, cb=cb, ts_=ts_: e.tensor_tensor(out=mT[:, cb, ts_], in0=sa, in1=sb_, op=ALU.add),
                  reads=[sak, sbk], writes=[mT_key(cb, 4 * tb + q) for q in range(4)])

    if stop == 4:
        return finish_debug([(mT.rearrange("p k t -> p (k t)"), 16384)])
    w13_buf = [arena[:, (R_PAIR.start + i * 4 * KB) // 2:(R_PAIR.start + (i + 1) * 4 * KB) // 2].rearrange(
        "p (a k n) -> p a k n", a=2, n=128) for i in range(2)]

    def load_w13(c):
        extra = [S.fence()] if c < 2 else []
        S.dma("pool", "w13_%d" % (c % 2), lambda e, c=c: e.dma_start(out=w13_buf[c % 2], in_=w13_d[c]),
              writes=["w13_%d" % (c % 2)] + extra)

    if stop is None:
        load_w13(0)
        load_w13(1)
    def x1_key(i):
        return "X1_%d" % i

    h2T = mT

    def h2_key(k, i):
        return mT_key(k, i)

    junk2 = _rp(48, 2)
    a2 = {}

    def a2_square(i):
        n = nctr[0]
        nctr[0] += 1
        c = dict(n=n, i=i, ss=stat[:, 32 + (n % 4) * 2:33 + (n % 4) * 2], rs=stat[:, 33 + (n % 4) * 2:34 + (n % 4) * 2],
                 sk="nstat%d" % (n % 4), xn=xnb2[n % 3], xk="xn3_%d" % (n % 3))
        S.act(lambda e: e.activation(out=junk2, in_=x1[:, i, :], func=AF.Square, accum_out=c["ss"]),
              reads=[x1_key(i)], writes=["junk", c["sk"]])
        return c

    def a2_ts(c):
        S.dve(lambda e: e.tensor_scalar(out=c["rs"], in0=c["ss"], scalar1=1.0 / D, scalar2=EPS, op0=ALU.mult, op1=ALU.add),
              reads=[c["sk"]], writes=[c["sk"] + "r"])
        S.act(lambda e: e.activation(out=c["rs"], in_=c["rs"], func=AF.Sqrt), reads=[c["sk"] + "r"], writes=[c["sk"] + "r"])

    def a2_fin(c):
        S.dve(lambda e: e.reciprocal(out=c["rs"], in_=c["rs"]), reads=[c["sk"] + "r"], writes=[c["sk"] + "r"])
        S.dve(lambda e: e.tensor_scalar(out=c["xn"], in0=x1[:, c["i"], :], scalar1=c["rs"], scalar2=None, op0=ALU.mult),
              reads=[x1_key(c["i"]), c["sk"] + "r"], writes=[c["xk"]])

    xnb2 = [_rp(50, 2), _rp(60, 2), arena[:, (R_YP.start + 14 * KB) // 2:(R_YP.start + 16 * KB) // 2]]
    pend = None
    for i in range(NT):
        pa = (i % 2) * 2
        xr = xres[i % 2]
        xrk = "xres%d" % (i % 2)
        S.dma("sp", xrk, lambda e, xr=xr, i=i: e.dma_start(out=xr, in_=x_d[i * 128:(i + 1) * 128, :]),
              writes=[xrk])
        for hf in range(2):
            for k in range(8):
                S.pe(lambda e, k=k, hf=hf, i=i, pa=pa: e.matmul(bank(pa + hf), lhsT=mT[:, k, i * 128:(i + 1) * 128],
                                                                rhs=wo_sb[:, k, hf * 512:(hf + 1) * 512],
                                                                start=(k == 0), stop=(k == 7)),
                     reads=[mT_key(k, i), "wo%d" % k], writes=[PK(pa + hf)])
        S.dve(lambda e, i=i, xr=xr, pa=pa: e.tensor_tensor(out=x1[:, i, :], in0=ps[:, pa * 512:(pa + 2) * 512], in1=xr, op=ALU.add),
              reads=[PK(pa), PK(pa + 1), xrk],
              writes=[x1_key(i)] + [(hT_key(i, t) if i < 8 else ygT_key(i - 8, t)) for t in range(NT)])
        if stop != 5:
            a2[i] = a2_square(i)
            if i >= 1:
                a2_ts(a2[i - 1])
            if i >= 3:
                norm_tr((a2[i - 3]["n"], a2[i - 3]["xn"], a2[i - 3]["xk"]), A_F, B_F, h2T, h2_key, i - 3, 4)
            if i >= 1:
                a2_fin(a2[i - 1])
    if stop != 5:
        a2_ts(a2[NT - 1])
        norm_tr((a2[NT - 3]["n"], a2[NT - 3]["xn"], a2[NT - 3]["xk"]), A_F, B_F, h2T, h2_key, NT - 3, 4)
        a2_fin(a2[NT - 1])
        for i_ in (NT - 2, NT - 1):
            norm_tr((a2[i_]["n"], a2[i_]["xn"], a2[i_]["xk"]), A_F, B_F, h2T, h2_key, i_, 4)

    if stop == 5:
        return finish_debug([(x1.rearrange("p i f -> p (i f)"), 16384)])
    R_PAIR.reset()
    R_PAIR.carve(8 * KB)
    actT = [R_PAIR.carve(20 * KB).rearrange("p (c t) -> p c t", t=L) for _ in range(2)]
    R_YP.reset()
    w2_sb = R_YP.carve(10 * KB).rearrange("p (c n) -> p c n", n=D)
    sa_sb = [R_YP.carve(2 * KB, F32) for _ in range(2)]
    w2_stage = [qdec, mask]

    S.dma("sp", "c6", lambda e: e.dma_start(out=nf_bc, in_=rows_d[0:1, RO_NF:RO_NF + D].to_broadcast([128, D])),
          writes=["nf_bc", S.fence()] + hcT_keys)
    fctr = 0
    for gi, (c0, ng) in enumerate(FF_GROUPS):
        at = actT[gi % 2]
        atk = "actT%d" % (gi % 2)
        for cc in range(ng):
            c = c0 + cc
            wb = w13_buf[c % 2]
            wbk = "w13_%d" % (c % 2)
            if 2 <= c + 1 < NFF:
                load_w13(c + 1)
            st = w2_stage[c % 2]
            stk = "w2st%d" % (c % 2)
            S.dma("sp", stk, lambda e, st=st, c=c: e.dma_start(out=st, in_=w2_d[c]), writes=[stk, "qdec" if c % 2 == 0 else "mask"])
            S.dve(lambda e, st=st, cc=cc: e.tensor_tensor(out=w2_sb[:, cc, :], in0=st, in1=gf_bc, op=ALU.mult),
                  reads=[stk, "gf_bc"], writes=["w2sb%d" % cc])
            for tb in range(4):
                ts_ = slice(tb * 512, (tb + 1) * 512)
                par = fctr % 2
                fctr += 1
                pa_, pb_ = par, 2 + par
                for a in range(2):
                    pbk = pa_ if a == 0 else pb_
                    for k in range(8):
                        S.pe(lambda e, a=a, k=k, ts_=ts_, pbk=pbk, wb=wb: e.matmul(bank(pbk), lhsT=wb[:, a, k, :], rhs=h2T[:, k, ts_],
                                                                                  start=(k == 0), stop=(k == 7)),
                             reads=[wbk] + [h2_key(k, 4 * tb + q) for q in range(4)], writes=[PK(pbk)])
                sa = sa_sb[par]
                sak = "ffsa%d" % par
                S.act(lambda e, sa=sa, pa_=pa_: e.activation(out=sa, in_=bank(pa_), func=AF.Silu), reads=[PK(pa_)], writes=[sak])
                S.dve(lambda e, sa=sa, pb_=pb_, cc=cc, ts_=ts_, at=at: e.tensor_tensor(out=at[:, cc, ts_], in0=sa, in1=bank(pb_), op=ALU.mult),
                      reads=[sak, PK(pb_)], writes=[atk + "_%d_%d" % (cc, tb)])
        last = (gi == len(FF_GROUPS) - 1)
        for i in range(NT):
            pa = 4 + (i % 2) * 2
            for hf in range(2):
                for cc in range(ng):
                    S.pe(lambda e, cc=cc, hf=hf, i=i, pa=pa, at=at: e.matmul(bank(pa + hf), lhsT=at[:, cc, i * 128:(i + 1) * 128],
                                                                             rhs=w2_sb[:, cc, hf * 512:(hf + 1) * 512],
                                                                             start=(cc == 0), stop=(cc == ng - 1)),
                         reads=[atk + "_%d_%d" % (cc, i // 4), "w2sb%d" % cc], writes=[PK(pa + hf)])
            S.dve(lambda e, i=i, pa=pa: e.tensor_tensor(out=x1[:, i, :], in0=ps[:, pa * 512:(pa + 2) * 512], in1=x1[:, i, :], op=ALU.add),
                  reads=[PK(pa), PK(pa + 1), x1_key(i)], writes=[x1_key(i)])
            if last:
                def fin_a(i):
                    ss = stat[:, 32 + (i % 4) * 2:33 + (i % 4) * 2]
                    S.act(lambda e: e.activation(out=junk2, in_=x1[:, i, :], func=AF.Square, accum_out=ss),
                          reads=[x1_key(i)], writes=["junk2", "fstat%d" % (i % 4)])

                def fin_b(i):
                    ss = stat[:, 32 + (i % 4) * 2:33 + (i % 4) * 2]
                    rs = stat[:, 33 + (i % 4) * 2:34 + (i % 4) * 2]
                    sk = "fstat%d" % (i % 4)
                    S.dve(lambda e: e.tensor_scalar(out=rs, in0=ss, scalar1=1.0 / D, scalar2=EPS, op0=ALU.mult, op1=ALU.add),
                          reads=[sk], writes=[sk + "r"])
                    S.act(lambda e: e.activation(out=rs, in_=rs, func=AF.Sqrt), reads=[sk + "r"], writes=[sk + "r"])

                def fin_c(i):
                    rs = stat[:, 33 + (i % 4) * 2:34 + (i % 4) * 2]
                    sk = "fstat%d" % (i % 4)
                    S.dve(lambda e: e.reciprocal(out=rs, in_=rs), reads=[sk + "r"], writes=[sk + "r"])
                    S.dve(lambda e: e.scalar_tensor_tensor(out=x1[:, i, :], in0=x1[:, i, :], scalar=rs, in1=nf_bc,
                                                           op0=ALU.mult, op1=ALU.mult),
                          reads=[x1_key(i), sk + "r", "nf_bc"], writes=[x1_key(i)])
                    S.dma("sp", "out", lambda e: e.dma_start(out=out_d[i * 128:(i + 1) * 128, :], in_=x1[:, i, :]),
                          reads=[x1_key(i)])

                fin_a(i)
                if i >= 1:
                    fin_b(i - 1)
                if i >= 2:
                    fin_c(i - 2)
                if i == NT - 1:
                    fin_b(i)
                    fin_c(i - 1)
                    fin_c(i)

    S.emit(nc, final_dma_keys=["out"])
    return nc


_Q_OFF, _K_OFF, _V_OFF, _G_OFF, _GA_OFF, _GB_OFF = 512, 1024, 1536, 2560, 3584, 4608


def _kchunk(w):
    kk = w.shape[0] // 128
    return np.ascontiguousarray(w.reshape(kk, 128, w.shape[1]).transpose(1, 0, 2))


def _prep(x, c, ctx, c_ctx, w_ada, b_ada, norm_mix, norm_ffn, w_in, w_pool, pool_scale,
          ret_decay_f, ret_decay_b, ret_gn_w, w_pa, w_rb, w_o, w_ff1, w_ff3, w_ff2, norm_final):
    f32 = np.float32
    x = np.asarray(x, f32)
    B = x.shape[0]
    pblk, plan = _get_pool()

    w_in0 = np.asarray(w_in, f32)[0]
    wada = _kchunk(np.asarray(w_ada, f32)[0])
    wu = _kchunk(w_in0[:, 0:512])
    wpairs = []
    for p in range(4):
        cols = np.concatenate([
            w_in0[:, _Q_OFF + p * 128:_Q_OFF + (p + 1) * 128],
            w_in0[:, _K_OFF + p * 128:_K_OFF + (p + 1) * 128],
            w_in0[:, _V_OFF + p * 256:_V_OFF + (p + 1) * 256],
            w_in0[:, _G_OFF + p * 256:_G_OFF + (p + 1) * 256]], axis=1)
        wpairs.append(_kchunk(cols))
    wpairs = np.stack(wpairs, 0)
    w_rb0 = np.asarray(w_rb, f32)[0]
    w_pa0 = np.asarray(w_pa, f32)[0]
    wtail = []
    for cb in range(8):
        cs = slice(cb * 128, (cb + 1) * 128)
        ga = _kchunk(w_in0[:, _GA_OFF:_GA_OFF + D][:, cs])
        gb = _kchunk(w_in0[:, _GB_OFF:_GB_OFF + D][:, cs])
        rb = _kchunk(w_rb0[:, cs])
        pa = _kchunk(w_pa0[:, cs])
        wtail.append(np.concatenate([ga, gb, rb, pa], axis=1))
    wtail = np.ascontiguousarray(np.stack(wtail, 0))
    wpool = np.ascontiguousarray(np.asarray(w_pool, f32)[0].transpose(1, 0, 2))
    wo = np.ascontiguousarray(np.asarray(w_o, f32)[0].reshape(8, 128, D))
    w1 = np.asarray(w_ff1, f32)[0]
    w3 = np.asarray(w_ff3, f32)[0]
    w13 = np.stack([np.stack([_kchunk(w1[:, cc * 128:(cc + 1) * 128]), _kchunk(w3[:, cc * 128:(cc + 1) * 128])], axis=1)
                    for cc in range(NFF)], 0)
    w13 = np.ascontiguousarray(w13)
    w2 = np.ascontiguousarray(np.asarray(w_ff2, f32)[0].reshape(NFF, 128, D))
    consts = _host_consts()
    ident = np.eye(128, dtype=np.float32).astype(ml_dtypes.bfloat16)

    def pp(v, k):
        return np.asarray(v, f32).reshape(k, 128).T

    b_ada0 = np.asarray(b_ada, f32)[0]
    rows = np.zeros((1, NROW), f32)
    rows[0, RO_BGM:RO_BGM + D] = b_ada0[2 * D:3 * D]
    rows[0, RO_BGF:RO_BGF + D] = b_ada0[5 * D:6 * D]
    rows[0, RO_NF:RO_NF + D] = np.asarray(norm_final, f32)
    rows[0, RO_DEC:RO_DEC + 8] = np.asarray(ret_decay_f, f32)[0]
    rows[0, RO_DEC + 8:RO_DEC + 16] = np.asarray(ret_decay_b, f32)[0]

    in_maps = []
    for b in range(B):
        vecs = np.zeros((128, NVEC), f32)
        vecs[:, VO_C:VO_C + 16:2] = pp(np.asarray(c, f32)[b], 8)
        vecs[:, VO_C + 1:VO_C + 16:2] = pp(np.asarray(c_ctx, f32), 8)
        vecs[:, VO_NMIX:VO_NMIX + 8] = pp(np.asarray(norm_mix, f32)[0], 8)
        vecs[:, VO_NFFN:VO_NFFN + 8] = pp(np.asarray(norm_ffn, f32)[0], 8)
        vecs[:, VO_PSC:VO_PSC + 4] = pp(np.asarray(pool_scale, f32)[0], 4)
        vecs[:, VO_GNW:VO_GNW + 8] = pp(np.asarray(ret_gn_w, f32)[0], 8)
        vecs[:, VO_BADA:VO_BADA + 48] = pp(b_ada0, 48)
        in_maps.append({
            "x": np.ascontiguousarray(x[b]), "ctx": np.ascontiguousarray(np.asarray(ctx, f32)[b]),
            "vecs": vecs, "rows": rows, "consts": consts, "ident": ident,
            "wada": wada, "wu": wu, "wpairs": wpairs, "wtail": wtail, "wpool": wpool, "wo": wo,
            "w13": w13, "w2": w2, "pblk": pblk,
        })
    return in_maps


def kernel(**inputs):
    in_maps = _prep(**inputs)
    pblk, plan = _get_pool()
    nc = build_program(plan, pblk.shape[1])
    B = len(in_maps)
    res = run_bass_kernel_spmd(nc, in_maps, core_ids=list(range(B)))
    out = np.stack([np.asarray(res.results[b]["out"], np.float32) for b in range(B)], 0)
    return out
```

```python
import contextlib
import os
import types
import numpy as np
import ml_dtypes
import concourse.bass as bass
import concourse.mybir as mybir
from concourse.bass_utils import run_bass_kernel_spmd

F32 = mybir.dt.float32
BF16 = mybir.dt.bfloat16
AF = mybir.ActivationFunctionType
ALU = mybir.AluOpType
AX = mybir.AxisListType

D = 1024
L = 2048
NT = 16
LC = 256
C = 128
GRID_W = 64
H = 8
DK = 64
DV = 128
DFF = 2816
NFF = 22
EPS = 1e-6
K_SCALE = DK ** -0.5
POOL_WINDOWS = (2, 4, 8, 16)
FF_GROUPS = ((0, 5), (5, 5), (10, 5), (15, 5), (20, 2))


class Op:
    __slots__ = ("eng", "fn", "reads", "writes", "dma_key", "waits", "signal", "cnt", "n_dma")

    def __init__(self, eng, fn, reads, writes, dma_key, n_dma):
        self.eng = eng
        self.fn = fn
        self.reads = reads
        self.writes = writes
        self.dma_key = dma_key
        self.n_dma = n_dma
        self.waits = []
        self.signal = False
        self.cnt = None


class Sched:
    ENGS = ("pe", "act", "dve", "pool", "sp")

    def __init__(self):
        self.ops = []
        self.last_writer = {}
        self.readers = {}
        self.dma_cum = {}
        self.bank_rd = {}

    def _dep(self, op, d, kind):
        if d is op:
            return
        if d.dma_key is not None:
            op.waits.append(("dma:" + d.dma_key, self.dma_cum[d.dma_key]))
            return
        if d.eng == op.eng and op.dma_key is None:
            if d.eng == "pe" or kind != "RAW":
                return
        d.signal = True
        op.waits.append(("eng:" + d.eng, d))

    @staticmethod
    def _snapshot(fn):
        if fn.__closure__ is None:
            return fn
        cells = []
        for c in fn.__closure__:
            try:
                cells.append(types.CellType(c.cell_contents))
            except ValueError:
                cells.append(c)
        return types.FunctionType(fn.__code__, fn.__globals__, fn.__name__, fn.__defaults__, tuple(cells))

    def fence(self):
        self._nfence = getattr(self, "_nfence", 0) + 1
        key = "__fence%d" % self._nfence
        last = {}
        for o in self.ops:
            last[o.dma_key if o.dma_key is not None else "eng:" + o.eng] = o
        self.readers[key] = list(last.values())
        return key

    @staticmethod
    def _expand(keys):
        out = []
        for k in keys:
            if isinstance(k, str) and len(k) == 3 and k.startswith("ps"):
                out.extend((k + "a", k + "b"))
            else:
                out.append(k)
        return tuple(out)

    def op(self, eng, fn, reads=(), writes=(), dma_key=None, n_dma=1):
        fn = self._snapshot(fn)
        o = Op(eng, fn, self._expand(reads), self._expand(writes), dma_key, n_dma)
        for k in o.reads:
            w = self.last_writer.get(k)
            if w is not None:
                self._dep(o, w, "RAW")
            if isinstance(k, str) and k.startswith("ps") and eng in ("act", "dve"):
                bk = k[:3]
                lr = self.bank_rd.setdefault(bk, {})
                for e2, r in lr.items():
                    if e2 != eng:
                        self._dep(o, r, "XRD")
                lr[eng] = o
        for k in o.writes:
            w = self.last_writer.get(k)
            if w is not None:
                self._dep(o, w, "WAW")
            for r in self.readers.get(k, ()):
                self._dep(o, r, "WAR")
        for k in o.reads:
            self.readers.setdefault(k, []).append(o)
        for k in o.writes:
            self.last_writer[k] = o
            self.readers[k] = []
        if dma_key is not None:
            self.dma_cum[dma_key] = self.dma_cum.get(dma_key, 0) + 16 * n_dma
            o.cnt = self.dma_cum[dma_key]
        self.ops.append(o)
        return o

    def pe(self, fn, reads=(), writes=()):
        return self.op("pe", fn, reads, writes)

    def act(self, fn, reads=(), writes=()):
        return self.op("act", fn, reads, writes)

    def dve(self, fn, reads=(), writes=()):
        return self.op("dve", fn, reads, writes)

    def pool(self, fn, reads=(), writes=()):
        return self.op("pool", fn, reads, writes)

    def pool_or(self, tag, alt, fn, reads=(), writes=()):
        on = os.environ.get("KPOOL", "").split(",")
        return self.op("pool" if tag in on else alt, fn, reads, writes)

    def dma(self, eng, key, fn, reads=(), writes=()):
        return self.op(eng, fn, reads, writes, dma_key=key)

    def emit(self, nc, final_dma_keys=()):
        cnt = {e: 0 for e in self.ENGS}
        for o in self.ops:
            if o.dma_key is None and o.signal:
                cnt[o.eng] += 1
                o.cnt = cnt[o.eng]
        semnames = set()
        for o in self.ops:
            for (s, v) in o.waits:
                semnames.add(s)
            if o.dma_key is not None:
                semnames.add("dma:" + o.dma_key)
            elif o.signal:
                semnames.add("eng:" + o.eng)
        with contextlib.ExitStack() as es:
            sems = {}
            for s in sorted(semnames):
                sems[s] = es.enter_context(nc.semaphore(s.replace(":", "_")))
            block = es.enter_context(nc.Block())
            streams = {e: [o for o in self.ops if o.eng == e] for e in self.ENGS}

            def run(engname, e):
                waited = {}
                for o in streams[engname]:
                    need = {}
                    for (s, v) in o.waits:
                        val = v.cnt if isinstance(v, Op) else v
                        if val > need.get(s, 0):
                            need[s] = val
                    for s, val in need.items():
                        if waited.get(s, 0) >= val:
                            continue
                        e.wait_ge(sems[s], val)
                        waited[s] = val
                    ins = o.fn(e)
                    if o.dma_key is not None:
                        ins.then_inc(sems["dma:" + o.dma_key], 16 * o.n_dma)
                    elif o.signal:
                        ins.then_inc(sems["eng:" + o.eng], 1)
                if engname == "sp":
                    for k in final_dma_keys:
                        e.wait_ge(sems["dma:" + k], self.dma_cum[k])

            @block.tensor
            def _(e):
                run("pe", e)

            @block.scalar
            def _(e):
                run("act", e)

            @block.vector
            def _(e):
                run("dve", e)

            @block.gpsimd
            def _(e):
                run("pool", e)

            @block.sync
            def _(e):
                run("sp", e)


def _box_matrix(n, w):
    pos = np.arange(n)
    lo = np.clip(pos - w // 2, 0, n)
    hi = np.clip(pos + (w - w // 2), 0, n)
    a = np.zeros((n, n), np.float64)
    for t in range(n):
        a[t, lo[t]:hi[t]] = 1.0 / (hi[t] - lo[t])
    return a


def _pool_blocks():
    rows = L // GRID_W
    blocks = []
    seen = {}
    plan = []
    for g, w in enumerate(POOL_WINDOWS):
        a = np.kron(_box_matrix(rows, w), _box_matrix(GRID_W, w)) - np.eye(L)
        at = a.T
        pg = []
        for j in range(4):
            lst = []
            for t in range(NT):
                blk = at[t * 128:(t + 1) * 128, j * 512:(j + 1) * 512]
                if np.any(blk != 0.0):
                    b32 = np.ascontiguousarray(blk.astype(np.float32))
                    key = (g, b32.tobytes())
                    if key not in seen:
                        seen[key] = len(blocks)
                        blocks.append(b32)
                    lst.append((t, seen[key]))
            pg.append(lst)
        plan.append(pg)
    arr = np.stack(blocks, axis=1)
    return np.ascontiguousarray(arr).astype(ml_dtypes.bfloat16), plan


def _rope_tables():
    t = np.arange(L)
    row = (t // GRID_W).astype(np.float32)
    col = (t % GRID_W).astype(np.float32)
    n_freq = DK // 4
    inv_freq = (10000.0 ** (-np.arange(n_freq, dtype=np.float32) / n_freq)).astype(np.float32)
    ang = np.concatenate([row[:, None] * inv_freq, col[:, None] * inv_freq], axis=-1).astype(np.float32)
    cos = np.cos(ang).astype(np.float32).reshape(NT, 128, 32).transpose(1, 0, 2)
    sin = np.sin(ang).astype(np.float32).reshape(NT, 128, 32).transpose(1, 0, 2)
    return np.ascontiguousarray(cos), np.ascontiguousarray(sin)


_POOL_CACHE = None


def _get_pool():
    global _POOL_CACHE
    if _POOL_CACHE is None:
        _POOL_CACHE = _pool_blocks()
    return _POOL_CACHE


CO_COS = 0
CO_SIN = CO_COS + NT * 32
CO_DPOS = CO_SIN + NT * 32
CO_DNEG = CO_DPOS + 128
CO_POSQ = CO_DNEG + 128
CO_SM = CO_POSQ + 128
NCONST = CO_SM + 8

VO_C = 0
VO_NMIX = 16
VO_NFFN = 24
VO_PSC = 32
VO_GNW = 36
VO_BADA = 44
NVEC = VO_BADA + 48

RO_BGM = 0
RO_BGF = 1024
RO_NF = 2048
RO_DEC = 3072
NROW = RO_DEC + 16


def _host_consts():
    cos, sin = _rope_tables()
    c = np.zeros((128, NCONST), np.float32)
    c[:, CO_COS:CO_COS + NT * 32] = cos.reshape(128, -1)
    c[:, CO_SIN:CO_SIN + NT * 32] = sin.reshape(128, -1)
    i = np.arange(128, dtype=np.float32)
    dmat = i[None, :] - i[:, None]
    c[:, CO_DPOS:CO_DPOS + 128] = np.maximum(dmat, 0)
    c[:, CO_DNEG:CO_DNEG + 128] = np.maximum(-dmat, 0)
    c[0:64, CO_POSQ:CO_POSQ + 128] = (i + 1.0)[None, :]
    c[64:128, CO_POSQ:CO_POSQ + 128] = (C - i)[None, :]
    c[:, CO_SM + 0] = C - 1.0 - i
    c[:, CO_SM + 1] = i
    c[:, CO_SM + 2] = LC - 1.0 - i
    c[:, CO_SM + 3] = LC - 1.0 - (i + 128)
    c[:, CO_SM + 4] = i
    c[:, CO_SM + 5] = i + 128
    return c


def build_program(pool_plan, n_pool_blk, stop=None):
    nc = bass.Bass("TRN2", target_bir_lowering=False)
    DBGN = 16384

    def din(name, shape, dt=F32):
        return nc.dram_tensor(name, list(shape), dt, kind="ExternalInput").ap()

    x_d = din("x", [L, D])
    ctx_d = din("ctx", [LC, D])
    vecs_d = din("vecs", [128, NVEC])
    rows_d = din("rows", [1, NROW])
    consts_d = din("consts", [128, NCONST])
    ident_d = din("ident", [128, 128], BF16)
    wada_d = din("wada", [128, 8, 6 * D])
    wu_d = din("wu", [128, 8, 512])
    wpairs_d = din("wpairs", [4, 128, 8, 768])
    wtail_d = din("wtail", [8, 128, 28, 128])
    wpool_d = din("wpool", [128, 4, 128])
    wo_d = din("wo", [8, 128, D])
    w13_d = din("w13", [NFF, 128, 2, 8, 128])
    w2_d = din("w2", [NFF, 128, D])
    pblk_d = din("pblk", [128, n_pool_blk, 512], BF16)
    out_d = nc.dram_tensor("out", [L, D], F32, kind="ExternalOutput").ap()
    dbg_d = nc.dram_tensor("dbg", [128, DBGN], F32, kind="ExternalOutput").ap() if stop is not None else None

    S = Sched()
    SB_BYTES = 207 * 1024
    arena = nc.alloc_sbuf_tensor("arena", [128, SB_BYTES // 2], BF16).ap()
    ps = nc.alloc_psum_tensor("ps", [128, 4096], F32).ap()

    class Region:
        def __init__(self, start, size):
            self.start = start
            self.size = size
            self.off = 0

        def carve(self, nbytes, dt=BF16):
            want = nbytes
            nbytes = (nbytes + 31) // 32 * 32
            assert self.off + nbytes <= self.size, (self.off, nbytes, self.size)
            a = (self.start + self.off) // 2
            self.off += nbytes
            v = arena[:, a:a + want // 2]
            return v if dt == BF16 else v.bitcast(dt)

        def reset(self):
            self.off = 0

    KB = 1024
    R_PERS = Region(0, 32 * KB)
    R_X = Region(32 * KB, 64 * KB)
    R_M = Region(96 * KB, 32 * KB)
    R_PAIR = Region(128 * KB, 63 * KB)
    R_YP = Region(191 * KB, 16 * KB)
    assert 207 * KB <= SB_BYTES

    def bank(b, dt=F32):
        v = ps[:, b * 512:(b + 1) * 512]
        return v if dt == F32 else v.bitcast(dt)

    def PK(b):
        return "ps%d" % b

    def finish_debug(items):
        off = 0
        fk = S.fence()
        for ap, n, in items:
            for c0 in range(0, n, 1024):
                c1 = min(n, c0 + 1024)
                S.dma("pool", "dbg", lambda e, ap=ap, off=off, c0=c0, c1=c1: e.dma_start(
                    out=dbg_d[:, off + c0:off + c1], in_=ap[:, c0:c1]), writes=[fk])
            off += n
        S.dma("sp", "out", lambda e: e.dma_start(out=out_d[0:128, :], in_=gm_bc), writes=[fk])
        S.emit(nc, final_dma_keys=["out", "dbg"])
        return nc

    vecs = R_PERS.carve(NVEC * 4, F32)
    consts = R_PERS.carve(NCONST * 4, F32)
    ident = R_PERS.carve(256)
    decbc = R_PERS.carve(64, F32)
    lgbc = R_PERS.carve(64, F32)
    lgsel = R_PERS.carve(32, F32)
    cdec = R_PERS.carve(32, F32)
    kdec = R_PERS.carve(64, F32)
    ctxw = R_PERS.carve(128, F32)
    modT = R_PERS.carve(48 * 2 * 4, F32)
    modAB = R_PERS.carve(6 * 8 * 4, F32)
    scT = R_PERS.carve(8 * 2 * 2)
    scbc = R_PERS.carve(8 * 128 * 2)
    qdec = R_PERS.carve(8 * 128 * 4, F32)
    mask = R_PERS.carve(8 * 128 * 4, F32)
    gm_bc = R_PERS.carve(D * 4, F32)
    gf_bc = R_PERS.carve(D * 4, F32)
    hcT = R_PERS.carve(8 * LC * 2)
    nf_bc = hcT.bitcast(F32)
    wpool = R_PERS.carve(4 * 128 * 2)
    stat = R_PERS.carve(64 * 4, F32)
    epsb = R_PERS.carve(32, F32)
    gnw128 = R_PERS.carve(32, F32)
    sfp = R_PERS.carve(384 * 4, F32)

    modT3 = modT.rearrange("p (j t) -> p j t", t=2)
    modAB3 = modAB.rearrange("p (a k) -> p a k", k=8)
    scT3 = scT.rearrange("p (k t) -> p k t", t=2)
    scbc3 = scbc.rearrange("p (k m) -> p k m", m=128)
    qdec3 = qdec.rearrange("p (h t) -> p h t", t=128)
    mask3 = mask.rearrange("p (h t) -> p h t", t=128)
    hcT3 = hcT.rearrange("p (k t) -> p k t", t=LC)
    wpool3 = wpool.rearrange("p (g n) -> p g n", n=128)
    kdec3 = kdec.rearrange("p (d h) -> p d h", h=8)
    ctxw4 = ctxw.rearrange("p (t d h) -> p t d h", d=2, h=8)
    cos3 = consts[:, CO_COS:CO_COS + NT * 32].rearrange("p (i f) -> p i f", f=32)
    sin3 = consts[:, CO_SIN:CO_SIN + NT * 32].rearrange("p (i f) -> p i f", f=32)
    dpos = consts[:, CO_DPOS:CO_DPOS + 128]
    dneg = consts[:, CO_DNEG:CO_DNEG + 128]
    posq = consts[:, CO_POSQ:CO_POSQ + 128]

    def csm(i):
        return consts[:, CO_SM + i:CO_SM + i + 1]

    A_M, B_M, A_C, B_C, A_F, B_F = range(6)

    hT = R_X.carve(32 * KB).rearrange("p (k t) -> p k t", t=L)
    ygT = R_X.carve(32 * KB).rearrange("p (k t) -> p k t", t=L)
    R_X.reset()
    x1 = R_X.carve(64 * KB, F32).rearrange("p (i f) -> p i f", f=D)
    mT = R_M.carve(32 * KB).rearrange("p (k t) -> p k t", t=L)
    R_M.reset()
    _ux = R_X.start + 32 * KB
    u_tok = arena[:, _ux // 2:(_ux + 16 * KB) // 2].rearrange("p (i c) -> p i c", c=512)
    dT_sb = [arena[:, (_ux + (16 + i) * KB) // 2:(_ux + (17 + i) * KB) // 2] for i in range(2)]
    R_M.reset()
    wpair_buf = [R_M.carve(12 * KB).rearrange("p (k n) -> p k n", n=768) for _ in range(2)]
    wada_buf = [arena[:, (R_M.start + i * 8 * KB) // 2:(R_M.start + (i + 1) * 8 * KB) // 2].rearrange("p (k n) -> p k n", n=512)
                for i in range(4)]
    wada_buf += [arena[:, (R_YP.start + i * 8 * KB) // 2:(R_YP.start + (i + 1) * 8 * KB) // 2].rearrange("p (k n) -> p k n", n=512)
                 for i in range(2)]
    ypT = R_YP.carve(16 * KB).rearrange("p (g t) -> p g t", t=L)

    S.dma("sp", "c0", lambda e: e.dma_start(out=vecs, in_=vecs_d), writes=["vecs"])
    S.dma("sp", "c1", lambda e: e.dma_start(out=consts, in_=consts_d), writes=["consts"])
    S.dma("sp", "c2", lambda e: e.dma_start(out=ident, in_=ident_d), writes=["ident"])
    S.dma("sp", "c3", lambda e: e.dma_start(out=decbc, in_=rows_d[0:1, RO_DEC:RO_DEC + 16].to_broadcast([128, 16])),
          writes=["decbc"])
    S.dma("sp", "c4", lambda e: e.dma_start(out=gm_bc, in_=rows_d[0:1, RO_BGM:RO_BGM + D].to_broadcast([128, D])),
          writes=["gm_bc"])
    S.dma("sp", "c5", lambda e: e.dma_start(out=gf_bc, in_=rows_d[0:1, RO_BGF:RO_BGF + D].to_broadcast([128, D])),
          writes=["gf_bc"])
    S.dma("pool", "wpool", lambda e: e.dma_start(out=wpool3, in_=wpool_d), writes=["wpool"])

    S.dve(lambda e: e.memset(epsb, float(DV * DV) * EPS), writes=["epsb"])
    S.dve(lambda e: e.tensor_scalar(out=gnw128, in0=vecs[:, VO_GNW:VO_GNW + 8], scalar1=float(DV), scalar2=None, op0=ALU.mult),
          reads=["vecs"], writes=["gnw128"])
    cv3 = vecs[:, VO_C:VO_C + 16].rearrange("p (k t) -> p k t", t=2)
    S.act(lambda e: e.activation(out=scT3, in_=cv3, func=AF.Silu), reads=["vecs"], writes=["scT"])
    S.act(lambda e: e.activation(out=scbc3, in_=cv3[:, :, 0:1].to_broadcast([128, 8, 128]), func=AF.Silu),
          reads=["vecs"], writes=["scbc"])

    def wada_group(gi):
        buf = wada_buf[gi % 6]
        bk = "wada%d" % (gi % 6)
        S.dma("pool", bk, lambda e, buf=buf, gi=gi: e.dma_start(out=buf, in_=wada_d[:, :, gi * 512:(gi + 1) * 512]),
              writes=[bk])
        return buf, bk

    def wada_compute(gi, buf, bk):
        if gi in (4, 5, 10, 11):
            dst = gm_bc if gi in (4, 5) else gf_bc
            dk = "gm_bc" if gi in (4, 5) else "gf_bc"
            half = gi % 2 if gi in (4, 5) else (gi - 10)
            pb = 5 + (gi % 2)
            for k in range(8):
                S.pe(lambda e, k=k, buf=buf, pb=pb: e.matmul(bank(pb), lhsT=scbc3[:, k, :], rhs=buf[:, k, :],
                                                              start=(k == 0), stop=(k == 7)),
                     reads=[bk, "scbc"], writes=[PK(pb)])
            S.dve(lambda e, dst=dst, half=half, pb=pb: e.tensor_tensor(
                out=dst[:, half * 512:(half + 1) * 512], in0=bank(pb), in1=dst[:, half * 512:(half + 1) * 512], op=ALU.add),
                reads=[PK(pb), dk], writes=[dk])
        else:
            for jj in range(4):
                j = gi * 4 + jj
                for k in range(8):
                    S.pe(lambda e, k=k, jj=jj, j=j, buf=buf: e.matmul(
                        bank(7)[:, 2 * j:2 * j + 2], lhsT=buf[:, k, jj * 128:(jj + 1) * 128], rhs=scT3[:, k, :],
                        start=(k == 0), stop=(k == 7)),
                        reads=[bk, "scT"], writes=[PK(7)])

    def modT_evac(j0, j1):
        S.dve(lambda e, j0=j0, j1=j1: e.tensor_tensor(
            out=modT3[:, j0:j1, :], in0=bank(7)[:, 2 * j0:2 * j1].rearrange("p (j t) -> p j t", t=2),
            in1=vecs[:, VO_BADA + j0:VO_BADA + j1].unsqueeze(2).to_broadcast([128, j1 - j0, 2]), op=ALU.add),
            reads=[PK(7), "vecs"], writes=["modT"])

    def mod_ab(ai, bi, nvo, sh_blk, sc_blk, col):
        S.dve(lambda e: e.scalar_tensor_tensor(out=modAB3[:, ai, :], in0=modT3[:, sc_blk:sc_blk + 8, col], scalar=1.0,
                                               in1=vecs[:, nvo:nvo + 8], op0=ALU.add, op1=ALU.mult),
              reads=["modT", "vecs"], writes=["modAB%d" % ai])
        S.dve(lambda e: e.tensor_copy(out=modAB3[:, bi, :], in_=modT3[:, sh_blk:sh_blk + 8, col]),
              reads=["modT"], writes=["modAB%d" % bi])

    wg = [wada_group(gi) for gi in range(4)]

    def setup_mod_mix():
        for gi in range(4):
            wada_compute(gi, *wg[gi])
        modT_evac(0, 16)
        mod_ab(A_M, B_M, VO_NMIX, 0, 8, 0)
        mod_ab(A_C, B_C, VO_NMIX, 0, 8, 1)

    if stop == 0:
        setup_mod_mix()

    ee = stat[:, 0:16]
    tt = stat[:, 16:32]
    S.act(lambda e: e.activation(out=ee, in_=decbc, func=AF.Exp, scale=-1.0), reads=["decbc"], writes=["ee"])
    S.dve(lambda e: e.tensor_scalar(out=tt, in0=ee, scalar1=-1.0 / 7, scalar2=1.0 / 6, op0=ALU.mult, op1=ALU.add),
          reads=["ee"], writes=["tt"])
    for cf in (1.0 / 5, 1.0 / 4, 1.0 / 3, 1.0 / 2, 1.0):
        S.dve(lambda e: e.tensor_tensor(out=tt, in0=tt, in1=ee, op=ALU.mult), reads=["tt", "ee"], writes=["tt"])
        S.dve(lambda e, cf=cf: e.tensor_scalar(out=tt, in0=tt, scalar1=-1.0, scalar2=cf, op0=ALU.mult, op1=ALU.add),
              reads=["tt"], writes=["tt"])
    S.dve(lambda e: e.scalar_tensor_tensor(out=lgbc, in0=tt, scalar=-1.0, in1=ee, op0=ALU.mult, op1=ALU.mult),
          reads=["tt", "ee"], writes=["lgbc"])
    S.dve(lambda e: e.tensor_copy(out=lgsel[0:64, :], in_=lgbc[0:64, 0:8]), reads=["lgbc"], writes=["lgsel"])
    S.dve(lambda e: e.tensor_copy(out=lgsel[64:128, :], in_=lgbc[64:128, 8:16]), reads=["lgbc"], writes=["lgsel"])
    S.act(lambda e: e.activation(out=cdec, in_=lgsel, func=AF.Exp, scale=float(C)), reads=["lgsel"], writes=["cdec"])
    for d in range(2):
        S.act(lambda e, d=d: e.activation(out=kdec3[:, d, :], in_=lgbc[:, d * 8:(d + 1) * 8], func=AF.Exp, scale=csm(d)),
              reads=["lgbc", "consts"], writes=["kdec"])
        for t in range(2):
            S.act(lambda e, d=d, t=t: e.activation(out=ctxw4[:, t, d, :], in_=lgbc[:, d * 8:(d + 1) * 8], func=AF.Exp,
                                                   scale=csm(2 + 2 * d + t)),
                  reads=["lgbc", "consts"], writes=["ctxw"])
    S.dve(lambda e: e.tensor_scalar(out=kdec, in0=kdec, scalar1=K_SCALE, scalar2=None, op0=ALU.mult),
          reads=["kdec"], writes=["kdec"])
    S.dve(lambda e: e.tensor_scalar(out=ctxw, in0=ctxw, scalar1=K_SCALE, scalar2=None, op0=ALU.mult),
          reads=["ctxw"], writes=["ctxw"])
    for h in range(H):
        S.act(lambda e, h=h: e.activation(out=qdec3[:, h, :], in_=posq, func=AF.Exp, scale=lgsel[:, h:h + 1]),
              reads=["lgsel", "consts"], writes=["qdec"])
        S.dve(lambda e, h=h: e.tensor_scalar(out=mask3[:, h, :], in0=dpos, scalar1=lgbc[:, h:h + 1], scalar2=None,
                                             op0=ALU.mult), reads=["lgbc", "consts"], writes=["mask"])
        S.dve(lambda e, h=h: e.scalar_tensor_tensor(out=mask3[:, h, :], in0=dneg, scalar=lgbc[:, 8 + h:9 + h],
                                                    in1=mask3[:, h, :], op0=ALU.mult, op1=ALU.add),
              reads=["lgbc", "consts", "mask"], writes=["mask"])
    S.act(lambda e: e.activation(out=mask, in_=mask, func=AF.Exp), reads=["mask"], writes=["mask"])
    S.dve(lambda e: e.tensor_scalar(out=mask, in0=mask, scalar1=K_SCALE, scalar2=None, op0=ALU.mult),
          reads=["mask"], writes=["mask"])

    if stop == 0:
        for gi in range(4, 12):
            wg.append(wada_group(gi))
            wada_compute(gi, *wg[gi])
        modT_evac(24, 40)
        mod_ab(A_F, B_F, VO_NFFN, 24, 32, 0)
        return finish_debug([(modAB, 48), (lgbc, 16), (mask, 1024), (qdec, 1024), (gm_bc, 1024), (gf_bc, 1024),
                             (kdec, 16), (ctxw, 32), (cdec, 8)])
    R_PAIR.reset()
    xbuf = [R_PAIR.carve(4 * KB, F32) for _ in range(3)]
    junk = R_PAIR.carve(2 * KB)
    xnb = [R_PAIR.carve(2 * KB) for _ in range(2)]
    nctr = [0]

    def norm_stats(src_ap, src_key, junk, xnb):
        n = nctr[0]
        nctr[0] += 1
        ss = stat[:, 32 + (n % 4) * 2:33 + (n % 4) * 2]
        rs = stat[:, 33 + (n % 4) * 2:34 + (n % 4) * 2]
        sk = "nstat%d" % (n % 4)
        xn = xnb[n % len(xnb)]
        xk = "xn%d_%d" % (len(xnb), n % len(xnb))
        S.act(lambda e: e.activation(out=junk, in_=src_ap, func=AF.Square, accum_out=ss),
              reads=[src_key], writes=["junk", sk])
        S.dve(lambda e: e.tensor_scalar(out=rs, in0=ss, scalar1=1.0 / D, scalar2=EPS, op0=ALU.mult, op1=ALU.add),
              reads=[sk], writes=[sk + "r"])
        S.act(lambda e: e.activation(out=rs, in_=rs, func=AF.Sqrt), reads=[sk + "r"], writes=[sk + "r"])
        S.dve(lambda e: e.reciprocal(out=rs, in_=rs), reads=[sk + "r"], writes=[sk + "r"])
        S.dve(lambda e: e.tensor_scalar(out=xn, in0=src_ap, scalar1=rs, scalar2=None, op0=ALU.mult),
              reads=[src_key, sk + "r"], writes=[xk])
        return (n, xn, xk)

    ACT_K = (1, 4, 6)

    def norm_tr(st, ai, bi, dst3, dst_key, tcol, pbase=0, fused=True):
        n, xn, xk = st
        par = n % 2
        pbD, pbA = pbase + 2 * par, pbase + 2 * par + 1
        pTD = bank(pbD, BF16)[:, 0:640].rearrange("p (k t) -> p k t", t=128)
        pTA = bank(pbA, BF16)[:, 0:384].rearrange("p (k t) -> p k t", t=128)
        slot = {}
        na = nd = 0
        for k in range(8):
            if k in ACT_K:
                slot[k] = (pTA, pbA, na)
                na += 1
            else:
                slot[k] = (pTD, pbD, nd)
                nd += 1
        for k in range(8):
            pt_, pbk, sl = slot[k]
            S.pe(lambda e, k=k, pt_=pt_, sl=sl: e.transpose(out=pt_[:, sl, :], in_=xn[:, k * 128:(k + 1) * 128], identity=ident),
                 reads=[xk, "ident"], writes=[PK(pbk)])
        for k in range(8):
            o = dst3[:, k, tcol * 128:(tcol + 1) * 128]
            pt_, pbk, sl = slot[k]
            if not fused:
                if k not in ACT_K:
                    S.dve(lambda e, o=o, pt_=pt_, sl=sl: e.tensor_copy(out=o, in_=pt_[:, sl, :]),
                          reads=[PK(pbk)], writes=[dst_key(k, tcol)])
                else:
                    S.act(lambda e, o=o, pt_=pt_, sl=sl: e.activation(out=o, in_=pt_[:, sl, :], func=AF.Copy),
                          reads=[PK(pbk)], writes=[dst_key(k, tcol)])
            elif k not in ACT_K:
                S.dve(lambda e, k=k, o=o, pt_=pt_, sl=sl: e.tensor_scalar(out=o, in0=pt_[:, sl, :], scalar1=modAB3[:, ai, k:k + 1],
                                                                          scalar2=modAB3[:, bi, k:k + 1], op0=ALU.mult, op1=ALU.add),
                      reads=[PK(pbk), "modAB%d" % ai, "modAB%d" % bi], writes=[dst_key(k, tcol)])
            else:
                S.act(lambda e, k=k, o=o, pt_=pt_, sl=sl: e.activation(out=o, in_=pt_[:, sl, :], func=AF.Identity,
                                                                       scale=modAB3[:, ai, k:k + 1], bias=modAB3[:, bi, k:k + 1]),
                      reads=[PK(pbk), "modAB%d" % ai, "modAB%d" % bi], writes=[dst_key(k, tcol)])

    def hT_key(k, i):
        return "XA_%d_%d" % (k, i)

    pendA = []
    xnb = xnb + [arena[:, R_YP.start // 2:(R_YP.start + 2 * KB) // 2]]
    xbuf = xbuf + [arena[:, (R_YP.start + (4 + 4 * i) * KB) // 2:(R_YP.start + (8 + 4 * i) * KB) // 2].bitcast(F32) for i in range(3)]
    xbuf = xbuf + [arena[:, (R_X.start + (52 + 4 * i) * KB) // 2:(R_X.start + (56 + 4 * i) * KB) // 2].bitcast(F32) for i in range(3)]
    for t in range(2 + NT):
        sbi = t % len(xbuf)
        xb = xbuf[sbi]
        if t < 2:
            S.dma("sp", "xb%d" % sbi, lambda e, xb=xb, t=t: e.dma_start(out=xb, in_=ctx_d[t * 128:(t + 1) * 128, :]),
                  writes=["xb%d" % sbi])
            args = (A_C, B_C, hcT3, (lambda k, tc: "hcT%d" % k), t)
        else:
            i = t - 2
            S.dma("sp", "xb%d" % sbi, lambda e, xb=xb, i=i: e.dma_start(out=xb, in_=x_d[i * 128:(i + 1) * 128, :]),
                  writes=["xb%d" % sbi])
            args = (A_M, B_M, hT, hT_key, i)
        st = norm_stats(xb, "xb%d" % sbi, junk, xnb)
        pendA.append((st,) + args)
        if len(pendA) > 2:
            norm_tr(*pendA.pop(0), fused=False)
    while pendA:
        norm_tr(*pendA.pop(0), fused=False)
    setup_mod_mix()
    for hf in range(2):
        for k in range(8):
            S.dve(lambda e, k=k, hf=hf: e.tensor_scalar(out=hT[:, k, hf * 1024:(hf + 1) * 1024], in0=hT[:, k, hf * 1024:(hf + 1) * 1024],
                                                        scalar1=modAB3[:, A_M, k:k + 1], scalar2=modAB3[:, B_M, k:k + 1],
                                                        op0=ALU.mult, op1=ALU.add),
                  reads=[hT_key(k, i) for i in range(hf * 8, hf * 8 + 8)] + ["modAB%d" % A_M, "modAB%d" % B_M],
                  writes=[hT_key(k, i) for i in range(hf * 8, hf * 8 + 8)])
    for k in range(8):
        S.dve(lambda e, k=k: e.tensor_scalar(out=hcT3[:, k, :], in0=hcT3[:, k, :], scalar1=modAB3[:, A_C, k:k + 1],
                                             scalar2=modAB3[:, B_C, k:k + 1], op0=ALU.mult, op1=ALU.add),
              reads=["hcT%d" % k, "modAB%d" % A_C, "modAB%d" % B_C], writes=["hcT%d" % k])
    wlate = arena[:, (R_M.start + 24 * KB) // 2:(R_M.start + 32 * KB) // 2].rearrange("p (k n) -> p k n", n=512)
    WL_ALIAS = ["ysb0", "ysb1", "ysb2", "ysq"]

    def wada_late_load(gi):
        S.dma("pool", "wlate", lambda e: e.dma_start(out=wlate, in_=wada_d[:, :, gi * 512:(gi + 1) * 512]),
              writes=["wlate"] + WL_ALIAS)

    def wada_late_compute(gi):
        rd = ["wlate"] + WL_ALIAS
        if gi in (4, 5, 10, 11):
            dst = gm_bc if gi in (4, 5) else gf_bc
            dk = "gm_bc" if gi in (4, 5) else "gf_bc"
            half = gi % 2 if gi in (4, 5) else (gi - 10)
            for k in range(8):
                S.pe(lambda e, k=k: e.matmul(bank(7), lhsT=scbc3[:, k, :], rhs=wlate[:, k, :], start=(k == 0), stop=(k == 7)),
                     reads=rd + ["scbc"], writes=[PK(7)])
            S.dve(lambda e: e.tensor_tensor(out=dst[:, half * 512:(half + 1) * 512], in0=bank(7),
                                            in1=dst[:, half * 512:(half + 1) * 512], op=ALU.add),
                  reads=[PK(7), dk], writes=[dk])
        else:
            for jj in range(4):
                j = gi * 4 + jj
                for k in range(8):
                    S.pe(lambda e, k=k, jj=jj, j=j: e.matmul(bank(7)[:, 2 * j:2 * j + 2], lhsT=wlate[:, k, jj * 128:(jj + 1) * 128],
                                                             rhs=scT3[:, k, :], start=(k == 0), stop=(k == 7)),
                         reads=rd + ["scT"], writes=[PK(7)])
            modT_evac(gi * 4, gi * 4 + 4)
    hcT_keys = ["hcT%d" % k for k in range(8)]

    if stop == 1:
        return finish_debug([(hcT, 2048), (hT.rearrange("p k t -> p (k t)")[:, 0:8192], 8192)])

    def load_pair(p):
        buf = wpair_buf[p % 2]
        extra = [S.fence()] if p < 2 else []
        S.dma("pool", "wpair%d" % (p % 2), lambda e, buf=buf, p=p: e.dma_start(out=buf, in_=wpairs_d[p]),
              writes=["wpair%d" % (p % 2)] + extra)

    wu_sb = R_PAIR.carve(8 * KB).rearrange("p (k n) -> p k n", n=512)
    NPST = 9
    pstage = [R_PAIR.carve(4 * KB).rearrange("p (b t) -> p b t", t=512) for _ in range(NPST)]
    S.dma("pool", "wu", lambda e: e.dma_start(out=wu_sb, in_=wu_d), writes=["wu"])
    load_pair(0)
    for i in range(NT):
        pb = 2 + (i % 2)
        for k in range(8):
            S.pe(lambda e, k=k, i=i, pb=pb: e.matmul(bank(pb), lhsT=hT[:, k, i * 128:(i + 1) * 128], rhs=wu_sb[:, k, :],
                                                     start=(k == 0), stop=(k == 7)),
                 reads=[hT_key(k, i), "wu"], writes=[PK(pb)])
        if i % 2 == 0:
            S.act(lambda e, i=i, pb=pb: e.activation(out=u_tok[:, i, :], in_=bank(pb), func=AF.Copy),
                  reads=[PK(pb)], writes=["u%d" % i])
        else:
            S.dve(lambda e, i=i, pb=pb: e.tensor_copy(out=u_tok[:, i, :], in_=bank(pb)),
                  reads=[PK(pb)], writes=["u%d" % i])
    pst_n = [0]
    resident = {}

    def pool_head(pctr, g, j):
        lst = pool_plan[g][j]
        pd = 4 + (pctr % 2)
        dsb = dT_sb[pctr % 2]
        dk = "dT%d" % (pctr % 2)
        runs = []
        for ent in lst:
            if runs and len(runs[-1]) < 4 and runs[-1][-1][1] + 1 == ent[1]:
                runs[-1].append(ent)
            else:
                runs.append([ent])
        pos = 0
        for grp in runs:
            b0 = grp[0][1]
            nb = len(grp)
            assert all(grp[q][1] == b0 + q for q in range(nb))
            hit = resident.get((b0, nb))
            if hit is not None and pst_n[0] - hit < NPST:
                st = pstage[hit % NPST]
                sk = "pst%d" % (hit % NPST)
            else:
                st = pstage[pst_n[0] % NPST]
                sk = "pst%d" % (pst_n[0] % NPST)
                resident[(b0, nb)] = pst_n[0]
                pst_n[0] += 1
                S.dma("sp", sk, lambda e: e.dma_start(out=st[:, 0:nb, :], in_=pblk_d[:, b0:b0 + nb, :]), writes=[sk])
            for q, (t, bi_) in enumerate(grp):
                first = (pos == 0)
                last = (pos == len(lst) - 1)
                pos += 1
                S.pe(lambda e, t=t, q=q, first=first, last=last: e.matmul(
                    bank(pd), lhsT=u_tok[:, t, g * 128:(g + 1) * 128], rhs=st[:, q, :], start=first, stop=last),
                    reads=["u%d" % t, sk], writes=[PK(pd)])
        S.dve(lambda e: e.tensor_copy(out=dsb, in_=bank(pd)), reads=[PK(pd)], writes=[dk])

    def pool_tail(pctr, g, j):
        py = 6 + (pctr % 2)
        dsb = dT_sb[pctr % 2]
        dk = "dT%d" % (pctr % 2)
        S.pe(lambda e: e.matmul(bank(py), lhsT=wpool3[:, g, :], rhs=dsb, start=True, stop=True),
             reads=[dk, "wpool"], writes=[PK(py)])
        S.act(lambda e: e.activation(out=ypT[:, g, j * 512:(j + 1) * 512], in_=bank(py), func=AF.Copy,
                                     scale=vecs[:, VO_PSC + g:VO_PSC + g + 1]),
              reads=[PK(py), "vecs"], writes=["ypT"])

    pblocks = [(g, j) for g in range(4) for j in range(4)]
    pool_head(0, *pblocks[0])
    for c_ in range(len(pblocks)):
        if c_ + 1 < len(pblocks):
            pool_head(c_ + 1, *pblocks[c_ + 1])
        pool_tail(c_, *pblocks[c_])

    if stop == 2:
        return finish_debug([(ypT.rearrange("p g t -> p (g t)"), 8192)])
    R_PAIR.reset()
    kT2 = R_PAIR.carve(4 * KB)
    qTz = R_PAIR.carve(8 * KB).rearrange("p (n h t) -> p n h t", h=2, t=128)
    zf = S.fence()
    S.dve(lambda e: e.memset(qTz[64:128, :, 0, :], 0.0), writes=["qTz_zero", zf])
    S.dve(lambda e: e.memset(qTz[0:64, :, 1, :], 0.0), writes=["qTz_zero", zf])
    q2 = R_PAIR.carve(8 * KB).rearrange("p (h t) -> p h t", t=L)
    kfb = R_PAIR.carve(8 * KB).rearrange("p (i d c) -> p i d c", d=2, c=128)
    v_tok = R_PAIR.carve(8 * KB).rearrange("p (i c) -> p i c", c=256)
    sg_tok = R_PAIR.carve(8 * KB).rearrange("p (i c) -> p i c", c=256)
    s_all = R_PAIR.carve(8 * KB).rearrange("p (n h v) -> p n h v", h=2, v=128)
    rot = [R_PAIR.carve(1 * KB).rearrange("p (t a c) -> p t a c", t=2, c=64) for _ in range(2)]
    qdup = R_PAIR.carve(1 * KB).rearrange("p (t a u c) -> p t a u c", t=2, u=2, c=64)
    _rt0 = R_PAIR.off
    rtmp = [R_PAIR.carve(1 * KB, F32).rearrange("p (t a f) -> p t a f", t=2, f=32) for _ in range(4)]
    _rt1 = R_PAIR.off
    R_PAIR.off = _rt0
    kcw = R_PAIR.carve(1 * KB).rearrange("p (t d c) -> p t d c", d=2, c=128)
    vc = R_PAIR.carve(1 * KB).rearrange("p (t c) -> p t c", c=256)
    R_PAIR.off = _rt1
    pmat = [R_PAIR.carve(1 * KB).rearrange("p (c h t) -> p c h t", c=2, t=128) for _ in range(2)]
    ygb = [R_PAIR.carve(1 * KB).rearrange("p (c h v) -> p c h v", c=2, v=128) for _ in range(2)]
    _rm = R_M.start + 24 * KB
    y_sb = [arena[:, (_rm + i * 2 * KB) // 2:(_rm + (i + 1) * 2 * KB) // 2].bitcast(F32).rearrange("p (c h v) -> p c h v", c=2, v=128)
            for i in range(3)]
    ysq = arena[:, (_rm + 6 * KB) // 2:(_rm + 8 * KB) // 2].bitcast(F32).rearrange("p (c h v) -> p c h v", c=2, v=128)
    sfp3 = sfp[:, 0:256].rearrange("p (h v) -> p h v", v=128)
    gst = sfp[:, 256:384]

    def ygT_key(k, i):
        return "XB_%d_%d" % (k, i)

    def make_pair(p):
        wp = wpair_buf[p % 2]
        wk = "wpair%d" % (p % 2)

        def ctx_pe():
            if 1 <= p + 1 < 4:
                load_pair(p + 1)
            for t in range(2):
                pb = 6 + t
                for k in range(8):
                    S.pe(lambda e, k=k, t=t, pb=pb: e.matmul(bank(pb)[:, 0:384], lhsT=hcT3[:, k, t * 128:(t + 1) * 128],
                                                             rhs=wp[:, k, 128:512], start=(k == 0), stop=(k == 7)),
                         reads=["hcT%d" % k, wk], writes=[PK(pb)])

        def ctx_rest():
            for t in range(2):
                pb = 6 + t
                for d in range(2):
                    S.dve(lambda e, t=t, d=d, pb=pb: e.tensor_tensor(
                        out=kcw[:, t, d, :].rearrange("p (h c) -> p h c", c=64),
                        in0=bank(pb)[:, 0:128].rearrange("p (h c) -> p h c", c=64),
                        in1=ctxw4[:, t, d, 2 * p:2 * p + 2].unsqueeze(2).to_broadcast([128, 2, 64]), op=ALU.mult),
                        reads=[PK(pb), "ctxw"], writes=["kcw"])
                S.dve(lambda e, t=t, pb=pb: e.tensor_copy(out=vc[:, t, :], in_=bank(pb)[:, 128:384]),
                      reads=[PK(pb)], writes=["vc"])
            pS = bank(1)[:, 0:256].rearrange("p (h v) -> p h v", v=128)
            for h2 in range(2):
                for d in range(2):
                    for t in range(2):
                        S.pe(lambda e, h2=h2, d=d, t=t: e.matmul(pS[d * 64:(d + 1) * 64, h2, :],
                                                                 lhsT=kcw[:, t, d, h2 * 64:(h2 + 1) * 64],
                                                                 rhs=vc[:, t, h2 * 128:(h2 + 1) * 128],
                                                                 start=(t == 0), stop=(t == 1)),
                             reads=["kcw", "vc"], writes=["ps1a", "ps1b"])
            S.dve(lambda e: e.tensor_copy(out=sfp3, in_=pS), reads=["ps1a", "ps1b"], writes=["sfp"])
            S.dve(lambda e: e.tensor_copy(out=s_all[0:64, 0, :, :], in_=pS[0:64, :, :]),
                  reads=["ps1a", "ps1b"], writes=["sall_f0"])
            S.dve(lambda e: e.tensor_copy(out=s_all[64:128, NT - 1, :, :], in_=pS[64:128, :, :]),
                  reads=["ps1a", "ps1b"], writes=["sall_b%d" % (NT - 1)])


        def sb_proj(it, i, pe_only=False):
            par = it % 2
            pq = 2 + par
            for t in range(2):
                for k in range(8):
                    S.pe(lambda e, k=k, t=t: e.matmul(bank(pq)[:, t * 256:(t + 1) * 256], lhsT=hT[:, k, (i + t) * 128:(i + t + 1) * 128],
                                                      rhs=wp[:, k, 0:256], start=(k == 0), stop=(k == 7)),
                         reads=[hT_key(k, i + t), wk], writes=[PK(pq)])
            for t in range(2):
                for k in range(8):
                    S.pe(lambda e, k=k, t=t: e.matmul(bank(4 + t), lhsT=hT[:, k, (i + t) * 128:(i + t + 1) * 128],
                                                      rhs=wp[:, k, 256:768], start=(k == 0), stop=(k == 7)),
                         reads=[hT_key(k, i + t), wk], writes=[PK(4 + t)])
            if not pe_only:
                sb_proj_evac(it, i)

        def sb_proj_evac(it, i):
            vg = ps[:, 4 * 512:6 * 512].rearrange("p (t c) -> p t c", t=2)
            S.act(lambda e: e.activation(out=v_tok[:, i:i + 2, :], in_=vg[:, :, 0:256], func=AF.Copy),
                  reads=[PK(4), PK(5)], writes=["v%d" % i, "v%d" % (i + 1)])
            S.act(lambda e: e.activation(out=sg_tok[:, i:i + 2, :], in_=vg[:, :, 256:512], func=AF.Silu),
                  reads=[PK(4), PK(5)], writes=["sg%d" % i, "sg%d" % (i + 1)])

        def sb_rope(it, i):
            par = it % 2
            pq = 2 + par
            qk4 = bank(pq).rearrange("p (t a c) -> p t a c", t=2, c=64)
            t1 = qk4[:, :, :, 0:32]
            t2 = qk4[:, :, :, 32:64]
            cs = cos3[:, i:i + 2, :].unsqueeze(2).to_broadcast([128, 2, 4, 32])
            sn = sin3[:, i:i + 2, :].unsqueeze(2).to_broadcast([128, 2, 4, 32])
            rt = rot[par]
            rk = "rot%d" % par
            ta, tb_, tc_, td = rtmp
            S.dve(lambda e: e.tensor_tensor(out=ta, in0=t1, in1=cs, op=ALU.mult), reads=[PK(pq), "consts"], writes=["rta", "kcw", "vc"])
            S.dve(lambda e: e.tensor_tensor(out=tb_, in0=t2, in1=sn, op=ALU.mult), reads=[PK(pq), "consts"], writes=["rtb", "kcw", "vc"])
            S.dve(lambda e: e.tensor_tensor(out=tc_, in0=t1, in1=sn, op=ALU.mult), reads=[PK(pq), "consts"], writes=["rtc", "kcw", "vc"])
            S.dve(lambda e: e.tensor_tensor(out=td, in0=t2, in1=cs, op=ALU.mult), reads=[PK(pq), "consts"], writes=["rtd", "kcw", "vc"])
            S.dve(lambda e: e.tensor_tensor(out=rt[:, :, :, 0:32], in0=ta, in1=tb_, op=ALU.subtract),
                  reads=["rta", "rtb"], writes=[rk])
            S.dve(lambda e: e.tensor_tensor(out=rt[:, :, :, 32:64], in0=tc_, in1=td, op=ALU.add),
                  reads=["rtc", "rtd"], writes=[rk])
            for t in range(2):
                S.act(lambda e, t=t: e.activation(out=qdup[:, t, :, :, :], in_=rt[:, t, 0:2, :].unsqueeze(2).to_broadcast([128, 2, 2, 64]),
                                                  func=AF.Copy), reads=[rk], writes=["qdup"])
            for d in range(2):
                S.dve(lambda e, d=d: e.tensor_tensor(
                    out=kfb[:, i:i + 2, d, :].rearrange("p t (h c) -> p t h c", c=64), in0=rt[:, :, 2:4, :],
                    in1=kdec3[:, d, 2 * p:2 * p + 2].unsqueeze(1).unsqueeze(3).to_broadcast([128, 2, 2, 64]), op=ALU.mult),
                    reads=[rk, "kdec"], writes=["kfb%d" % i, "kfb%d" % (i + 1)])

        def sb_tr(it, i):
            par = it % 2
            rt = rot[par]
            rk = "rot%d" % par
            sfx = "ab"[par]
            pTA = bank(0, BF16)[:, par * 512:(par + 1) * 512].rearrange("p (t a x) -> p t a x", t=2, x=128)
            pTD = bank(1, BF16)[:, par * 512:(par + 1) * 512].rearrange("p (t h x) -> p t h x", t=2, x=128)
            for t in range(2):
                S.pe(lambda e, t=t: e.transpose(out=pTA[:, t, 0, :], in_=rt[:, t, 0:2, :].rearrange("p a c -> p (a c)"), identity=ident),
                     reads=[rk, "ident"], writes=["ps0" + sfx])
                S.pe(lambda e, t=t: e.transpose(out=pTA[:, t, 1, :], in_=rt[:, t, 2:4, :].rearrange("p a c -> p (a c)"), identity=ident),
                     reads=[rk, "ident"], writes=["ps0" + sfx])
                for h2 in range(2):
                    S.pe(lambda e, t=t, h2=h2: e.transpose(out=pTD[:, t, h2, :], in_=qdup[:, t, h2, :, :].rearrange("p u c -> p (u c)"),
                                                           identity=ident),
                         reads=["qdup", "ident"], writes=["ps1" + sfx])
            qk_keys = ["qkT%d" % i, "qkT%d" % (i + 1)]
            S.act(lambda e: e.activation(out=kT2[:, i * 128:(i + 2) * 128].rearrange("p (t x) -> p t x", t=2), in_=pTA[:, :, 1, :], func=AF.Copy),
                  reads=["ps0" + sfx], writes=qk_keys)
            S.act(lambda e: e.activation(out=qTz[0:64, i:i + 2, 0, :], in_=pTA[0:64, :, 0, :], func=AF.Copy),
                  reads=["ps0" + sfx, "qTz_zero"], writes=qk_keys)
            S.act(lambda e: e.activation(out=qTz[64:128, i:i + 2, 1, :], in_=pTA[64:128, :, 0, :], func=AF.Copy),
                  reads=["ps0" + sfx, "qTz_zero"], writes=qk_keys)
            S.dve(lambda e: e.tensor_tensor(out=q2[:, :, i * 128:(i + 2) * 128].rearrange("p h (t x) -> p t h x", t=2), in0=pTD,
                                            in1=qdec3[:, 2 * p:2 * p + 2, :].unsqueeze(1).to_broadcast([128, 2, 2, 128]), op=ALU.mult),
                  reads=["ps1" + sfx, "qdec"], writes=["q2_%d" % i, "q2_%d" % (i + 1)])

        def scan_pe(j):
            sfx = "ab"[j % 2]
            pD = bank(6)[:, (j % 2) * 256:(j % 2) * 256 + 256].rearrange("p (h v) -> p h v", v=128)
            nf, nb = j, NT - 1 - j
            for h2 in range(2):
                S.pe(lambda e, h2=h2: e.matmul(pD[0:64, h2, :], lhsT=kfb[:, nf, 0, h2 * 64:(h2 + 1) * 64],
                                               rhs=v_tok[:, nf, h2 * 128:(h2 + 1) * 128], start=True, stop=True),
                     reads=["kfb%d" % nf, "v%d" % nf], writes=["ps6" + sfx])
                S.pe(lambda e, h2=h2: e.matmul(pD[64:128, h2, :], lhsT=kfb[:, nb, 1, h2 * 64:(h2 + 1) * 64],
                                               rhs=v_tok[:, nb, h2 * 128:(h2 + 1) * 128], start=True, stop=True),
                     reads=["kfb%d" % nb, "v%d" % nb], writes=["ps6" + sfx])

        def scan_ew(j):
            sfx = "ab"[j % 2]
            pD = bank(6)[:, (j % 2) * 256:(j % 2) * 256 + 256].rearrange("p (h v) -> p h v", v=128)
            nf, nb = j, NT - 1 - j
            for h2 in range(2):
                S.dve(lambda e, h2=h2: e.scalar_tensor_tensor(
                    out=sfp3[:, h2, :], in0=sfp3[:, h2, :], scalar=cdec[:, 2 * p + h2:2 * p + h2 + 1], in1=pD[:, h2, :],
                    op0=ALU.mult, op1=ALU.add), reads=["sfp", "ps6" + sfx, "cdec"], writes=["sfp"])
            if os.environ.get("KDBG_CAST", "dve") == "dve":
                S.dve(lambda e: e.tensor_copy(out=s_all[0:64, nf + 1, :, :], in_=sfp3[0:64, :, :]),
                      reads=["sfp"], writes=["sall_f%d" % (nf + 1)])
                S.dve(lambda e: e.tensor_copy(out=s_all[64:128, nb - 1, :, :], in_=sfp3[64:128, :, :]),
                      reads=["sfp"], writes=["sall_b%d" % (nb - 1)])
            else:
                S.act(lambda e: e.activation(out=s_all[0:64, nf + 1, :, :], in_=sfp3[0:64, :, :], func=AF.Copy),
                      reads=["sfp"], writes=["sall_f%d" % (nf + 1)])
                S.act(lambda e: e.activation(out=s_all[64:128, nb - 1, :, :], in_=sfp3[64:128, :, :], func=AF.Copy),
                      reads=["sfp"], writes=["sall_b%d" % (nb - 1)])

        def scan_step(j):
            scan_pe(j)
            scan_ew(j)

        class OutStep:
            def __init__(self, it, n0):
                self.it, self.n0 = it, n0
                par = it % 2
                self.psc = 2 + par
                self.pyb = 4 + par
                self.pS4 = bank(self.psc).rearrange("p (c h t) -> p c h t", c=2, t=128)
                self.pY4 = bank(self.pyb).rearrange("p (c h v) -> p c h v", c=2, v=128)
                self.pm = pmat[par]
                self.pmk = "pmat%d" % par
                self.ysb = y_sb[it % 3]
                self.yk = "ysb%d" % (it % 3)
                self.gs = gst[:, (it % 4) * 32:(it % 4) * 32 + 32]
                self.gk = "gst%d" % (it % 4)
                self.yg = ygb[par]
                self.ygk = "ygb%d" % par
                self.sfx = "ab"[par]
                self.pT4 = bank(0, BF16)[:, par * 512:(par + 1) * 512].rearrange("p (h c t) -> p h c t", h=2, t=128)

            def scores(o):
                for c in range(2):
                    n = o.n0 + c
                    tsl = slice(n * 128, (n + 1) * 128)
                    S.pe(lambda e, c=c, n=n, tsl=tsl: e.matmul(bank(o.psc)[:, c * 256:(c + 1) * 256], lhsT=kT2[:, tsl],
                                                               rhs=qTz[:, n, :, :].rearrange("p h t -> p (h t)"), start=True, stop=True),
                         reads=["qkT%d" % n, "qTz_zero"], writes=[PK(o.psc)])

            def mask(o):
                S.dve(lambda e: e.tensor_tensor(out=o.pm, in0=o.pS4,
                                                in1=mask3[:, 2 * p:2 * p + 2, :].unsqueeze(1).to_broadcast([128, 2, 2, 128]),
                                                op=ALU.mult), reads=[PK(o.psc), "mask"], writes=[o.pmk])

            def av(o):
                for c in range(2):
                    n = o.n0 + c
                    tsl = slice(n * 128, (n + 1) * 128)
                    for h2 in range(2):
                        S.pe(lambda e, c=c, n=n, h2=h2: e.matmul(o.pY4[:, c, h2, :], lhsT=o.pm[:, c, h2, :],
                                                                 rhs=v_tok[:, n, h2 * 128:(h2 + 1) * 128], start=True, stop=False),
                             reads=[o.pmk, "v%d" % n], writes=[PK(o.pyb)])
                        S.pe(lambda e, c=c, n=n, h2=h2, tsl=tsl: e.matmul(o.pY4[:, c, h2, :], lhsT=q2[:, h2, tsl],
                                                                          rhs=s_all[:, n, h2, :], start=False, stop=True),
                             reads=["q2_%d" % n, "sall_f%d" % n, "sall_b%d" % n], writes=[PK(o.pyb)])

            def copy_sq(o):
                S.act(lambda e: e.activation(out=o.ysb, in_=o.pY4, func=AF.Copy), reads=[PK(o.pyb)], writes=[o.yk])
                for c in range(2):
                    for h2 in range(2):
                        q_ = c * 2 + h2
                        S.act(lambda e, c=c, h2=h2, q_=q_: e.activation(out=ysq[:, c, h2, :], in_=o.pY4[:, c, h2, :], func=AF.Square,
                                                                        accum_out=o.gs[:, 4 + q_:5 + q_]),
                              reads=[PK(o.pyb)], writes=["ysq", o.gk + "q"])

            def reduce(o):
                S.dve(lambda e: e.tensor_reduce(out=o.gs[:, 0:4], in_=o.ysb.rearrange("p c h v -> p (c h) v"), axis=AX.X, op=ALU.add),
                      reads=[o.yk], writes=[o.gk + "s"])

            def var(o):
                S.dve(lambda e: e.tensor_tensor(out=o.gs[:, 8:12], in0=o.gs[:, 0:4], in1=o.gs[:, 0:4], op=ALU.mult),
                      reads=[o.gk + "s"], writes=[o.gk + "ss"])
                S.dve(lambda e: e.scalar_tensor_tensor(out=o.gs[:, 12:16], in0=o.gs[:, 4:8], scalar=float(DV), in1=o.gs[:, 8:12],
                                                       op0=ALU.mult, op1=ALU.subtract),
                      reads=[o.gk + "q", o.gk + "ss"], writes=[o.gk + "v"])

            def sqrt(o):
                S.act(lambda e: e.activation(out=o.gs[:, 16:20], in_=o.gs[:, 12:16], func=AF.Sqrt, bias=epsb[:, 0:1]),
                      reads=[o.gk + "v", "epsb"], writes=[o.gk + "sd"])

            def rstd(o):
                S.dve(lambda e: e.reciprocal(out=o.gs[:, 24:28], in_=o.gs[:, 16:20]), reads=[o.gk + "sd"], writes=[o.gk + "r"])
                S.dve(lambda e: e.scalar_tensor_tensor(out=o.gs[:, 28:32], in0=o.gs[:, 0:4], scalar=-1.0 / DV, in1=o.gs[:, 24:28],
                                                       op0=ALU.mult, op1=ALU.mult),
                      reads=[o.gk + "s", o.gk + "r"], writes=[o.gk + "nb"])

            def normalize(o):
                for c in range(2):
                    for h2 in range(2):
                        q_ = c * 2 + h2
                        S.act(lambda e, c=c, h2=h2, q_=q_: e.activation(out=o.yg[:, c, h2, :], in_=o.ysb[:, c, h2, :], func=AF.Identity,
                                                                        scale=o.gs[:, 24 + q_:25 + q_], bias=o.gs[:, 28 + q_:29 + q_]),
                              reads=[o.yk, o.gk + "r", o.gk + "nb"], writes=[o.ygk])

            def gate(o):
                S.dve(lambda e: e.tensor_tensor(out=o.yg.rearrange("p c h v -> p c (h v)"),
                                                in0=o.yg.rearrange("p c h v -> p c (h v)"),
                                                in1=sg_tok[:, o.n0:o.n0 + 2, :], op=ALU.mult),
                      reads=[o.ygk, "sg%d" % o.n0, "sg%d" % (o.n0 + 1)], writes=[o.ygk])

            def transposes(o):
                for c in range(2):
                    for h2 in range(2):
                        S.pe(lambda e, c=c, h2=h2: e.transpose(out=o.pT4[:, h2, c, :], in_=o.yg[:, c, h2, :], identity=ident),
                             reads=[o.ygk, "ident"], writes=["ps0" + o.sfx])

            def evac(o):
                for h2 in range(2):
                    kk = 2 * p + h2
                    S.act(lambda e, h2=h2, kk=kk: e.activation(out=ygT[:, kk, o.n0 * 128:(o.n0 + 2) * 128],
                                                               in_=o.pT4[:, h2, :, :].rearrange("p c t -> p (c t)"), func=AF.Copy,
                                                               scale=gnw128[:, kk:kk + 1]),
                          reads=["ps0" + o.sfx, "gnw128"], writes=[ygT_key(kk, o.n0), ygT_key(kk, o.n0 + 1)])

        order_b = []
        for j in range(NT // 4):
            order_b += [2 * j, NT - 2 - 2 * j]
        def head_pe():
            sb_proj(0, order_b[0], pe_only=True)

        def head_rest():
            sb_proj_evac(0, order_b[0])
            sb_rope(0, order_b[0])

        def body(nxt):
            wada_late_load(4 + 2 * p)
            for it, i in enumerate(order_b):
                if it + 1 < len(order_b):
                    sb_proj(it + 1, order_b[it + 1])
                if it == 2:
                    wada_late_compute(4 + 2 * p)
                    wada_late_load(5 + 2 * p)
                if it == 5:
                    wada_late_compute(5 + 2 * p)
                    if p == 2:
                        mod_ab(A_F, B_F, VO_NFFN, 24, 32, 0)
                sb_tr(it, i)
                if it + 1 < len(order_b):
                    sb_rope(it + 1, order_b[it + 1])
                if it % 2 == 1:
                    scan_step(it - 1)
                    scan_step(it)
            order_o = [6, 8, 4, 10, 2, 12, 0, 14]
            NO = len(order_o)
            steps = [OutStep(it, order_o[it]) for it in range(NO)]

            def g(k):
                return steps[k] if 0 <= k < NO else None

            scan_step(NT // 2)
            def iteration(it):
                    a_, b_, c_, d_ = g(it), g(it - 1), g(it - 2), g(it - 3)
                    sj = NT // 2 + 1 + it if NT // 2 + 1 + it <= NT - 2 else None
                    ordm = os.environ.get("KDBG_ORD", "m1c")
                    if ordm == "r7":
                        if c_:
                            c_.var(); c_.sqrt(); c_.rstd()
                        if d_:
                            d_.normalize(); d_.gate(); d_.transposes(); d_.evac()
                        if b_:
                            b_.copy_sq(); b_.reduce()
                        if sj is not None:
                            scan_step(sj)
                        if a_:
                            a_.scores(); a_.mask(); a_.av()
                        return
                    if ordm == "m1":
                        if a_:
                            a_.scores()
                        if sj is not None:
                            scan_pe(sj)
                        if c_:
                            c_.var(); c_.sqrt(); c_.rstd()
                        if d_:
                            d_.normalize(); d_.gate(); d_.transposes(); d_.evac()
                        if b_:
                            b_.copy_sq(); b_.reduce()
                        if sj is not None:
                            scan_ew(sj)
                        if a_:
                            a_.mask(); a_.av()
                        return
                    if ordm == "m1c":
                        if a_:
                            a_.scores()
                        if sj is not None:
                            scan_pe(sj)
                        if c_:
                            c_.var(); c_.sqrt()
                        if sj is not None:
                            scan_ew(sj)
                        if c_:
                            c_.rstd()
                        if d_:
                            d_.normalize()
                        if b_:
                            b_.copy_sq()
                        if d_:
                            d_.gate(); d_.transposes()
                        if b_:
                            b_.reduce()
                        if d_:
                            d_.evac()
                        if a_:
                            a_.mask(); a_.av()
                        return
                    if ordm == "m1b":
                        if a_:
                            a_.scores()
                        if sj is not None:
                            scan_pe(sj)
                        if c_:
                            c_.var(); c_.sqrt(); c_.rstd()
                        if d_:
                            d_.normalize()
                        if b_:
                            b_.copy_sq()
                        if d_:
                            d_.gate(); d_.transposes()
                        if b_:
                            b_.reduce()
                        if d_:
                            d_.evac()
                        if sj is not None:
                            scan_ew(sj)
                        if a_:
                            a_.mask(); a_.av()
                        return
                    if ordm in ("m3", "m4"):
                        if a_:
                            a_.scores()
                        if sj is not None:
                            scan_pe(sj)
                        if c_:
                            c_.var(); c_.sqrt(); c_.rstd()
                        if ordm == "m4" and a_:
                            a_.mask(); a_.av()
                        if d_:
                            d_.normalize(); d_.gate(); d_.transposes(); d_.evac()
                        if ordm == "m3" and a_:
                            a_.mask(); a_.av()
                        if b_:
                            b_.copy_sq(); b_.reduce()
                        if sj is not None:
                            scan_ew(sj)
                        return
                    if ordm == "m2":
                        if a_:
                            a_.scores()
                        if sj is not None:
                            scan_pe(sj)
                        if c_:
                            c_.var(); c_.sqrt()
                        if a_:
                            a_.mask(); a_.av()
                        if c_:
                            c_.rstd()
                        if d_:
                            d_.normalize(); d_.gate(); d_.transposes(); d_.evac()
                        if b_:
                            b_.copy_sq(); b_.reduce()
                        if sj is not None:
                            scan_ew(sj)
                        return
                    if a_:
                        a_.scores()
                    if sj is not None:
                        scan_pe(sj)
                    if c_:
                        c_.var()
                        c_.sqrt()
                    if d_:
                        d_.normalize()
                    if a_:
                        a_.mask()
                        a_.av()
                    if c_:
                        c_.rstd()
                    if sj is not None:
                        scan_ew(sj)
                    if b_:
                        b_.copy_sq()
                    if d_:
                        d_.gate()
                        d_.transposes()
                    if b_:
                        b_.reduce()
                    if d_:
                        d_.evac()

            for it in range(NO + 3):
                iteration(it)
                if nxt is not None and it == NO:
                    nxt.ctx_pe()
                if nxt is not None and it == NO + 1:
                    nxt.head_pe()
            if nxt is not None:
                nxt.ctx_rest()
                nxt.head_rest()

        class _P:
            pass
        o_ = _P()
        o_.ctx_pe, o_.ctx_rest, o_.head_pe, o_.head_rest, o_.body = ctx_pe, ctx_rest, head_pe, head_rest, body
        return o_

    pairs = [make_pair(p) for p in range(4)]
    pairs[0].ctx_pe()
    pairs[0].ctx_rest()
    pairs[0].head_pe()
    pairs[0].head_rest()
    for p in range(4):
        pairs[p].body(pairs[p + 1] if p < 3 else None)

    if stop == 3:
        return finish_debug([(ygT.rearrange("p k t -> p (k t)"), 16384)])
    R_PAIR.reset()
    def _rp(off_kb, size_kb, dt=BF16):
        v = arena[:, (R_PAIR.start + off_kb * KB) // 2:(R_PAIR.start + (off_kb + size_kb) * KB) // 2]
        return v if dt == BF16 else v.bitcast(dt)

    wt_buf = [_rp(20, 7).rearrange("p (r n) -> p r n", n=128), _rp(0, 7).rearrange("p (r n) -> p r n", n=128)]
    wt_buf = [wt_buf[0], wt_buf[1]]
    sgab = [_rp(7 + 2 * i, 2, F32) for i in range(4)]
    wo_sb = _rp(28, 16).rearrange("p (k n) -> p k n", n=D)
    wo_stage = [_rp(44 + 4 * i, 4, F32) for i in range(2)]
    xres = [_rp(52 + 4 * i, 4, F32) for i in range(2)]
    R_PAIR.off = 60 * KB

    def mT_key(k, i):
        return "M_%d_%d" % (k, i)

    def load_tail(cb):
        extra = [S.fence()] if cb == 1 else (["kfb%d" % i for i in range(NT)] if cb == 0 else [])
        S.dma("pool", "wt%d" % (cb % 2), lambda e, cb=cb: e.dma_start(out=wt_buf[cb % 2], in_=wtail_d[cb]),
              writes=["wt%d" % (cb % 2)] + extra)

    def prep_wo(k):
        st = wo_stage[k % 2]
        sk = "wost%d" % (k % 2)
        S.dma("sp", sk, lambda e, st=st, k=k: e.dma_start(out=st, in_=wo_d[k]), writes=[sk] + ([S.fence()] if k < 2 else []))
        S.dve(lambda e, st=st, k=k: e.tensor_tensor(out=wo_sb[:, k, :], in0=st, in1=gm_bc, op=ALU.mult),
              reads=[sk, "gm_bc"], writes=["wo%d" % k])

    load_tail(0)
    ctr = 0
    for cb in range(8):
        wt = wt_buf[cb % 2]
        wtk = "wt%d" % (cb % 2)
        if cb + 1 < 8:
            load_tail(cb + 1)
        prep_wo(cb)
        for tb in range(4):
            ts_ = slice(tb * 512, (tb + 1) * 512)
            par = ctr % 2
            ctr += 1
            pga, pgb, pba, pbb = par, 2 + par, 4 + par, 6 + par
            hreads = lambda k: [hT_key(k, 4 * tb + q) for q in range(4)]
            for k in range(8):
                S.pe(lambda e, k=k, ts_=ts_, pga=pga, wt=wt: e.matmul(bank(pga), lhsT=wt[:, k, :], rhs=hT[:, k, ts_],
                                                                     start=(k == 0), stop=(k == 7)),
                     reads=[wtk] + hreads(k), writes=[PK(pga)])
            for g in range(4):
                S.pe(lambda e, g=g, ts_=ts_, pba=pba, wt=wt: e.matmul(bank(pba), lhsT=wt[:, 24 + g, :], rhs=ypT[:, g, ts_],
                                                                     start=(g == 0), stop=(g == 3)),
                     reads=[wtk, "ypT"], writes=[PK(pba)])
            for k in range(8):
                S.pe(lambda e, k=k, ts_=ts_, pgb=pgb, wt=wt: e.matmul(bank(pgb), lhsT=wt[:, 8 + k, :], rhs=hT[:, k, ts_],
                                                                     start=(k == 0), stop=(k == 7)),
                     reads=[wtk] + hreads(k), writes=[PK(pgb)])
            for k in range(8):
                S.pe(lambda e, k=k, ts_=ts_, pbb=pbb, wt=wt: e.matmul(bank(pbb), lhsT=wt[:, 16 + k, :], rhs=ygT[:, k, ts_],
                                                                     start=(k == 0), stop=(k == 7)),
                     reads=[wtk] + [ygT_key(k, 4 * tb + q) for q in range(4)], writes=[PK(pbb)])
            sa = sgab[par * 2]
            sb_ = sgab[par * 2 + 1]
            sak = "sga%d" % par
            sbk = "sgb%d" % par
            S.act(lambda e, sa=sa, pga=pga: e.activation(out=sa, in_=bank(pga), func=AF.Sigmoid), reads=[PK(pga)], writes=[sak])
            S.act(lambda e, sb_=sb_, pgb=pgb: e.activation(out=sb_, in_=bank(pgb), func=AF.Sigmoid), reads=[PK(pgb)], writes=[sbk])
            S.dve(lambda e, sa=sa, pba=pba: e.tensor_tensor(out=sa, in0=sa, in1=bank(pba), op=ALU.mult),
                  reads=[sak, PK(pba)], writes=[sak])
            S.dve(lambda e, sb_=sb_, pbb=pbb: e.tensor_tensor(out=sb_, in0=sb_, in1=bank(pbb), op=ALU.mult),
                  reads=[sbk, PK(pbb)], writes=[sbk])
            S.dve(lambda e, sa=sa, sb_=sb_, cb=cb, ts_=ts_: e.tensor_tensor(out=mT[:, cb, ts_], in0=sa, in1=sb_, op=ALU.add),
                  reads=[sak, sbk], writes=[mT_key(cb, 4 * tb + q) for q in range(4)])

    if stop == 4:
        return finish_debug([(mT.rearrange("p k t -> p (k t)"), 16384)])
    w13_buf = [arena[:, (R_PAIR.start + i * 4 * KB) // 2:(R_PAIR.start + (i + 1) * 4 * KB) // 2].rearrange(
        "p (a k n) -> p a k n", a=2, n=128) for i in range(2)]

    def load_w13(c):
        extra = [S.fence()] if c < 2 else []
        S.dma("pool", "w13_%d" % (c % 2), lambda e, c=c: e.dma_start(out=w13_buf[c % 2], in_=w13_d[c]),
              writes=["w13_%d" % (c % 2)] + extra)

    if stop is None:
        load_w13(0)
        load_w13(1)
    def x1_key(i):
        return "X1_%d" % i

    h2T = mT

    def h2_key(k, i):
        return mT_key(k, i)

    junk2 = _rp(48, 2)
    a2 = {}

    def a2_square(i):
        n = nctr[0]
        nctr[0] += 1
        c = dict(n=n, i=i, ss=stat[:, 32 + (n % 4) * 2:33 + (n % 4) * 2], rs=stat[:, 33 + (n % 4) * 2:34 + (n % 4) * 2],
                 sk="nstat%d" % (n % 4), xn=xnb2[n % 3], xk="xn3_%d" % (n % 3))
        S.act(lambda e: e.activation(out=junk2, in_=x1[:, i, :], func=AF.Square, accum_out=c["ss"]),
              reads=[x1_key(i)], writes=["junk", c["sk"]])
        return c

    def a2_ts(c):
        S.dve(lambda e: e.tensor_scalar(out=c["rs"], in0=c["ss"], scalar1=1.0 / D, scalar2=EPS, op0=ALU.mult, op1=ALU.add),
              reads=[c["sk"]], writes=[c["sk"] + "r"])
        S.act(lambda e: e.activation(out=c["rs"], in_=c["rs"], func=AF.Sqrt), reads=[c["sk"] + "r"], writes=[c["sk"] + "r"])

    def a2_fin(c):
        S.dve(lambda e: e.reciprocal(out=c["rs"], in_=c["rs"]), reads=[c["sk"] + "r"], writes=[c["sk"] + "r"])
        S.dve(lambda e: e.tensor_scalar(out=c["xn"], in0=x1[:, c["i"], :], scalar1=c["rs"], scalar2=None, op0=ALU.mult),
              reads=[x1_key(c["i"]), c["sk"] + "r"], writes=[c["xk"]])

    xnb2 = [_rp(50, 2), _rp(60, 2), arena[:, (R_YP.start + 14 * KB) // 2:(R_YP.start + 16 * KB) // 2]]
    pend = None
    for i in range(NT):
        pa = (i % 2) * 2
        xr = xres[i % 2]
        xrk = "xres%d" % (i % 2)
        S.dma("sp", xrk, lambda e, xr=xr, i=i: e.dma_start(out=xr, in_=x_d[i * 128:(i + 1) * 128, :]),
              writes=[xrk])
        for hf in range(2):
            for k in range(8):
                S.pe(lambda e, k=k, hf=hf, i=i, pa=pa: e.matmul(bank(pa + hf), lhsT=mT[:, k, i * 128:(i + 1) * 128],
                                                                rhs=wo_sb[:, k, hf * 512:(hf + 1) * 512],
                                                                start=(k == 0), stop=(k == 7)),
                     reads=[mT_key(k, i), "wo%d" % k], writes=[PK(pa + hf)])
        S.dve(lambda e, i=i, xr=xr, pa=pa: e.tensor_tensor(out=x1[:, i, :], in0=ps[:, pa * 512:(pa + 2) * 512], in1=xr, op=ALU.add),
              reads=[PK(pa), PK(pa + 1), xrk],
              writes=[x1_key(i)] + [(hT_key(i, t) if i < 8 else ygT_key(i - 8, t)) for t in range(NT)])
        if stop != 5:
            a2[i] = a2_square(i)
            if i >= 1:
                a2_ts(a2[i - 1])
            if i >= 3:
                norm_tr((a2[i - 3]["n"], a2[i - 3]["xn"], a2[i - 3]["xk"]), A_F, B_F, h2T, h2_key, i - 3, 4)
            if i >= 1:
                a2_fin(a2[i - 1])
    if stop != 5:
        a2_ts(a2[NT - 1])
        norm_tr((a2[NT - 3]["n"], a2[NT - 3]["xn"], a2[NT - 3]["xk"]), A_F, B_F, h2T, h2_key, NT - 3, 4)
        a2_fin(a2[NT - 1])
        for i_ in (NT - 2, NT - 1):
            norm_tr((a2[i_]["n"], a2[i_]["xn"], a2[i_]["xk"]), A_F, B_F, h2T, h2_key, i_, 4)

    if stop == 5:
        return finish_debug([(x1.rearrange("p i f -> p (i f)"), 16384)])
    R_PAIR.reset()
    R_PAIR.carve(8 * KB)
    actT = [R_PAIR.carve(20 * KB).rearrange("p (c t) -> p c t", t=L) for _ in range(2)]
    R_YP.reset()
    w2_sb = R_YP.carve(10 * KB).rearrange("p (c n) -> p c n", n=D)
    sa_sb = [R_YP.carve(2 * KB, F32) for _ in range(2)]
    w2_stage = [qdec, mask]

    S.dma("sp", "c6", lambda e: e.dma_start(out=nf_bc, in_=rows_d[0:1, RO_NF:RO_NF + D].to_broadcast([128, D])),
          writes=["nf_bc", S.fence()] + hcT_keys)
    fctr = 0
    for gi, (c0, ng) in enumerate(FF_GROUPS):
        at = actT[gi % 2]
        atk = "actT%d" % (gi % 2)
        for cc in range(ng):
            c = c0 + cc
            wb = w13_buf[c % 2]
            wbk = "w13_%d" % (c % 2)
            if 2 <= c + 1 < NFF:
                load_w13(c + 1)
            st = w2_stage[c % 2]
            stk = "w2st%d" % (c % 2)
            S.dma("sp", stk, lambda e, st=st, c=c: e.dma_start(out=st, in_=w2_d[c]), writes=[stk, "qdec" if c % 2 == 0 else "mask"])
            S.dve(lambda e, st=st, cc=cc: e.tensor_tensor(out=w2_sb[:, cc, :], in0=st, in1=gf_bc, op=ALU.mult),
                  reads=[stk, "gf_bc"], writes=["w2sb%d" % cc])
            for tb in range(4):
                ts_ = slice(tb * 512, (tb + 1) * 512)
                par = fctr % 2
                fctr += 1
                pa_, pb_ = par, 2 + par
                for a in range(2):
                    pbk = pa_ if a == 0 else pb_
                    for k in range(8):
                        S.pe(lambda e, a=a, k=k, ts_=ts_, pbk=pbk, wb=wb: e.matmul(bank(pbk), lhsT=wb[:, a, k, :], rhs=h2T[:, k, ts_],
                                                                                  start=(k == 0), stop=(k == 7)),
                             reads=[wbk] + [h2_key(k, 4 * tb + q) for q in range(4)], writes=[PK(pbk)])
                sa = sa_sb[par]
                sak = "ffsa%d" % par
                S.act(lambda e, sa=sa, pa_=pa_: e.activation(out=sa, in_=bank(pa_), func=AF.Silu), reads=[PK(pa_)], writes=[sak])
                S.dve(lambda e, sa=sa, pb_=pb_, cc=cc, ts_=ts_, at=at: e.tensor_tensor(out=at[:, cc, ts_], in0=sa, in1=bank(pb_), op=ALU.mult),
                      reads=[sak, PK(pb_)], writes=[atk + "_%d_%d" % (cc, tb)])
        last = (gi == len(FF_GROUPS) - 1)
        for i in range(NT):
            pa = 4 + (i % 2) * 2
            for hf in range(2):
                for cc in range(ng):
                    S.pe(lambda e, cc=cc, hf=hf, i=i, pa=pa, at=at: e.matmul(bank(pa + hf), lhsT=at[:, cc, i * 128:(i + 1) * 128],
                                                                             rhs=w2_sb[:, cc, hf * 512:(hf + 1) * 512],
                                                                             start=(cc == 0), stop=(cc == ng - 1)),
                         reads=[atk + "_%d_%d" % (cc, i // 4), "w2sb%d" % cc], writes=[PK(pa + hf)])
            S.dve(lambda e, i=i, pa=pa: e.tensor_tensor(out=x1[:, i, :], in0=ps[:, pa * 512:(pa + 2) * 512], in1=x1[:, i, :], op=ALU.add),
                  reads=[PK(pa), PK(pa + 1), x1_key(i)], writes=[x1_key(i)])
            if last:
                def fin_a(i):
                    ss = stat[:, 32 + (i % 4) * 2:33 + (i % 4) * 2]
                    S.act(lambda e: e.activation(out=junk2, in_=x1[:, i, :], func=AF.Square, accum_out=ss),
                          reads=[x1_key(i)], writes=["junk2", "fstat%d" % (i % 4)])

                def fin_b(i):
                    ss = stat[:, 32 + (i % 4) * 2:33 + (i % 4) * 2]
                    rs = stat[:, 33 + (i % 4) * 2:34 + (i % 4) * 2]
                    sk = "fstat%d" % (i % 4)
                    S.dve(lambda e: e.tensor_scalar(out=rs, in0=ss, scalar1=1.0 / D, scalar2=EPS, op0=ALU.mult, op1=ALU.add),
                          reads=[sk], writes=[sk + "r"])
                    S.act(lambda e: e.activation(out=rs, in_=rs, func=AF.Sqrt), reads=[sk + "r"], writes=[sk + "r"])

                def fin_c(i):
                    rs = stat[:, 33 + (i % 4) * 2:34 + (i % 4) * 2]
                    sk = "fstat%d" % (i % 4)
                    S.dve(lambda e: e.reciprocal(out=rs, in_=rs), reads=[sk + "r"], writes=[sk + "r"])
                    S.dve(lambda e: e.scalar_tensor_tensor(out=x1[:, i, :], in0=x1[:, i, :], scalar=rs, in1=nf_bc,
                                                           op0=ALU.mult, op1=ALU.mult),
                          reads=[x1_key(i), sk + "r", "nf_bc"], writes=[x1_key(i)])
                    S.dma("sp", "out", lambda e: e.dma_start(out=out_d[i * 128:(i + 1) * 128, :], in_=x1[:, i, :]),
                          reads=[x1_key(i)])

                fin_a(i)
                if i >= 1:
                    fin_b(i - 1)
                if i >= 2:
                    fin_c(i - 2)
                if i == NT - 1:
                    fin_b(i)
                    fin_c(i - 1)
                    fin_c(i)

    S.emit(nc, final_dma_keys=["out"])
    return nc


_Q_OFF, _K_OFF, _V_OFF, _G_OFF, _GA_OFF, _GB_OFF = 512, 1024, 1536, 2560, 3584, 4608


def _kchunk(w):
    kk = w.shape[0] // 128
    return np.ascontiguousarray(w.reshape(kk, 128, w.shape[1]).transpose(1, 0, 2))


def _prep(x, c, ctx, c_ctx, w_ada, b_ada, norm_mix, norm_ffn, w_in, w_pool, pool_scale,
          ret_decay_f, ret_decay_b, ret_gn_w, w_pa, w_rb, w_o, w_ff1, w_ff3, w_ff2, norm_final):
    f32 = np.float32
    x = np.asarray(x, f32)
    B = x.shape[0]
    pblk, plan = _get_pool()

    w_in0 = np.asarray(w_in, f32)[0]
    wada = _kchunk(np.asarray(w_ada, f32)[0])
    wu = _kchunk(w_in0[:, 0:512])
    wpairs = []
    for p in range(4):
        cols = np.concatenate([
            w_in0[:, _Q_OFF + p * 128:_Q_OFF + (p + 1) * 128],
            w_in0[:, _K_OFF + p * 128:_K_OFF + (p + 1) * 128],
            w_in0[:, _V_OFF + p * 256:_V_OFF + (p + 1) * 256],
            w_in0[:, _G_OFF + p * 256:_G_OFF + (p + 1) * 256]], axis=1)
        wpairs.append(_kchunk(cols))
    wpairs = np.stack(wpairs, 0)
    w_rb0 = np.asarray(w_rb, f32)[0]
    w_pa0 = np.asarray(w_pa, f32)[0]
    wtail = []
    for cb in range(8):
        cs = slice(cb * 128, (cb + 1) * 128)
        ga = _kchunk(w_in0[:, _GA_OFF:_GA_OFF + D][:, cs])
        gb = _kchunk(w_in0[:, _GB_OFF:_GB_OFF + D][:, cs])
        rb = _kchunk(w_rb0[:, cs])
        pa = _kchunk(w_pa0[:, cs])
        wtail.append(np.concatenate([ga, gb, rb, pa], axis=1))
    wtail = np.ascontiguousarray(np.stack(wtail, 0))
    wpool = np.ascontiguousarray(np.asarray(w_pool, f32)[0].transpose(1, 0, 2))
    wo = np.ascontiguousarray(np.asarray(w_o, f32)[0].reshape(8, 128, D))
    w1 = np.asarray(w_ff1, f32)[0]
    w3 = np.asarray(w_ff3, f32)[0]
    w13 = np.stack([np.stack([_kchunk(w1[:, cc * 128:(cc + 1) * 128]), _kchunk(w3[:, cc * 128:(cc + 1) * 128])], axis=1)
                    for cc in range(NFF)], 0)
    w13 = np.ascontiguousarray(w13)
    w2 = np.ascontiguousarray(np.asarray(w_ff2, f32)[0].reshape(NFF, 128, D))
    consts = _host_consts()
    ident = np.eye(128, dtype=np.float32).astype(ml_dtypes.bfloat16)

    def pp(v, k):
        return np.asarray(v, f32).reshape(k, 128).T

    b_ada0 = np.asarray(b_ada, f32)[0]
    rows = np.zeros((1, NROW), f32)
    rows[0, RO_BGM:RO_BGM + D] = b_ada0[2 * D:3 * D]
    rows[0, RO_BGF:RO_BGF + D] = b_ada0[5 * D:6 * D]
    rows[0, RO_NF:RO_NF + D] = np.asarray(norm_final, f32)
    rows[0, RO_DEC:RO_DEC + 8] = np.asarray(ret_decay_f, f32)[0]
    rows[0, RO_DEC + 8:RO_DEC + 16] = np.asarray(ret_decay_b, f32)[0]

    in_maps = []
    for b in range(B):
        vecs = np.zeros((128, NVEC), f32)
        vecs[:, VO_C:VO_C + 16:2] = pp(np.asarray(c, f32)[b], 8)
        vecs[:, VO_C + 1:VO_C + 16:2] = pp(np.asarray(c_ctx, f32), 8)
        vecs[:, VO_NMIX:VO_NMIX + 8] = pp(np.asarray(norm_mix, f32)[0], 8)
        vecs[:, VO_NFFN:VO_NFFN + 8] = pp(np.asarray(norm_ffn, f32)[0], 8)
        vecs[:, VO_PSC:VO_PSC + 4] = pp(np.asarray(pool_scale, f32)[0], 4)
        vecs[:, VO_GNW:VO_GNW + 8] = pp(np.asarray(ret_gn_w, f32)[0], 8)
        vecs[:, VO_BADA:VO_BADA + 48] = pp(b_ada0, 48)
        in_maps.append({
            "x": np.ascontiguousarray(x[b]), "ctx": np.ascontiguousarray(np.asarray(ctx, f32)[b]),
            "vecs": vecs, "rows": rows, "consts": consts, "ident": ident,
            "wada": wada, "wu": wu, "wpairs": wpairs, "wtail": wtail, "wpool": wpool, "wo": wo,
            "w13": w13, "w2": w2, "pblk": pblk,
        })
    return in_maps


def kernel(**inputs):
    in_maps = _prep(**inputs)
    pblk, plan = _get_pool()
    nc = build_program(plan, pblk.shape[1])
    B = len(in_maps)
    res = run_bass_kernel_spmd(nc, in_maps, core_ids=list(range(B)))
    out = np.stack([np.asarray(res.results[b]["out"], np.float32) for b in range(B)], 0)
    return out
```

```python
import contextlib
import os
import types
import numpy as np
import ml_dtypes
import concourse.bass as bass
import concourse.mybir as mybir
from concourse.bass_utils import run_bass_kernel_spmd

F32 = mybir.dt.float32
BF16 = mybir.dt.bfloat16
AF = mybir.ActivationFunctionType
ALU = mybir.AluOpType
AX = mybir.AxisListType

D = 1024
L = 2048
NT = 16
LC = 256
C = 128
GRID_W = 64
H = 8
DK = 64
DV = 128
DFF = 2816
NFF = 22
EPS = 1e-6
K_SCALE = DK ** -0.5
POOL_WINDOWS = (2, 4, 8, 16)
FF_GROUPS = ((0, 5), (5, 5), (10, 5), (15, 5), (20, 2))


class Op:
    __slots__ = ("eng", "fn", "reads", "writes", "dma_key", "waits", "signal", "cnt", "n_dma")

    def __init__(self, eng, fn, reads, writes, dma_key, n_dma):
        self.eng = eng
        self.fn = fn
        self.reads = reads
        self.writes = writes
        self.dma_key = dma_key
        self.n_dma = n_dma
        self.waits = []
        self.signal = False
        self.cnt = None


class Sched:
    ENGS = ("pe", "act", "dve", "pool", "sp")

    def __init__(self):
        self.ops = []
        self.last_writer = {}
        self.readers = {}
        self.dma_cum = {}
        self.bank_rd = {}

    def _dep(self, op, d, kind):
        if d is op:
            return
        if d.dma_key is not None:
            op.waits.append(("dma:" + d.dma_key, self.dma_cum[d.dma_key]))
            return
        if d.eng == op.eng and op.dma_key is None:
            if d.eng == "pe" or kind != "RAW":
                return
        d.signal = True
        op.waits.append(("eng:" + d.eng, d))

    @staticmethod
    def _snapshot(fn):
        if fn.__closure__ is None:
            return fn
        cells = []
        for c in fn.__closure__:
            try:
                cells.append(types.CellType(c.cell_contents))
            except ValueError:
                cells.append(c)
        return types.FunctionType(fn.__code__, fn.__globals__, fn.__name__, fn.__defaults__, tuple(cells))

    def fence(self):
        self._nfence = getattr(self, "_nfence", 0) + 1
        key = "__fence%d" % self._nfence
        last = {}
        for o in self.ops:
            last[o.dma_key if o.dma_key is not None else "eng:" + o.eng] = o
        self.readers[key] = list(last.values())
        return key

    @staticmethod
    def _expand(keys):
        out = []
        for k in keys:
            if isinstance(k, str) and len(k) == 3 and k.startswith("ps"):
                out.extend((k + "a", k + "b"))
            else:
                out.append(k)
        return tuple(out)

    def op(self, eng, fn, reads=(), writes=(), dma_key=None, n_dma=1):
        fn = self._snapshot(fn)
        o = Op(eng, fn, self._expand(reads), self._expand(writes), dma_key, n_dma)
        for k in o.reads:
            w = self.last_writer.get(k)
            if w is not None:
                self._dep(o, w, "RAW")
            if isinstance(k, str) and k.startswith("ps") and eng in ("act", "dve"):
                bk = k[:3]
                lr = self.bank_rd.setdefault(bk, {})
                for e2, r in lr.items():
                    if e2 != eng:
                        self._dep(o, r, "XRD")
                lr[eng] = o
        for k in o.writes:
            w = self.last_writer.get(k)
            if w is not None:
                self._dep(o, w, "WAW")
            for r in self.readers.get(k, ()):
                self._dep(o, r, "WAR")
        for k in o.reads:
            self.readers.setdefault(k, []).append(o)
        for k in o.writes:
            self.last_writer[k] = o
            self.readers[k] = []
        if dma_key is not None:
            self.dma_cum[dma_key] = self.dma_cum.get(dma_key, 0) + 16 * n_dma
            o.cnt = self.dma_cum[dma_key]
        self.ops.append(o)
        return o

    def pe(self, fn, reads=(), writes=()):
        return self.op("pe", fn, reads, writes)

    def act(self, fn, reads=(), writes=()):
        return self.op("act", fn, reads, writes)

    def dve(self, fn, reads=(), writes=()):
        return self.op("dve", fn, reads, writes)

    def pool(self, fn, reads=(), writes=()):
        return self.op("pool", fn, reads, writes)

    def pool_or(self, tag, alt, fn, reads=(), writes=()):
        on = os.environ.get("KPOOL", "").split(",")
        return self.op("pool" if tag in on else alt, fn, reads, writes)

    def dma(self, eng, key, fn, reads=(), writes=()):
        return self.op(eng, fn, reads, writes, dma_key=key)

    def emit(self, nc, final_dma_keys=()):
        cnt = {e: 0 for e in self.ENGS}
        for o in self.ops:
            if o.dma_key is None and o.signal:
                cnt[o.eng] += 1
                o.cnt = cnt[o.eng]
        semnames = set()
        for o in self.ops:
            for (s, v) in o.waits:
                semnames.add(s)
            if o.dma_key is not None:
                semnames.add("dma:" + o.dma_key)
            elif o.signal:
                semnames.add("eng:" + o.eng)
        with contextlib.ExitStack() as es:
            sems = {}
            for s in sorted(semnames):
                sems[s] = es.enter_context(nc.semaphore(s.replace(":", "_")))
            block = es.enter_context(nc.Block())
            streams = {e: [o for o in self.ops if o.eng == e] for e in self.ENGS}

            def run(engname, e):
                waited = {}
                for o in streams[engname]:
                    need = {}
                    for (s, v) in o.waits:
                        val = v.cnt if isinstance(v, Op) else v
                        if val > need.get(s, 0):
                            need[s] = val
                    for s, val in need.items():
                        if waited.get(s, 0) >= val:
                            continue
                        e.wait_ge(sems[s], val)
                        waited[s] = val
                    ins = o.fn(e)
                    if o.dma_key is not None:
                        ins.then_inc(sems["dma:" + o.dma_key], 16 * o.n_dma)
                    elif o.signal:
                        ins.then_inc(sems["eng:" + o.eng], 1)
                if engname == "sp":
                    for k in final_dma_keys:
                        e.wait_ge(sems["dma:" + k], self.dma_cum[k])

            @block.tensor
            def _(e):
                run("pe", e)

            @block.scalar
            def _(e):
                run("act", e)

            @block.vector
            def _(e):
                run("dve", e)

            @block.gpsimd
            def _(e):
                run("pool", e)

            @block.sync
            def _(e):
                run("sp", e)


def _box_matrix(n, w):
    pos = np.arange(n)
    lo = np.clip(pos - w // 2, 0, n)
    hi = np.clip(pos + (w - w // 2), 0, n)
    a = np.zeros((n, n), np.float64)
    for t in range(n):
        a[t, lo[t]:hi[t]] = 1.0 / (hi[t] - lo[t])
    return a


def _pool_blocks():
    rows = L // GRID_W
    blocks = []
    seen = {}
    plan = []
    for g, w in enumerate(POOL_WINDOWS):
        a = np.kron(_box_matrix(rows, w), _box_matrix(GRID_W, w)) - np.eye(L)
        at = a.T
        pg = []
        for j in range(4):
            lst = []
            for t in range(NT):
                blk = at[t * 128:(t + 1) * 128, j * 512:(j + 1) * 512]
                if np.any(blk != 0.0):
                    b32 = np.ascontiguousarray(blk.astype(np.float32))
                    key = (g, b32.tobytes())
                    if key not in seen:
                        seen[key] = len(blocks)
                        blocks.append(b32)
                    lst.append((t, seen[key]))
            pg.append(lst)
        plan.append(pg)
    arr = np.stack(blocks, axis=1)
    return np.ascontiguousarray(arr).astype(ml_dtypes.bfloat16), plan


def _rope_tables():
    t = np.arange(L)
    row = (t // GRID_W).astype(np.float32)
    col = (t % GRID_W).astype(np.float32)
    n_freq = DK // 4
    inv_freq = (10000.0 ** (-np.arange(n_freq, dtype=np.float32) / n_freq)).astype(np.float32)
    ang = np.concatenate([row[:, None] * inv_freq, col[:, None] * inv_freq], axis=-1).astype(np.float32)
    cos = np.cos(ang).astype(np.float32).reshape(NT, 128, 32).transpose(1, 0, 2)
    sin = np.sin(ang).astype(np.float32).reshape(NT, 128, 32).transpose(1, 0, 2)
    return np.ascontiguousarray(cos), np.ascontiguousarray(sin)


_POOL_CACHE = None


def _get_pool():
    global _POOL_CACHE
    if _POOL_CACHE is None:
        _POOL_CACHE = _pool_blocks()
    return _POOL_CACHE


CO_COS = 0
CO_SIN = CO_COS + NT * 32
CO_DPOS = CO_SIN + NT * 32
CO_DNEG = CO_DPOS + 128
CO_POSQ = CO_DNEG + 128
CO_SM = CO_POSQ + 128
NCONST = CO_SM + 8

VO_C = 0
VO_NMIX = 16
VO_NFFN = 24
VO_PSC = 32
VO_GNW = 36
VO_BADA = 44
NVEC = VO_BADA + 48

RO_BGM = 0
RO_BGF = 1024
RO_NF = 2048
RO_DEC = 3072
NROW = RO_DEC + 16


def _host_consts():
    cos, sin = _rope_tables()
    c = np.zeros((128, NCONST), np.float32)
    c[:, CO_COS:CO_COS + NT * 32] = cos.reshape(128, -1)
    c[:, CO_SIN:CO_SIN + NT * 32] = sin.reshape(128, -1)
    i = np.arange(128, dtype=np.float32)
    dmat = i[None, :] - i[:, None]
    c[:, CO_DPOS:CO_DPOS + 128] = np.maximum(dmat, 0)
    c[:, CO_DNEG:CO_DNEG + 128] = np.maximum(-dmat, 0)
    c[0:64, CO_POSQ:CO_POSQ + 128] = (i + 1.0)[None, :]
    c[64:128, CO_POSQ:CO_POSQ + 128] = (C - i)[None, :]
    c[:, CO_SM + 0] = C - 1.0 - i
    c[:, CO_SM + 1] = i
    c[:, CO_SM + 2] = LC - 1.0 - i
    c[:, CO_SM + 3] = LC - 1.0 - (i + 128)
    c[:, CO_SM + 4] = i
    c[:, CO_SM + 5] = i + 128
    return c


def build_program(pool_plan, n_pool_blk, stop=None):
    nc = bass.Bass("TRN2", target_bir_lowering=False)
    DBGN = 16384

    def din(name, shape, dt=F32):
        return nc.dram_tensor(name, list(shape), dt, kind="ExternalInput").ap()

    x_d = din("x", [L, D])
    ctx_d = din("ctx", [LC, D])
    vecs_d = din("vecs", [128, NVEC])
    rows_d = din("rows", [1, NROW])
    consts_d = din("consts", [128, NCONST])
    ident_d = din("ident", [128, 128], BF16)
    wada_d = din("wada", [128, 8, 6 * D])
    wu_d = din("wu", [128, 8, 512])
    wpairs_d = din("wpairs", [4, 128, 8, 768])
    wtail_d = din("wtail", [8, 128, 28, 128])
    wpool_d = din("wpool", [128, 4, 128])
    wo_d = din("wo", [8, 128, D])
    w13_d = din("w13", [NFF, 128, 2, 8, 128])
    w2_d = din("w2", [NFF, 128, D])
    pblk_d = din("pblk", [128, n_pool_blk, 512], BF16)
    out_d = nc.dram_tensor("out", [L, D], F32, kind="ExternalOutput").ap()
    dbg_d = nc.dram_tensor("dbg", [128, DBGN], F32, kind="ExternalOutput").ap() if stop is not None else None

    S = Sched()
    SB_BYTES = 207 * 1024
    arena = nc.alloc_sbuf_tensor("arena", [128, SB_BYTES // 2], BF16).ap()
    ps = nc.alloc_psum_tensor("ps", [128, 4096], F32).ap()

    class Region:
        def __init__(self, start, size):
            self.start = start
            self.size = size
            self.off = 0

        def carve(self, nbytes, dt=BF16):
            want = nbytes
            nbytes = (nbytes + 31) // 32 * 32
            assert self.off + nbytes <= self.size, (self.off, nbytes, self.size)
            a = (self.start + self.off) // 2
            self.off += nbytes
            v = arena[:, a:a + want // 2]
            return v if dt == BF16 else v.bitcast(dt)

        def reset(self):
            self.off = 0

    KB = 1024
    R_PERS = Region(0, 32 * KB)
    R_X = Region(32 * KB, 64 * KB)
    R_M = Region(96 * KB, 32 * KB)
    R_PAIR = Region(128 * KB, 63 * KB)
    R_YP = Region(191 * KB, 16 * KB)
    assert 207 * KB <= SB_BYTES

    def bank(b, dt=F32):
        v = ps[:, b * 512:(b + 1) * 512]
        return v if dt == F32 else v.bitcast(dt)

    def PK(b):
        return "ps%d" % b

    def finish_debug(items):
        off = 0
        fk = S.fence()
        for ap, n, in items:
            for c0 in range(0, n, 1024):
                c1 = min(n, c0 + 1024)
                S.dma("pool", "dbg", lambda e, ap=ap, off=off, c0=c0, c1=c1: e.dma_start(
                    out=dbg_d[:, off + c0:off + c1], in_=ap[:, c0:c1]), writes=[fk])
            off += n
        S.dma("sp", "out", lambda e: e.dma_start(out=out_d[0:128, :], in_=gm_bc), writes=[fk])
        S.emit(nc, final_dma_keys=["out", "dbg"])
        return nc

    vecs = R_PERS.carve(NVEC * 4, F32)
    consts = R_PERS.carve(NCONST * 4, F32)
    ident = R_PERS.carve(256)
    decbc = R_PERS.carve(64, F32)
    lgbc = R_PERS.carve(64, F32)
    lgsel = R_PERS.carve(32, F32)
    cdec = R_PERS.carve(32, F32)
    kdec = R_PERS.carve(64, F32)
    ctxw = R_PERS.carve(128, F32)
    modT = R_PERS.carve(48 * 2 * 4, F32)
    modAB = R_PERS.carve(6 * 8 * 4, F32)
    scT = R_PERS.carve(8 * 2 * 2)
    scbc = R_PERS.carve(8 * 128 * 2)
    qdec = R_PERS.carve(8 * 128 * 4, F32)
    mask = R_PERS.carve(8 * 128 * 4, F32)
    gm_bc = R_PERS.carve(D * 4, F32)
    gf_bc = R_PERS.carve(D * 4, F32)
    hcT = R_PERS.carve(8 * LC * 2)
    nf_bc = hcT.bitcast(F32)
    wpool = R_PERS.carve(4 * 128 * 2)
    stat = R_PERS.carve(64 * 4, F32)
    epsb = R_PERS.carve(32, F32)
    gnw128 = R_PERS.carve(32, F32)
    sfp = R_PERS.carve(384 * 4, F32)

    modT3 = modT.rearrange("p (j t) -> p j t", t=2)
    modAB3 = modAB.rearrange("p (a k) -> p a k", k=8)
    scT3 = scT.rearrange("p (k t) -> p k t", t=2)
    scbc3 = scbc.rearrange("p (k m) -> p k m", m=128)
    qdec3 = qdec.rearrange("p (h t) -> p h t", t=128)
    mask3 = mask.rearrange("p (h t) -> p h t", t=128)
    hcT3 = hcT.rearrange("p (k t) -> p k t", t=LC)
    wpool3 = wpool.rearrange("p (g n) -> p g n", n=128)
    kdec3 = kdec.rearrange("p (d h) -> p d h", h=8)
    ctxw4 = ctxw.rearrange("p (t d h) -> p t d h", d=2, h=8)
    cos3 = consts[:, CO_COS:CO_COS + NT * 32].rearrange("p (i f) -> p i f", f=32)
    sin3 = consts[:, CO_SIN:CO_SIN + NT * 32].rearrange("p (i f) -> p i f", f=32)
    dpos = consts[:, CO_DPOS:CO_DPOS + 128]
    dneg = consts[:, CO_DNEG:CO_DNEG + 128]
    posq = consts[:, CO_POSQ:CO_POSQ + 128]

    def csm(i):
        return consts[:, CO_SM + i:CO_SM + i + 1]

    A_M, B_M, A_C, B_C, A_F, B_F = range(6)

    hT = R_X.carve(32 * KB).rearrange("p (k t) -> p k t", t=L)
    ygT = R_X.carve(32 * KB).rearrange("p (k t) -> p k t", t=L)
    R_X.reset()
    x1 = R_X.carve(64 * KB, F32).rearrange("p (i f) -> p i f", f=D)
    mT = R_M.carve(32 * KB).rearrange("p (k t) -> p k t", t=L)
    R_M.reset()
    _ux = R_X.start + 32 * KB
    u_tok = arena[:, _ux // 2:(_ux + 16 * KB) // 2].rearrange("p (i c) -> p i c", c=512)
    dT_sb = [arena[:, (_ux + (16 + i) * KB) // 2:(_ux + (17 + i) * KB) // 2] for i in range(2)]
    R_M.reset()
    wpair_buf = [R_M.carve(12 * KB).rearrange("p (k n) -> p k n", n=768) for _ in range(2)]
    wada_buf = [arena[:, (R_M.start + i * 8 * KB) // 2:(R_M.start + (i + 1) * 8 * KB) // 2].rearrange("p (k n) -> p k n", n=512)
                for i in range(4)]
    wada_buf += [arena[:, (R_YP.start + i * 8 * KB) // 2:(R_YP.start + (i + 1) * 8 * KB) // 2].rearrange("p (k n) -> p k n", n=512)
                 for i in range(2)]
    ypT = R_YP.carve(16 * KB).rearrange("p (g t) -> p g t", t=L)

    S.dma("sp", "c0", lambda e: e.dma_start(out=vecs, in_=vecs_d), writes=["vecs"])
    S.dma("sp", "c1", lambda e: e.dma_start(out=consts, in_=consts_d), writes=["consts"])
    S.dma("sp", "c2", lambda e: e.dma_start(out=ident, in_=ident_d), writes=["ident"])
    S.dma("sp", "c3", lambda e: e.dma_start(out=decbc, in_=rows_d[0:1, RO_DEC:RO_DEC + 16].to_broadcast([128, 16])),
          writes=["decbc"])
    S.dma("sp", "c4", lambda e: e.dma_start(out=gm_bc, in_=rows_d[0:1, RO_BGM:RO_BGM + D].to_broadcast([128, D])),
          writes=["gm_bc"])
    S.dma("sp", "c5", lambda e: e.dma_start(out=gf_bc, in_=rows_d[0:1, RO_BGF:RO_BGF + D].to_broadcast([128, D])),
          writes=["gf_bc"])
    S.dma("pool", "wpool", lambda e: e.dma_start(out=wpool3, in_=wpool_d), writes=["wpool"])

    S.dve(lambda e: e.memset(epsb, float(DV * DV) * EPS), writes=["epsb"])
    S.dve(lambda e: e.tensor_scalar(out=gnw128, in0=vecs[:, VO_GNW:VO_GNW + 8], scalar1=float(DV), scalar2=None, op0=ALU.mult),
          reads=["vecs"], writes=["gnw128"])
    cv3 = vecs[:, VO_C:VO_C + 16].rearrange("p (k t) -> p k t", t=2)
    S.act(lambda e: e.activation(out=scT3, in_=cv3, func=AF.Silu), reads=["vecs"], writes=["scT"])
    S.act(lambda e: e.activation(out=scbc3, in_=cv3[:, :, 0:1].to_broadcast([128, 8, 128]), func=AF.Silu),
          reads=["vecs"], writes=["scbc"])

    def wada_group(gi):
        buf = wada_buf[gi % 6]
        bk = "wada%d" % (gi % 6)
        S.dma("pool", bk, lambda e, buf=buf, gi=gi: e.dma_start(out=buf, in_=wada_d[:, :, gi * 512:(gi + 1) * 512]),
              writes=[bk])
        return buf, bk

    def wada_compute(gi, buf, bk):
        if gi in (4, 5, 10, 11):
            dst = gm_bc if gi in (4, 5) else gf_bc
            dk = "gm_bc" if gi in (4, 5) else "gf_bc"
            half = gi % 2 if gi in (4, 5) else (gi - 10)
            pb = 5 + (gi % 2)
            for k in range(8):
                S.pe(lambda e, k=k, buf=buf, pb=pb: e.matmul(bank(pb), lhsT=scbc3[:, k, :], rhs=buf[:, k, :],
                                                              start=(k == 0), stop=(k == 7)),
                     reads=[bk, "scbc"], writes=[PK(pb)])
            S.dve(lambda e, dst=dst, half=half, pb=pb: e.tensor_tensor(
                out=dst[:, half * 512:(half + 1) * 512], in0=bank(pb), in1=dst[:, half * 512:(half + 1) * 512], op=ALU.add),
                reads=[PK(pb), dk], writes=[dk])
        else:
            for jj in range(4):
                j = gi * 4 + jj
                for k in range(8):
                    S.pe(lambda e, k=k, jj=jj, j=j, buf=buf: e.matmul(
                        bank(7)[:, 2 * j:2 * j + 2], lhsT=buf[:, k, jj * 128:(jj + 1) * 128], rhs=scT3[:, k, :],
                        start=(k == 0), stop=(k == 7)),
                        reads=[bk, "scT"], writes=[PK(7)])

    def modT_evac(j0, j1):
        S.dve(lambda e, j0=j0, j1=j1: e.tensor_tensor(
            out=modT3[:, j0:j1, :], in0=bank(7)[:, 2 * j0:2 * j1].rearrange("p (j t) -> p j t", t=2),
            in1=vecs[:, VO_BADA + j0:VO_BADA + j1].unsqueeze(2).to_broadcast([128, j1 - j0, 2]), op=ALU.add),
            reads=[PK(7), "vecs"], writes=["modT"])

    def mod_ab(ai, bi, nvo, sh_blk, sc_blk, col):
        S.dve(lambda e: e.scalar_tensor_tensor(out=modAB3[:, ai, :], in0=modT3[:, sc_blk:sc_blk + 8, col], scalar=1.0,
                                               in1=vecs[:, nvo:nvo + 8], op0=ALU.add, op1=ALU.mult),
              reads=["modT", "vecs"], writes=["modAB%d" % ai])
        S.dve(lambda e: e.tensor_copy(out=modAB3[:, bi, :], in_=modT3[:, sh_blk:sh_blk + 8, col]),
              reads=["modT"], writes=["modAB%d" % bi])

    wg = [wada_group(gi) for gi in range(4)]

    def setup_mod_mix():
        for gi in range(4):
            wada_compute(gi, *wg[gi])
        modT_evac(0, 16)
        mod_ab(A_M, B_M, VO_NMIX, 0, 8, 0)
        mod_ab(A_C, B_C, VO_NMIX, 0, 8, 1)

    if stop == 0:
        setup_mod_mix()

    ee = stat[:, 0:16]
    tt = stat[:, 16:32]
    S.act(lambda e: e.activation(out=ee, in_=decbc, func=AF.Exp, scale=-1.0), reads=["decbc"], writes=["ee"])
    S.dve(lambda e: e.tensor_scalar(out=tt, in0=ee, scalar1=-1.0 / 7, scalar2=1.0 / 6, op0=ALU.mult, op1=ALU.add),
          reads=["ee"], writes=["tt"])
    for cf in (1.0 / 5, 1.0 / 4, 1.0 / 3, 1.0 / 2, 1.0):
        S.dve(lambda e: e.tensor_tensor(out=tt, in0=tt, in1=ee, op=ALU.mult), reads=["tt", "ee"], writes=["tt"])
        S.dve(lambda e, cf=cf: e.tensor_scalar(out=tt, in0=tt, scalar1=-1.0, scalar2=cf, op0=ALU.mult, op1=ALU.add),
              reads=["tt"], writes=["tt"])
    S.dve(lambda e: e.scalar_tensor_tensor(out=lgbc, in0=tt, scalar=-1.0, in1=ee, op0=ALU.mult, op1=ALU.mult),
          reads=["tt", "ee"], writes=["lgbc"])
    S.dve(lambda e: e.tensor_copy(out=lgsel[0:64, :], in_=lgbc[0:64, 0:8]), reads=["lgbc"], writes=["lgsel"])
    S.dve(lambda e: e.tensor_copy(out=lgsel[64:128, :], in_=lgbc[64:128, 8:16]), reads=["lgbc"], writes=["lgsel"])
    S.act(lambda e: e.activation(out=cdec, in_=lgsel, func=AF.Exp, scale=float(C)), reads=["lgsel"], writes=["cdec"])
    for d in range(2):
        S.act(lambda e, d=d: e.activation(out=kdec3[:, d, :], in_=lgbc[:, d * 8:(d + 1) * 8], func=AF.Exp, scale=csm(d)),
              reads=["lgbc", "consts"], writes=["kdec"])
        for t in range(2):
            S.act(lambda e, d=d, t=t: e.activation(out=ctxw4[:, t, d, :], in_=lgbc[:, d * 8:(d + 1) * 8], func=AF.Exp,
                                                   scale=csm(2 + 2 * d + t)),
                  reads=["lgbc", "consts"], writes=["ctxw"])
    S.dve(lambda e: e.tensor_scalar(out=kdec, in0=kdec, scalar1=K_SCALE, scalar2=None, op0=ALU.mult),
          reads=["kdec"], writes=["kdec"])
    S.dve(lambda e: e.tensor_scalar(out=ctxw, in0=ctxw, scalar1=K_SCALE, scalar2=None, op0=ALU.mult),
          reads=["ctxw"], writes=["ctxw"])
    for h in range(H):
        S.act(lambda e, h=h: e.activation(out=qdec3[:, h, :], in_=posq, func=AF.Exp, scale=lgsel[:, h:h + 1]),
              reads=["lgsel", "consts"], writes=["qdec"])
        S.dve(lambda e, h=h: e.tensor_scalar(out=mask3[:, h, :], in0=dpos, scalar1=lgbc[:, h:h + 1], scalar2=None,
                                             op0=ALU.mult), reads=["lgbc", "consts"], writes=["mask"])
        S.dve(lambda e, h=h: e.scalar_tensor_tensor(out=mask3[:, h, :], in0=dneg, scalar=lgbc[:, 8 + h:9 + h],
                                                    in1=mask3[:, h, :], op0=ALU.mult, op1=ALU.add),
              reads=["lgbc", "consts", "mask"], writes=["mask"])
    S.act(lambda e: e.activation(out=mask, in_=mask, func=AF.Exp), reads=["mask"], writes=["mask"])
    S.dve(lambda e: e.tensor_scalar(out=mask, in0=mask, scalar1=K_SCALE, scalar2=None, op0=ALU.mult),
          reads=["mask"], writes=["mask"])

    if stop == 0:
        for gi in range(4, 12):
            wg.append(wada_group(gi))
            wada_compute(gi, *wg[gi])
        modT_evac(24, 40)
        mod_ab(A_F, B_F, VO_NFFN, 24, 32, 0)
        return finish_debug([(modAB, 48), (lgbc, 16), (mask, 1024), (qdec, 1024), (gm_bc, 1024), (gf_bc, 1024),
                             (kdec, 16), (ctxw, 32), (cdec, 8)])
    R_PAIR.reset()
    xbuf = [R_PAIR.carve(4 * KB, F32) for _ in range(3)]
    junk = R_PAIR.carve(2 * KB)
    xnb = [R_PAIR.carve(2 * KB) for _ in range(2)]
    nctr = [0]

    def norm_stats(src_ap, src_key, junk, xnb):
        n = nctr[0]
        nctr[0] += 1
        ss = stat[:, 32 + (n % 4) * 2:33 + (n % 4) * 2]
        rs = stat[:, 33 + (n % 4) * 2:34 + (n % 4) * 2]
        sk = "nstat%d" % (n % 4)
        xn = xnb[n % len(xnb)]
        xk = "xn%d_%d" % (len(xnb), n % len(xnb))
        S.act(lambda e: e.activation(out=junk, in_=src_ap, func=AF.Square, accum_out=ss),
              reads=[src_key], writes=["junk", sk])
        S.dve(lambda e: e.tensor_scalar(out=rs, in0=ss, scalar1=1.0 / D, scalar2=EPS, op0=ALU.mult, op1=ALU.add),
              reads=[sk], writes=[sk + "r"])
        S.act(lambda e: e.activation(out=rs, in_=rs, func=AF.Sqrt), reads=[sk + "r"], writes=[sk + "r"])
        S.dve(lambda e: e.reciprocal(out=rs, in_=rs), reads=[sk + "r"], writes=[sk + "r"])
        S.dve(lambda e: e.tensor_scalar(out=xn, in0=src_ap, scalar1=rs, scalar2=None, op0=ALU.mult),
              reads=[src_key, sk + "r"], writes=[xk])
        return (n, xn, xk)

    ACT_K = (1, 4, 6)

    def norm_tr(st, ai, bi, dst3, dst_key, tcol, pbase=0, fused=True, ACT_K=ACT_K):
        n, xn, xk = st
        par = n % 2
        pbD, pbA = pbase + 2 * par, pbase + 2 * par + 1
        pTD = bank(pbD, BF16)[:, 0:768].rearrange("p (k t) -> p k t", t=128)
        pTA = bank(pbA, BF16)[:, 0:384].rearrange("p (k t) -> p k t", t=128)
        slot = {}
        na = nd = 0
        for k in range(8):
            if k in ACT_K:
                slot[k] = (pTA, pbA, na)
                na += 1
            else:
                slot[k] = (pTD, pbD, nd)
                nd += 1
        for k in range(8):
            pt_, pbk, sl = slot[k]
            S.pe(lambda e, k=k, pt_=pt_, sl=sl: e.transpose(out=pt_[:, sl, :], in_=xn[:, k * 128:(k + 1) * 128], identity=ident),
                 reads=[xk, "ident"], writes=[PK(pbk)])
        for k in range(8):
            o = dst3[:, k, tcol * 128:(tcol + 1) * 128]
            pt_, pbk, sl = slot[k]
            if not fused:
                if k not in ACT_K:
                    S.dve(lambda e, o=o, pt_=pt_, sl=sl: e.tensor_copy(out=o, in_=pt_[:, sl, :]),
                          reads=[PK(pbk)], writes=[dst_key(k, tcol)])
                else:
                    S.act(lambda e, o=o, pt_=pt_, sl=sl: e.activation(out=o, in_=pt_[:, sl, :], func=AF.Copy),
                          reads=[PK(pbk)], writes=[dst_key(k, tcol)])
            elif k not in ACT_K:
                S.dve(lambda e, k=k, o=o, pt_=pt_, sl=sl: e.tensor_scalar(out=o, in0=pt_[:, sl, :], scalar1=modAB3[:, ai, k:k + 1],
                                                                          scalar2=modAB3[:, bi, k:k + 1], op0=ALU.mult, op1=ALU.add),
                      reads=[PK(pbk), "modAB%d" % ai, "modAB%d" % bi], writes=[dst_key(k, tcol)])
            else:
                S.act(lambda e, k=k, o=o, pt_=pt_, sl=sl: e.activation(out=o, in_=pt_[:, sl, :], func=AF.Identity,
                                                                       scale=modAB3[:, ai, k:k + 1], bias=modAB3[:, bi, k:k + 1]),
                      reads=[PK(pbk), "modAB%d" % ai, "modAB%d" % bi], writes=[dst_key(k, tcol)])

    def hT_key(k, i):
        return "XA_%d_%d" % (k, i)

    pendA = []
    xnb = xnb + [arena[:, R_YP.start // 2:(R_YP.start + 2 * KB) // 2]]
    xbuf = xbuf + [arena[:, (R_YP.start + (4 + 4 * i) * KB) // 2:(R_YP.start + (8 + 4 * i) * KB) // 2].bitcast(F32) for i in range(3)]
    for t in range(2 + NT):
        sbi = t % len(xbuf)
        xb = xbuf[sbi]
        if t < 2:
            S.dma("sp", "xb%d" % sbi, lambda e, xb=xb, t=t: e.dma_start(out=xb, in_=ctx_d[t * 128:(t + 1) * 128, :]),
                  writes=["xb%d" % sbi])
            args = (A_C, B_C, hcT3, (lambda k, tc: "hcT%d" % k), t)
        else:
            i = t - 2
            S.dma("sp", "xb%d" % sbi, lambda e, xb=xb, i=i: e.dma_start(out=xb, in_=x_d[i * 128:(i + 1) * 128, :]),
                  writes=["xb%d" % sbi])
            args = (A_M, B_M, hT, hT_key, i)
        st = norm_stats(xb, "xb%d" % sbi, junk, xnb)
        pendA.append((st,) + args)
        if len(pendA) > 2:
            norm_tr(*pendA.pop(0), fused=False, ACT_K=(2, 5))
    while pendA:
        norm_tr(*pendA.pop(0), fused=False, ACT_K=(2, 5))
    setup_mod_mix()
    for hf in range(2):
        for k in range(8):
            S.dve(lambda e, k=k, hf=hf: e.tensor_scalar(out=hT[:, k, hf * 1024:(hf + 1) * 1024], in0=hT[:, k, hf * 1024:(hf + 1) * 1024],
                                                        scalar1=modAB3[:, A_M, k:k + 1], scalar2=modAB3[:, B_M, k:k + 1],
                                                        op0=ALU.mult, op1=ALU.add),
                  reads=[hT_key(k, i) for i in range(hf * 8, hf * 8 + 8)] + ["modAB%d" % A_M, "modAB%d" % B_M],
                  writes=[hT_key(k, i) for i in range(hf * 8, hf * 8 + 8)])
    for k in range(8):
        S.dve(lambda e, k=k: e.tensor_scalar(out=hcT3[:, k, :], in0=hcT3[:, k, :], scalar1=modAB3[:, A_C, k:k + 1],
                                             scalar2=modAB3[:, B_C, k:k + 1], op0=ALU.mult, op1=ALU.add),
              reads=["hcT%d" % k, "modAB%d" % A_C, "modAB%d" % B_C], writes=["hcT%d" % k])
    wlate = arena[:, (R_M.start + 24 * KB) // 2:(R_M.start + 32 * KB) // 2].rearrange("p (k n) -> p k n", n=512)
    WL_ALIAS = ["ysb0", "ysb1", "ysb2", "ysq"]

    def wada_late_load(gi):
        S.dma("pool", "wlate", lambda e: e.dma_start(out=wlate, in_=wada_d[:, :, gi * 512:(gi + 1) * 512]),
              writes=["wlate"] + WL_ALIAS)

    def wada_late_compute(gi):
        rd = ["wlate"] + WL_ALIAS
        if gi in (4, 5, 10, 11):
            dst = gm_bc if gi in (4, 5) else gf_bc
            dk = "gm_bc" if gi in (4, 5) else "gf_bc"
            half = gi % 2 if gi in (4, 5) else (gi - 10)
            for k in range(8):
                S.pe(lambda e, k=k: e.matmul(bank(7), lhsT=scbc3[:, k, :], rhs=wlate[:, k, :], start=(k == 0), stop=(k == 7)),
                     reads=rd + ["scbc"], writes=[PK(7)])
            S.dve(lambda e: e.tensor_tensor(out=dst[:, half * 512:(half + 1) * 512], in0=bank(7),
                                            in1=dst[:, half * 512:(half + 1) * 512], op=ALU.add),
                  reads=[PK(7), dk], writes=[dk])
        else:
            for jj in range(4):
                j = gi * 4 + jj
                for k in range(8):
                    S.pe(lambda e, k=k, jj=jj, j=j: e.matmul(bank(7)[:, 2 * j:2 * j + 2], lhsT=wlate[:, k, jj * 128:(jj + 1) * 128],
                                                             rhs=scT3[:, k, :], start=(k == 0), stop=(k == 7)),
                         reads=rd + ["scT"], writes=[PK(7)])
            modT_evac(gi * 4, gi * 4 + 4)
    hcT_keys = ["hcT%d" % k for k in range(8)]

    if stop == 1:
        return finish_debug([(hcT, 2048), (hT.rearrange("p k t -> p (k t)")[:, 0:8192], 8192)])

    def load_pair(p):
        buf = wpair_buf[p % 2]
        extra = [S.fence()] if p < 2 else []
        S.dma("pool", "wpair%d" % (p % 2), lambda e, buf=buf, p=p: e.dma_start(out=buf, in_=wpairs_d[p]),
              writes=["wpair%d" % (p % 2)] + extra)

    wu_sb = R_PAIR.carve(8 * KB).rearrange("p (k n) -> p k n", n=512)
    NPST = 9
    pstage = [R_PAIR.carve(4 * KB).rearrange("p (b t) -> p b t", t=512) for _ in range(NPST)]
    S.dma("pool", "wu", lambda e: e.dma_start(out=wu_sb, in_=wu_d), writes=["wu"])
    load_pair(0)
    for i in range(NT):
        pb = 2 + (i % 2)
        for k in range(8):
            S.pe(lambda e, k=k, i=i, pb=pb: e.matmul(bank(pb), lhsT=hT[:, k, i * 128:(i + 1) * 128], rhs=wu_sb[:, k, :],
                                                     start=(k == 0), stop=(k == 7)),
                 reads=[hT_key(k, i), "wu"], writes=[PK(pb)])
        if i % 2 == 0:
            S.act(lambda e, i=i, pb=pb: e.activation(out=u_tok[:, i, :], in_=bank(pb), func=AF.Copy),
                  reads=[PK(pb)], writes=["u%d" % i])
        else:
            S.dve(lambda e, i=i, pb=pb: e.tensor_copy(out=u_tok[:, i, :], in_=bank(pb)),
                  reads=[PK(pb)], writes=["u%d" % i])
    pst_n = [0]
    resident = {}

    def pool_head(pctr, g, j):
        lst = pool_plan[g][j]
        pd = 4 + (pctr % 2)
        dsb = dT_sb[pctr % 2]
        dk = "dT%d" % (pctr % 2)
        runs = []
        for ent in lst:
            if runs and len(runs[-1]) < 4 and runs[-1][-1][1] + 1 == ent[1]:
                runs[-1].append(ent)
            else:
                runs.append([ent])
        pos = 0
        for grp in runs:
            b0 = grp[0][1]
            nb = len(grp)
            assert all(grp[q][1] == b0 + q for q in range(nb))
            hit = resident.get((b0, nb))
            if hit is not None and pst_n[0] - hit < NPST:
                st = pstage[hit % NPST]
                sk = "pst%d" % (hit % NPST)
            else:
                st = pstage[pst_n[0] % NPST]
                sk = "pst%d" % (pst_n[0] % NPST)
                resident[(b0, nb)] = pst_n[0]
                pst_n[0] += 1
                S.dma("sp", sk, lambda e: e.dma_start(out=st[:, 0:nb, :], in_=pblk_d[:, b0:b0 + nb, :]), writes=[sk])
            for q, (t, bi_) in enumerate(grp):
                first = (pos == 0)
                last = (pos == len(lst) - 1)
                pos += 1
                S.pe(lambda e, t=t, q=q, first=first, last=last: e.matmul(
                    bank(pd), lhsT=u_tok[:, t, g * 128:(g + 1) * 128], rhs=st[:, q, :], start=first, stop=last),
                    reads=["u%d" % t, sk], writes=[PK(pd)])
        S.dve(lambda e: e.tensor_copy(out=dsb, in_=bank(pd)), reads=[PK(pd)], writes=[dk])

    def pool_tail(pctr, g, j):
        py = 6 + (pctr % 2)
        dsb = dT_sb[pctr % 2]
        dk = "dT%d" % (pctr % 2)
        S.pe(lambda e: e.matmul(bank(py), lhsT=wpool3[:, g, :], rhs=dsb, start=True, stop=True),
             reads=[dk, "wpool"], writes=[PK(py)])
        S.act(lambda e: e.activation(out=ypT[:, g, j * 512:(j + 1) * 512], in_=bank(py), func=AF.Copy,
                                     scale=vecs[:, VO_PSC + g:VO_PSC + g + 1]),
              reads=[PK(py), "vecs"], writes=["ypT"])

    pblocks = [(g, j) for g in range(4) for j in range(4)]
    pool_head(0, *pblocks[0])
    for c_ in range(len(pblocks)):
        if c_ + 1 < len(pblocks):
            pool_head(c_ + 1, *pblocks[c_ + 1])
        pool_tail(c_, *pblocks[c_])

    if stop == 2:
        return finish_debug([(ypT.rearrange("p g t -> p (g t)"), 8192)])
    R_PAIR.reset()
    kT2 = R_PAIR.carve(4 * KB)
    qTz = R_PAIR.carve(8 * KB).rearrange("p (n h t) -> p n h t", h=2, t=128)
    zf = S.fence()
    S.dve(lambda e: e.memset(qTz[64:128, :, 0, :], 0.0), writes=["qTz_zero", zf])
    S.dve(lambda e: e.memset(qTz[0:64, :, 1, :], 0.0), writes=["qTz_zero", zf])
    q2 = R_PAIR.carve(8 * KB).rearrange("p (h t) -> p h t", t=L)
    kfb = R_PAIR.carve(8 * KB).rearrange("p (i d c) -> p i d c", d=2, c=128)
    v_tok = R_PAIR.carve(8 * KB).rearrange("p (i c) -> p i c", c=256)
    sg_tok = R_PAIR.carve(8 * KB).rearrange("p (i c) -> p i c", c=256)
    s_all = R_PAIR.carve(8 * KB).rearrange("p (n h v) -> p n h v", h=2, v=128)
    rot = [R_PAIR.carve(1 * KB).rearrange("p (t a c) -> p t a c", t=2, c=64) for _ in range(2)]
    qdup = R_PAIR.carve(1 * KB).rearrange("p (t a u c) -> p t a u c", t=2, u=2, c=64)
    _rt0 = R_PAIR.off
    rtmp = [R_PAIR.carve(1 * KB, F32).rearrange("p (t a f) -> p t a f", t=2, f=32) for _ in range(4)]
    _rt1 = R_PAIR.off
    R_PAIR.off = _rt0
    kcw = R_PAIR.carve(1 * KB).rearrange("p (t d c) -> p t d c", d=2, c=128)
    vc = R_PAIR.carve(1 * KB).rearrange("p (t c) -> p t c", c=256)
    R_PAIR.off = _rt1
    pmat = [R_PAIR.carve(1 * KB).rearrange("p (c h t) -> p c h t", c=2, t=128) for _ in range(2)]
    ygb = [R_PAIR.carve(1 * KB).rearrange("p (c h v) -> p c h v", c=2, v=128) for _ in range(2)]
    _rm = R_M.start + 24 * KB
    y_sb = [arena[:, (_rm + i * 2 * KB) // 2:(_rm + (i + 1) * 2 * KB) // 2].bitcast(F32).rearrange("p (c h v) -> p c h v", c=2, v=128)
            for i in range(3)]
    ysq = arena[:, (_rm + 6 * KB) // 2:(_rm + 8 * KB) // 2].bitcast(F32).rearrange("p (c h v) -> p c h v", c=2, v=128)
    sfp3 = sfp[:, 0:256].rearrange("p (h v) -> p h v", v=128)
    gst = sfp[:, 256:384]

    def ygT_key(k, i):
        return "XB_%d_%d" % (k, i)

    def make_pair(p):
        wp = wpair_buf[p % 2]
        wk = "wpair%d" % (p % 2)

        def ctx_pe():
            if 1 <= p + 1 < 4:
                load_pair(p + 1)
            for t in range(2):
                pb = 6 + t
                for k in range(8):
                    S.pe(lambda e, k=k, t=t, pb=pb: e.matmul(bank(pb)[:, 0:384], lhsT=hcT3[:, k, t * 128:(t + 1) * 128],
                                                             rhs=wp[:, k, 128:512], start=(k == 0), stop=(k == 7)),
                         reads=["hcT%d" % k, wk], writes=[PK(pb)])

        def ctx_rest():
            for t in range(2):
                pb = 6 + t
                for d in range(2):
                    S.dve(lambda e, t=t, d=d, pb=pb: e.tensor_tensor(
                        out=kcw[:, t, d, :].rearrange("p (h c) -> p h c", c=64),
                        in0=bank(pb)[:, 0:128].rearrange("p (h c) -> p h c", c=64),
                        in1=ctxw4[:, t, d, 2 * p:2 * p + 2].unsqueeze(2).to_broadcast([128, 2, 64]), op=ALU.mult),
                        reads=[PK(pb), "ctxw"], writes=["kcw"])
                S.dve(lambda e, t=t, pb=pb: e.tensor_copy(out=vc[:, t, :], in_=bank(pb)[:, 128:384]),
                      reads=[PK(pb)], writes=["vc"])
            pS = bank(1)[:, 0:256].rearrange("p (h v) -> p h v", v=128)
            for h2 in range(2):
                for d in range(2):
                    for t in range(2):
                        S.pe(lambda e, h2=h2, d=d, t=t: e.matmul(pS[d * 64:(d + 1) * 64, h2, :],
                                                                 lhsT=kcw[:, t, d, h2 * 64:(h2 + 1) * 64],
                                                                 rhs=vc[:, t, h2 * 128:(h2 + 1) * 128],
                                                                 start=(t == 0), stop=(t == 1)),
                             reads=["kcw", "vc"], writes=["ps1a", "ps1b"])
            S.dve(lambda e: e.tensor_copy(out=sfp3, in_=pS), reads=["ps1a", "ps1b"], writes=["sfp"])
            S.dve(lambda e: e.tensor_copy(out=s_all[0:64, 0, :, :], in_=pS[0:64, :, :]),
                  reads=["ps1a", "ps1b"], writes=["sall_f0"])
            S.dve(lambda e: e.tensor_copy(out=s_all[64:128, NT - 1, :, :], in_=pS[64:128, :, :]),
                  reads=["ps1a", "ps1b"], writes=["sall_b%d" % (NT - 1)])


        def sb_proj(it, i, pe_only=False):
            par = it % 2
            pq = 2 + par
            for t in range(2):
                for k in range(8):
                    S.pe(lambda e, k=k, t=t: e.matmul(bank(pq)[:, t * 256:(t + 1) * 256], lhsT=hT[:, k, (i + t) * 128:(i + t + 1) * 128],
                                                      rhs=wp[:, k, 0:256], start=(k == 0), stop=(k == 7)),
                         reads=[hT_key(k, i + t), wk], writes=[PK(pq)])
            for t in range(2):
                for k in range(8):
                    S.pe(lambda e, k=k, t=t: e.matmul(bank(4 + t), lhsT=hT[:, k, (i + t) * 128:(i + t + 1) * 128],
                                                      rhs=wp[:, k, 256:768], start=(k == 0), stop=(k == 7)),
                         reads=[hT_key(k, i + t), wk], writes=[PK(4 + t)])
            if not pe_only:
                sb_proj_evac(it, i)

        def sb_proj_evac(it, i):
            vg = ps[:, 4 * 512:6 * 512].rearrange("p (t c) -> p t c", t=2)
            S.act(lambda e: e.activation(out=v_tok[:, i:i + 2, :], in_=vg[:, :, 0:256], func=AF.Copy),
                  reads=[PK(4), PK(5)], writes=["v%d" % i, "v%d" % (i + 1)])
            S.act(lambda e: e.activation(out=sg_tok[:, i:i + 2, :], in_=vg[:, :, 256:512], func=AF.Silu),
                  reads=[PK(4), PK(5)], writes=["sg%d" % i, "sg%d" % (i + 1)])

        def sb_rope(it, i):
            par = it % 2
            pq = 2 + par
            qk4 = bank(pq).rearrange("p (t a c) -> p t a c", t=2, c=64)
            t1 = qk4[:, :, :, 0:32]
            t2 = qk4[:, :, :, 32:64]
            cs = cos3[:, i:i + 2, :].unsqueeze(2).to_broadcast([128, 2, 4, 32])
            sn = sin3[:, i:i + 2, :].unsqueeze(2).to_broadcast([128, 2, 4, 32])
            rt = rot[par]
            rk = "rot%d" % par
            ta, tb_, tc_, td = rtmp
            S.dve(lambda e: e.tensor_tensor(out=ta, in0=t1, in1=cs, op=ALU.mult), reads=[PK(pq), "consts"], writes=["rta", "kcw", "vc"])
            S.dve(lambda e: e.tensor_tensor(out=tb_, in0=t2, in1=sn, op=ALU.mult), reads=[PK(pq), "consts"], writes=["rtb", "kcw", "vc"])
            S.dve(lambda e: e.tensor_tensor(out=tc_, in0=t1, in1=sn, op=ALU.mult), reads=[PK(pq), "consts"], writes=["rtc", "kcw", "vc"])
            S.dve(lambda e: e.tensor_tensor(out=td, in0=t2, in1=cs, op=ALU.mult), reads=[PK(pq), "consts"], writes=["rtd", "kcw", "vc"])
            S.dve(lambda e: e.tensor_tensor(out=rt[:, :, :, 0:32], in0=ta, in1=tb_, op=ALU.subtract),
                  reads=["rta", "rtb"], writes=[rk])
            S.dve(lambda e: e.tensor_tensor(out=rt[:, :, :, 32:64], in0=tc_, in1=td, op=ALU.add),
                  reads=["rtc", "rtd"], writes=[rk])
            for t in range(2):
                S.act(lambda e, t=t: e.activation(out=qdup[:, t, :, :, :], in_=rt[:, t, 0:2, :].unsqueeze(2).to_broadcast([128, 2, 2, 64]),
                                                  func=AF.Copy), reads=[rk], writes=["qdup"])
            for d in range(2):
                S.dve(lambda e, d=d: e.tensor_tensor(
                    out=kfb[:, i:i + 2, d, :].rearrange("p t (h c) -> p t h c", c=64), in0=rt[:, :, 2:4, :],
                    in1=kdec3[:, d, 2 * p:2 * p + 2].unsqueeze(1).unsqueeze(3).to_broadcast([128, 2, 2, 64]), op=ALU.mult),
                    reads=[rk, "kdec"], writes=["kfb%d" % i, "kfb%d" % (i + 1)])

        def sb_tr(it, i):
            par = it % 2
            rt = rot[par]
            rk = "rot%d" % par
            sfx = "ab"[par]
            pTA = bank(0, BF16)[:, par * 512:(par + 1) * 512].rearrange("p (t a x) -> p t a x", t=2, x=128)
            pTD = bank(1, BF16)[:, par * 512:(par + 1) * 512].rearrange("p (t h x) -> p t h x", t=2, x=128)
            for t in range(2):
                S.pe(lambda e, t=t: e.transpose(out=pTA[:, t, 0, :], in_=rt[:, t, 0:2, :].rearrange("p a c -> p (a c)"), identity=ident),
                     reads=[rk, "ident"], writes=["ps0" + sfx])
                S.pe(lambda e, t=t: e.transpose(out=pTA[:, t, 1, :], in_=rt[:, t, 2:4, :].rearrange("p a c -> p (a c)"), identity=ident),
                     reads=[rk, "ident"], writes=["ps0" + sfx])
                for h2 in range(2):
                    S.pe(lambda e, t=t, h2=h2: e.transpose(out=pTD[:, t, h2, :], in_=qdup[:, t, h2, :, :].rearrange("p u c -> p (u c)"),
                                                           identity=ident),
                         reads=["qdup", "ident"], writes=["ps1" + sfx])
            qk_keys = ["qkT%d" % i, "qkT%d" % (i + 1)]
            S.act(lambda e: e.activation(out=kT2[:, i * 128:(i + 2) * 128].rearrange("p (t x) -> p t x", t=2), in_=pTA[:, :, 1, :], func=AF.Copy),
                  reads=["ps0" + sfx], writes=qk_keys)
            S.act(lambda e: e.activation(out=qTz[0:64, i:i + 2, 0, :], in_=pTA[0:64, :, 0, :], func=AF.Copy),
                  reads=["ps0" + sfx, "qTz_zero"], writes=qk_keys)
            S.act(lambda e: e.activation(out=qTz[64:128, i:i + 2, 1, :], in_=pTA[64:128, :, 0, :], func=AF.Copy),
                  reads=["ps0" + sfx, "qTz_zero"], writes=qk_keys)
            S.dve(lambda e: e.tensor_tensor(out=q2[:, :, i * 128:(i + 2) * 128].rearrange("p h (t x) -> p t h x", t=2), in0=pTD,
                                            in1=qdec3[:, 2 * p:2 * p + 2, :].unsqueeze(1).to_broadcast([128, 2, 2, 128]), op=ALU.mult),
                  reads=["ps1" + sfx, "qdec"], writes=["q2_%d" % i, "q2_%d" % (i + 1)])

        def scan_pe(j):
            sfx = "ab"[j % 2]
            pD = bank(6)[:, (j % 2) * 256:(j % 2) * 256 + 256].rearrange("p (h v) -> p h v", v=128)
            nf, nb = j, NT - 1 - j
            for h2 in range(2):
                S.pe(lambda e, h2=h2: e.matmul(pD[0:64, h2, :], lhsT=kfb[:, nf, 0, h2 * 64:(h2 + 1) * 64],
                                               rhs=v_tok[:, nf, h2 * 128:(h2 + 1) * 128], start=True, stop=True),
                     reads=["kfb%d" % nf, "v%d" % nf], writes=["ps6" + sfx])
                S.pe(lambda e, h2=h2: e.matmul(pD[64:128, h2, :], lhsT=kfb[:, nb, 1, h2 * 64:(h2 + 1) * 64],
                                               rhs=v_tok[:, nb, h2 * 128:(h2 + 1) * 128], start=True, stop=True),
                     reads=["kfb%d" % nb, "v%d" % nb], writes=["ps6" + sfx])

        def scan_ew(j):
            sfx = "ab"[j % 2]
            pD = bank(6)[:, (j % 2) * 256:(j % 2) * 256 + 256].rearrange("p (h v) -> p h v", v=128)
            nf, nb = j, NT - 1 - j
            for h2 in range(2):
                S.dve(lambda e, h2=h2: e.scalar_tensor_tensor(
                    out=sfp3[:, h2, :], in0=sfp3[:, h2, :], scalar=cdec[:, 2 * p + h2:2 * p + h2 + 1], in1=pD[:, h2, :],
                    op0=ALU.mult, op1=ALU.add), reads=["sfp", "ps6" + sfx, "cdec"], writes=["sfp"])
            if os.environ.get("KDBG_CAST", "dve") == "dve":
                S.dve(lambda e: e.tensor_copy(out=s_all[0:64, nf + 1, :, :], in_=sfp3[0:64, :, :]),
                      reads=["sfp"], writes=["sall_f%d" % (nf + 1)])
                S.dve(lambda e: e.tensor_copy(out=s_all[64:128, nb - 1, :, :], in_=sfp3[64:128, :, :]),
                      reads=["sfp"], writes=["sall_b%d" % (nb - 1)])
            else:
                S.act(lambda e: e.activation(out=s_all[0:64, nf + 1, :, :], in_=sfp3[0:64, :, :], func=AF.Copy),
                      reads=["sfp"], writes=["sall_f%d" % (nf + 1)])
                S.act(lambda e: e.activation(out=s_all[64:128, nb - 1, :, :], in_=sfp3[64:128, :, :], func=AF.Copy),
                      reads=["sfp"], writes=["sall_b%d" % (nb - 1)])

        def scan_step(j):
            scan_pe(j)
            scan_ew(j)

        class OutStep:
            def __init__(self, it, n0):
                self.it, self.n0 = it, n0
                par = it % 2
                self.psc = 2 + par
                self.pyb = 4 + par
                self.pS4 = bank(self.psc).rearrange("p (c h t) -> p c h t", c=2, t=128)
                self.pY4 = bank(self.pyb).rearrange("p (c h v) -> p c h v", c=2, v=128)
                self.pm = pmat[par]
                self.pmk = "pmat%d" % par
                self.ysb = y_sb[it % 3]
                self.yk = "ysb%d" % (it % 3)
                self.gs = gst[:, (it % 4) * 32:(it % 4) * 32 + 32]
                self.gk = "gst%d" % (it % 4)
                self.yg = ygb[par]
                self.ygk = "ygb%d" % par
                self.sfx = "ab"[par]
                self.pT4 = bank(0, BF16)[:, par * 512:(par + 1) * 512].rearrange("p (h c t) -> p h c t", h=2, t=128)

            def scores(o):
                for c in range(2):
                    n = o.n0 + c
                    tsl = slice(n * 128, (n + 1) * 128)
                    S.pe(lambda e, c=c, n=n, tsl=tsl: e.matmul(bank(o.psc)[:, c * 256:(c + 1) * 256], lhsT=kT2[:, tsl],
                                                               rhs=qTz[:, n, :, :].rearrange("p h t -> p (h t)"), start=True, stop=True),
                         reads=["qkT%d" % n, "qTz_zero"], writes=[PK(o.psc)])

            def mask(o):
                S.dve(lambda e: e.tensor_tensor(out=o.pm, in0=o.pS4,
                                                in1=mask3[:, 2 * p:2 * p + 2, :].unsqueeze(1).to_broadcast([128, 2, 2, 128]),
                                                op=ALU.mult), reads=[PK(o.psc), "mask"], writes=[o.pmk])

            def av(o):
                for c in range(2):
                    n = o.n0 + c
                    tsl = slice(n * 128, (n + 1) * 128)
                    for h2 in range(2):
                        S.pe(lambda e, c=c, n=n, h2=h2: e.matmul(o.pY4[:, c, h2, :], lhsT=o.pm[:, c, h2, :],
                                                                 rhs=v_tok[:, n, h2 * 128:(h2 + 1) * 128], start=True, stop=False),
                             reads=[o.pmk, "v%d" % n], writes=[PK(o.pyb)])
                        S.pe(lambda e, c=c, n=n, h2=h2, tsl=tsl: e.matmul(o.pY4[:, c, h2, :], lhsT=q2[:, h2, tsl],
                                                                          rhs=s_all[:, n, h2, :], start=False, stop=True),
                             reads=["q2_%d" % n, "sall_f%d" % n, "sall_b%d" % n], writes=[PK(o.pyb)])

            def copy_sq(o):
                S.act(lambda e: e.activation(out=o.ysb, in_=o.pY4, func=AF.Copy), reads=[PK(o.pyb)], writes=[o.yk])
                for c in range(2):
                    for h2 in range(2):
                        q_ = c * 2 + h2
                        S.act(lambda e, c=c, h2=h2, q_=q_: e.activation(out=ysq[:, c, h2, :], in_=o.pY4[:, c, h2, :], func=AF.Square,
                                                                        accum_out=o.gs[:, 4 + q_:5 + q_]),
                              reads=[PK(o.pyb)], writes=["ysq", o.gk + "q"])

            def reduce(o):
                S.dve(lambda e: e.tensor_reduce(out=o.gs[:, 0:4], in_=o.ysb.rearrange("p c h v -> p (c h) v"), axis=AX.X, op=ALU.add),
                      reads=[o.yk], writes=[o.gk + "s"])

            def var(o):
                S.dve(lambda e: e.tensor_tensor(out=o.gs[:, 8:12], in0=o.gs[:, 0:4], in1=o.gs[:, 0:4], op=ALU.mult),
                      reads=[o.gk + "s"], writes=[o.gk + "ss"])
                S.dve(lambda e: e.scalar_tensor_tensor(out=o.gs[:, 12:16], in0=o.gs[:, 4:8], scalar=float(DV), in1=o.gs[:, 8:12],
                                                       op0=ALU.mult, op1=ALU.subtract),
                      reads=[o.gk + "q", o.gk + "ss"], writes=[o.gk + "v"])

            def sqrt(o):
                S.act(lambda e: e.activation(out=o.gs[:, 16:20], in_=o.gs[:, 12:16], func=AF.Sqrt, bias=epsb[:, 0:1]),
                      reads=[o.gk + "v", "epsb"], writes=[o.gk + "sd"])

            def rstd(o):
                S.dve(lambda e: e.reciprocal(out=o.gs[:, 24:28], in_=o.gs[:, 16:20]), reads=[o.gk + "sd"], writes=[o.gk + "r"])
                S.dve(lambda e: e.scalar_tensor_tensor(out=o.gs[:, 28:32], in0=o.gs[:, 0:4], scalar=-1.0 / DV, in1=o.gs[:, 24:28],
                                                       op0=ALU.mult, op1=ALU.mult),
                      reads=[o.gk + "s", o.gk + "r"], writes=[o.gk + "nb"])

            def normalize(o):
                for c in range(2):
                    for h2 in range(2):
                        q_ = c * 2 + h2
                        S.act(lambda e, c=c, h2=h2, q_=q_: e.activation(out=o.yg[:, c, h2, :], in_=o.ysb[:, c, h2, :], func=AF.Identity,
                                                                        scale=o.gs[:, 24 + q_:25 + q_], bias=o.gs[:, 28 + q_:29 + q_]),
                              reads=[o.yk, o.gk + "r", o.gk + "nb"], writes=[o.ygk])

            def gate(o):
                S.dve(lambda e: e.tensor_tensor(out=o.yg.rearrange("p c h v -> p c (h v)"),
                                                in0=o.yg.rearrange("p c h v -> p c (h v)"),
                                                in1=sg_tok[:, o.n0:o.n0 + 2, :], op=ALU.mult),
                      reads=[o.ygk, "sg%d" % o.n0, "sg%d" % (o.n0 + 1)], writes=[o.ygk])

            def transposes(o):
                for c in range(2):
                    for h2 in range(2):
                        S.pe(lambda e, c=c, h2=h2: e.transpose(out=o.pT4[:, h2, c, :], in_=o.yg[:, c, h2, :], identity=ident),
                             reads=[o.ygk, "ident"], writes=["ps0" + o.sfx])

            def evac(o):
                for h2 in range(2):
                    kk = 2 * p + h2
                    S.act(lambda e, h2=h2, kk=kk: e.activation(out=ygT[:, kk, o.n0 * 128:(o.n0 + 2) * 128],
                                                               in_=o.pT4[:, h2, :, :].rearrange("p c t -> p (c t)"), func=AF.Copy,
                                                               scale=gnw128[:, kk:kk + 1]),
                          reads=["ps0" + o.sfx, "gnw128"], writes=[ygT_key(kk, o.n0), ygT_key(kk, o.n0 + 1)])

        order_b = []
        for j in range(NT // 4):
            order_b += [2 * j, NT - 2 - 2 * j]
        def head_pe():
            sb_proj(0, order_b[0], pe_only=True)

        def head_rest():
            sb_proj_evac(0, order_b[0])
            sb_rope(0, order_b[0])

        def body(nxt):
            wada_late_load(4 + 2 * p)
            for it, i in enumerate(order_b):
                if it + 1 < len(order_b):
                    sb_proj(it + 1, order_b[it + 1])
                if it == 2:
                    wada_late_compute(4 + 2 * p)
                    wada_late_load(5 + 2 * p)
                if it == 5:
                    wada_late_compute(5 + 2 * p)
                    if p == 2:
                        mod_ab(A_F, B_F, VO_NFFN, 24, 32, 0)
                sb_tr(it, i)
                if it + 1 < len(order_b):
                    sb_rope(it + 1, order_b[it + 1])
                if it % 2 == 1:
                    scan_step(it - 1)
                    scan_step(it)
            order_o = [6, 8, 4, 10, 2, 12, 0, 14]
            NO = len(order_o)
            steps = [OutStep(it, order_o[it]) for it in range(NO)]

            def g(k):
                return steps[k] if 0 <= k < NO else None

            scan_step(NT // 2)
            def iteration(it):
                    a_, b_, c_, d_ = g(it), g(it - 1), g(it - 2), g(it - 3)
                    sj = NT // 2 + 1 + it if NT // 2 + 1 + it <= NT - 2 else None
                    ordm = os.environ.get("KDBG_ORD", "m1c")
                    if ordm == "r7":
                        if c_:
                            c_.var(); c_.sqrt(); c_.rstd()
                        if d_:
                            d_.normalize(); d_.gate(); d_.transposes(); d_.evac()
                        if b_:
                            b_.copy_sq(); b_.reduce()
                        if sj is not None:
                            scan_step(sj)
                        if a_:
                            a_.scores(); a_.mask(); a_.av()
                        return
                    if ordm == "m1":
                        if a_:
                            a_.scores()
                        if sj is not None:
                            scan_pe(sj)
                        if c_:
                            c_.var(); c_.sqrt(); c_.rstd()
                        if d_:
                            d_.normalize(); d_.gate(); d_.transposes(); d_.evac()
                        if b_:
                            b_.copy_sq(); b_.reduce()
                        if sj is not None:
                            scan_ew(sj)
                        if a_:
                            a_.mask(); a_.av()
                        return
                    if ordm == "m1c":
                        if a_:
                            a_.scores()
                        if sj is not None:
                            scan_pe(sj)
                        if c_:
                            c_.var(); c_.sqrt()
                        if sj is not None:
                            scan_ew(sj)
                        if c_:
                            c_.rstd()
                        if d_:
                            d_.normalize()
                        if b_:
                            b_.copy_sq()
                        if d_:
                            d_.gate(); d_.transposes()
                        if b_:
                            b_.reduce()
                        if d_:
                            d_.evac()
                        if a_:
                            a_.mask(); a_.av()
                        return
                    if ordm == "m1b":
                        if a_:
                            a_.scores()
                        if sj is not None:
                            scan_pe(sj)
                        if c_:
                            c_.var(); c_.sqrt(); c_.rstd()
                        if d_:
                            d_.normalize()
                        if b_:
                            b_.copy_sq()
                        if d_:
                            d_.gate(); d_.transposes()
                        if b_:
                            b_.reduce()
                        if d_:
                            d_.evac()
                        if sj is not None:
                            scan_ew(sj)
                        if a_:
                            a_.mask(); a_.av()
                        return
                    if ordm in ("m3", "m4"):
                        if a_:
                            a_.scores()
                        if sj is not None:
                            scan_pe(sj)
                        if c_:
                            c_.var(); c_.sqrt(); c_.rstd()
                        if ordm == "m4" and a_:
                            a_.mask(); a_.av()
                        if d_:
                            d_.normalize(); d_.gate(); d_.transposes(); d_.evac()
                        if ordm == "m3" and a_:
                            a_.mask(); a_.av()
                        if b_:
                            b_.copy_sq(); b_.reduce()
                        if sj is not None:
                            scan_ew(sj)
                        return
                    if ordm == "m2":
                        if a_:
                            a_.scores()
                        if sj is not None:
                            scan_pe(sj)
                        if c_:
                            c_.var(); c_.sqrt()
                        if a_:
                            a_.mask(); a_.av()
                        if c_:
                            c_.rstd()
                        if d_:
                            d_.normalize(); d_.gate(); d_.transposes(); d_.evac()
                        if b_:
                            b_.copy_sq(); b_.reduce()
                        if sj is not None:
                            scan_ew(sj)
                        return
                    if a_:
                        a_.scores()
                    if sj is not None:
                        scan_pe(sj)
                    if c_:
                        c_.var()
                        c_.sqrt()
                    if d_:
                        d_.normalize()
                    if a_:
                        a_.mask()
                        a_.av()
                    if c_:
                        c_.rstd()
                    if sj is not None:
                        scan_ew(sj)
                    if b_:
                        b_.copy_sq()
                    if d_:
                        d_.gate()
                        d_.transposes()
                    if b_:
                        b_.reduce()
                    if d_:
                        d_.evac()

            for it in range(NO + 3):
                iteration(it)
                if nxt is not None and it == NO:
                    nxt.ctx_pe()
                if nxt is not None and it == NO + 1:
                    nxt.head_pe()
            if nxt is not None:
                nxt.ctx_rest()
                nxt.head_rest()

        class _P:
            pass
        o_ = _P()
        o_.ctx_pe, o_.ctx_rest, o_.head_pe, o_.head_rest, o_.body = ctx_pe, ctx_rest, head_pe, head_rest, body
        return o_

    pairs = [make_pair(p) for p in range(4)]
    pairs[0].ctx_pe()
    pairs[0].ctx_rest()
    pairs[0].head_pe()
    pairs[0].head_rest()
    for p in range(4):
        pairs[p].body(pairs[p + 1] if p < 3 else None)

    if stop == 3:
        return finish_debug([(ygT.rearrange("p k t -> p (k t)"), 16384)])
    R_PAIR.reset()
    def _rp(off_kb, size_kb, dt=BF16):
        v = arena[:, (R_PAIR.start + off_kb * KB) // 2:(R_PAIR.start + (off_kb + size_kb) * KB) // 2]
        return v if dt == BF16 else v.bitcast(dt)

    wt_buf = [_rp(20, 7).rearrange("p (r n) -> p r n", n=128), _rp(0, 7).rearrange("p (r n) -> p r n", n=128)]
    wt_buf = [wt_buf[0], wt_buf[1]]
    sgab = [_rp(7 + 2 * i, 2, F32) for i in range(4)]
    wo_sb = _rp(28, 16).rearrange("p (k n) -> p k n", n=D)
    wo_stage = [_rp(44 + 4 * i, 4, F32) for i in range(2)]
    xres = [_rp(52 + 4 * i, 4, F32) for i in range(2)]
    R_PAIR.off = 60 * KB

    def mT_key(k, i):
        return "M_%d_%d" % (k, i)

    def load_tail(cb):
        extra = [S.fence()] if cb == 1 else (["kfb%d" % i for i in range(NT)] if cb == 0 else [])
        S.dma("pool", "wt%d" % (cb % 2), lambda e, cb=cb: e.dma_start(out=wt_buf[cb % 2], in_=wtail_d[cb]),
              writes=["wt%d" % (cb % 2)] + extra)

    def prep_wo(k):
        st = wo_stage[k % 2]
        sk = "wost%d" % (k % 2)
        S.dma("sp", sk, lambda e, st=st, k=k: e.dma_start(out=st, in_=wo_d[k]), writes=[sk] + ([S.fence()] if k < 2 else []))
        S.dve(lambda e, st=st, k=k: e.tensor_tensor(out=wo_sb[:, k, :], in0=st, in1=gm_bc, op=ALU.mult),
              reads=[sk, "gm_bc"], writes=["wo%d" % k])

    load_tail(0)
    ctr = 0
    for cb in range(8):
        wt = wt_buf[cb % 2]
        wtk = "wt%d" % (cb % 2)
        if cb + 1 < 8:
            load_tail(cb + 1)
        prep_wo(cb)
        for tb in range(4):
            ts_ = slice(tb * 512, (tb + 1) * 512)
            par = ctr % 2
            ctr += 1
            pga, pgb, pba, pbb = par, 2 + par, 4 + par, 6 + par
            hreads = lambda k: [hT_key(k, 4 * tb + q) for q in range(4)]
            for k in range(8):
                S.pe(lambda e, k=k, ts_=ts_, pga=pga, wt=wt: e.matmul(bank(pga), lhsT=wt[:, k, :], rhs=hT[:, k, ts_],
                                                                     start=(k == 0), stop=(k == 7)),
                     reads=[wtk] + hreads(k), writes=[PK(pga)])
            for g in range(4):
                S.pe(lambda e, g=g, ts_=ts_, pba=pba, wt=wt: e.matmul(bank(pba), lhsT=wt[:, 24 + g, :], rhs=ypT[:, g, ts_],
                                                                     start=(g == 0), stop=(g == 3)),
                     reads=[wtk, "ypT"], writes=[PK(pba)])
            for k in range(8):
                S.pe(lambda e, k=k, ts_=ts_, pgb=pgb, wt=wt: e.matmul(bank(pgb), lhsT=wt[:, 8 + k, :], rhs=hT[:, k, ts_],
                                                                     start=(k == 0), stop=(k == 7)),
                     reads=[wtk] + hreads(k), writes=[PK(pgb)])
            for k in range(8):
                S.pe(lambda e, k=k, ts_=ts_, pbb=pbb, wt=wt: e.matmul(bank(pbb), lhsT=wt[:, 16 + k, :], rhs=ygT[:, k, ts_],
                                                                     start=(k == 0), stop=(k == 7)),
                     reads=[wtk] + [ygT_key(k, 4 * tb + q) for q in range(4)], writes=[PK(pbb)])
            sa = sgab[par * 2]
            sb_ = sgab[par * 2 + 1]
            sak = "sga%d" % par
            sbk = "sgb%d" % par
            S.act(lambda e, sa=sa, pga=pga: e.activation(out=sa, in_=bank(pga), func=AF.Sigmoid), reads=[PK(pga)], writes=[sak])
            S.act(lambda e, sb_=sb_, pgb=pgb: e.activation(out=sb_, in_=bank(pgb), func=AF.Sigmoid), reads=[PK(pgb)], writes=[sbk])
            S.dve(lambda e, sa=sa, pba=pba: e.tensor_tensor(out=sa, in0=sa, in1=bank(pba), op=ALU.mult),
                  reads=[sak, PK(pba)], writes=[sak])
            S.dve(lambda e, sb_=sb_, pbb=pbb: e.tensor_tensor(out=sb_, in0=sb_, in1=bank(pbb), op=ALU.mult),
                  reads=[sbk, PK(pbb)], writes=[sbk])
            S.dve(lambda e, sa=sa, sb_=sb_, cb=cb, ts_=ts_: e.tensor_tensor(out=mT[:, cb, ts_], in0=sa, in1=sb_, op=ALU.add),
                  reads=[sak, sbk], writes=[mT_key(cb, 4 * tb + q) for q in range(4)])

    if stop == 4:
        return finish_debug([(mT.rearrange("p k t -> p (k t)"), 16384)])
    w13_buf = [arena[:, (R_PAIR.start + i * 4 * KB) // 2:(R_PAIR.start + (i + 1) * 4 * KB) // 2].rearrange(
        "p (a k n) -> p a k n", a=2, n=128) for i in range(2)]

    def load_w13(c):
        extra = [S.fence()] if c < 2 else []
        S.dma("pool", "w13_%d" % (c % 2), lambda e, c=c: e.dma_start(out=w13_buf[c % 2], in_=w13_d[c]),
              writes=["w13_%d" % (c % 2)] + extra)

    if stop is None:
        load_w13(0)
        load_w13(1)
    def x1_key(i):
        return "X1_%d" % i

    h2T = mT

    def h2_key(k, i):
        return mT_key(k, i)

    junk2 = _rp(48, 2)
    a2 = {}

    def a2_square(i):
        n = nctr[0]
        nctr[0] += 1
        c = dict(n=n, i=i, ss=stat[:, 32 + (n % 4) * 2:33 + (n % 4) * 2], rs=stat[:, 33 + (n % 4) * 2:34 + (n % 4) * 2],
                 sk="nstat%d" % (n % 4), xn=xnb2[n % 3], xk="xn3_%d" % (n % 3))
        S.act(lambda e: e.activation(out=junk2, in_=x1[:, i, :], func=AF.Square, accum_out=c["ss"]),
              reads=[x1_key(i)], writes=["junk", c["sk"]])
        return c

    def a2_ts(c):
        S.dve(lambda e: e.tensor_scalar(out=c["rs"], in0=c["ss"], scalar1=1.0 / D, scalar2=EPS, op0=ALU.mult, op1=ALU.add),
              reads=[c["sk"]], writes=[c["sk"] + "r"])
        S.act(lambda e: e.activation(out=c["rs"], in_=c["rs"], func=AF.Sqrt), reads=[c["sk"] + "r"], writes=[c["sk"] + "r"])

    def a2_fin(c):
        S.dve(lambda e: e.reciprocal(out=c["rs"], in_=c["rs"]), reads=[c["sk"] + "r"], writes=[c["sk"] + "r"])
        S.dve(lambda e: e.tensor_scalar(out=c["xn"], in0=x1[:, c["i"], :], scalar1=c["rs"], scalar2=None, op0=ALU.mult),
              reads=[x1_key(c["i"]), c["sk"] + "r"], writes=[c["xk"]])

    xnb2 = [_rp(50, 2), _rp(60, 2), arena[:, (R_YP.start + 14 * KB) // 2:(R_YP.start + 16 * KB) // 2]]
    pend = None
    for i in range(NT):
        pa = (i % 2) * 2
        xr = xres[i % 2]
        xrk = "xres%d" % (i % 2)
        S.dma("sp", xrk, lambda e, xr=xr, i=i: e.dma_start(out=xr, in_=x_d[i * 128:(i + 1) * 128, :]),
              writes=[xrk])
        for hf in range(2):
            for k in range(8):
                S.pe(lambda e, k=k, hf=hf, i=i, pa=pa: e.matmul(bank(pa + hf), lhsT=mT[:, k, i * 128:(i + 1) * 128],
                                                                rhs=wo_sb[:, k, hf * 512:(hf + 1) * 512],
                                                                start=(k == 0), stop=(k == 7)),
                     reads=[mT_key(k, i), "wo%d" % k], writes=[PK(pa + hf)])
        S.dve(lambda e, i=i, xr=xr, pa=pa: e.tensor_tensor(out=x1[:, i, :], in0=ps[:, pa * 512:(pa + 2) * 512], in1=xr, op=ALU.add),
              reads=[PK(pa), PK(pa + 1), xrk],
              writes=[x1_key(i)] + [(hT_key(i, t) if i < 8 else ygT_key(i - 8, t)) for t in range(NT)])
        if stop != 5:
            a2[i] = a2_square(i)
            if i >= 1:
                a2_ts(a2[i - 1])
            if i >= 3:
                norm_tr((a2[i - 3]["n"], a2[i - 3]["xn"], a2[i - 3]["xk"]), A_F, B_F, h2T, h2_key, i - 3, 4)
            if i >= 1:
                a2_fin(a2[i - 1])
    if stop != 5:
        a2_ts(a2[NT - 1])
        norm_tr((a2[NT - 3]["n"], a2[NT - 3]["xn"], a2[NT - 3]["xk"]), A_F, B_F, h2T, h2_key, NT - 3, 4)
        a2_fin(a2[NT - 1])
        for i_ in (NT - 2, NT - 1):
            norm_tr((a2[i_]["n"], a2[i_]["xn"], a2[i_]["xk"]), A_F, B_F, h2T, h2_key, i_, 4)

    if stop == 5:
        return finish_debug([(x1.rearrange("p i f -> p (i f)"), 16384)])
    R_PAIR.reset()
    R_PAIR.carve(8 * KB)
    actT = [R_PAIR.carve(20 * KB).rearrange("p (c t) -> p c t", t=L) for _ in range(2)]
    R_YP.reset()
    w2_sb = R_YP.carve(10 * KB).rearrange("p (c n) -> p c n", n=D)
    sa_sb = [R_YP.carve(2 * KB, F32) for _ in range(2)]
    w2_stage = [qdec, mask]

    S.dma("sp", "c6", lambda e: e.dma_start(out=nf_bc, in_=rows_d[0:1, RO_NF:RO_NF + D].to_broadcast([128, D])),
          writes=["nf_bc", S.fence()] + hcT_keys)
    fctr = 0
    for gi, (c0, ng) in enumerate(FF_GROUPS):
        at = actT[gi % 2]
        atk = "actT%d" % (gi % 2)
        for cc in range(ng):
            c = c0 + cc
            wb = w13_buf[c % 2]
            wbk = "w13_%d" % (c % 2)
            if 2 <= c + 1 < NFF:
                load_w13(c + 1)
            st = w2_stage[c % 2]
            stk = "w2st%d" % (c % 2)
            S.dma("sp", stk, lambda e, st=st, c=c: e.dma_start(out=st, in_=w2_d[c]), writes=[stk, "qdec" if c % 2 == 0 else "mask"])
            S.dve(lambda e, st=st, cc=cc: e.tensor_tensor(out=w2_sb[:, cc, :], in0=st, in1=gf_bc, op=ALU.mult),
                  reads=[stk, "gf_bc"], writes=["w2sb%d" % cc])
            for tb in range(4):
                ts_ = slice(tb * 512, (tb + 1) * 512)
                par = fctr % 2
                fctr += 1
                pa_, pb_ = par, 2 + par
                for a in range(2):
                    pbk = pa_ if a == 0 else pb_
                    for k in range(8):
                        S.pe(lambda e, a=a, k=k, ts_=ts_, pbk=pbk, wb=wb: e.matmul(bank(pbk), lhsT=wb[:, a, k, :], rhs=h2T[:, k, ts_],
                                                                                  start=(k == 0), stop=(k == 7)),
                             reads=[wbk] + [h2_key(k, 4 * tb + q) for q in range(4)], writes=[PK(pbk)])
                sa = sa_sb[par]
                sak = "ffsa%d" % par
                S.act(lambda e, sa=sa, pa_=pa_: e.activation(out=sa, in_=bank(pa_), func=AF.Silu), reads=[PK(pa_)], writes=[sak])
                S.dve(lambda e, sa=sa, pb_=pb_, cc=cc, ts_=ts_, at=at: e.tensor_tensor(out=at[:, cc, ts_], in0=sa, in1=bank(pb_), op=ALU.mult),
                      reads=[sak, PK(pb_)], writes=[atk + "_%d_%d" % (cc, tb)])
        last = (gi == len(FF_GROUPS) - 1)
        for i in range(NT):
            pa = 4 + (i % 2) * 2
            for hf in range(2):
                for cc in range(ng):
                    S.pe(lambda e, cc=cc, hf=hf, i=i, pa=pa, at=at: e.matmul(bank(pa + hf), lhsT=at[:, cc, i * 128:(i + 1) * 128],
                                                                             rhs=w2_sb[:, cc, hf * 512:(hf + 1) * 512],
                                                                             start=(cc == 0), stop=(cc == ng - 1)),
                         reads=[atk + "_%d_%d" % (cc, i // 4), "w2sb%d" % cc], writes=[PK(pa + hf)])
            S.dve(lambda e, i=i, pa=pa: e.tensor_tensor(out=x1[:, i, :], in0=ps[:, pa * 512:(pa + 2) * 512], in1=x1[:, i, :], op=ALU.add),
                  reads=[PK(pa), PK(pa + 1), x1_key(i)], writes=[x1_key(i)])
            if last:
                def fin_a(i):
                    ss = stat[:, 32 + (i % 4) * 2:33 + (i % 4) * 2]
                    S.act(lambda e: e.activation(out=junk2, in_=x1[:, i, :], func=AF.Square, accum_out=ss),
                          reads=[x1_key(i)], writes=["junk2", "fstat%d" % (i % 4)])

                def fin_b(i):
                    ss = stat[:, 32 + (i % 4) * 2:33 + (i % 4) * 2]
                    rs = stat[:, 33 + (i % 4) * 2:34 + (i % 4) * 2]
                    sk = "fstat%d" % (i % 4)
                    S.dve(lambda e: e.tensor_scalar(out=rs, in0=ss, scalar1=1.0 / D, scalar2=EPS, op0=ALU.mult, op1=ALU.add),
                          reads=[sk], writes=[sk + "r"])
                    S.act(lambda e: e.activation(out=rs, in_=rs, func=AF.Sqrt), reads=[sk + "r"], writes=[sk + "r"])

                def fin_c(i):
                    rs = stat[:, 33 + (i % 4) * 2:34 + (i % 4) * 2]
                    sk = "fstat%d" % (i % 4)
                    S.dve(lambda e: e.reciprocal(out=rs, in_=rs), reads=[sk + "r"], writes=[sk + "r"])
                    S.dve(lambda e: e.scalar_tensor_tensor(out=x1[:, i, :], in0=x1[:, i, :], scalar=rs, in1=nf_bc,
                                                           op0=ALU.mult, op1=ALU.mult),
                          reads=[x1_key(i), sk + "r", "nf_bc"], writes=[x1_key(i)])
                    S.dma("sp", "out", lambda e: e.dma_start(out=out_d[i * 128:(i + 1) * 128, :], in_=x1[:, i, :]),
                          reads=[x1_key(i)])

                fin_a(i)
                if i >= 1:
                    fin_b(i - 1)
                if i >= 2:
                    fin_c(i - 2)
                if i == NT - 1:
                    fin_b(i)
                    fin_c(i - 1)
                    fin_c(i)

    S.emit(nc, final_dma_keys=["out"])
    return nc


_Q_OFF, _K_OFF, _V_OFF, _G_OFF, _GA_OFF, _GB_OFF = 512, 1024, 1536, 2560, 3584, 4608


def _kchunk(w):
    kk = w.shape[0] // 128
    return np.ascontiguousarray(w.reshape(kk, 128, w.shape[1]).transpose(1, 0, 2))


def _prep(x, c, ctx, c_ctx, w_ada, b_ada, norm_mix, norm_ffn, w_in, w_pool, pool_scale,
          ret_decay_f, ret_decay_b, ret_gn_w, w_pa, w_rb, w_o, w_ff1, w_ff3, w_ff2, norm_final):
    f32 = np.float32
    x = np.asarray(x, f32)
    B = x.shape[0]
    pblk, plan = _get_pool()

    w_in0 = np.asarray(w_in, f32)[0]
    wada = _kchunk(np.asarray(w_ada, f32)[0])
    wu = _kchunk(w_in0[:, 0:512])
    wpairs = []
    for p in range(4):
        cols = np.concatenate([
            w_in0[:, _Q_OFF + p * 128:_Q_OFF + (p + 1) * 128],
            w_in0[:, _K_OFF + p * 128:_K_OFF + (p + 1) * 128],
            w_in0[:, _V_OFF + p * 256:_V_OFF + (p + 1) * 256],
            w_in0[:, _G_OFF + p * 256:_G_OFF + (p + 1) * 256]], axis=1)
        wpairs.append(_kchunk(cols))
    wpairs = np.stack(wpairs, 0)
    w_rb0 = np.asarray(w_rb, f32)[0]
    w_pa0 = np.asarray(w_pa, f32)[0]
    wtail = []
    for cb in range(8):
        cs = slice(cb * 128, (cb + 1) * 128)
        ga = _kchunk(w_in0[:, _GA_OFF:_GA_OFF + D][:, cs])
        gb = _kchunk(w_in0[:, _GB_OFF:_GB_OFF + D][:, cs])
        rb = _kchunk(w_rb0[:, cs])
        pa = _kchunk(w_pa0[:, cs])
        wtail.append(np.concatenate([ga, gb, rb, pa], axis=1))
    wtail = np.ascontiguousarray(np.stack(wtail, 0))
    wpool = np.ascontiguousarray(np.asarray(w_pool, f32)[0].transpose(1, 0, 2))
    wo = np.ascontiguousarray(np.asarray(w_o, f32)[0].reshape(8, 128, D))
    w1 = np.asarray(w_ff1, f32)[0]
    w3 = np.asarray(w_ff3, f32)[0]
    w13 = np.stack([np.stack([_kchunk(w1[:, cc * 128:(cc + 1) * 128]), _kchunk(w3[:, cc * 128:(cc + 1) * 128])], axis=1)
                    for cc in range(NFF)], 0)
    w13 = np.ascontiguousarray(w13)
    w2 = np.ascontiguousarray(np.asarray(w_ff2, f32)[0].reshape(NFF, 128, D))
    consts = _host_consts()
    ident = np.eye(128, dtype=np.float32).astype(ml_dtypes.bfloat16)

    def pp(v, k):
        return np.asarray(v, f32).reshape(k, 128).T

    b_ada0 = np.asarray(b_ada, f32)[0]
    rows = np.zeros((1, NROW), f32)
    rows[0, RO_BGM:RO_BGM + D] = b_ada0[2 * D:3 * D]
    rows[0, RO_BGF:RO_BGF + D] = b_ada0[5 * D:6 * D]
    rows[0, RO_NF:RO_NF + D] = np.asarray(norm_final, f32)
    rows[0, RO_DEC:RO_DEC + 8] = np.asarray(ret_decay_f, f32)[0]
    rows[0, RO_DEC + 8:RO_DEC + 16] = np.asarray(ret_decay_b, f32)[0]

    in_maps = []
    for b in range(B):
        vecs = np.zeros((128, NVEC), f32)
        vecs[:, VO_C:VO_C + 16:2] = pp(np.asarray(c, f32)[b], 8)
        vecs[:, VO_C + 1:VO_C + 16:2] = pp(np.asarray(c_ctx, f32), 8)
        vecs[:, VO_NMIX:VO_NMIX + 8] = pp(np.asarray(norm_mix, f32)[0], 8)
        vecs[:, VO_NFFN:VO_NFFN + 8] = pp(np.asarray(norm_ffn, f32)[0], 8)
        vecs[:, VO_PSC:VO_PSC + 4] = pp(np.asarray(pool_scale, f32)[0], 4)
        vecs[:, VO_GNW:VO_GNW + 8] = pp(np.asarray(ret_gn_w, f32)[0], 8)
        vecs[:, VO_BADA:VO_BADA + 48] = pp(b_ada0, 48)
        in_maps.append({
            "x": np.ascontiguousarray(x[b]), "ctx": np.ascontiguousarray(np.asarray(ctx, f32)[b]),
            "vecs": vecs, "rows": rows, "consts": consts, "ident": ident,
            "wada": wada, "wu": wu, "wpairs": wpairs, "wtail": wtail, "wpool": wpool, "wo": wo,
            "w13": w13, "w2": w2, "pblk": pblk,
        })
    return in_maps


def kernel(**inputs):
    in_maps = _prep(**inputs)
    pblk, plan = _get_pool()
    nc = build_program(plan, pblk.shape[1])
    B = len(in_maps)
    res = run_bass_kernel_spmd(nc, in_maps, core_ids=list(range(B)))
    out = np.stack([np.asarray(res.results[b]["out"], np.float32) for b in range(B)], 0)
    return out
```

```python
import contextlib
import os
import types
import numpy as np
import ml_dtypes
import concourse.bass as bass
import concourse.mybir as mybir
from concourse.bass_utils import run_bass_kernel_spmd

F32 = mybir.dt.float32
BF16 = mybir.dt.bfloat16
AF = mybir.ActivationFunctionType
ALU = mybir.AluOpType
AX = mybir.AxisListType

D = 1024
L = 2048
NT = 16
LC = 256
C = 128
GRID_W = 64
H = 8
DK = 64
DV = 128
DFF = 2816
NFF = 22
EPS = 1e-6
K_SCALE = DK ** -0.5
POOL_WINDOWS = (2, 4, 8, 16)
FF_GROUPS = ((0, 5), (5, 5), (10, 5), (15, 5), (20, 2))


class Op:
    __slots__ = ("eng", "fn", "reads", "writes", "dma_key", "waits", "signal", "cnt", "n_dma")

    def __init__(self, eng, fn, reads, writes, dma_key, n_dma):
        self.eng = eng
        self.fn = fn
        self.reads = reads
        self.writes = writes
        self.dma_key = dma_key
        self.n_dma = n_dma
        self.waits = []
        self.signal = False
        self.cnt = None


class Sched:
    ENGS = ("pe", "act", "dve", "pool", "sp")

    def __init__(self):
        self.ops = []
        self.last_writer = {}
        self.readers = {}
        self.dma_cum = {}
        self.bank_rd = {}

    def _dep(self, op, d, kind):
        if d is op:
            return
        if d.dma_key is not None:
            op.waits.append(("dma:" + d.dma_key, self.dma_cum[d.dma_key]))
            return
        if d.eng == op.eng and op.dma_key is None:
            if d.eng == "pe" or kind != "RAW":
                return
        d.signal = True
        op.waits.append(("eng:" + d.eng, d))

    @staticmethod
    def _snapshot(fn):
        if fn.__closure__ is None:
            return fn
        cells = []
        for c in fn.__closure__:
            try:
                cells.append(types.CellType(c.cell_contents))
            except ValueError:
                cells.append(c)
        return types.FunctionType(fn.__code__, fn.__globals__, fn.__name__, fn.__defaults__, tuple(cells))

    def fence(self):
        self._nfence = getattr(self, "_nfence", 0) + 1
        key = "__fence%d" % self._nfence
        last = {}
        for o in self.ops:
            last[o.dma_key if o.dma_key is not None else "eng:" + o.eng] = o
        self.readers[key] = list(last.values())
        return key

    @staticmethod
    def _expand(keys):
        out = []
        for k in keys:
            if isinstance(k, str) and len(k) == 3 and k.startswith("ps"):
                out.extend((k + "a", k + "b"))
            else:
                out.append(k)
        return tuple(out)

    def op(self, eng, fn, reads=(), writes=(), dma_key=None, n_dma=1):
        fn = self._snapshot(fn)
        o = Op(eng, fn, self._expand(reads), self._expand(writes), dma_key, n_dma)
        for k in o.reads:
            w = self.last_writer.get(k)
            if w is not None:
                self._dep(o, w, "RAW")
            if isinstance(k, str) and k.startswith("ps") and eng in ("act", "dve"):
                bk = k[:3]
                lr = self.bank_rd.setdefault(bk, {})
                for e2, r in lr.items():
                    if e2 != eng:
                        self._dep(o, r, "XRD")
                lr[eng] = o
        for k in o.writes:
            w = self.last_writer.get(k)
            if w is not None:
                self._dep(o, w, "WAW")
            for r in self.readers.get(k, ()):
                self._dep(o, r, "WAR")
        for k in o.reads:
            self.readers.setdefault(k, []).append(o)
        for k in o.writes:
            self.last_writer[k] = o
            self.readers[k] = []
        if dma_key is not None:
            self.dma_cum[dma_key] = self.dma_cum.get(dma_key, 0) + 16 * n_dma
            o.cnt = self.dma_cum[dma_key]
        self.ops.append(o)
        return o

    def pe(self, fn, reads=(), writes=()):
        return self.op("pe", fn, reads, writes)

    def act(self, fn, reads=(), writes=()):
        return self.op("act", fn, reads, writes)

    def dve(self, fn, reads=(), writes=()):
        return self.op("dve", fn, reads, writes)

    def pool(self, fn, reads=(), writes=()):
        return self.op("pool", fn, reads, writes)

    def pool_or(self, tag, alt, fn, reads=(), writes=()):
        on = os.environ.get("KPOOL", "").split(",")
        return self.op("pool" if tag in on else alt, fn, reads, writes)

    def dma(self, eng, key, fn, reads=(), writes=()):
        return self.op(eng, fn, reads, writes, dma_key=key)

    def emit(self, nc, final_dma_keys=()):
        cnt = {e: 0 for e in self.ENGS}
        for o in self.ops:
            if o.dma_key is None and o.signal:
                cnt[o.eng] += 1
                o.cnt = cnt[o.eng]
        semnames = set()
        for o in self.ops:
            for (s, v) in o.waits:
                semnames.add(s)
            if o.dma_key is not None:
                semnames.add("dma:" + o.dma_key)
            elif o.signal:
                semnames.add("eng:" + o.eng)
        with contextlib.ExitStack() as es:
            sems = {}
            for s in sorted(semnames):
                sems[s] = es.enter_context(nc.semaphore(s.replace(":", "_")))
            block = es.enter_context(nc.Block())
            streams = {e: [o for o in self.ops if o.eng == e] for e in self.ENGS}

            def run(engname, e):
                waited = {}
                for o in streams[engname]:
                    need = {}
                    for (s, v) in o.waits:
                        val = v.cnt if isinstance(v, Op) else v
                        if val > need.get(s, 0):
                            need[s] = val
                    todo = [(s, val) for s, val in need.items() if waited.get(s, 0) < val]
                    attach = None
                    if todo and o.dma_key is None and engname in ("pe", "act", "dve"):
                        attach = todo.pop()
                    for s, val in todo:
                        e.wait_ge(sems[s], val)
                        waited[s] = val
                    ins = o.fn(e)
                    if attach is not None:
                        ins._wait_ge(sems[attach[0]], attach[1])
                        waited[attach[0]] = attach[1]
                    if o.dma_key is not None:
                        ins.then_inc(sems["dma:" + o.dma_key], 16 * o.n_dma)
                    elif o.signal:
                        ins.then_inc(sems["eng:" + o.eng], 1)
                if engname == "sp":
                    for k in final_dma_keys:
                        e.wait_ge(sems["dma:" + k], self.dma_cum[k])

            @block.tensor
            def _(e):
                run("pe", e)

            @block.scalar
            def _(e):
                run("act", e)

            @block.vector
            def _(e):
                run("dve", e)

            @block.gpsimd
            def _(e):
                run("pool", e)

            @block.sync
            def _(e):
                run("sp", e)


def _box_matrix(n, w):
    pos = np.arange(n)
    lo = np.clip(pos - w // 2, 0, n)
    hi = np.clip(pos + (w - w // 2), 0, n)
    a = np.zeros((n, n), np.float64)
    for t in range(n):
        a[t, lo[t]:hi[t]] = 1.0 / (hi[t] - lo[t])
    return a


def _pool_blocks():
    rows = L // GRID_W
    blocks = []
    seen = {}
    plan = []
    for g, w in enumerate(POOL_WINDOWS):
        a = np.kron(_box_matrix(rows, w), _box_matrix(GRID_W, w)) - np.eye(L)
        at = a.T
        pg = []
        for j in range(4):
            lst = []
            for t in range(NT):
                blk = at[t * 128:(t + 1) * 128, j * 512:(j + 1) * 512]
                if np.any(blk != 0.0):
                    b32 = np.ascontiguousarray(blk.astype(np.float32))
                    key = (g, b32.tobytes())
                    if key not in seen:
                        seen[key] = len(blocks)
                        blocks.append(b32)
                    lst.append((t, seen[key]))
            pg.append(lst)
        plan.append(pg)
    arr = np.stack(blocks, axis=1)
    return np.ascontiguousarray(arr).astype(ml_dtypes.bfloat16), plan


def _rope_tables():
    t = np.arange(L)
    row = (t // GRID_W).astype(np.float32)
    col = (t % GRID_W).astype(np.float32)
    n_freq = DK // 4
    inv_freq = (10000.0 ** (-np.arange(n_freq, dtype=np.float32) / n_freq)).astype(np.float32)
    ang = np.concatenate([row[:, None] * inv_freq, col[:, None] * inv_freq], axis=-1).astype(np.float32)
    cos = np.cos(ang).astype(np.float32).reshape(NT, 128, 32).transpose(1, 0, 2)
    sin = np.sin(ang).astype(np.float32).reshape(NT, 128, 32).transpose(1, 0, 2)
    return np.ascontiguousarray(cos), np.ascontiguousarray(sin)


_POOL_CACHE = None


def _get_pool():
    global _POOL_CACHE
    if _POOL_CACHE is None:
        _POOL_CACHE = _pool_blocks()
    return _POOL_CACHE


CO_COS = 0
CO_SIN = CO_COS + NT * 32
CO_DPOS = CO_SIN + NT * 32
CO_DNEG = CO_DPOS + 128
CO_POSQ = CO_DNEG + 128
CO_SM = CO_POSQ + 128
NCONST = CO_SM + 8

VO_C = 0
VO_NMIX = 16
VO_NFFN = 24
VO_PSC = 32
VO_GNW = 36
VO_BADA = 44
NVEC = VO_BADA + 48

RO_BGM = 0
RO_BGF = 1024
RO_NF = 2048
RO_DEC = 3072
NROW = RO_DEC + 16


def _host_consts():
    cos, sin = _rope_tables()
    c = np.zeros((128, NCONST), np.float32)
    c[:, CO_COS:CO_COS + NT * 32] = cos.reshape(128, -1)
    c[:, CO_SIN:CO_SIN + NT * 32] = sin.reshape(128, -1)
    i = np.arange(128, dtype=np.float32)
    dmat = i[None, :] - i[:, None]
    c[:, CO_DPOS:CO_DPOS + 128] = np.maximum(dmat, 0)
    c[:, CO_DNEG:CO_DNEG + 128] = np.maximum(-dmat, 0)
    c[0:64, CO_POSQ:CO_POSQ + 128] = (i + 1.0)[None, :]
    c[64:128, CO_POSQ:CO_POSQ + 128] = (C - i)[None, :]
    c[:, CO_SM + 0] = C - 1.0 - i
    c[:, CO_SM + 1] = i
    c[:, CO_SM + 2] = LC - 1.0 - i
    c[:, CO_SM + 3] = LC - 1.0 - (i + 128)
    c[:, CO_SM + 4] = i
    c[:, CO_SM + 5] = i + 128
    return c


def build_program(pool_plan, n_pool_blk, stop=None):
    nc = bass.Bass("TRN2", target_bir_lowering=False)
    DBGN = 16384

    def din(name, shape, dt=F32):
        return nc.dram_tensor(name, list(shape), dt, kind="ExternalInput").ap()

    x_d = din("x", [L, D])
    ctx_d = din("ctx", [LC, D])
    vecs_d = din("vecs", [128, NVEC])
    rows_d = din("rows", [1, NROW])
    consts_d = din("consts", [128, NCONST])
    ident_d = din("ident", [128, 128], BF16)
    wada_d = din("wada", [128, 8, 6 * D])
    wu_d = din("wu", [128, 8, 512])
    wpairs_d = din("wpairs", [4, 128, 8, 768])
    wtail_d = din("wtail", [8, 128, 28, 128])
    wpool_d = din("wpool", [128, 4, 128])
    wo_d = din("wo", [8, 128, D])
    w13_d = din("w13", [NFF, 128, 2, 8, 128])
    w2_d = din("w2", [NFF, 128, D])
    pblk_d = din("pblk", [128, n_pool_blk, 512], BF16)
    out_d = nc.dram_tensor("out", [L, D], F32, kind="ExternalOutput").ap()
    dbg_d = nc.dram_tensor("dbg", [128, DBGN], F32, kind="ExternalOutput").ap() if stop is not None else None

    S = Sched()
    SB_BYTES = 207 * 1024
    arena = nc.alloc_sbuf_tensor("arena", [128, SB_BYTES // 2], BF16).ap()
    ps = nc.alloc_psum_tensor("ps", [128, 4096], F32).ap()

    class Region:
        def __init__(self, start, size):
            self.start = start
            self.size = size
            self.off = 0

        def carve(self, nbytes, dt=BF16):
            want = nbytes
            nbytes = (nbytes + 31) // 32 * 32
            assert self.off + nbytes <= self.size, (self.off, nbytes, self.size)
            a = (self.start + self.off) // 2
            self.off += nbytes
            v = arena[:, a:a + want // 2]
            return v if dt == BF16 else v.bitcast(dt)

        def reset(self):
            self.off = 0

    KB = 1024
    R_PERS = Region(0, 32 * KB)
    R_X = Region(32 * KB, 64 * KB)
    R_M = Region(96 * KB, 32 * KB)
    R_PAIR = Region(128 * KB, 63 * KB)
    R_YP = Region(191 * KB, 16 * KB)
    assert 207 * KB <= SB_BYTES

    def bank(b, dt=F32):
        v = ps[:, b * 512:(b + 1) * 512]
        return v if dt == F32 else v.bitcast(dt)

    def PK(b):
        return "ps%d" % b

    def finish_debug(items):
        off = 0
        fk = S.fence()
        for ap, n, in items:
            for c0 in range(0, n, 1024):
                c1 = min(n, c0 + 1024)
                S.dma("pool", "dbg", lambda e, ap=ap, off=off, c0=c0, c1=c1: e.dma_start(
                    out=dbg_d[:, off + c0:off + c1], in_=ap[:, c0:c1]), writes=[fk])
            off += n
        S.dma("sp", "out", lambda e: e.dma_start(out=out_d[0:128, :], in_=gm_bc), writes=[fk])
        S.emit(nc, final_dma_keys=["out", "dbg"])
        return nc

    vecs = R_PERS.carve(NVEC * 4, F32)
    consts = R_PERS.carve(NCONST * 4, F32)
    ident = R_PERS.carve(256)
    decbc = R_PERS.carve(64, F32)
    lgbc = R_PERS.carve(64, F32)
    lgsel = R_PERS.carve(32, F32)
    cdec = R_PERS.carve(32, F32)
    kdec = R_PERS.carve(64, F32)
    ctxw = R_PERS.carve(128, F32)
    modT = R_PERS.carve(48 * 2 * 4, F32)
    modAB = R_PERS.carve(6 * 8 * 4, F32)
    scT = R_PERS.carve(8 * 2 * 2)
    scbc = R_PERS.carve(8 * 128 * 2)
    qdec = R_PERS.carve(8 * 128 * 4, F32)
    mask = R_PERS.carve(8 * 128 * 4, F32)
    gm_bc = R_PERS.carve(D * 4, F32)
    gf_bc = R_PERS.carve(D * 4, F32)
    hcT = R_PERS.carve(8 * LC * 2)
    nf_bc = hcT.bitcast(F32)
    wpool = R_PERS.carve(4 * 128 * 2)
    stat = R_PERS.carve(64 * 4, F32)
    epsb = R_PERS.carve(32, F32)
    gnw128 = R_PERS.carve(32, F32)
    sfp = R_PERS.carve(384 * 4, F32)

    modT3 = modT.rearrange("p (j t) -> p j t", t=2)
    modAB3 = modAB.rearrange("p (a k) -> p a k", k=8)
    scT3 = scT.rearrange("p (k t) -> p k t", t=2)
    scbc3 = scbc.rearrange("p (k m) -> p k m", m=128)
    qdec3 = qdec.rearrange("p (h t) -> p h t", t=128)
    mask3 = mask.rearrange("p (h t) -> p h t", t=128)
    hcT3 = hcT.rearrange("p (k t) -> p k t", t=LC)
    wpool3 = wpool.rearrange("p (g n) -> p g n", n=128)
    kdec3 = kdec.rearrange("p (d h) -> p d h", h=8)
    ctxw4 = ctxw.rearrange("p (t d h) -> p t d h", d=2, h=8)
    cos3 = consts[:, CO_COS:CO_COS + NT * 32].rearrange("p (i f) -> p i f", f=32)
    sin3 = consts[:, CO_SIN:CO_SIN + NT * 32].rearrange("p (i f) -> p i f", f=32)
    dpos = consts[:, CO_DPOS:CO_DPOS + 128]
    dneg = consts[:, CO_DNEG:CO_DNEG + 128]
    posq = consts[:, CO_POSQ:CO_POSQ + 128]

    def csm(i):
        return consts[:, CO_SM + i:CO_SM + i + 1]

    A_M, B_M, A_C, B_C, A_F, B_F = range(6)

    hT = R_X.carve(32 * KB).rearrange("p (k t) -> p k t", t=L)
    ygT = R_X.carve(32 * KB).rearrange("p (k t) -> p k t", t=L)
    R_X.reset()
    x1 = R_X.carve(64 * KB, F32).rearrange("p (i f) -> p i f", f=D)
    mT = R_M.carve(32 * KB).rearrange("p (k t) -> p k t", t=L)
    R_M.reset()
    _ux = R_X.start + 32 * KB
    u_tok = arena[:, _ux // 2:(_ux + 16 * KB) // 2].rearrange("p (i c) -> p i c", c=512)
    dT_sb = [arena[:, (_ux + (16 + i) * KB) // 2:(_ux + (17 + i) * KB) // 2] for i in range(2)]
    R_M.reset()
    wpair_buf = [R_M.carve(12 * KB).rearrange("p (k n) -> p k n", n=768) for _ in range(2)]
    wada_buf = [arena[:, (R_M.start + i * 8 * KB) // 2:(R_M.start + (i + 1) * 8 * KB) // 2].rearrange("p (k n) -> p k n", n=512)
                for i in range(4)]
    wada_buf += [arena[:, (R_YP.start + i * 8 * KB) // 2:(R_YP.start + (i + 1) * 8 * KB) // 2].rearrange("p (k n) -> p k n", n=512)
                 for i in range(2)]
    ypT = R_YP.carve(16 * KB).rearrange("p (g t) -> p g t", t=L)

    S.dma("sp", "c0", lambda e: e.dma_start(out=vecs, in_=vecs_d), writes=["vecs"])
    S.dma("sp", "c1", lambda e: e.dma_start(out=consts, in_=consts_d), writes=["consts"])
    S.dma("sp", "c2", lambda e: e.dma_start(out=ident, in_=ident_d), writes=["ident"])
    S.dma("sp", "c3", lambda e: e.dma_start(out=decbc, in_=rows_d[0:1, RO_DEC:RO_DEC + 16].to_broadcast([128, 16])),
          writes=["decbc"])
    S.dma("sp", "c4", lambda e: e.dma_start(out=gm_bc, in_=rows_d[0:1, RO_BGM:RO_BGM + D].to_broadcast([128, D])),
          writes=["gm_bc"])
    S.dma("sp", "c5", lambda e: e.dma_start(out=gf_bc, in_=rows_d[0:1, RO_BGF:RO_BGF + D].to_broadcast([128, D])),
          writes=["gf_bc"])
    S.dma("pool", "wpool", lambda e: e.dma_start(out=wpool3, in_=wpool_d), writes=["wpool"])

    S.dve(lambda e: e.memset(epsb, float(DV * DV) * EPS), writes=["epsb"])
    S.dve(lambda e: e.tensor_scalar(out=gnw128, in0=vecs[:, VO_GNW:VO_GNW + 8], scalar1=float(DV), scalar2=None, op0=ALU.mult),
          reads=["vecs"], writes=["gnw128"])
    cv3 = vecs[:, VO_C:VO_C + 16].rearrange("p (k t) -> p k t", t=2)
    S.act(lambda e: e.activation(out=scT3, in_=cv3, func=AF.Silu), reads=["vecs"], writes=["scT"])
    S.act(lambda e: e.activation(out=scbc3, in_=cv3[:, :, 0:1].to_broadcast([128, 8, 128]), func=AF.Silu),
          reads=["vecs"], writes=["scbc"])

    def wada_group(gi):
        buf = wada_buf[gi % 6]
        bk = "wada%d" % (gi % 6)
        S.dma("pool", bk, lambda e, buf=buf, gi=gi: e.dma_start(out=buf, in_=wada_d[:, :, gi * 512:(gi + 1) * 512]),
              writes=[bk])
        return buf, bk

    def wada_compute(gi, buf, bk):
        if gi in (4, 5, 10, 11):
            dst = gm_bc if gi in (4, 5) else gf_bc
            dk = "gm_bc" if gi in (4, 5) else "gf_bc"
            half = gi % 2 if gi in (4, 5) else (gi - 10)
            pb = 5 + (gi % 2)
            for k in range(8):
                S.pe(lambda e, k=k, buf=buf, pb=pb: e.matmul(bank(pb), lhsT=scbc3[:, k, :], rhs=buf[:, k, :],
                                                              start=(k == 0), stop=(k == 7)),
                     reads=[bk, "scbc"], writes=[PK(pb)])
            S.dve(lambda e, dst=dst, half=half, pb=pb: e.tensor_tensor(
                out=dst[:, half * 512:(half + 1) * 512], in0=bank(pb), in1=dst[:, half * 512:(half + 1) * 512], op=ALU.add),
                reads=[PK(pb), dk], writes=[dk])
        else:
            for jj in range(4):
                j = gi * 4 + jj
                for k in range(8):
                    S.pe(lambda e, k=k, jj=jj, j=j, buf=buf: e.matmul(
                        bank(7)[:, 2 * j:2 * j + 2], lhsT=buf[:, k, jj * 128:(jj + 1) * 128], rhs=scT3[:, k, :],
                        start=(k == 0), stop=(k == 7)),
                        reads=[bk, "scT"], writes=[PK(7)])

    def modT_evac(j0, j1):
        S.dve(lambda e, j0=j0, j1=j1: e.tensor_tensor(
            out=modT3[:, j0:j1, :], in0=bank(7)[:, 2 * j0:2 * j1].rearrange("p (j t) -> p j t", t=2),
            in1=vecs[:, VO_BADA + j0:VO_BADA + j1].unsqueeze(2).to_broadcast([128, j1 - j0, 2]), op=ALU.add),
            reads=[PK(7), "vecs"], writes=["modT"])

    def mod_ab(ai, bi, nvo, sh_blk, sc_blk, col):
        S.dve(lambda e: e.scalar_tensor_tensor(out=modAB3[:, ai, :], in0=modT3[:, sc_blk:sc_blk + 8, col], scalar=1.0,
                                               in1=vecs[:, nvo:nvo + 8], op0=ALU.add, op1=ALU.mult),
              reads=["modT", "vecs"], writes=["modAB%d" % ai])
        S.dve(lambda e: e.tensor_copy(out=modAB3[:, bi, :], in_=modT3[:, sh_blk:sh_blk + 8, col]),
              reads=["modT"], writes=["modAB%d" % bi])

    wg = [wada_group(gi) for gi in range(4)]

    def setup_mod_mix():
        for gi in range(4):
            wada_compute(gi, *wg[gi])
        modT_evac(0, 16)
        mod_ab(A_M, B_M, VO_NMIX, 0, 8, 0)
        mod_ab(A_C, B_C, VO_NMIX, 0, 8, 1)

    if stop == 0:
        setup_mod_mix()

    ee = stat[:, 0:16]
    tt = stat[:, 16:32]
    S.act(lambda e: e.activation(out=ee, in_=decbc, func=AF.Exp, scale=-1.0), reads=["decbc"], writes=["ee"])
    S.dve(lambda e: e.tensor_scalar(out=tt, in0=ee, scalar1=-1.0 / 7, scalar2=1.0 / 6, op0=ALU.mult, op1=ALU.add),
          reads=["ee"], writes=["tt"])
    for cf in (1.0 / 5, 1.0 / 4, 1.0 / 3, 1.0 / 2, 1.0):
        S.dve(lambda e: e.tensor_tensor(out=tt, in0=tt, in1=ee, op=ALU.mult), reads=["tt", "ee"], writes=["tt"])
        S.dve(lambda e, cf=cf: e.tensor_scalar(out=tt, in0=tt, scalar1=-1.0, scalar2=cf, op0=ALU.mult, op1=ALU.add),
              reads=["tt"], writes=["tt"])
    S.dve(lambda e: e.scalar_tensor_tensor(out=lgbc, in0=tt, scalar=-1.0, in1=ee, op0=ALU.mult, op1=ALU.mult),
          reads=["tt", "ee"], writes=["lgbc"])
    S.dve(lambda e: e.tensor_copy(out=lgsel[0:64, :], in_=lgbc[0:64, 0:8]), reads=["lgbc"], writes=["lgsel"])
    S.dve(lambda e: e.tensor_copy(out=lgsel[64:128, :], in_=lgbc[64:128, 8:16]), reads=["lgbc"], writes=["lgsel"])
    S.act(lambda e: e.activation(out=cdec, in_=lgsel, func=AF.Exp, scale=float(C)), reads=["lgsel"], writes=["cdec"])
    for d in range(2):
        S.act(lambda e, d=d: e.activation(out=kdec3[:, d, :], in_=lgbc[:, d * 8:(d + 1) * 8], func=AF.Exp, scale=csm(d)),
              reads=["lgbc", "consts"], writes=["kdec"])
        for t in range(2):
            S.act(lambda e, d=d, t=t: e.activation(out=ctxw4[:, t, d, :], in_=lgbc[:, d * 8:(d + 1) * 8], func=AF.Exp,
                                                   scale=csm(2 + 2 * d + t)),
                  reads=["lgbc", "consts"], writes=["ctxw"])
    S.dve(lambda e: e.tensor_scalar(out=kdec, in0=kdec, scalar1=K_SCALE, scalar2=None, op0=ALU.mult),
          reads=["kdec"], writes=["kdec"])
    S.dve(lambda e: e.tensor_scalar(out=ctxw, in0=ctxw, scalar1=K_SCALE, scalar2=None, op0=ALU.mult),
          reads=["ctxw"], writes=["ctxw"])
    for h in range(H):
        S.act(lambda e, h=h: e.activation(out=qdec3[:, h, :], in_=posq, func=AF.Exp, scale=lgsel[:, h:h + 1]),
              reads=["lgsel", "consts"], writes=["qdec"])
        S.dve(lambda e, h=h: e.tensor_scalar(out=mask3[:, h, :], in0=dpos, scalar1=lgbc[:, h:h + 1], scalar2=None,
                                             op0=ALU.mult), reads=["lgbc", "consts"], writes=["mask"])
        S.dve(lambda e, h=h: e.scalar_tensor_tensor(out=mask3[:, h, :], in0=dneg, scalar=lgbc[:, 8 + h:9 + h],
                                                    in1=mask3[:, h, :], op0=ALU.mult, op1=ALU.add),
              reads=["lgbc", "consts", "mask"], writes=["mask"])
    S.act(lambda e: e.activation(out=mask, in_=mask, func=AF.Exp), reads=["mask"], writes=["mask"])
    S.dve(lambda e: e.tensor_scalar(out=mask, in0=mask, scalar1=K_SCALE, scalar2=None, op0=ALU.mult),
          reads=["mask"], writes=["mask"])

    if stop == 0:
        for gi in range(4, 12):
            wg.append(wada_group(gi))
            wada_compute(gi, *wg[gi])
        modT_evac(24, 40)
        mod_ab(A_F, B_F, VO_NFFN, 24, 32, 0)
        return finish_debug([(modAB, 48), (lgbc, 16), (mask, 1024), (qdec, 1024), (gm_bc, 1024), (gf_bc, 1024),
                             (kdec, 16), (ctxw, 32), (cdec, 8)])
    R_PAIR.reset()
    xbuf = [R_PAIR.carve(4 * KB, F32) for _ in range(3)]
    junk = R_PAIR.carve(2 * KB)
    xnb = [R_PAIR.carve(2 * KB) for _ in range(2)]
    nctr = [0]

    def norm_stats(src_ap, src_key, junk, xnb):
        n = nctr[0]
        nctr[0] += 1
        ss = stat[:, 32 + (n % 4) * 2:33 + (n % 4) * 2]
        rs = stat[:, 33 + (n % 4) * 2:34 + (n % 4) * 2]
        sk = "nstat%d" % (n % 4)
        xn = xnb[n % len(xnb)]
        xk = "xn%d_%d" % (len(xnb), n % len(xnb))
        S.act(lambda e: e.activation(out=junk, in_=src_ap, func=AF.Square, accum_out=ss),
              reads=[src_key], writes=["junk", sk])
        S.dve(lambda e: e.tensor_scalar(out=rs, in0=ss, scalar1=1.0 / D, scalar2=EPS, op0=ALU.mult, op1=ALU.add),
              reads=[sk], writes=[sk + "r"])
        S.act(lambda e: e.activation(out=rs, in_=rs, func=AF.Sqrt), reads=[sk + "r"], writes=[sk + "r"])
        S.dve(lambda e: e.reciprocal(out=rs, in_=rs), reads=[sk + "r"], writes=[sk + "r"])
        S.dve(lambda e: e.tensor_scalar(out=xn, in0=src_ap, scalar1=rs, scalar2=None, op0=ALU.mult),
              reads=[src_key, sk + "r"], writes=[xk])
        return (n, xn, xk)

    ACT_K = (1, 4, 6)

    def norm_tr(st, ai, bi, dst3, dst_key, tcol, pbase=0, fused=True):
        n, xn, xk = st
        par = n % 2
        pbD, pbA = pbase + 2 * par, pbase + 2 * par + 1
        pTD = bank(pbD, BF16)[:, 0:640].rearrange("p (k t) -> p k t", t=128)
        pTA = bank(pbA, BF16)[:, 0:384].rearrange("p (k t) -> p k t", t=128)
        slot = {}
        na = nd = 0
        for k in range(8):
            if k in ACT_K:
                slot[k] = (pTA, pbA, na)
                na += 1
            else:
                slot[k] = (pTD, pbD, nd)
                nd += 1
        for k in range(8):
            pt_, pbk, sl = slot[k]
            S.pe(lambda e, k=k, pt_=pt_, sl=sl: e.transpose(out=pt_[:, sl, :], in_=xn[:, k * 128:(k + 1) * 128], identity=ident),
                 reads=[xk, "ident"], writes=[PK(pbk)])
        for k in range(8):
            o = dst3[:, k, tcol * 128:(tcol + 1) * 128]
            pt_, pbk, sl = slot[k]
            if not fused:
                if k not in ACT_K:
                    S.dve(lambda e, o=o, pt_=pt_, sl=sl: e.tensor_copy(out=o, in_=pt_[:, sl, :]),
                          reads=[PK(pbk)], writes=[dst_key(k, tcol)])
                else:
                    S.act(lambda e, o=o, pt_=pt_, sl=sl: e.activation(out=o, in_=pt_[:, sl, :], func=AF.Copy),
                          reads=[PK(pbk)], writes=[dst_key(k, tcol)])
            elif k not in ACT_K:
                S.dve(lambda e, k=k, o=o, pt_=pt_, sl=sl: e.tensor_scalar(out=o, in0=pt_[:, sl, :], scalar1=modAB3[:, ai, k:k + 1],
                                                                          scalar2=modAB3[:, bi, k:k + 1], op0=ALU.mult, op1=ALU.add),
                      reads=[PK(pbk), "modAB%d" % ai, "modAB%d" % bi], writes=[dst_key(k, tcol)])
            else:
                S.act(lambda e, k=k, o=o, pt_=pt_, sl=sl: e.activation(out=o, in_=pt_[:, sl, :], func=AF.Identity,
                                                                       scale=modAB3[:, ai, k:k + 1], bias=modAB3[:, bi, k:k + 1]),
                      reads=[PK(pbk), "modAB%d" % ai, "modAB%d" % bi], writes=[dst_key(k, tcol)])

    def hT_key(k, i):
        return "XA_%d_%d" % (k, i)

    pendA = []
    xnb = xnb + [arena[:, R_YP.start // 2:(R_YP.start + 2 * KB) // 2]]
    xbuf = xbuf + [arena[:, (R_YP.start + (4 + 4 * i) * KB) // 2:(R_YP.start + (8 + 4 * i) * KB) // 2].bitcast(F32) for i in range(3)]
    for t in range(2 + NT):
        sbi = t % len(xbuf)
        xb = xbuf[sbi]
        if t < 2:
            S.dma("sp", "xb%d" % sbi, lambda e, xb=xb, t=t: e.dma_start(out=xb, in_=ctx_d[t * 128:(t + 1) * 128, :]),
                  writes=["xb%d" % sbi])
            args = (A_C, B_C, hcT3, (lambda k, tc: "hcT%d" % k), t)
        else:
            i = t - 2
            S.dma("sp", "xb%d" % sbi, lambda e, xb=xb, i=i: e.dma_start(out=xb, in_=x_d[i * 128:(i + 1) * 128, :]),
                  writes=["xb%d" % sbi])
            args = (A_M, B_M, hT, hT_key, i)
        st = norm_stats(xb, "xb%d" % sbi, junk, xnb)
        pendA.append((st,) + args)
        if len(pendA) > 2:
            norm_tr(*pendA.pop(0), fused=False)
    while pendA:
        norm_tr(*pendA.pop(0), fused=False)
    setup_mod_mix()
    for hf in range(2):
        for k in range(8):
            S.dve(lambda e, k=k, hf=hf: e.tensor_scalar(out=hT[:, k, hf * 1024:(hf + 1) * 1024], in0=hT[:, k, hf * 1024:(hf + 1) * 1024],
                                                        scalar1=modAB3[:, A_M, k:k + 1], scalar2=modAB3[:, B_M, k:k + 1],
                                                        op0=ALU.mult, op1=ALU.add),
                  reads=[hT_key(k, i) for i in range(hf * 8, hf * 8 + 8)] + ["modAB%d" % A_M, "modAB%d" % B_M],
                  writes=[hT_key(k, i) for i in range(hf * 8, hf * 8 + 8)])
    for k in range(8):
        S.dve(lambda e, k=k: e.tensor_scalar(out=hcT3[:, k, :], in0=hcT3[:, k, :], scalar1=modAB3[:, A_C, k:k + 1],
                                             scalar2=modAB3[:, B_C, k:k + 1], op0=ALU.mult, op1=ALU.add),
              reads=["hcT%d" % k, "modAB%d" % A_C, "modAB%d" % B_C], writes=["hcT%d" % k])
    wlate = arena[:, (R_M.start + 24 * KB) // 2:(R_M.start + 32 * KB) // 2].rearrange("p (k n) -> p k n", n=512)
    WL_ALIAS = ["ysb0", "ysb1", "ysb2", "ysq"]

    def wada_late_load(gi):
        S.dma("pool", "wlate", lambda e: e.dma_start(out=wlate, in_=wada_d[:, :, gi * 512:(gi + 1) * 512]),
              writes=["wlate"] + WL_ALIAS)

    def wada_late_compute(gi):
        rd = ["wlate"] + WL_ALIAS
        if gi in (4, 5, 10, 11):
            dst = gm_bc if gi in (4, 5) else gf_bc
            dk = "gm_bc" if gi in (4, 5) else "gf_bc"
            half = gi % 2 if gi in (4, 5) else (gi - 10)
            for k in range(8):
                S.pe(lambda e, k=k: e.matmul(bank(7), lhsT=scbc3[:, k, :], rhs=wlate[:, k, :], start=(k == 0), stop=(k == 7)),
                     reads=rd + ["scbc"], writes=[PK(7)])
            S.dve(lambda e: e.tensor_tensor(out=dst[:, half * 512:(half + 1) * 512], in0=bank(7),
                                            in1=dst[:, half * 512:(half + 1) * 512], op=ALU.add),
                  reads=[PK(7), dk], writes=[dk])
        else:
            for jj in range(4):
                j = gi * 4 + jj
                for k in range(8):
                    S.pe(lambda e, k=k, jj=jj, j=j: e.matmul(bank(7)[:, 2 * j:2 * j + 2], lhsT=wlate[:, k, jj * 128:(jj + 1) * 128],
                                                             rhs=scT3[:, k, :], start=(k == 0), stop=(k == 7)),
                         reads=rd + ["scT"], writes=[PK(7)])
            modT_evac(gi * 4, gi * 4 + 4)
    hcT_keys = ["hcT%d" % k for k in range(8)]

    if stop == 1:
        return finish_debug([(hcT, 2048), (hT.rearrange("p k t -> p (k t)")[:, 0:8192], 8192)])

    def load_pair(p):
        buf = wpair_buf[p % 2]
        extra = [S.fence()] if p < 2 else []
        S.dma("pool", "wpair%d" % (p % 2), lambda e, buf=buf, p=p: e.dma_start(out=buf, in_=wpairs_d[p]),
              writes=["wpair%d" % (p % 2)] + extra)

    wu_sb = R_PAIR.carve(8 * KB).rearrange("p (k n) -> p k n", n=512)
    NPST = 9
    pstage = [R_PAIR.carve(4 * KB).rearrange("p (b t) -> p b t", t=512) for _ in range(NPST)]
    S.dma("pool", "wu", lambda e: e.dma_start(out=wu_sb, in_=wu_d), writes=["wu"])
    load_pair(0)
    for i in range(NT):
        pb = 2 + (i % 2)
        for k in range(8):
            S.pe(lambda e, k=k, i=i, pb=pb: e.matmul(bank(pb), lhsT=hT[:, k, i * 128:(i + 1) * 128], rhs=wu_sb[:, k, :],
                                                     start=(k == 0), stop=(k == 7)),
                 reads=[hT_key(k, i), "wu"], writes=[PK(pb)])
        if i % 2 == 0:
            S.act(lambda e, i=i, pb=pb: e.activation(out=u_tok[:, i, :], in_=bank(pb), func=AF.Copy),
                  reads=[PK(pb)], writes=["u%d" % i])
        else:
            S.dve(lambda e, i=i, pb=pb: e.tensor_copy(out=u_tok[:, i, :], in_=bank(pb)),
                  reads=[PK(pb)], writes=["u%d" % i])
    pst_n = [0]
    resident = {}

    def pool_head(pctr, g, j):
        lst = pool_plan[g][j]
        pd = 4 + (pctr % 2)
        dsb = dT_sb[pctr % 2]
        dk = "dT%d" % (pctr % 2)
        runs = []
        for ent in lst:
            if runs and len(runs[-1]) < 4 and runs[-1][-1][1] + 1 == ent[1]:
                runs[-1].append(ent)
            else:
                runs.append([ent])
        pos = 0
        for grp in runs:
            b0 = grp[0][1]
            nb = len(grp)
            assert all(grp[q][1] == b0 + q for q in range(nb))
            hit = resident.get((b0, nb))
            if hit is not None and pst_n[0] - hit < NPST:
                st = pstage[hit % NPST]
                sk = "pst%d" % (hit % NPST)
            else:
                st = pstage[pst_n[0] % NPST]
                sk = "pst%d" % (pst_n[0] % NPST)
                resident[(b0, nb)] = pst_n[0]
                pst_n[0] += 1
                S.dma("sp", sk, lambda e: e.dma_start(out=st[:, 0:nb, :], in_=pblk_d[:, b0:b0 + nb, :]), writes=[sk])
            for q, (t, bi_) in enumerate(grp):
                first = (pos == 0)
                last = (pos == len(lst) - 1)
                pos += 1
                S.pe(lambda e, t=t, q=q, first=first, last=last: e.matmul(
                    bank(pd), lhsT=u_tok[:, t, g * 128:(g + 1) * 128], rhs=st[:, q, :], start=first, stop=last),
                    reads=["u%d" % t, sk], writes=[PK(pd)])
        S.dve(lambda e: e.tensor_copy(out=dsb, in_=bank(pd)), reads=[PK(pd)], writes=[dk])

    def pool_tail(pctr, g, j):
        py = 6 + (pctr % 2)
        dsb = dT_sb[pctr % 2]
        dk = "dT%d" % (pctr % 2)
        S.pe(lambda e: e.matmul(bank(py), lhsT=wpool3[:, g, :], rhs=dsb, start=True, stop=True),
             reads=[dk, "wpool"], writes=[PK(py)])
        S.act(lambda e: e.activation(out=ypT[:, g, j * 512:(j + 1) * 512], in_=bank(py), func=AF.Copy,
                                     scale=vecs[:, VO_PSC + g:VO_PSC + g + 1]),
              reads=[PK(py), "vecs"], writes=["ypT"])

    pblocks = [(g, j) for g in range(4) for j in range(4)]
    pool_head(0, *pblocks[0])
    for c_ in range(len(pblocks)):
        if c_ + 1 < len(pblocks):
            pool_head(c_ + 1, *pblocks[c_ + 1])
        pool_tail(c_, *pblocks[c_])

    if stop == 2:
        return finish_debug([(ypT.rearrange("p g t -> p (g t)"), 8192)])
    R_PAIR.reset()
    kT2 = R_PAIR.carve(4 * KB)
    qTz = R_PAIR.carve(8 * KB).rearrange("p (n h t) -> p n h t", h=2, t=128)
    zf = S.fence()
    S.dve(lambda e: e.memset(qTz[64:128, :, 0, :], 0.0), writes=["qTz_zero", zf])
    S.dve(lambda e: e.memset(qTz[0:64, :, 1, :], 0.0), writes=["qTz_zero", zf])
    q2 = R_PAIR.carve(8 * KB).rearrange("p (h t) -> p h t", t=L)
    kfb = R_PAIR.carve(8 * KB).rearrange("p (i d c) -> p i d c", d=2, c=128)
    v_tok = R_PAIR.carve(8 * KB).rearrange("p (i c) -> p i c", c=256)
    sg_tok = R_PAIR.carve(8 * KB).rearrange("p (i c) -> p i c", c=256)
    s_all = R_PAIR.carve(8 * KB).rearrange("p (n h v) -> p n h v", h=2, v=128)
    rot = [R_PAIR.carve(1 * KB).rearrange("p (t a c) -> p t a c", t=2, c=64) for _ in range(2)]
    qdup = R_PAIR.carve(1 * KB).rearrange("p (t a u c) -> p t a u c", t=2, u=2, c=64)
    _rt0 = R_PAIR.off
    rtmp = [R_PAIR.carve(1 * KB, F32).rearrange("p (t a f) -> p t a f", t=2, f=32) for _ in range(4)]
    _rt1 = R_PAIR.off
    R_PAIR.off = _rt0
    kcw = R_PAIR.carve(1 * KB).rearrange("p (t d c) -> p t d c", d=2, c=128)
    vc = R_PAIR.carve(1 * KB).rearrange("p (t c) -> p t c", c=256)
    R_PAIR.off = _rt1
    pmat = [R_PAIR.carve(1 * KB).rearrange("p (c h t) -> p c h t", c=2, t=128) for _ in range(2)]
    ygb = [R_PAIR.carve(1 * KB).rearrange("p (c h v) -> p c h v", c=2, v=128) for _ in range(2)]
    _rm = R_M.start + 24 * KB
    y_sb = [arena[:, (_rm + i * 2 * KB) // 2:(_rm + (i + 1) * 2 * KB) // 2].bitcast(F32).rearrange("p (c h v) -> p c h v", c=2, v=128)
            for i in range(3)]
    ysq = arena[:, (_rm + 6 * KB) // 2:(_rm + 8 * KB) // 2].bitcast(F32).rearrange("p (c h v) -> p c h v", c=2, v=128)
    sfp3 = sfp[:, 0:256].rearrange("p (h v) -> p h v", v=128)
    gst = sfp[:, 256:384]

    def ygT_key(k, i):
        return "XB_%d_%d" % (k, i)

    def make_pair(p):
        wp = wpair_buf[p % 2]
        wk = "wpair%d" % (p % 2)

        def ctx_pe():
            if 1 <= p + 1 < 4:
                load_pair(p + 1)
            for t in range(2):
                pb = 6 + t
                for k in range(8):
                    S.pe(lambda e, k=k, t=t, pb=pb: e.matmul(bank(pb)[:, 0:384], lhsT=hcT3[:, k, t * 128:(t + 1) * 128],
                                                             rhs=wp[:, k, 128:512], start=(k == 0), stop=(k == 7)),
                         reads=["hcT%d" % k, wk], writes=[PK(pb)])

        def ctx_rest():
            for t in range(2):
                pb = 6 + t
                for d in range(2):
                    S.dve(lambda e, t=t, d=d, pb=pb: e.tensor_tensor(
                        out=kcw[:, t, d, :].rearrange("p (h c) -> p h c", c=64),
                        in0=bank(pb)[:, 0:128].rearrange("p (h c) -> p h c", c=64),
                        in1=ctxw4[:, t, d, 2 * p:2 * p + 2].unsqueeze(2).to_broadcast([128, 2, 64]), op=ALU.mult),
                        reads=[PK(pb), "ctxw"], writes=["kcw"])
                S.dve(lambda e, t=t, pb=pb: e.tensor_copy(out=vc[:, t, :], in_=bank(pb)[:, 128:384]),
                      reads=[PK(pb)], writes=["vc"])
            pS = bank(1)[:, 0:256].rearrange("p (h v) -> p h v", v=128)
            for h2 in range(2):
                for d in range(2):
                    for t in range(2):
                        S.pe(lambda e, h2=h2, d=d, t=t: e.matmul(pS[d * 64:(d + 1) * 64, h2, :],
                                                                 lhsT=kcw[:, t, d, h2 * 64:(h2 + 1) * 64],
                                                                 rhs=vc[:, t, h2 * 128:(h2 + 1) * 128],
                                                                 start=(t == 0), stop=(t == 1)),
                             reads=["kcw", "vc"], writes=["ps1a", "ps1b"])
            S.dve(lambda e: e.tensor_copy(out=sfp3, in_=pS), reads=["ps1a", "ps1b"], writes=["sfp"])
            S.dve(lambda e: e.tensor_copy(out=s_all[0:64, 0, :, :], in_=pS[0:64, :, :]),
                  reads=["ps1a", "ps1b"], writes=["sall_f0"])
            S.dve(lambda e: e.tensor_copy(out=s_all[64:128, NT - 1, :, :], in_=pS[64:128, :, :]),
                  reads=["ps1a", "ps1b"], writes=["sall_b%d" % (NT - 1)])


        def sb_proj(it, i, pe_only=False):
            par = it % 2
            pq = 2 + par
            for t in range(2):
                for k in range(8):
                    S.pe(lambda e, k=k, t=t: e.matmul(bank(pq)[:, t * 256:(t + 1) * 256], lhsT=hT[:, k, (i + t) * 128:(i + t + 1) * 128],
                                                      rhs=wp[:, k, 0:256], start=(k == 0), stop=(k == 7)),
                         reads=[hT_key(k, i + t), wk], writes=[PK(pq)])
            for t in range(2):
                for k in range(8):
                    S.pe(lambda e, k=k, t=t: e.matmul(bank(4 + t), lhsT=hT[:, k, (i + t) * 128:(i + t + 1) * 128],
                                                      rhs=wp[:, k, 256:768], start=(k == 0), stop=(k == 7)),
                         reads=[hT_key(k, i + t), wk], writes=[PK(4 + t)])
            if not pe_only:
                sb_proj_evac(it, i)

        def sb_proj_evac(it, i):
            vg = ps[:, 4 * 512:6 * 512].rearrange("p (t c) -> p t c", t=2)
            S.act(lambda e: e.activation(out=v_tok[:, i:i + 2, :], in_=vg[:, :, 0:256], func=AF.Copy),
                  reads=[PK(4), PK(5)], writes=["v%d" % i, "v%d" % (i + 1)])
            S.act(lambda e: e.activation(out=sg_tok[:, i:i + 2, :], in_=vg[:, :, 256:512], func=AF.Silu),
                  reads=[PK(4), PK(5)], writes=["sg%d" % i, "sg%d" % (i + 1)])

        def sb_rope(it, i):
            par = it % 2
            pq = 2 + par
            qk4 = bank(pq).rearrange("p (t a c) -> p t a c", t=2, c=64)
            t1 = qk4[:, :, :, 0:32]
            t2 = qk4[:, :, :, 32:64]
            cs = cos3[:, i:i + 2, :].unsqueeze(2).to_broadcast([128, 2, 4, 32])
            sn = sin3[:, i:i + 2, :].unsqueeze(2).to_broadcast([128, 2, 4, 32])
            rt = rot[par]
            rk = "rot%d" % par
            ta, tb_, tc_, td = rtmp
            S.dve(lambda e: e.tensor_tensor(out=ta, in0=t1, in1=cs, op=ALU.mult), reads=[PK(pq), "consts"], writes=["rta", "kcw", "vc"])
            S.dve(lambda e: e.tensor_tensor(out=tb_, in0=t2, in1=sn, op=ALU.mult), reads=[PK(pq), "consts"], writes=["rtb", "kcw", "vc"])
            S.dve(lambda e: e.tensor_tensor(out=tc_, in0=t1, in1=sn, op=ALU.mult), reads=[PK(pq), "consts"], writes=["rtc", "kcw", "vc"])
            S.dve(lambda e: e.tensor_tensor(out=td, in0=t2, in1=cs, op=ALU.mult), reads=[PK(pq), "consts"], writes=["rtd", "kcw", "vc"])
            S.dve(lambda e: e.tensor_tensor(out=rt[:, :, :, 0:32], in0=ta, in1=tb_, op=ALU.subtract),
                  reads=["rta", "rtb"], writes=[rk])
            S.dve(lambda e: e.tensor_tensor(out=rt[:, :, :, 32:64], in0=tc_, in1=td, op=ALU.add),
                  reads=["rtc", "rtd"], writes=[rk])
            for t in range(2):
                S.act(lambda e, t=t: e.activation(out=qdup[:, t, :, :, :], in_=rt[:, t, 0:2, :].unsqueeze(2).to_broadcast([128, 2, 2, 64]),
                                                  func=AF.Copy), reads=[rk], writes=["qdup"])
            for d in range(2):
                S.dve(lambda e, d=d: e.tensor_tensor(
                    out=kfb[:, i:i + 2, d, :].rearrange("p t (h c) -> p t h c", c=64), in0=rt[:, :, 2:4, :],
                    in1=kdec3[:, d, 2 * p:2 * p + 2].unsqueeze(1).unsqueeze(3).to_broadcast([128, 2, 2, 64]), op=ALU.mult),
                    reads=[rk, "kdec"], writes=["kfb%d" % i, "kfb%d" % (i + 1)])

        def sb_tr(it, i):
            par = it % 2
            rt = rot[par]
            rk = "rot%d" % par
            sfx = "ab"[par]
            pTA = bank(0, BF16)[:, par * 512:(par + 1) * 512].rearrange("p (t a x) -> p t a x", t=2, x=128)
            pTD = bank(1, BF16)[:, par * 512:(par + 1) * 512].rearrange("p (t h x) -> p t h x", t=2, x=128)
            for t in range(2):
                S.pe(lambda e, t=t: e.transpose(out=pTA[:, t, 0, :], in_=rt[:, t, 0:2, :].rearrange("p a c -> p (a c)"), identity=ident),
                     reads=[rk, "ident"], writes=["ps0" + sfx])
                S.pe(lambda e, t=t: e.transpose(out=pTA[:, t, 1, :], in_=rt[:, t, 2:4, :].rearrange("p a c -> p (a c)"), identity=ident),
                     reads=[rk, "ident"], writes=["ps0" + sfx])
                for h2 in range(2):
                    S.pe(lambda e, t=t, h2=h2: e.transpose(out=pTD[:, t, h2, :], in_=qdup[:, t, h2, :, :].rearrange("p u c -> p (u c)"),
                                                           identity=ident),
                         reads=["qdup", "ident"], writes=["ps1" + sfx])
            qk_keys = ["qkT%d" % i, "qkT%d" % (i + 1)]
            S.act(lambda e: e.activation(out=kT2[:, i * 128:(i + 2) * 128].rearrange("p (t x) -> p t x", t=2), in_=pTA[:, :, 1, :], func=AF.Copy),
                  reads=["ps0" + sfx], writes=qk_keys)
            S.act(lambda e: e.activation(out=qTz[0:64, i:i + 2, 0, :], in_=pTA[0:64, :, 0, :], func=AF.Copy),
                  reads=["ps0" + sfx, "qTz_zero"], writes=qk_keys)
            S.act(lambda e: e.activation(out=qTz[64:128, i:i + 2, 1, :], in_=pTA[64:128, :, 0, :], func=AF.Copy),
                  reads=["ps0" + sfx, "qTz_zero"], writes=qk_keys)
            S.dve(lambda e: e.tensor_tensor(out=q2[:, :, i * 128:(i + 2) * 128].rearrange("p h (t x) -> p t h x", t=2), in0=pTD,
                                            in1=qdec3[:, 2 * p:2 * p + 2, :].unsqueeze(1).to_broadcast([128, 2, 2, 128]), op=ALU.mult),
                  reads=["ps1" + sfx, "qdec"], writes=["q2_%d" % i, "q2_%d" % (i + 1)])

        def scan_pe(j):
            sfx = "ab"[j % 2]
            pD = bank(6)[:, (j % 2) * 256:(j % 2) * 256 + 256].rearrange("p (h v) -> p h v", v=128)
            nf, nb = j, NT - 1 - j
            for h2 in range(2):
                S.pe(lambda e, h2=h2: e.matmul(pD[0:64, h2, :], lhsT=kfb[:, nf, 0, h2 * 64:(h2 + 1) * 64],
                                               rhs=v_tok[:, nf, h2 * 128:(h2 + 1) * 128], start=True, stop=True),
                     reads=["kfb%d" % nf, "v%d" % nf], writes=["ps6" + sfx])
                S.pe(lambda e, h2=h2: e.matmul(pD[64:128, h2, :], lhsT=kfb[:, nb, 1, h2 * 64:(h2 + 1) * 64],
                                               rhs=v_tok[:, nb, h2 * 128:(h2 + 1) * 128], start=True, stop=True),
                     reads=["kfb%d" % nb, "v%d" % nb], writes=["ps6" + sfx])

        def scan_ew(j):
            sfx = "ab"[j % 2]
            pD = bank(6)[:, (j % 2) * 256:(j % 2) * 256 + 256].rearrange("p (h v) -> p h v", v=128)
            nf, nb = j, NT - 1 - j
            for h2 in range(2):
                S.dve(lambda e, h2=h2: e.scalar_tensor_tensor(
                    out=sfp3[:, h2, :], in0=sfp3[:, h2, :], scalar=cdec[:, 2 * p + h2:2 * p + h2 + 1], in1=pD[:, h2, :],
                    op0=ALU.mult, op1=ALU.add), reads=["sfp", "ps6" + sfx, "cdec"], writes=["sfp"])
            if os.environ.get("KDBG_CAST", "dve") == "dve":
                S.dve(lambda e: e.tensor_copy(out=s_all[0:64, nf + 1, :, :], in_=sfp3[0:64, :, :]),
                      reads=["sfp"], writes=["sall_f%d" % (nf + 1)])
                S.dve(lambda e: e.tensor_copy(out=s_all[64:128, nb - 1, :, :], in_=sfp3[64:128, :, :]),
                      reads=["sfp"], writes=["sall_b%d" % (nb - 1)])
            else:
                S.act(lambda e: e.activation(out=s_all[0:64, nf + 1, :, :], in_=sfp3[0:64, :, :], func=AF.Copy),
                      reads=["sfp"], writes=["sall_f%d" % (nf + 1)])
                S.act(lambda e: e.activation(out=s_all[64:128, nb - 1, :, :], in_=sfp3[64:128, :, :], func=AF.Copy),
                      reads=["sfp"], writes=["sall_b%d" % (nb - 1)])

        def scan_step(j):
            scan_pe(j)
            scan_ew(j)

        class OutStep:
            def __init__(self, it, n0):
                self.it, self.n0 = it, n0
                par = it % 2
                self.psc = 2 + par
                self.pyb = 4 + par
                self.pS4 = bank(self.psc).rearrange("p (c h t) -> p c h t", c=2, t=128)
                self.pY4 = bank(self.pyb).rearrange("p (c h v) -> p c h v", c=2, v=128)
                self.pm = pmat[par]
                self.pmk = "pmat%d" % par
                self.ysb = y_sb[it % 3]
                self.yk = "ysb%d" % (it % 3)
                self.gs = gst[:, (it % 4) * 32:(it % 4) * 32 + 32]
                self.gk = "gst%d" % (it % 4)
                self.yg = ygb[par]
                self.ygk = "ygb%d" % par
                self.sfx = "ab"[par]
                self.pT4 = bank(0, BF16)[:, par * 512:(par + 1) * 512].rearrange("p (h c t) -> p h c t", h=2, t=128)

            def scores(o):
                for c in range(2):
                    n = o.n0 + c
                    tsl = slice(n * 128, (n + 1) * 128)
                    S.pe(lambda e, c=c, n=n, tsl=tsl: e.matmul(bank(o.psc)[:, c * 256:(c + 1) * 256], lhsT=kT2[:, tsl],
                                                               rhs=qTz[:, n, :, :].rearrange("p h t -> p (h t)"), start=True, stop=True),
                         reads=["qkT%d" % n, "qTz_zero"], writes=[PK(o.psc)])

            def mask(o):
                S.dve(lambda e: e.tensor_tensor(out=o.pm, in0=o.pS4,
                                                in1=mask3[:, 2 * p:2 * p + 2, :].unsqueeze(1).to_broadcast([128, 2, 2, 128]),
                                                op=ALU.mult), reads=[PK(o.psc), "mask"], writes=[o.pmk])

            def av(o):
                for c in range(2):
                    n = o.n0 + c
                    tsl = slice(n * 128, (n + 1) * 128)
                    for h2 in range(2):
                        S.pe(lambda e, c=c, n=n, h2=h2: e.matmul(o.pY4[:, c, h2, :], lhsT=o.pm[:, c, h2, :],
                                                                 rhs=v_tok[:, n, h2 * 128:(h2 + 1) * 128], start=True, stop=False),
                             reads=[o.pmk, "v%d" % n], writes=[PK(o.pyb)])
                        S.pe(lambda e, c=c, n=n, h2=h2, tsl=tsl: e.matmul(o.pY4[:, c, h2, :], lhsT=q2[:, h2, tsl],
                                                                          rhs=s_all[:, n, h2, :], start=False, stop=True),
                             reads=["q2_%d" % n, "sall_f%d" % n, "sall_b%d" % n], writes=[PK(o.pyb)])

            def copy_sq(o):
                S.act(lambda e: e.activation(out=o.ysb, in_=o.pY4, func=AF.Copy), reads=[PK(o.pyb)], writes=[o.yk])
                for c in range(2):
                    for h2 in range(2):
                        q_ = c * 2 + h2
                        S.act(lambda e, c=c, h2=h2, q_=q_: e.activation(out=ysq[:, c, h2, :], in_=o.pY4[:, c, h2, :], func=AF.Square,
                                                                        accum_out=o.gs[:, 4 + q_:5 + q_]),
                              reads=[PK(o.pyb)], writes=["ysq", o.gk + "q"])

            def reduce(o):
                S.dve(lambda e: e.tensor_reduce(out=o.gs[:, 0:4], in_=o.ysb.rearrange("p c h v -> p (c h) v"), axis=AX.X, op=ALU.add),
                      reads=[o.yk], writes=[o.gk + "s"])

            def var(o):
                S.dve(lambda e: e.tensor_tensor(out=o.gs[:, 8:12], in0=o.gs[:, 0:4], in1=o.gs[:, 0:4], op=ALU.mult),
                      reads=[o.gk + "s"], writes=[o.gk + "ss"])
                S.dve(lambda e: e.scalar_tensor_tensor(out=o.gs[:, 12:16], in0=o.gs[:, 4:8], scalar=float(DV), in1=o.gs[:, 8:12],
                                                       op0=ALU.mult, op1=ALU.subtract),
                      reads=[o.gk + "q", o.gk + "ss"], writes=[o.gk + "v"])

            def sqrt(o):
                S.act(lambda e: e.activation(out=o.gs[:, 16:20], in_=o.gs[:, 12:16], func=AF.Sqrt, bias=epsb[:, 0:1]),
                      reads=[o.gk + "v", "epsb"], writes=[o.gk + "sd"])

            def rstd(o):
                S.dve(lambda e: e.reciprocal(out=o.gs[:, 24:28], in_=o.gs[:, 16:20]), reads=[o.gk + "sd"], writes=[o.gk + "r"])
                S.dve(lambda e: e.scalar_tensor_tensor(out=o.gs[:, 28:32], in0=o.gs[:, 0:4], scalar=-1.0 / DV, in1=o.gs[:, 24:28],
                                                       op0=ALU.mult, op1=ALU.mult),
                      reads=[o.gk + "s", o.gk + "r"], writes=[o.gk + "nb"])

            def normalize(o):
                for c in range(2):
                    for h2 in range(2):
                        q_ = c * 2 + h2
                        S.act(lambda e, c=c, h2=h2, q_=q_: e.activation(out=o.yg[:, c, h2, :], in_=o.ysb[:, c, h2, :], func=AF.Identity,
                                                                        scale=o.gs[:, 24 + q_:25 + q_], bias=o.gs[:, 28 + q_:29 + q_]),
                              reads=[o.yk, o.gk + "r", o.gk + "nb"], writes=[o.ygk])

            def gate(o):
                S.dve(lambda e: e.tensor_tensor(out=o.yg.rearrange("p c h v -> p c (h v)"),
                                                in0=o.yg.rearrange("p c h v -> p c (h v)"),
                                                in1=sg_tok[:, o.n0:o.n0 + 2, :], op=ALU.mult),
                      reads=[o.ygk, "sg%d" % o.n0, "sg%d" % (o.n0 + 1)], writes=[o.ygk])

            def transposes(o):
                for c in range(2):
                    for h2 in range(2):
                        S.pe(lambda e, c=c, h2=h2: e.transpose(out=o.pT4[:, h2, c, :], in_=o.yg[:, c, h2, :], identity=ident),
                             reads=[o.ygk, "ident"], writes=["ps0" + o.sfx])

            def evac(o):
                for h2 in range(2):
                    kk = 2 * p + h2
                    S.act(lambda e, h2=h2, kk=kk: e.activation(out=ygT[:, kk, o.n0 * 128:(o.n0 + 2) * 128],
                                                               in_=o.pT4[:, h2, :, :].rearrange("p c t -> p (c t)"), func=AF.Copy,
                                                               scale=gnw128[:, kk:kk + 1]),
                          reads=["ps0" + o.sfx, "gnw128"], writes=[ygT_key(kk, o.n0), ygT_key(kk, o.n0 + 1)])

        order_b = []
        for j in range(NT // 4):
            order_b += [2 * j, NT - 2 - 2 * j]
        def head_pe():
            sb_proj(0, order_b[0], pe_only=True)

        def head_rest():
            sb_proj_evac(0, order_b[0])
            sb_rope(0, order_b[0])

        def body(nxt):
            wada_late_load(4 + 2 * p)
            for it, i in enumerate(order_b):
                if it + 1 < len(order_b):
                    sb_proj(it + 1, order_b[it + 1])
                if it == 2:
                    wada_late_compute(4 + 2 * p)
                    wada_late_load(5 + 2 * p)
                if it == 5:
                    wada_late_compute(5 + 2 * p)
                    if p == 2:
                        mod_ab(A_F, B_F, VO_NFFN, 24, 32, 0)
                sb_tr(it, i)
                if it + 1 < len(order_b):
                    sb_rope(it + 1, order_b[it + 1])
                if it % 2 == 1:
                    scan_step(it - 1)
                    scan_step(it)
            order_o = [6, 8, 4, 10, 2, 12, 0, 14]
            NO = len(order_o)
            steps = [OutStep(it, order_o[it]) for it in range(NO)]

            def g(k):
                return steps[k] if 0 <= k < NO else None

            scan_step(NT // 2)
            def iteration(it):
                    a_, b_, c_, d_ = g(it), g(it - 1), g(it - 2), g(it - 3)
                    sj = NT // 2 + 1 + it if NT // 2 + 1 + it <= NT - 2 else None
                    ordm = os.environ.get("KDBG_ORD", "m1c")
                    if ordm == "r7":
                        if c_:
                            c_.var(); c_.sqrt(); c_.rstd()
                        if d_:
                            d_.normalize(); d_.gate(); d_.transposes(); d_.evac()
                        if b_:
                            b_.copy_sq(); b_.reduce()
                        if sj is not None:
                            scan_step(sj)
                        if a_:
                            a_.scores(); a_.mask(); a_.av()
                        return
                    if ordm == "m1":
                        if a_:
                            a_.scores()
                        if sj is not None:
                            scan_pe(sj)
                        if c_:
                            c_.var(); c_.sqrt(); c_.rstd()
                        if d_:
                            d_.normalize(); d_.gate(); d_.transposes(); d_.evac()
                        if b_:
                            b_.copy_sq(); b_.reduce()
                        if sj is not None:
                            scan_ew(sj)
                        if a_:
                            a_.mask(); a_.av()
                        return
                    if ordm == "m1c":
                        if a_:
                            a_.scores()
                        if sj is not None:
                            scan_pe(sj)
                        if c_:
                            c_.var(); c_.sqrt()
                        if sj is not None:
                            scan_ew(sj)
                        if c_:
                            c_.rstd()
                        if d_:
                            d_.normalize()
                        if b_:
                            b_.copy_sq()
                        if d_:
                            d_.gate(); d_.transposes()
                        if b_:
                            b_.reduce()
                        if d_:
                            d_.evac()
                        if a_:
                            a_.mask(); a_.av()
                        return
                    if ordm == "m1b":
                        if a_:
                            a_.scores()
                        if sj is not None:
                            scan_pe(sj)
                        if c_:
                            c_.var(); c_.sqrt(); c_.rstd()
                        if d_:
                            d_.normalize()
                        if b_:
                            b_.copy_sq()
                        if d_:
                            d_.gate(); d_.transposes()
                        if b_:
                            b_.reduce()
                        if d_:
                            d_.evac()
                        if sj is not None:
                            scan_ew(sj)
                        if a_:
                            a_.mask(); a_.av()
                        return
                    if ordm in ("m3", "m4"):
                        if a_:
                            a_.scores()
                        if sj is not None:
                            scan_pe(sj)
                        if c_:
                            c_.var(); c_.sqrt(); c_.rstd()
                        if ordm == "m4" and a_:
                            a_.mask(); a_.av()
                        if d_:
                            d_.normalize(); d_.gate(); d_.transposes(); d_.evac()
                        if ordm == "m3" and a_:
                            a_.mask(); a_.av()
                        if b_:
                            b_.copy_sq(); b_.reduce()
                        if sj is not None:
                            scan_ew(sj)
                        return
                    if ordm == "m2":
                        if a_:
                            a_.scores()
                        if sj is not None:
                            scan_pe(sj)
                        if c_:
                            c_.var(); c_.sqrt()
                        if a_:
                            a_.mask(); a_.av()
                        if c_:
                            c_.rstd()
                        if d_:
                            d_.normalize(); d_.gate(); d_.transposes(); d_.evac()
                        if b_:
                            b_.copy_sq(); b_.reduce()
                        if sj is not None:
                            scan_ew(sj)
                        return
                    if a_:
                        a_.scores()
                    if sj is not None:
                        scan_pe(sj)
                    if c_:
                        c_.var()
                        c_.sqrt()
                    if d_:
                        d_.normalize()
                    if a_:
                        a_.mask()
                        a_.av()
                    if c_:
                        c_.rstd()
                    if sj is not None:
                        scan_ew(sj)
                    if b_:
                        b_.copy_sq()
                    if d_:
                        d_.gate()
                        d_.transposes()
                    if b_:
                        b_.reduce()
                    if d_:
                        d_.evac()

            for it in range(NO + 3):
                iteration(it)
                if nxt is not None and it == NO:
                    nxt.ctx_pe()
                if nxt is not None and it == NO + 1:
                    nxt.head_pe()
            if nxt is not None:
                nxt.ctx_rest()
                nxt.head_rest()

        class _P:
            pass
        o_ = _P()
        o_.ctx_pe, o_.ctx_rest, o_.head_pe, o_.head_rest, o_.body = ctx_pe, ctx_rest, head_pe, head_rest, body
        return o_

    pairs = [make_pair(p) for p in range(4)]
    pairs[0].ctx_pe()
    pairs[0].ctx_rest()
    pairs[0].head_pe()
    pairs[0].head_rest()
    for p in range(4):
        pairs[p].body(pairs[p + 1] if p < 3 else None)

    if stop == 3:
        return finish_debug([(ygT.rearrange("p k t -> p (k t)"), 16384)])
    R_PAIR.reset()
    def _rp(off_kb, size_kb, dt=BF16):
        v = arena[:, (R_PAIR.start + off_kb * KB) // 2:(R_PAIR.start + (off_kb + size_kb) * KB) // 2]
        return v if dt == BF16 else v.bitcast(dt)

    wt_buf = [_rp(20, 7).rearrange("p (r n) -> p r n", n=128), _rp(0, 7).rearrange("p (r n) -> p r n", n=128)]
    wt_buf = [wt_buf[0], wt_buf[1]]
    sgab = [_rp(7 + 2 * i, 2, F32) for i in range(4)]
    wo_sb = _rp(28, 16).rearrange("p (k n) -> p k n", n=D)
    wo_stage = [_rp(44 + 4 * i, 4, F32) for i in range(2)]
    xres = [_rp(52 + 4 * i, 4, F32) for i in range(2)]
    R_PAIR.off = 60 * KB

    def mT_key(k, i):
        return "M_%d_%d" % (k, i)

    def load_tail(cb):
        extra = [S.fence()] if cb == 1 else (["kfb%d" % i for i in range(NT)] if cb == 0 else [])
        S.dma("pool", "wt%d" % (cb % 2), lambda e, cb=cb: e.dma_start(out=wt_buf[cb % 2], in_=wtail_d[cb]),
              writes=["wt%d" % (cb % 2)] + extra)

    def prep_wo(k):
        st = wo_stage[k % 2]
        sk = "wost%d" % (k % 2)
        S.dma("sp", sk, lambda e, st=st, k=k: e.dma_start(out=st, in_=wo_d[k]), writes=[sk] + ([S.fence()] if k < 2 else []))
        S.dve(lambda e, st=st, k=k: e.tensor_tensor(out=wo_sb[:, k, :], in0=st, in1=gm_bc, op=ALU.mult),
              reads=[sk, "gm_bc"], writes=["wo%d" % k])

    load_tail(0)
    ctr = 0
    for cb in range(8):
        wt = wt_buf[cb % 2]
        wtk = "wt%d" % (cb % 2)
        if cb + 1 < 8:
            load_tail(cb + 1)
        prep_wo(cb)
        for tb in range(4):
            ts_ = slice(tb * 512, (tb + 1) * 512)
            par = ctr % 2
            ctr += 1
            pga, pgb, pba, pbb = par, 2 + par, 4 + par, 6 + par
            hreads = lambda k: [hT_key(k, 4 * tb + q) for q in range(4)]
            for k in range(8):
                S.pe(lambda e, k=k, ts_=ts_, pga=pga, wt=wt: e.matmul(bank(pga), lhsT=wt[:, k, :], rhs=hT[:, k, ts_],
                                                                     start=(k == 0), stop=(k == 7)),
                     reads=[wtk] + hreads(k), writes=[PK(pga)])
            for g in range(4):
                S.pe(lambda e, g=g, ts_=ts_, pba=pba, wt=wt: e.matmul(bank(pba), lhsT=wt[:, 24 + g, :], rhs=ypT[:, g, ts_],
                                                                     start=(g == 0), stop=(g == 3)),
                     reads=[wtk, "ypT"], writes=[PK(pba)])
            for k in range(8):
                S.pe(lambda e, k=k, ts_=ts_, pgb=pgb, wt=wt: e.matmul(bank(pgb), lhsT=wt[:, 8 + k, :], rhs=hT[:, k, ts_],
                                                                     start=(k == 0), stop=(k == 7)),
                     reads=[wtk] + hreads(k), writes=[PK(pgb)])
            for k in range(8):
                S.pe(lambda e, k=k, ts_=ts_, pbb=pbb, wt=wt: e.matmul(bank(pbb), lhsT=wt[:, 16 + k, :], rhs=ygT[:, k, ts_],
                                                                     start=(k == 0), stop=(k == 7)),
                     reads=[wtk] + [ygT_key(k, 4 * tb + q) for q in range(4)], writes=[PK(pbb)])
            sa = sgab[par * 2]
            sb_ = sgab[par * 2 + 1]
            sak = "sga%d" % par
            sbk = "sgb%d" % par
            S.act(lambda e, sa=sa, pga=pga: e.activation(out=sa, in_=bank(pga), func=AF.Sigmoid), reads=[PK(pga)], writes=[sak])
            S.act(lambda e, sb_=sb_, pgb=pgb: e.activation(out=sb_, in_=bank(pgb), func=AF.Sigmoid), reads=[PK(pgb)], writes=[sbk])
            S.dve(lambda e, sa=sa, pba=pba: e.tensor_tensor(out=sa, in0=sa, in1=bank(pba), op=ALU.mult),
                  reads=[sak, PK(pba)], writes=[sak])
            S.dve(lambda e, sb_=sb_, pbb=pbb: e.tensor_tensor(out=sb_, in0=sb_, in1=bank(pbb), op=ALU.mult),
                  reads=[sbk, PK(pbb)], writes=[sbk])
            S.dve(lambda e, sa=sa, sb_=sb_, cb=cb, ts_=ts_: e.tensor_tensor(out=mT[:, cb, ts_], in0=sa, in1=sb_, op=ALU.add),
                  reads=[sak, sbk], writes=[mT_key(cb, 4 * tb + q) for q in range(4)])

    if stop == 4:
        return finish_debug([(mT.rearrange("p k t -> p (k t)"), 16384)])
    w13_buf = [arena[:, (R_PAIR.start + i * 4 * KB) // 2:(R_PAIR.start + (i + 1) * 4 * KB) // 2].rearrange(
        "p (a k n) -> p a k n", a=2, n=128) for i in range(2)]

    def load_w13(c):
        extra = [S.fence()] if c < 2 else []
        S.dma("pool", "w13_%d" % (c % 2), lambda e, c=c: e.dma_start(out=w13_buf[c % 2], in_=w13_d[c]),
              writes=["w13_%d" % (c % 2)] + extra)

    if stop is None:
        load_w13(0)
        load_w13(1)
    def x1_key(i):
        return "X1_%d" % i

    h2T = mT

    def h2_key(k, i):
        return mT_key(k, i)

    junk2 = _rp(48, 2)
    a2 = {}

    def a2_square(i):
        n = nctr[0]
        nctr[0] += 1
        c = dict(n=n, i=i, ss=stat[:, 32 + (n % 4) * 2:33 + (n % 4) * 2], rs=stat[:, 33 + (n % 4) * 2:34 + (n % 4) * 2],
                 sk="nstat%d" % (n % 4), xn=xnb2[n % 3], xk="xn3_%d" % (n % 3))
        S.act(lambda e: e.activation(out=junk2, in_=x1[:, i, :], func=AF.Square, accum_out=c["ss"]),
              reads=[x1_key(i)], writes=["junk", c["sk"]])
        return c

    def a2_ts(c):
        S.dve(lambda e: e.tensor_scalar(out=c["rs"], in0=c["ss"], scalar1=1.0 / D, scalar2=EPS, op0=ALU.mult, op1=ALU.add),
              reads=[c["sk"]], writes=[c["sk"] + "r"])
        S.act(lambda e: e.activation(out=c["rs"], in_=c["rs"], func=AF.Sqrt), reads=[c["sk"] + "r"], writes=[c["sk"] + "r"])

    def a2_fin(c):
        S.dve(lambda e: e.reciprocal(out=c["rs"], in_=c["rs"]), reads=[c["sk"] + "r"], writes=[c["sk"] + "r"])
        S.dve(lambda e: e.tensor_scalar(out=c["xn"], in0=x1[:, c["i"], :], scalar1=c["rs"], scalar2=None, op0=ALU.mult),
              reads=[x1_key(c["i"]), c["sk"] + "r"], writes=[c["xk"]])

    xnb2 = [_rp(50, 2), _rp(60, 2), arena[:, (R_YP.start + 14 * KB) // 2:(R_YP.start + 16 * KB) // 2]]
    pend = None
    for i in range(NT):
        pa = (i % 2) * 2
        xr = xres[i % 2]
        xrk = "xres%d" % (i % 2)
        S.dma("sp", xrk, lambda e, xr=xr, i=i: e.dma_start(out=xr, in_=x_d[i * 128:(i + 1) * 128, :]),
              writes=[xrk])
        for hf in range(2):
            for k in range(8):
                S.pe(lambda e, k=k, hf=hf, i=i, pa=pa: e.matmul(bank(pa + hf), lhsT=mT[:, k, i * 128:(i + 1) * 128],
                                                                rhs=wo_sb[:, k, hf * 512:(hf + 1) * 512],
                                                                start=(k == 0), stop=(k == 7)),
                     reads=[mT_key(k, i), "wo%d" % k], writes=[PK(pa + hf)])
        S.dve(lambda e, i=i, xr=xr, pa=pa: e.tensor_tensor(out=x1[:, i, :], in0=ps[:, pa * 512:(pa + 2) * 512], in1=xr, op=ALU.add),
              reads=[PK(pa), PK(pa + 1), xrk],
              writes=[x1_key(i)] + [(hT_key(i, t) if i < 8 else ygT_key(i - 8, t)) for t in range(NT)])
        if stop != 5:
            a2[i] = a2_square(i)
            if i >= 1:
                a2_ts(a2[i - 1])
            if i >= 3:
                norm_tr((a2[i - 3]["n"], a2[i - 3]["xn"], a2[i - 3]["xk"]), A_F, B_F, h2T, h2_key, i - 3, 4)
            if i >= 1:
                a2_fin(a2[i - 1])
    if stop != 5:
        a2_ts(a2[NT - 1])
        norm_tr((a2[NT - 3]["n"], a2[NT - 3]["xn"], a2[NT - 3]["xk"]), A_F, B_F, h2T, h2_key, NT - 3, 4)
        a2_fin(a2[NT - 1])
        for i_ in (NT - 2, NT - 1):
            norm_tr((a2[i_]["n"], a2[i_]["xn"], a2[i_]["xk"]), A_F, B_F, h2T, h2_key, i_, 4)

    if stop == 5:
        return finish_debug([(x1.rearrange("p i f -> p (i f)"), 16384)])
    R_PAIR.reset()
    R_PAIR.carve(8 * KB)
    actT = [R_PAIR.carve(20 * KB).rearrange("p (c t) -> p c t", t=L) for _ in range(2)]
    R_YP.reset()
    w2_sb = R_YP.carve(10 * KB).rearrange("p (c n) -> p c n", n=D)
    sa_sb = [R_YP.carve(2 * KB, F32) for _ in range(2)]
    w2_stage = [qdec, mask]

    S.dma("sp", "c6", lambda e: e.dma_start(out=nf_bc, in_=rows_d[0:1, RO_NF:RO_NF + D].to_broadcast([128, D])),
          writes=["nf_bc", S.fence()] + hcT_keys)
    fctr = 0
    for gi, (c0, ng) in enumerate(FF_GROUPS):
        at = actT[gi % 2]
        atk = "actT%d" % (gi % 2)
        for cc in range(ng):
            c = c0 + cc
            wb = w13_buf[c % 2]
            wbk = "w13_%d" % (c % 2)
            if 2 <= c + 1 < NFF:
                load_w13(c + 1)
            st = w2_stage[c % 2]
            stk = "w2st%d" % (c % 2)
            S.dma("sp", stk, lambda e, st=st, c=c: e.dma_start(out=st, in_=w2_d[c]), writes=[stk, "qdec" if c % 2 == 0 else "mask"])
            S.dve(lambda e, st=st, cc=cc: e.tensor_tensor(out=w2_sb[:, cc, :], in0=st, in1=gf_bc, op=ALU.mult),
                  reads=[stk, "gf_bc"], writes=["w2sb%d" % cc])
            for tb in range(4):
                ts_ = slice(tb * 512, (tb + 1) * 512)
                par = fctr % 2
                fctr += 1
                pa_, pb_ = par, 2 + par
                for a in range(2):
                    pbk = pa_ if a == 0 else pb_
                    for k in range(8):
                        S.pe(lambda e, a=a, k=k, ts_=ts_, pbk=pbk, wb=wb: e.matmul(bank(pbk), lhsT=wb[:, a, k, :], rhs=h2T[:, k, ts_],
                                                                                  start=(k == 0), stop=(k == 7)),
                             reads=[wbk] + [h2_key(k, 4 * tb + q) for q in range(4)], writes=[PK(pbk)])
                sa = sa_sb[par]
                sak = "ffsa%d" % par
                S.act(lambda e, sa=sa, pa_=pa_: e.activation(out=sa, in_=bank(pa_), func=AF.Silu), reads=[PK(pa_)], writes=[sak])
                S.dve(lambda e, sa=sa, pb_=pb_, cc=cc, ts_=ts_, at=at: e.tensor_tensor(out=at[:, cc, ts_], in0=sa, in1=bank(pb_), op=ALU.mult),
                      reads=[sak, PK(pb_)], writes=[atk + "_%d_%d" % (cc, tb)])
        last = (gi == len(FF_GROUPS) - 1)
        for i in range(NT):
            pa = 4 + (i % 2) * 2
            for hf in range(2):
                for cc in range(ng):
                    S.pe(lambda e, cc=cc, hf=hf, i=i, pa=pa, at=at: e.matmul(bank(pa + hf), lhsT=at[:, cc, i * 128:(i + 1) * 128],
                                                                             rhs=w2_sb[:, cc, hf * 512:(hf + 1) * 512],
                                                                             start=(cc == 0), stop=(cc == ng - 1)),
                         reads=[atk + "_%d_%d" % (cc, i // 4), "w2sb%d" % cc], writes=[PK(pa + hf)])
            S.dve(lambda e, i=i, pa=pa: e.tensor_tensor(out=x1[:, i, :], in0=ps[:, pa * 512:(pa + 2) * 512], in1=x1[:, i, :], op=ALU.add),
                  reads=[PK(pa), PK(pa + 1), x1_key(i)], writes=[x1_key(i)])
            if last:
                def fin_a(i):
                    ss = stat[:, 32 + (i % 4) * 2:33 + (i % 4) * 2]
                    S.act(lambda e: e.activation(out=junk2, in_=x1[:, i, :], func=AF.Square, accum_out=ss),
                          reads=[x1_key(i)], writes=["junk2", "fstat%d" % (i % 4)])

                def fin_b(i):
                    ss = stat[:, 32 + (i % 4) * 2:33 + (i % 4) * 2]
                    rs = stat[:, 33 + (i % 4) * 2:34 + (i % 4) * 2]
                    sk = "fstat%d" % (i % 4)
                    S.dve(lambda e: e.tensor_scalar(out=rs, in0=ss, scalar1=1.0 / D, scalar2=EPS, op0=ALU.mult, op1=ALU.add),
                          reads=[sk], writes=[sk + "r"])
                    S.act(lambda e: e.activation(out=rs, in_=rs, func=AF.Sqrt), reads=[sk + "r"], writes=[sk + "r"])

                def fin_c(i):
                    rs = stat[:, 33 + (i % 4) * 2:34 + (i % 4) * 2]
                    sk = "fstat%d" % (i % 4)
                    S.dve(lambda e: e.reciprocal(out=rs, in_=rs), reads=[sk + "r"], writes=[sk + "r"])
                    S.dve(lambda e: e.scalar_tensor_tensor(out=x1[:, i, :], in0=x1[:, i, :], scalar=rs, in1=nf_bc,
                                                           op0=ALU.mult, op1=ALU.mult),
                          reads=[x1_key(i), sk + "r", "nf_bc"], writes=[x1_key(i)])
                    S.dma("sp", "out", lambda e: e.dma_start(out=out_d[i * 128:(i + 1) * 128, :], in_=x1[:, i, :]),
                          reads=[x1_key(i)])

                fin_a(i)
                if i >= 1:
                    fin_b(i - 1)
                if i >= 2:
                    fin_c(i - 2)
                if i == NT - 1:
                    fin_b(i)
                    fin_c(i - 1)
                    fin_c(i)

    S.emit(nc, final_dma_keys=["out"])
    return nc


_Q_OFF, _K_OFF, _V_OFF, _G_OFF, _GA_OFF, _GB_OFF = 512, 1024, 1536, 2560, 3584, 4608


def _kchunk(w):
    kk = w.shape[0] // 128
    return np.ascontiguousarray(w.reshape(kk, 128, w.shape[1]).transpose(1, 0, 2))


def _prep(x, c, ctx, c_ctx, w_ada, b_ada, norm_mix, norm_ffn, w_in, w_pool, pool_scale,
          ret_decay_f, ret_decay_b, ret_gn_w, w_pa, w_rb, w_o, w_ff1, w_ff3, w_ff2, norm_final):
    f32 = np.float32
    x = np.asarray(x, f32)
    B = x.shape[0]
    pblk, plan = _get_pool()

    w_in0 = np.asarray(w_in, f32)[0]
    wada = _kchunk(np.asarray(w_ada, f32)[0])
    wu = _kchunk(w_in0[:, 0:512])
    wpairs = []
    for p in range(4):
        cols = np.concatenate([
            w_in0[:, _Q_OFF + p * 128:_Q_OFF + (p + 1) * 128],
            w_in0[:, _K_OFF + p * 128:_K_OFF + (p + 1) * 128],
            w_in0[:, _V_OFF + p * 256:_V_OFF + (p + 1) * 256],
            w_in0[:, _G_OFF + p * 256:_G_OFF + (p + 1) * 256]], axis=1)
        wpairs.append(_kchunk(cols))
    wpairs = np.stack(wpairs, 0)
    w_rb0 = np.asarray(w_rb, f32)[0]
    w_pa0 = np.asarray(w_pa, f32)[0]
    wtail = []
    for cb in range(8):
        cs = slice(cb * 128, (cb + 1) * 128)
        ga = _kchunk(w_in0[:, _GA_OFF:_GA_OFF + D][:, cs])
        gb = _kchunk(w_in0[:, _GB_OFF:_GB_OFF + D][:, cs])
        rb = _kchunk(w_rb0[:, cs])
        pa = _kchunk(w_pa0[:, cs])
        wtail.append(np.concatenate([ga, gb, rb, pa], axis=1))
    wtail = np.ascontiguousarray(np.stack(wtail, 0))
    wpool = np.ascontiguousarray(np.asarray(w_pool, f32)[0].transpose(1, 0, 2))
    wo = np.ascontiguousarray(np.asarray(w_o, f32)[0].reshape(8, 128, D))
    w1 = np.asarray(w_ff1, f32)[0]
    w3 = np.asarray(w_ff3, f32)[0]
    w13 = np.stack([np.stack([_kchunk(w1[:, cc * 128:(cc + 1) * 128]), _kchunk(w3[:, cc * 128:(cc + 1) * 128])], axis=1)
                    for cc in range(NFF)], 0)
    w13 = np.ascontiguousarray(w13)
    w2 = np.ascontiguousarray(np.asarray(w_ff2, f32)[0].reshape(NFF, 128, D))
    consts = _host_consts()
    ident = np.eye(128, dtype=np.float32).astype(ml_dtypes.bfloat16)

    def pp(v, k):
        return np.asarray(v, f32).reshape(k, 128).T

    b_ada0 = np.asarray(b_ada, f32)[0]
    rows = np.zeros((1, NROW), f32)
    rows[0, RO_BGM:RO_BGM + D] = b_ada0[2 * D:3 * D]
    rows[0, RO_BGF:RO_BGF + D] = b_ada0[5 * D:6 * D]
    rows[0, RO_NF:RO_NF + D] = np.asarray(norm_final, f32)
    rows[0, RO_DEC:RO_DEC + 8] = np.asarray(ret_decay_f, f32)[0]
    rows[0, RO_DEC + 8:RO_DEC + 16] = np.asarray(ret_decay_b, f32)[0]

    in_maps = []
    for b in range(B):
        vecs = np.zeros((128, NVEC), f32)
        vecs[:, VO_C:VO_C + 16:2] = pp(np.asarray(c, f32)[b], 8)
        vecs[:, VO_C + 1:VO_C + 16:2] = pp(np.asarray(c_ctx, f32), 8)
        vecs[:, VO_NMIX:VO_NMIX + 8] = pp(np.asarray(norm_mix, f32)[0], 8)
        vecs[:, VO_NFFN:VO_NFFN + 8] = pp(np.asarray(norm_ffn, f32)[0], 8)
        vecs[:, VO_PSC:VO_PSC + 4] = pp(np.asarray(pool_scale, f32)[0], 4)
        vecs[:, VO_GNW:VO_GNW + 8] = pp(np.asarray(ret_gn_w, f32)[0], 8)
        vecs[:, VO_BADA:VO_BADA + 48] = pp(b_ada0, 48)
        in_maps.append({
            "x": np.ascontiguousarray(x[b]), "ctx": np.ascontiguousarray(np.asarray(ctx, f32)[b]),
            "vecs": vecs, "rows": rows, "consts": consts, "ident": ident,
            "wada": wada, "wu": wu, "wpairs": wpairs, "wtail": wtail, "wpool": wpool, "wo": wo,
            "w13": w13, "w2": w2, "pblk": pblk,
        })
    return in_maps


def kernel(**inputs):
    in_maps = _prep(**inputs)
    pblk, plan = _get_pool()
    nc = build_program(plan, pblk.shape[1])
    B = len(in_maps)
    res = run_bass_kernel_spmd(nc, in_maps, core_ids=list(range(B)))
    out = np.stack([np.asarray(res.results[b]["out"], np.float32) for b in range(B)], 0)
    return out
```
